# Optimizing a Trainium2 kernel written in Bass

```python
import math
import jax
import jax.numpy as jnp
from jax import lax
import numpy as np

D_MODEL = 1024
BATCH = 16
SEQ = 2048
DEPTH = 4

GRID_W = 64
CTX_LEN = 256
N_BRANCH = 4
BR_WIDTH = D_MODEL // 4
DA_HEADS = 4
DA_DV = BR_WIDTH // DA_HEADS
DA_DH = DA_DV // 2
NA_HEADS = 4
NA_DH = BR_WIDTH // NA_HEADS
NA_WIN_H = 8
NA_WIN_W = 16
HG_HEADS = 4
HG_DK = BR_WIDTH // HG_HEADS
HG_DV = BR_WIDTH // HG_HEADS
HG_CHUNK = 64
LB_FLOOR = 1e-20
FT_GROUPS = 4
FT_DG = BR_WIDTH // FT_GROUPS
SPLIT_IDX = [3 * BR_WIDTH, 6 * BR_WIDTH, 10 * BR_WIDTH, 11 * BR_WIDTH]
IN_WIDTH = 15 * BR_WIDTH
Q_BLOCK = 128
ROPE_THETA = 10000.0
EPS = 1e-6
NEG_INF = -1e30

kernel_name = 'hybrid_diffusion_parallel_mixers'


def rmsnorm(x, gain):
    xf = x.astype(jnp.float32)
    y = xf * lax.rsqrt(jnp.mean(xf * xf, axis=-1, keepdims=True) + EPS)
    return (y * gain.astype(jnp.float32)).astype(x.dtype)


def axial_rope(n, dim):
    t = jnp.arange(n)
    row = (t // GRID_W).astype(jnp.float32)
    col = (t % GRID_W).astype(jnp.float32)
    d_axis = dim // 2
    inv = ROPE_THETA ** (-jnp.arange(0, d_axis, 2, dtype=jnp.float32) / d_axis)
    ang = jnp.concatenate([row[:, None] * inv, col[:, None] * inv], axis=-1)
    return jnp.cos(ang), jnp.sin(ang)


def apply_rope(x, cos, sin):
    x1, x2 = jnp.split(x, 2, axis=-1)
    c = cos[None, :, None, :].astype(x.dtype)
    s = sin[None, :, None, :].astype(x.dtype)
    return jnp.concatenate([x1 * c - x2 * s, x1 * s + x2 * c], axis=-1)


def softmax_attention(q, k, v):
    s = jnp.einsum('bqhd,bkhd->bhqk', q, k, preferred_element_type=jnp.float32) * (q.shape[-1] ** -0.5)
    p = jax.nn.softmax(s, axis=-1).astype(v.dtype)
    return jnp.einsum('bhqk,bkhd->bqhd', p, v)


def diff_softmax_attention(q, k, v, lam):
    b, nq = q.shape[:2]
    nk = k.shape[1]
    nb = nq // Q_BLOCK
    scale = DA_DH ** -0.5
    qb = q.reshape(b, nb, Q_BLOCK, 2 * DA_HEADS, DA_DH).swapaxes(0, 1)

    def block(qi):
        s = jnp.einsum('bqhd,bkhd->bhqk', qi, k, preferred_element_type=jnp.float32) * scale
        p = jax.nn.softmax(s, axis=-1).reshape(b, DA_HEADS, 2, Q_BLOCK, nk)
        a = p[:, :, 0] - lam * p[:, :, 1]
        return jnp.einsum('bhqk,bkhe->bqhe', a.astype(v.dtype), v)

    o = lax.map(block, qb)
    return o.swapaxes(0, 1).reshape(b, nq, DA_HEADS, DA_DV)


def mixer_diff_attn(z, zc, qk_gain, lam_vec, subln_gain, lam_init, cos, sin, need_ctx):
    def project(t):
        b, n, _ = t.shape
        q, k, v = jnp.split(t, 3, axis=-1)
        q = rmsnorm(q.reshape(b, n, 2 * DA_HEADS, DA_DH), qk_gain[0])
        k = rmsnorm(k.reshape(b, n, 2 * DA_HEADS, DA_DH), qk_gain[1])
        return q, k, v.reshape(b, n, DA_HEADS, DA_DV)

    lv = lam_vec.astype(jnp.float32)
    lam = jnp.exp(jnp.sum(lv[0] * lv[1])) - jnp.exp(jnp.sum(lv[2] * lv[3])) + lam_init
    q, k, v = project(z)
    q = apply_rope(q, cos, sin)
    k = apply_rope(k, cos, sin)
    qc, kc, vc = project(zc)

    def finish(o):
        b, n = o.shape[:2]
        return (rmsnorm(o, subln_gain) * (1.0 - lam_init)).reshape(b, n, BR_WIDTH)

    y = finish(diff_softmax_attention(q, jnp.concatenate([k, kc], axis=1), jnp.concatenate([v, vc], axis=1), lam))
    yc = finish(diff_softmax_attention(qc, kc, vc, lam)) if need_ctx else None
    return y, yc


def neighbourhood_attention(q, k, v, kc, vc, rpb):
    b, n, h, dh = q.shape
    rows = n // GRID_W
    wh = min(NA_WIN_H, rows)
    ww = NA_WIN_W
    scale = dh ** -0.5
    r = jnp.arange(rows)
    col = jnp.arange(GRID_W)
    r0 = jnp.clip(r - wh // 2, 0, rows - wh)
    key_rows = r0[:, None] + jnp.arange(wh)[None, :]
    c0 = jnp.clip(col - ww // 2, 0, GRID_W - ww)
    in_win = (col[None, :] >= c0[:, None]) & (col[None, :] < c0[:, None] + ww)
    dr = key_rows - r[:, None] + NA_WIN_H - 1
    dc = jnp.clip(col[None, :] - col[:, None], 1 - ww, ww - 1) + ww - 1
    bias = rpb.astype(jnp.float32)[:, dr[:, :, None, None], dc[None, None, :, :]]
    bias = jnp.where(in_win[None, None, None], bias, NEG_INF).transpose(1, 3, 0, 2, 4)
    qg = q.reshape(b, rows, GRID_W, h, dh)
    kg = k.reshape(b, rows, GRID_W, h, dh)[:, key_rows]
    vg = v.reshape(b, rows, GRID_W, h, dh)[:, key_rows]
    s_win = jnp.einsum('brqhd,brjkhd->brqhjk', qg, kg, preferred_element_type=jnp.float32) * scale + bias
    s_ctx = jnp.einsum('brqhd,blhd->brqhl', qg, kc, preferred_element_type=jnp.float32) * scale
    nw = wh * GRID_W
    s = jnp.concatenate([s_win.reshape(b, rows, GRID_W, h, nw), s_ctx], axis=-1)
    p = jax.nn.softmax(s, axis=-1).astype(v.dtype)
    p_win = p[..., :nw].reshape(b, rows, GRID_W, h, wh, GRID_W)
    o = jnp.einsum('brqhjk,brjkhd->brqhd', p_win, vg) + jnp.einsum('brqhl,blhd->brqhd', p[..., nw:], vc)
    return o.reshape(b, n, h, dh)


def mixer_neigh_attn(z, zc, qk_gain, rpb, need_ctx):
    def project(t):
        b, n, _ = t.shape
        q, k, v = [u.reshape(b, n, NA_HEADS, NA_DH) for u in jnp.split(t, 3, axis=-1)]
        return rmsnorm(q, qk_gain[0]), rmsnorm(k, qk_gain[1]), v

    b, n, _ = z.shape
    q, k, v = project(z)
    qc, kc, vc = project(zc)
    y = neighbourhood_attention(q, k, v, kc, vc, rpb).reshape(b, n, BR_WIDTH)
    yc = softmax_attention(qc, kc, vc).reshape(b, zc.shape[1], BR_WIDTH) if need_ctx else None
    return y, yc


def hgrn2_scan(q, v, log_f, s0, with_output=True):
    b, n, h, _ = q.shape
    nc = n // HG_CHUNK

    def chunks(t):
        return t.reshape(b, nc, HG_CHUNK, h, t.shape[-1]).transpose(1, 0, 3, 2, 4)

    causal = jnp.tril(jnp.ones((HG_CHUNK, HG_CHUNK), dtype=bool))[:, :, None]

    def step(S, inp):
        qi, vi, gi = inp
        bcum = jnp.cumsum(gi, axis=2)
        ki = -jnp.expm1(gi)
        b_last = bcum[:, :, -1:, :]
        S_new = jnp.exp(b_last)[:, :, 0, :, None] * S + jnp.einsum('bhsd,bhse->bhde', ki * jnp.exp(b_last - bcum), vi)
        if not with_output:
            return S_new, None
        rel = bcum[:, :, :, None, :] - bcum[:, :, None, :, :]
        decay = jnp.where(causal, jnp.exp(jnp.minimum(rel, 0.0)), 0.0)
        att = jnp.einsum('bhtd,bhsd,bhtsd->bhts', qi, ki, decay)
        o = jnp.einsum('bhts,bhse->bhte', att, vi) + jnp.einsum('bhtd,bhde->bhte', qi * jnp.exp(bcum), S)
        return S_new, o

    s_fin, o = lax.scan(step, s0, (chunks(q), chunks(v), chunks(log_f)))
    if not with_output:
        return None, s_fin
    return o.transpose(1, 0, 3, 2, 4).reshape(b, n, h, v.shape[-1]), s_fin


def mixer_hgrn2(z, zc, lb, onorm_gain, need_ctx):
    log_lb = jnp.log(jnp.maximum(lb, LB_FLOOR))
    log_1mlb = jnp.log1p(-lb)

    def prep(t):
        b, n, _ = t.shape
        q, ff, fb, i = jnp.split(t.astype(jnp.float32), 4, axis=-1)

        def log_forget(f, d):
            return jnp.logaddexp(log_lb[d], log_1mlb[d] + jax.nn.log_sigmoid(f)).reshape(b, n, HG_HEADS, HG_DK)

        return (jax.nn.silu(q).reshape(b, n, HG_HEADS, HG_DK), log_forget(ff, 0), log_forget(fb, 1),
                i.reshape(b, n, HG_HEADS, HG_DV))

    def flip(t):
        return jnp.flip(t, axis=1)

    b, n, _ = z.shape
    q, g_f, g_b, i = prep(z)
    qc, gc_f, gc_b, ic = prep(zc)
    s0 = jnp.zeros((b, HG_HEADS, HG_DK, HG_DV), jnp.float32)
    oc_f, sc_f = hgrn2_scan(qc, ic, gc_f, s0, need_ctx)
    oc_b, sc_b = hgrn2_scan(flip(qc), flip(ic), flip(gc_b), s0, need_ctx)
    o_f, _ = hgrn2_scan(q, i, g_f, sc_f)
    o_b, _ = hgrn2_scan(flip(q), flip(i), flip(g_b), sc_b)

    def finish(o, m):
        return rmsnorm(o, onorm_gain).reshape(b, m, BR_WIDTH).astype(z.dtype)

    y = finish(o_f + flip(o_b), n)
    yc = finish(oc_f + flip(oc_b), zc.shape[1]) if need_ctx else None
    return y, yc


def fourier_mix(u):
    b, n, _ = u.shape
    ug = u.astype(jnp.float32).reshape(b, n, FT_GROUPS, FT_DG)
    return jnp.fft.fft2(ug, axes=(1, 3), norm='ortho').real.reshape(b, n, BR_WIDTH).astype(u.dtype)


def merge_branches(h, ys, g, w_up_l, w_merge_l, w_out_l):
    gs = jnp.split(g, N_BRANCH, axis=-1)
    acc = None
    for i in range(N_BRANCH):
        branch = (ys[i] * jax.nn.silu(gs[i])) @ w_up_l[i]
        term = jax.nn.sigmoid(h @ w_merge_l[i]) * branch
        acc = term if acc is None else acc + term
    return acc @ w_out_l


def setup_inputs(seed: int = 0) -> dict:
    key = jax.random.key(seed)
    ks = jax.random.split(key, 18)

    def nrm(k, shape, s):
        return jax.random.normal(k, shape, jnp.float32) * s

    return {
        'x': nrm(ks[0], (BATCH, SEQ, D_MODEL), 1.0),
        'c': nrm(ks[1], (BATCH, D_MODEL), 1.0),
        'ctx': nrm(ks[2], (BATCH, CTX_LEN, D_MODEL), 1.0),
        'c_ctx': nrm(ks[3], (D_MODEL,), 1.0),
        'norm_gain': 1.0 + nrm(ks[4], (DEPTH, D_MODEL), 0.02),
        'w_mod': nrm(ks[5], (DEPTH, D_MODEL, 3 * D_MODEL), 0.5 * D_MODEL ** -0.5),
        'b_mod': nrm(ks[6], (DEPTH, 3 * D_MODEL), 0.01),
        'w_in': nrm(ks[7], (DEPTH, D_MODEL, IN_WIDTH), D_MODEL ** -0.5),
        'da_qk_gain': 1.0 + nrm(ks[8], (DEPTH, 2, DA_DH), 0.02),
        'da_lambda': nrm(ks[9], (DEPTH, 4, DA_DH), 0.1),
        'da_subln_gain': 1.0 + nrm(ks[10], (DEPTH, DA_DV), 0.02),
        'na_qk_gain': 1.0 + nrm(ks[11], (DEPTH, 2, NA_DH), 0.02),
        'na_rpb': nrm(ks[12], (DEPTH, NA_HEADS, 2 * NA_WIN_H - 1, 2 * NA_WIN_W - 1), 0.2),
        'hg_lb_logits': nrm(ks[13], (2, DEPTH, BR_WIDTH), 0.5),
        'hg_norm_gain': 1.0 + nrm(ks[14], (DEPTH, HG_DV), 0.02),
        'w_up': nrm(ks[15], (DEPTH, N_BRANCH, BR_WIDTH, D_MODEL), BR_WIDTH ** -0.5),
        'w_merge': nrm(ks[16], (DEPTH, N_BRANCH, D_MODEL, D_MODEL), D_MODEL ** -0.5),
        'w_out': nrm(ks[17], (DEPTH, D_MODEL, D_MODEL), D_MODEL ** -0.5),
    }


def reference(x, c, ctx, c_ctx, norm_gain, w_mod, b_mod, w_in, da_qk_gain, da_lambda, da_subln_gain,
              na_qk_gain, na_rpb, hg_lb_logits, hg_norm_gain, w_up, w_merge, w_out):
    n = x.shape[1]
    cos, sin = axial_rope(n, DA_DH)
    p_lb = jax.nn.softmax(hg_lb_logits.astype(jnp.float32), axis=1)
    lower_bounds = jnp.cumsum(p_lb, axis=1) - p_lb[:, :1]
    silu_c = jax.nn.silu(c)
    silu_cc = jax.nn.silu(c_ctx)
    xc = ctx
    for l in range(DEPTH):
        need_ctx = l < DEPTH - 1
        lam_init = 0.8 - 0.6 * math.exp(-0.3 * l)
        shift, scale, gate = jnp.split((silu_c @ w_mod[l] + b_mod[l])[:, None, :], 3, axis=-1)
        shift_c, scale_c, gate_c = jnp.split(silu_cc @ w_mod[l] + b_mod[l], 3)
        h = rmsnorm(x, norm_gain[l]) * (1.0 + scale) + shift
        hc = rmsnorm(xc, norm_gain[l]) * (1.0 + scale_c) + shift_c
        za, zb, zh, zf, zg = jnp.split(h @ w_in[l], SPLIT_IDX, axis=-1)
        zc = hc @ (w_in[l] if need_ctx else w_in[l][:, :SPLIT_IDX[2]])
        zc_parts = jnp.split(zc, SPLIT_IDX if need_ctx else SPLIT_IDX[:2], axis=-1)
        ya, yca = mixer_diff_attn(za, zc_parts[0], da_qk_gain[l], da_lambda[l], da_subln_gain[l], lam_init, cos, sin, need_ctx)
        yb, ycb = mixer_neigh_attn(zb, zc_parts[1], na_qk_gain[l], na_rpb[l], need_ctx)
        yh, ych = mixer_hgrn2(zh, zc_parts[2], lower_bounds[:, l], hg_norm_gain[l], need_ctx)
        yf = fourier_mix(zf)
        x = x + gate * merge_branches(h, (ya, yb, yh, yf), zg, w_up[l], w_merge[l], w_out[l])
        if need_ctx:
            ycf = fourier_mix(zc_parts[3])
            xc = xc + gate_c * merge_branches(hc, (yca, ycb, ych, ycf), zc_parts[4], w_up[l], w_merge[l], w_out[l])
    return x
```

```python
import math
from contextlib import ExitStack

import numpy as np
import ml_dtypes

import concourse.bass as bass
import concourse.mybir as mybir
from concourse.bass_utils import run_bass_kernel_spmd

F32 = mybir.dt.float32
BF16 = mybir.dt.bfloat16
ALU = mybir.AluOpType
AF = mybir.ActivationFunctionType
AX = mybir.AxisListType

NL = 4
D = 1024
NCORE = 8
LAT = 2048
CTX = 256
T = LAT + CTX
NT = T // 128
NCH = T // 64
EPS = 1e-6
BLKS = [(0, 512), (512, 512), (1024, 512), (1536, 512), (2048, 256)]
NEG = -30000.0


class Buf:
    def __init__(self, t=None, name=""):
        self.t = t
        self.name = name
        self.last_w = None
        self.readers = {}

    def __getitem__(self, k):
        return self.t[k]


class K:
    ENG = ("pe", "act", "dve", "pool", "sp")

    def __init__(self, nc, stack, n_dma_sems=48):
        self.nc = nc
        self.stack = stack
        self.cur = stack
        self.streams = {e: [] for e in self.ENG}
        self.sem = {}
        self.cnt = {}
        for e in self.ENG:
            self.sem[e] = stack.enter_context(nc.semaphore("s_" + e))
            self.cnt[e] = 0
        self.ndma = n_dma_sems
        for i in range(n_dma_sems):
            self.sem[("d", i)] = stack.enter_context(nc.semaphore("d%d" % i))
            self.cnt[("d", i)] = 0
        self.dma_rr = 0
        self.known = {e: {} for e in self.ENG}
        self.nbuf = 0

    def sb(self, shape, dtype, name=None):
        self.nbuf += 1
        name = "%s_%d" % (name or "sb", self.nbuf)
        t = self.cur.enter_context(self.nc.sbuf_tensor(name, list(shape), dtype))
        return Buf(t, name)

    def ps(self, shape, dtype, name=None):
        self.nbuf += 1
        name = "%s_%d" % (name or "ps", self.nbuf)
        t = self.cur.enter_context(self.nc.psum_tensor(name, list(shape), dtype))
        return Buf(t, name)

    def dram(self, name, shape, dtype, kind="Internal"):
        t = self.nc.dram_tensor(name, list(shape), dtype, kind=kind)
        return Buf(t.ap(), name)

    def _need(self, eng, reads, writes):
        need = {}

        def add(dep):
            if dep is None:
                return
            k, v = dep
            if need.get(k, 0) < v:
                need[k] = v

        for b in reads:
            add(b.last_w)
        for b in writes:
            add(b.last_w)
            for k, v in b.readers.items():
                add((k, v))
        waits = []
        kn = self.known[eng]
        for k, v in need.items():
            if k == "pe" and eng == "pe":
                continue
            if kn.get(k, 0) >= v:
                continue
            kn[k] = v
            waits.append((k, v))
        return waits

    def op(self, eng, reads, writes, fn, inc=True):
        waits = self._need(eng, reads, writes)
        if inc:
            self.cnt[eng] += 1
            v = self.cnt[eng]
            self.streams[eng].append((waits, fn, (eng, 1)))
        else:
            v = self.cnt[eng] + 1
            self.streams[eng].append((waits, fn, None))
        for b in reads:
            if b.readers.get(eng, 0) < v:
                b.readers[eng] = v
        for b in writes:
            b.last_w = (eng, v)
            b.readers = {}
        return v

    def dma(self, q, reads, writes, fn):
        i = self.dma_rr
        self.dma_rr = (self.dma_rr + 1) % self.ndma
        key = ("d", i)
        waits = self._need(q, reads, writes)
        prev = self.cnt[key]
        if prev > 0 and self.known[q].get(key, 0) < prev:
            self.known[q][key] = prev
            waits.append((key, prev))
        self.cnt[key] += 16
        v = self.cnt[key]
        self.streams[q].append((waits, fn, (key, 16)))
        for b in reads:
            b.readers[key] = v
        for b in writes:
            b.last_w = (key, v)
            b.readers = {}
        return key, v

    def barrier(self):
        for e in self.ENG:
            waits = []
            for k, c in self.cnt.items():
                if c > 0 and (k != e or e in ("act", "dve", "pool")) and self.known[e].get(k, 0) < c:
                    self.known[e][k] = c
                    waits.append((k, c))
            if waits:
                self.streams[e].append((waits, None, None))

    def emit(self):
        nc = self.nc
        waits = [(k, c) for k, c in self.cnt.items() if c > 0 and k != "sp"]
        self.streams["sp"].append((waits, None, None))
        with nc.Block() as block:
            def mk(ename):
                def body(engine):
                    for ws, fn, inc in self.streams[ename]:
                        for kk, v in ws:
                            engine.wait_ge(self.sem[kk], v)
                        if fn is not None:
                            ins = fn(engine)
                            if inc is not None:
                                ins.then_inc(self.sem[inc[0]], inc[1])
                return body
            block.tensor(mk("pe"))
            block.scalar(mk("act"))
            block.vector(mk("dve"))
            block.gpsimd(mk("pool"))
            block.sync(mk("sp"))


def build(nlayers=NL, dbg=False):
    nc = bass.Bass("TRN2", target_bir_lowering=False)
    st = ExitStack()
    with st:
        k = K(nc, st)

        def MM(out, lhsT, rhs, start, stop, reads, writes, tp=None):
            if tp is None:
                k.op("pe", reads, writes, lambda e: e.matmul(out, lhsT=lhsT, rhs=rhs, start=start, stop=stop), inc=bool(stop))
            else:
                k.op("pe", reads, writes, lambda e: e.matmul(out, lhsT=lhsT, rhs=rhs, start=start, stop=stop, tile_position=tp), inc=bool(stop))

        def RSQ(buf, ap, npart, n, inv_n):
            TS(ap, ap, inv_n, ALU.mult, [buf], [buf], s2=EPS, op1=ALU.add, eng="pool")
            TT(ap, ap, mhalf[0:npart, 0:n], ALU.pow, [buf, mhalf], [buf], eng="pool")

        def TR(out, in_, idn, reads, writes):
            k.op("pe", reads, writes, lambda e: e.transpose(out=out, in_=in_, identity=idn))

        def ACT(out, in_, func, reads, writes, scale=None, bias=None, accum=None):
            kw = {}
            if scale is not None:
                kw["scale"] = scale
            if bias is not None:
                kw["bias"] = bias
            if accum is not None:
                kw["accum_out"] = accum
            k.op("act", reads, writes, lambda e: e.activation(out=out, in_=in_, func=func, **kw))

        def TT(out, in0, in1, op, reads, writes, eng="dve"):
            k.op(eng, reads, writes, lambda e: e.tensor_tensor(out=out, in0=in0, in1=in1, op=op))

        def TS(out, in0, s1, op0, reads, writes, s2=None, op1=None, eng="dve"):
            if op1 is None:
                k.op(eng, reads, writes, lambda e: e.tensor_scalar(out=out, in0=in0, scalar1=s1, scalar2=None, op0=op0))
            else:
                k.op(eng, reads, writes, lambda e: e.tensor_scalar(out=out, in0=in0, scalar1=s1, scalar2=s2, op0=op0, op1=op1))

        def STT(out, in0, scalar, in1, op0, op1, reads, writes):
            k.op("dve", reads, writes, lambda e: e.scalar_tensor_tensor(out=out, in0=in0, scalar=scalar, in1=in1, op0=op0, op1=op1))

        def CP(out, in_, reads, writes, eng="dve"):
            if eng == "act":
                k.op("act", reads, writes, lambda e: e.activation(out=out, in_=in_, func=AF.Copy))
            else:
                k.op(eng, reads, writes, lambda e: e.tensor_copy(out=out, in_=in_))

        def RED(out, in_, reads, writes):
            k.op("dve", reads, writes, lambda e: e.tensor_reduce(out=out, in_=in_, axis=AX.X, op=ALU.add))

        def RECIP(out, in_, reads, writes):
            k.op("dve", reads, writes, lambda e: e.reciprocal(out=out, in_=in_))

        def MSET(ap, val, writes, eng="pool"):
            k.op(eng, [], writes, lambda e: e.memset(ap, val))

        def DMA(q, out, in_, reads, writes):
            k.dma(q, reads, writes, lambda e: e.dma_start(out=out, in_=in_))

        EI = "ExternalInput"
        x_d = k.dram("x", [2, LAT, D], F32, EI)
        ctx_d = k.dram("ctx", [2, CTX, D], F32, EI)
        cv_d = k.dram("cv", [128, 8, 4], F32, EI)
        wmod_d = k.dram("w_mod", [NL, D, 3 * D], F32, EI)
        bmod_d = k.dram("b_modT", [128, NL, 24], F32, EI)
        gain_d = k.dram("gainT", [128, NL, 8], F32, EI)
        win_d = k.dram("w_in", [NL, D, 3840], F32, EI)
        wup_d = k.dram("w_up", [NL, 4, 256, D], F32, EI)
        wmg_d = k.dram("w_merge", [NL, 4, D, D], F32, EI)
        wout_d = k.dram("w_out", [NL, D, D], F32, EI)
        sv_d = k.dram("smallv", [128, NL, 448], F32, EI)
        lb_d = k.dram("lbT", [128, 2, 2, 4], F32, EI)
        nb_d = k.dram("nbias", [NL, 100, 128, 128], F32, EI)
        rope_d = k.dram("rope", [128, 16, 32], F32, EI)
        cs64_d = k.dram("cs64", [128, 256], BF16, EI)
        dftl_d = k.dram("dftL", [128, 16, 2, LAT], BF16, EI)
        dftc_d = k.dram("dftC", [128, 2, 2, CTX], BF16, EI)
        idb_d = k.dram("identb", [128, 128], BF16, EI)
        idf_d = k.dram("identf", [128, 128], F32, EI)
        cm_d = k.dram("cmask", [64, 256], F32, EI)
        out_d = k.dram("out", [2, LAT, D], F32, "ExternalOutput")
        xs_d = k.dram("xs", [2, T, D], F32, "ExternalOutput" if dbg else "Internal")
        ybr_d = k.dram("ybr", [2, 4, 2, 128, T], BF16, "ExternalOutput" if dbg else "Internal")
        xs_tok = [[Buf() for _ in range(NT)] for _ in range(2)]
        ybr_tok = [[Buf() for _ in range(4)] for _ in range(2)]
        cin = Buf()

        identb = k.sb([128, 128], BF16, "identb")
        identf = k.sb([128, 128], F32, "identf")
        onesf = k.sb([128, 128], F32, "onesf")
        cmask = k.sb([64, 256], F32, "cmask")
        rope = k.sb([128, 16, 32], F32, "rope")
        cs64 = k.sb([128, 256], BF16, "cs64")
        dftc = k.sb([128, 2, 2, CTX], BF16, "dftc")
        sv = k.sb([128, NL, 448], F32, "sv")
        bmod = k.sb([128, NL, 24], F32, "bmod")
        gainT = k.sb([128, NL, 8], F32, "gainT")
        sc = k.sb([128, 8, 4], F32, "sc")
        segm = k.sb([128, 512], F32, "segm")
        mhalf = k.sb([128, 16], F32, "mhalf")
        lb_a = k.sb([128, 2, 2, 4], F32, "lb_a")
        lb_c = k.sb([128, 2, 2, 4], F32, "lb_c")
        lb_na = k.sb([128, 2, 2, 4], F32, "lb_na")
        mod = k.sb([128, 24, 4], F32, "mod")
        A1 = k.sb([128, 8, 4], F32, "A1")
        gate_bc = [k.sb([128, D], F32, "gatebc") for _ in range(3)]
        neglam = k.sb([128, 1], F32, "neglam")
        gA = k.sb([128, 2, 32], F32, "gA")
        gS = k.sb([128, 64], F32, "gS")
        gB = k.sb([128, 2, 64], F32, "gB")
        hT = k.sb([128, 8, T], BF16, "hT")
        hT_tok = [Buf() for _ in range(NT)]
        WB0 = k.sb([128, 8, 1024], BF16, "WB0")
        WB1 = k.sb([128, 8192], BF16, "WB1")

        def hts(t0, n):
            return hT_tok[t0 // 128:(t0 + n + 127) // 128]

        DMA("sp", identb[:], idb_d[:], [cin], [identb])
        DMA("sp", identf[:], idf_d[:], [cin], [identf])
        DMA("sp", cmask[:], cm_d[:], [cin], [cmask])
        DMA("sp", rope[:], rope_d[:], [cin], [rope])
        DMA("sp", cs64[:], cs64_d[:], [cin], [cs64])
        DMA("sp", dftc[:], dftc_d[:], [cin], [dftc])
        DMA("sp", sv[:], sv_d[:], [cin], [sv])
        DMA("sp", bmod[:], bmod_d[:], [cin], [bmod])
        DMA("sp", gainT[:], gain_d[:], [cin], [gainT])
        MSET(onesf[:], 1.0, [onesf])
        MSET(mhalf[:], -0.5, [mhalf])
        MSET(segm[:], 1.0, [segm])
        MSET(segm[:].rearrange("p (c s) -> p c s", s=64)[:, :, 0:1], 0.0, [segm])
        for b in range(2):
            DMA("sp", xs_d.t[b, 0:LAT, :], x_d.t[b], [cin], xs_tok[b][0:16])
            DMA("sp", xs_d.t[b, LAT:T, :], ctx_d.t[b], [cin], xs_tok[b][16:18])

        with ExitStack() as ph:
            k.cur = ph
            cvt = k.sb([128, 8, 4], F32)
            th0 = k.sb([128, 8, 4], F32)
            lbl = k.sb([128, 2, 2, 4], F32)
            lbe = k.sb([128, 2, 2, 4], F32)
            lbs = k.sb([128, 2, 2], F32)
            lbv = k.sb([128, 2, 2, 4], F32)
            lbm = k.sb([128, 2, 2, 4], F32)
            DMA("sp", cvt[:], cv_d[:], [cin], [cvt])
            DMA("sp", lbl[:], lb_d[:], [cin], [lbl])
            ACT(th0[:], cvt[:], AF.Tanh, [cvt], [th0], scale=0.5)
            TS(th0[:], th0[:], 0.5, ALU.mult, [th0], [th0], s2=0.5, op1=ALU.add)
            TT(sc[:], th0[:], cvt[:], ALU.mult, [th0, cvt], [sc])
            ACT(lbe[:], lbl[:], AF.Exp, [lbl], [lbe])
            RED(lbs[:], lbe[:], [lbe], [lbs])
            RECIP(lbs[:], lbs[:], [lbs], [lbs])
            TT(lbe[:], lbe[:], lbs[:].unsqueeze(3).broadcast_to([128, 2, 2, 4]), ALU.mult, [lbe, lbs], [lbe])
            MSET(lbv[:], 0.0, [lbv])
            for l in range(1, 4):
                TT(lbv[:, :, :, l:l + 1], lbv[:, :, :, l - 1:l], lbe[:, :, :, l:l + 1], ALU.add, [lbv, lbe], [lbv])
            TS(lb_a[:], lbv[:], 1e-20, ALU.max, [lbv], [lb_a])
            TS(lb_na[:], lbv[:], -1.0, ALU.mult, [lbv], [lb_na], s2=1.0, op1=ALU.add)
            TT(lb_c[:], lb_a[:], lb_na[:], ALU.add, [lb_a, lb_na], [lb_c])
            k.barrier()
        k.cur = st

        for l in range(nlayers):
            last = (l == NL - 1)
            lam_init = 0.8 - 0.6 * math.exp(-0.3 * l)
            nq = 16 if last else NT
            nck = 32 if last else NCH

            with ExitStack() as ph:
                k.cur = ph
                pm = k.ps([128, 24, 4], F32)
                pg = k.ps([128, 1024], F32)
                wm32 = [k.sb([128, 8, 512], F32) for _ in range(2)]
                diag = [k.sb([128, 128], F32) for _ in range(2)]
                lt = k.sb([128, 2, 32], F32)
                le = k.sb([128, 2], F32)
                for ch in range(6):
                    wb_ = wm32[ch % 2]
                    DMA("sp", wb_[:], wmod_d.t[l, :, ch * 512:(ch + 1) * 512].rearrange("(kc p) n -> p kc n", p=128), [cin], [wb_])
                    for jj in range(4):
                        for kc in range(8):
                            MM(pm[:, ch * 4 + jj, :], wb_[:, kc, jj * 128:(jj + 1) * 128], sc[:, kc, :], kc == 0, kc == 7, [wb_, sc], [pm])
                TT(mod[:], pm[:], bmod[:, l, :].unsqueeze(2).broadcast_to([128, 24, 4]), ALU.add, [pm, bmod], [mod])
                TS(A1[:], mod[:, 8:16, :], 1.0, ALU.add, [mod], [A1])
                TT(A1[:], A1[:], gainT[:, l, :].unsqueeze(2).broadcast_to([128, 8, 4]), ALU.mult, [A1, gainT], [A1])
                for v in range(3):
                    for t in range(8):
                        dg = diag[t % 2]
                        TS(dg[:], identf[:], mod[:, 16 + t, v:v + 1], ALU.mult, [identf, mod], [dg])
                        MM(pg[:, t * 128:(t + 1) * 128], onesf[:], dg[:], True, True, [onesf, dg], [pg])
                    CP(gate_bc[v][:, 0:512], pg[:, 0:512], [pg], [gate_bc[v]])
                    CP(gate_bc[v][:, 512:1024], pg[:, 512:1024], [pg], [gate_bc[v]], eng="act")
                lv = sv[:, l, 64:192].rearrange("p (a b d) -> p a b d", a=2, b=2)
                TT(lt[:], lv[:, :, 0, :], lv[:, :, 1, :], ALU.mult, [sv], [lt])
                RED(le[:], lt[:], [lt], [le])
                ACT(le[:], le[:], AF.Exp, [le], [le])
                TT(neglam[:], le[:, 1:2], le[:, 0:1], ALU.subtract, [le], [neglam])
                TS(neglam[:], neglam[:], -lam_init, ALU.add, [neglam], [neglam])
                CP(gA[:], sv[:, l, 0:64].rearrange("p (a d) -> p a d", a=2), [sv], [gA])
                TS(gA[:, 0, :], gA[:, 0, :], 32.0 ** -0.5, ALU.mult, [gA], [gA])
                TS(gS[:], sv[:, l, 192:256], 1.0 - lam_init, ALU.mult, [sv], [gS])
                CP(gB[:], sv[:, l, 256:384].rearrange("p (a d) -> p a d", a=2), [sv], [gB])
                TS(gB[:, 0, :], gB[:, 0, :], 0.125, ALU.mult, [gB], [gB])
                k.barrier()
            k.cur = st

            for b in range(2):
                with ExitStack() as ph:
                    k.cur = ph
                    xt = [k.sb([128, D], F32) for _ in range(2)]
                    sq = k.sb([128, D], F32)
                    ssq = [k.sb([128, 1], F32) for _ in range(2)]
                    xn = [k.sb([128, D], BF16) for _ in range(2)]
                    tmp = [k.sb([128, 8, 128], F32) for _ in range(2)]
                    pT = [k.ps([128, 8, 128], BF16) for _ in range(2)]
                    for t in range(NT):
                        v = b if t < 16 else 2
                        x_, s_, n_, p_, m_ = xt[t % 2], ssq[t % 2], xn[t % 2], pT[t % 2], tmp[t % 2]
                        DMA("sp", x_[:], xs_d.t[b, t * 128:(t + 1) * 128, :], [xs_tok[b][t]], [x_])
                        ACT(sq[:], x_[:], AF.Square, [x_], [sq, s_], accum=s_[:])
                        RSQ(s_, s_[:], 128, 1, 1.0 / D)
                        TS(n_[:], x_[:], s_[:, 0:1], ALU.mult, [x_, s_], [n_])
                        for kc in range(8):
                            TR(p_[:, kc, :], n_[:, kc * 128:(kc + 1) * 128], identb[:], [n_, identb], [p_])
                        TT(m_[:], p_[:], A1[:, :, v:v + 1].broadcast_to([128, 8, 128]), ALU.mult, [p_, A1], [m_])
                        TT(hT[:, :, t * 128:(t + 1) * 128], m_[:], mod[:, 0:8, v:v + 1].broadcast_to([128, 8, 128]), ALU.add,
                           [m_, mod], [hT_tok[t]], eng="pool")
                    k.barrier()
                k.cur = st

                for br in range(2):
                    with ExitStack() as ph:
                        k.cur = ph
                        isA = (br == 0)
                        dh = 32 if isA else 64
                        ng = 512 // dh
                        nh = ng // 2
                        nmap = 8 if isA else 4
                        c0 = 0 if isA else 768
                        gq = gA if isA else gB
                        qkT = k.sb([128, 4, T], BF16)
                        vaug = k.sb([128, NT, 4, 65], BF16)
                        sqt = [k.sb([128, 512], F32) for _ in range(2)]
                        ssg = [k.sb([128, 16], F32) for _ in range(2)]
                        qkn = [k.sb([128, 512], F32) for _ in range(2)]
                        qkb = [k.sb([128, 512], BF16) for _ in range(2)]
                        ta = k.sb([128, 256], F32)
                        tb = k.sb([128, 256], F32)
                        tc_ = k.sb([128, 256], F32)
                        td = k.sb([128, 256], F32)
                        Pt = [k.sb([128, 512], BF16) for _ in range(3)]
                        Ot2 = [k.sb([128, 8, 65], F32) for _ in range(2)]
                        rsum = k.sb([128, 8], F32)
                        On = k.sb([128, 8, 64], F32)
                        ot = k.sb([128, 256], F32)
                        o2 = k.sb([128, 256], F32)
                        rs4 = k.sb([128, 4], F32)
                        tht = k.sb([128, 256], F32)
                        sgt = k.sb([128, 256], F32)
                        ygb = k.sb([128, 256], BF16)
                        ybt = [k.sb([128, 2, 128], BF16) for _ in range(2)]
                        if isA:
                            OaT = k.sb([66, 8, 512], F32)
                            MSET(OaT[:], 0.0, [OaT])
                        else:
                            nbt = k.sb([128, 100, 128], BF16)
                            DMA("pool", nbt[:], nb_d.t[l].rearrange("n k q -> k n q"), [cin], [nbt])
                        DMA("pool", WB0[:, :, 0:768], win_d.t[l, :, c0:c0 + 768].rearrange("(kc p) n -> p kc n", p=128), [cin], [WB0])
                        g0 = 2816 + 256 * br
                        wg = WB1[:, 0:2048].rearrange("p (kc n) -> p kc n", kc=8)
                        DMA("pool", wg, win_d.t[l, :, g0:g0 + 256].rearrange("(kc p) n -> p kc n", p=128), [cin], [WB1])
                        MSET(vaug[:, :, :, 64:65], 1.0, [vaug])
                        with ExitStack() as ph2:
                            k.cur = ph2
                            pz0 = [k.ps([128, 512], F32) for _ in range(2)]
                            pz1 = [k.ps([128, 512], F32) for _ in range(2)]
                            ptrp = [k.ps([128, 4, 128], BF16) for _ in range(2)]
                            for t in range(NT):
                                ts_ = slice(t * 128, (t + 1) * 128)
                                z0, z1, pt_ = pz0[t % 2], pz1[t % 2], ptrp[t % 2]
                                sq_, sg_, qn_, qb = sqt[t % 2], ssg[t % 2], qkn[t % 2], qkb[t % 2]
                                for kc in range(8):
                                    MM(z0[:], hT[:, kc, ts_], WB0[:, kc, 0:512], kc == 0, kc == 7, [hT_tok[t], WB0], [z0])
                                for kc in range(8):
                                    MM(z1[:, 0:256], hT[:, kc, ts_], WB0[:, kc, 512:768], kc == 0, kc == 7, [hT_tok[t], WB0], [z1])
                                ACT(sq_[:], z0[:], AF.Square, [z0], [sq_])
                                RED(sg_[:, 0:ng], sq_[:].rearrange("p (g d) -> p g d", d=dh), [sq_], [sg_])
                                RSQ(sg_, sg_[:, 0:ng], 128, ng, 1.0 / dh)
                                TT(qn_[:].rearrange("p (g d) -> p g d", d=dh), z0[:].rearrange("p (g d) -> p g d", d=dh),
                                   sg_[:, 0:ng].unsqueeze(2).broadcast_to([128, ng, dh]), ALU.mult, [z0, sg_], [qn_])
                                if isA and t < 16:
                                    TT(qn_[:].rearrange("p (a h d) -> p a h d", a=2, d=dh), qn_[:].rearrange("p (a h d) -> p a h d", a=2, d=dh),
                                       gq[:].unsqueeze(2).broadcast_to([128, 2, nh, dh]), ALU.mult, [qn_, gq], [qn_], eng="pool")
                                    qv = qn_[:].rearrange("p (g two d) -> p g two d", two=2, d=16)
                                    ov = qb[:].rearrange("p (g two d) -> p g two d", two=2, d=16)
                                    cosb = rope[:, t, 0:16].unsqueeze(1).broadcast_to([128, 16, 16])
                                    sinb = rope[:, t, 16:32].unsqueeze(1).broadcast_to([128, 16, 16])
                                    t3 = lambda buf: buf[:].rearrange("p (g d) -> p g d", d=16)
                                    TT(t3(ta), qv[:, :, 0, :], cosb, ALU.mult, [qn_, rope], [ta])
                                    TT(t3(tb), qv[:, :, 1, :], sinb, ALU.mult, [qn_, rope], [tb])
                                    TT(ov[:, :, 0, :], t3(ta), t3(tb), ALU.subtract, [ta, tb], [qb])
                                    TT(t3(tc_), qv[:, :, 0, :], sinb, ALU.mult, [qn_, rope], [tc_], eng="pool")
                                    TT(t3(td), qv[:, :, 1, :], cosb, ALU.mult, [qn_, rope], [td], eng="pool")
                                    TT(ov[:, :, 1, :], t3(tc_), t3(td), ALU.add, [tc_, td], [qb], eng="pool")
                                else:
                                    TT(qb[:].rearrange("p (a h d) -> p a h d", a=2, d=dh), qn_[:].rearrange("p (a h d) -> p a h d", a=2, d=dh),
                                       gq[:].unsqueeze(2).broadcast_to([128, 2, nh, dh]), ALU.mult, [qn_, gq], [qb], eng="pool")
                                for i in range(4):
                                    TR(pt_[:, i, :], qb[:, i * 128:(i + 1) * 128], identb[:], [qb, identb], [pt_])
                                CP(qkT[:, :, ts_], pt_[:], [pt_], [qkT], eng="act")
                                CP(vaug[:, t, :, 0:64], z1[:, 0:256].rearrange("p (h d) -> p h d", d=64), [z1], [vaug], eng="act")
                            k.barrier()
                        k.cur = ph
                        Sp = [k.ps([128, 512], F32) for _ in range(2)]
                        if isA:
                            accT = [k.ps([65, 512], F32) for _ in range(2)]
                            pOa = k.ps([128, 4, 66], F32)
                            pOb = k.ps([128, 4, 66], F32)
                        else:
                            acc = [k.ps([128, 4, 65], F32) for _ in range(2)]
                        pgz = k.ps([128, 256], F32)
                        ptr = k.ps([128, 4, 128], BF16)

                        def finish(qt, Ot):
                            qs_ = slice(qt * 128, (qt + 1) * 128)
                            RECIP(rsum[:, 0:nmap], Ot[:, 0:nmap, 64], [Ot], [rsum])
                            TT(On[:, 0:nmap, :], Ot[:, 0:nmap, 0:64], rsum[:, 0:nmap].unsqueeze(2).broadcast_to([128, nmap, 64]), ALU.mult,
                               [Ot, rsum], [On])
                            if isA:
                                Onv = On[:].rearrange("p (h two) d -> p h two d", two=2)
                                STT(ot[:].rearrange("p (h d) -> p h d", d=64), Onv[:, :, 1, :], neglam[:, 0:1], Onv[:, :, 0, :],
                                    ALU.mult, ALU.add, [On, neglam], [ot])
                                TT(o2[:], ot[:], ot[:], ALU.mult, [ot], [o2], eng="pool")
                                RED(rs4[:], o2[:].rearrange("p (h d) -> p h d", d=64), [o2], [rs4])
                                RSQ(rs4, rs4[:], 128, 4, 1.0 / 64)
                                TT(o2[:].rearrange("p (h d) -> p h d", d=64), ot[:].rearrange("p (h d) -> p h d", d=64),
                                   rs4[:].unsqueeze(2).broadcast_to([128, 4, 64]), ALU.mult, [ot, rs4], [o2])
                                TT(ot[:].rearrange("p (h d) -> p h d", d=64), o2[:].rearrange("p (h d) -> p h d", d=64),
                                   gS[:].unsqueeze(1).broadcast_to([128, 4, 64]), ALU.mult, [o2, gS], [ot], eng="pool")
                                ysrc = ot[:]
                                ybuf = ot
                            else:
                                ysrc = On[:, 0:4, :].rearrange("p h d -> p (h d)")
                                ybuf = On
                            for kc in range(8):
                                MM(pgz[:], hT[:, kc, qs_], wg[:, kc, :], kc == 0, kc == 7, [hT_tok[qt], WB1], [pgz])
                            ACT(tht[:], pgz[:], AF.Tanh, [pgz], [tht], scale=0.5)
                            STT(sgt[:], tht[:], 1.0, pgz[:], ALU.add, ALU.mult, [tht, pgz], [sgt])
                            STT(ygb[:], sgt[:], 0.5, ysrc, ALU.mult, ALU.mult, [sgt, ybuf], [ygb])
                            yb_ = ybt[qt % 2]
                            for i in range(2):
                                TR(ptr[:, i, :], ygb[:, i * 128:(i + 1) * 128], identb[:], [ygb, identb], [ptr])
                            CP(yb_[:], ptr[:, 0:2, :], [ptr], [yb_])
                            DMA("sp", ybr_d.t[b, br, :, :, qs_].rearrange("c p n -> p c n"), yb_[:], [yb_], [ybr_tok[b][br]])

                        gi = 0
                        if isA:
                            qblocks = [(0, 512), (512, 512), (1024, 512), (1536, 512)] + ([] if last else [(2048, 256)])
                            for (q0, qn) in qblocks:
                                chunks = list(range(NT)) if q0 < LAT else [16, 17]
                                for m in range(8):
                                    slot, pb, vh = m // 4, 32 * (m % 4), m // 2
                                    a_ = accT[m % 2]
                                    for ci, c in enumerate(chunks):
                                        S_ = Sp[gi % 2]
                                        P_ = Pt[gi % 3]
                                        gi += 1
                                        MM(S_[:, 0:qn], qkT[pb:pb + 32, 2 + slot, c * 128:(c + 1) * 128], qkT[pb:pb + 32, slot, q0:q0 + qn],
                                           True, True, [qkT], [S_], tp=(pb, 0))
                                        ACT(P_[:, 0:qn], S_[:, 0:qn], AF.Exp, [S_], [P_])
                                        MM(a_[:, 0:qn], vaug[:, c, vh, :], P_[:, 0:qn], ci == 0, ci == len(chunks) - 1, [vaug, P_], [a_])
                                    CP(OaT[0:65, m, 0:qn], a_[:, 0:qn], [a_], [OaT])
                                for j in range(qn // 128):
                                    qt = q0 // 128 + j
                                    Ot = Ot2[qt % 2]
                                    for m in range(8):
                                        pO_ = pOa if m < 4 else pOb
                                        TR(pO_[:, m % 4, :], OaT[:, m, j * 128:(j + 1) * 128], identf[0:66, 0:66], [OaT, identf], [pO_])
                                    CP(Ot[:, 0:4, :], pOa[:, :, 0:65], [pOa], [Ot])
                                    CP(Ot[:, 4:8, :], pOb[:, :, 0:65], [pOb], [Ot])
                                    finish(qt, Ot)
                        else:
                            for qt in range(nq):
                                qs_ = slice(qt * 128, (qt + 1) * 128)
                                Ot = Ot2[qt % 2]
                                if qt >= 16:
                                    chunks = [(16, None), (17, None)]
                                else:
                                    cs = min(max(qt - 2, 0), 11)
                                    typ = 0 if qt == 0 else 1 if qt == 1 else 3 if qt == 14 else 4 if qt == 15 else 2
                                    chunks = [(cs + i, typ * 5 + i) for i in range(5)] + [(16, None), (17, None)]
                                groups = [chunks[i:i + 4] for i in range(0, len(chunks), 4)]
                                for m in range(4):
                                    slot, pb, pw, vh = m // 2, 64 * (m % 2), 64, m
                                    ab = acc[0]
                                    first = True
                                    for g in groups:
                                        S_ = Sp[gi % 2][:].rearrange("p (c q) -> p c q", q=128)
                                        Sb = Sp[gi % 2]
                                        P_ = Pt[gi % 3][:].rearrange("p (c q) -> p c q", q=128)
                                        Pb = Pt[gi % 3]
                                        gi += 1
                                        for i, (c, bi) in enumerate(g):
                                            kT_ = qkT[pb:pb + pw, 2 + slot, c * 128:(c + 1) * 128]
                                            qT_ = qkT[pb:pb + pw, slot, qs_]
                                            if bi is not None:
                                                MM(S_[:, i, :], identb[:], nbt[:, bi * 4 + m, :], True, False, [identb, nbt], [Sb])
                                                MM(S_[:, i, :], kT_, qT_, False, True, [qkT], [Sb], tp=(pb, 0))
                                            else:
                                                MM(S_[:, i, :], kT_, qT_, True, True, [qkT], [Sb], tp=(pb, 0))
                                        n = len(g)
                                        ACT(P_[:, 0:n, :], S_[:, 0:n, :], AF.Exp, [Sb], [Pb])
                                        for i, (c, bi) in enumerate(g):
                                            lastc = (g is groups[-1]) and (i == n - 1)
                                            MM(ab[:, m, :], P_[:, i, :], vaug[:, c, vh, :], first, lastc, [Pb, vaug], [ab])
                                            first = False
                                CP(Ot[:, 0:4, :], acc[0][:], [acc[0]], [Ot])
                                acc.reverse()
                                finish(qt, Ot)
                        k.barrier()
                    k.cur = st

                DMA("pool", WB0[:], win_d.t[l, :, 1536:2560].rearrange("(kc p) n -> p kc n", p=128), [cin], [WB0])
                wg = WB1[:, 0:2048].rearrange("p (kc n) -> p kc n", kc=8)
                DMA("pool", wg, win_d.t[l, :, 3328:3584].rearrange("(kc p) n -> p kc n", p=128), [cin], [WB1])
                for ct in range(2):
                    with ExitStack() as ph:
                        k.cur = ph
                        KT = [k.sb([128, T], BF16) for _ in range(2)]
                        QBD = [k.sb([128, NCH, 128], BF16) for _ in range(2)]
                        Ktm = k.sb([64, NCH, 128], BF16)
                        Vc = k.sb([64, NCH, 128], BF16)
                        Spr = [k.sb([128, NCH, 64], BF16) for _ in range(2)]
                        er = [k.sb([128, NCH], F32) for _ in range(2)]
                        Sst = k.sb([128, 64], F32)
                        Sp32 = k.sb([128, 64], F32)
                        tht = k.sb([128, 512], F32)
                        qh = k.sb([128, 512], F32)
                        gg = k.sb([128, 512], F32)
                        kk = k.sb([128, 512], F32)
                        uu = k.sb([128, 512], F32)
                        Bc = k.sb([128, 512], F32)
                        tmpb = k.sb([128, 512], F32)
                        tot = k.sb([128, 8], F32)
                        E1 = k.sb([128, 512], F32)
                        E2 = k.sb([128, 512], F32)
                        Am = k.sb([64, 2, 128], BF16)
                        of_ = k.sb([64, 512], F32)
                        o2 = k.sb([64, 512], F32)
                        rs8 = k.sb([64, 8], F32)
                        th2 = k.sb([64, 512], F32)
                        sg2 = k.sb([64, 512], F32)
                        yg2 = k.sb([64, 512], BF16)
                        ybt = [k.sb([128, 256], BF16) for _ in range(2)]
                        pq = [k.ps([128, 512], F32) for _ in range(2)]
                        pv = k.ps([64, 4, 128], F32)
                        pbk = k.ps([128, 1024], BF16)
                        ptk = pbk[0:64, :].rearrange("p (a b) -> p a b", b=128)
                        pTy = pbk[:, 0:256]
                        pu = k.ps([128, 128], F32)
                        pa = k.ps([64, 2, 128], F32)
                        po = k.ps([64, 4, 128], F32)
                        pgz = k.ps([64, 4, 128], F32)
                        for d in range(2):
                            MSET(QBD[d][:], 0.0, [QBD[d]])
                        for c4 in range(0, NCH, 4):
                            for i in range(4):
                                c = c4 + i
                                for kc in range(8):
                                    MM(pv[:, i, :], hT[:, kc, c * 64:(c + 1) * 64], WB0[:, kc, 768 + ct * 128:768 + (ct + 1) * 128],
                                       kc == 0, kc == 7, [hT_tok[c // 2], WB0], [pv])
                            CP(Vc[:, c4:c4 + 4, :], pv[:], [pv], [Vc], eng="act")
                        pi = 0
                        for (t0, n) in BLKS:
                            nc_ = n // 64
                            cb = t0 // 64
                            bs_ = slice(t0, t0 + n)
                            p_ = pq[pi % 2]
                            pi += 1
                            for kc in range(8):
                                MM(p_[:, 0:n], WB0[:, kc, ct * 128:(ct + 1) * 128], hT[:, kc, bs_], kc == 0, kc == 7, hts(t0, n) + [WB0], [p_])
                            ACT(tht[:, 0:n], p_[:, 0:n], AF.Tanh, [p_], [tht], scale=0.5)
                            STT(qh[:, 0:n], tht[:, 0:n], 1.0, p_[:, 0:n], ALU.add, ALU.mult, [tht, p_], [qh])
                            for d in range(2):
                                p_ = pq[pi % 2]
                                pi += 1
                                cc = 256 + d * 256 + ct * 128
                                for kc in range(8):
                                    MM(p_[:, 0:n], WB0[:, kc, cc:cc + 128], hT[:, kc, bs_], kc == 0, kc == 7, hts(t0, n) + [WB0], [p_])
                                ACT(uu[:, 0:n], p_[:, 0:n], AF.Exp, [p_], [uu], scale=-1.0)
                                ACT(E2[:, 0:n], uu[:, 0:n], AF.Ln, [uu], [E2], bias=1.0)
                                ACT(E1[:, 0:n], uu[:, 0:n], AF.Ln, [uu, lb_a, lb_c], [E1], scale=lb_a[:, d, ct, l:l + 1],
                                    bias=lb_c[:, d, ct, l:l + 1])
                                TT(gg[:, 0:n], E1[:, 0:n], E2[:, 0:n], ALU.subtract, [E1, E2], [gg], eng="pool")
                                TT(tht[:, 0:n], p_[:, 0:n], E2[:, 0:n], ALU.add, [p_, E2], [tht])
                                ACT(kk[:, 0:n], tht[:, 0:n], AF.Exp, [tht], [kk], scale=-1.0)
                                k.op("dve", [segm, gg], [Bc], lambda e, n=n: e.tensor_tensor_scan(
                                    out=Bc[:, 0:n], data0=segm[:, 0:n], data1=gg[:, 0:n], initial=0.0, op0=ALU.mult, op1=ALU.add))
                                B3 = Bc[:, 0:n].rearrange("p (c s) -> p c s", s=64)
                                CP(tot[:, 0:nc_], B3[:, :, 63], [Bc], [tot])
                                if d == 1:
                                    TT(tmpb[:, 0:n], gg[:, 0:n], Bc[:, 0:n], ALU.subtract, [gg, Bc], [tmpb], eng="pool")
                                    TT(B3, tmpb[:, 0:n].rearrange("p (c s) -> p c s", s=64),
                                       tot[:, 0:nc_].unsqueeze(2).broadcast_to([128, nc_, 64]), ALU.add, [tmpb, tot], [Bc])
                                ACT(er[d][:, cb:cb + nc_], tot[:, 0:nc_], AF.Exp, [tot], [er[d]], scale=0.5)
                                STT(tmpb[:, 0:n].rearrange("p (c s) -> p c s", s=64), tot[:, 0:nc_].unsqueeze(2).broadcast_to([128, nc_, 64]),
                                    -0.5, B3, ALU.mult, ALU.add, [tot, Bc], [tmpb])
                                TS(tmpb[:, 0:n], tmpb[:, 0:n], 43.0, ALU.min, [tmpb], [tmpb], s2=-43.0, op1=ALU.max, eng="pool")
                                ACT(E1[:, 0:n], tmpb[:, 0:n], AF.Exp, [tmpb], [E1])
                                ACT(E2[:, 0:n], tmpb[:, 0:n], AF.Exp, [tmpb], [E2], scale=-1.0)
                                for hh in range(2):
                                    ps_ = slice(64 * hh, 64 * hh + 64)
                                    STT(QBD[d][ps_, cb:cb + nc_, 64 * hh:64 * hh + 64], qh[ps_, 0:n].rearrange("p (c s) -> p c s", s=64), 0.5,
                                        E1[ps_, 0:n].rearrange("p (c s) -> p c s", s=64), ALU.mult, ALU.mult, [qh, E1], [QBD[d]])
                                STT(KT[d][:, bs_], kk[:, 0:n], lb_na[:, d, ct, l:l + 1], E2[:, 0:n], ALU.mult, ALU.mult, [kk, E2, lb_na], [KT[d]])
                        for d in range(2):
                            for c8 in range(0, NCH, 4):
                                for i in range(4):
                                    c = c8 + i
                                    TR(ptk[:, i, :], KT[d][:, c * 64:(c + 1) * 64], identb[:], [KT[d], identb], [pbk])
                                CP(Ktm[:, c8:c8 + 4, :], ptk[:, 0:4, :], [pbk], [Ktm], eng="act")
                            MSET(Sst[:], 0.0, [Sst], eng="dve")
                            order = ([32, 33, 34, 35] + list(range(32))) if d == 0 else ([35, 34, 33, 32] + list(range(31, -1, -1)))
                            for c in order:
                                MM(pu[:], Ktm[:, c, :], Vc[:, c, :], True, True, [Ktm, Vc], [pu])
                                e_ = er[d][:, c:c + 1]
                                TS(Sp32[:], Sst[:], e_, ALU.mult, [Sst, er[d]], [Sp32])
                                CP(Spr[d][:, c, :], Sp32[:], [Sp32], [Spr[d]], eng="pool")
                                for hh in range(2):
                                    ps_ = slice(64 * hh, 64 * hh + 64)
                                    TT(Sst[ps_, :], Sp32[ps_, :], pu[ps_, 64 * hh:64 * hh + 64], ALU.add, [Sp32, pu], [Sst])
                                TS(Sst[:], Sst[:], e_, ALU.mult, [Sst, er[d]], [Sst])
                        cm4 = cmask[:].rearrange("p (d h t) -> p d h t", d=2, h=2)
                        for c4 in range(0, nck, 4):
                            for i in range(4):
                                c = c4 + i
                                for d in range(2):
                                    MM(pa[:, d, :], KT[d][:, c * 64:(c + 1) * 64], QBD[d][:, c, :], True, True, [KT[d], QBD[d]], [pa])
                                TT(Am[:].rearrange("p d (h t) -> p d h t", h=2), pa[:].rearrange("p d (h t) -> p d h t", h=2), cm4, ALU.mult,
                                   [pa, cmask], [Am])
                                for hh in range(2):
                                    hs_ = slice(64 * hh, 64 * hh + 64)
                                    for d in range(2):
                                        MM(po[:, i, hs_], Am[:, d, hs_], Vc[:, c, hs_], d == 0, False, [Am, Vc], [po])
                                        MM(po[:, i, hs_], QBD[d][:, c, hs_], Spr[d][:, c, :], False, d == 1, [QBD[d], Spr[d]], [po])
                                for kc in range(8):
                                    MM(pgz[:, i, :], hT[:, kc, c * 64:(c + 1) * 64], wg[:, kc, ct * 128:(ct + 1) * 128], kc == 0, kc == 7,
                                       [hT_tok[c // 2], WB1], [pgz])
                            pof = po[:].rearrange("p c n -> p (c n)")
                            pgf = pgz[:].rearrange("p c n -> p (c n)")
                            CP(of_[:], pof, [po], [of_], eng="act")
                            ACT(o2[:], pof, AF.Square, [po], [o2])
                            RED(rs8[:], o2[:].rearrange("p (g d) -> p g d", d=64), [o2], [rs8])
                            RSQ(rs8, rs8[:], 64, 8, 1.0 / 64)
                            TT(o2[:].rearrange("p (g d) -> p g d", d=64), of_[:].rearrange("p (g d) -> p g d", d=64),
                               rs8[:].unsqueeze(2).broadcast_to([64, 8, 64]), ALU.mult, [of_, rs8], [o2])
                            TT(of_[:].rearrange("p (g d) -> p g d", d=64), o2[:].rearrange("p (g d) -> p g d", d=64),
                               sv[0:64, l, 384:448].unsqueeze(1).broadcast_to([64, 8, 64]), ALU.mult, [o2, sv], [of_], eng="pool")
                            ACT(th2[:], pgf, AF.Tanh, [pgz], [th2], scale=0.5)
                            STT(sg2[:], th2[:], 1.0, pgf, ALU.add, ALU.mult, [th2, pgz], [sg2])
                            STT(yg2[:], sg2[:], 0.5, of_[:], ALU.mult, ALU.mult, [sg2, of_], [yg2])
                            yb_ = ybt[(c4 // 4) % 2]
                            for i in range(4):
                                TR(pTy[:, i * 64:(i + 1) * 64], yg2[:, i * 128:(i + 1) * 128], identb[0:64, 0:64], [yg2, identb], [pbk])
                            CP(yb_[:], pTy, [pbk], [yb_], eng="act")
                            DMA("sp", ybr_d.t[b, 2, ct, :, c4 * 64:c4 * 64 + 256], yb_[:], [yb_], [ybr_tok[b][2]])
                        k.barrier()
                    k.cur = st

                with ExitStack() as ph:
                    k.cur = ph
                    uT = k.sb([128, 2, T], BF16)
                    PQ = k.sb([128, NT, 2, 256], BF16)
                    tab = [k.sb([128, 16, 2, 256], BF16) for _ in range(2)]
                    tht = k.sb([128, 256], F32)
                    sgt = k.sb([128, 256], F32)
                    ybt = [k.sb([128, 2, 256], BF16) for _ in range(2)]
                    pq = [k.ps([128, 512], F32) for _ in range(2)]
                    pp = [k.ps([128, 2, 256], F32) for _ in range(2)]
                    py = [k.ps([128, 256], F32) for _ in range(2)]
                    pgz = [k.ps([128, 256], F32) for _ in range(2)]
                    DMA("pool", WB0[:, :, 0:256], win_d.t[l, :, 2560:2816].rearrange("(kc p) n -> p kc n", p=128), [cin], [WB0])
                    wg = WB1[:, 0:2048].rearrange("p (kc n) -> p kc n", kc=8)
                    DMA("pool", wg, win_d.t[l, :, 3584:3840].rearrange("(kc p) n -> p kc n", p=128), [cin], [WB1])
                    pi = 0
                    for ct in range(2):
                        for (t0, n) in BLKS:
                            p_ = pq[pi % 2]
                            pi += 1
                            for kc in range(8):
                                MM(p_[:, 0:n], WB0[:, kc, ct * 128:(ct + 1) * 128], hT[:, kc, t0:t0 + n], kc == 0, kc == 7, hts(t0, n) + [WB0], [p_])
                            CP(uT[:, ct, t0:t0 + n], p_[:, 0:n], [p_], [uT], eng="act")
                    for t in range(NT):
                        p_ = pp[t % 2]
                        for ct in range(2):
                            MM(p_[:, ct, :], uT[:, ct, t * 128:(t + 1) * 128], cs64[:], True, True, [uT, cs64], [p_])
                        CP(PQ[:, t, :, :], p_[:], [p_], [PQ], eng=("act" if t % 2 else "dve"))
                    it = 0
                    nblk = 8 if last else 9
                    for nb in range(nblk):
                        if nb < 8:
                            tb_ = tab[nb % 2]
                            DMA("sp", tb_[:], dftl_d.t[:, :, :, nb * 256:(nb + 1) * 256], [cin], [tb_])
                            tcs = list(range(16))
                            tsrc = lambda tc, cs: tb_[:, tc, cs, :]
                            tread = [tb_]
                            t0 = nb * 256
                        else:
                            tcs = [16, 17]
                            tsrc = lambda tc, cs: dftc[:, tc - 16, cs, :]
                            tread = [dftc]
                            t0 = LAT
                        yb_ = ybt[nb % 2]
                        for ct in range(2):
                            y_ = py[it % 2]
                            g_ = pgz[it % 2]
                            it += 1
                            nmm = len(tcs) * 2
                            j = 0
                            for tc in tcs:
                                for cs in range(2):
                                    MM(y_[:], PQ[:, tc, ct, cs * 128:(cs + 1) * 128], tsrc(tc, cs), j == 0, j == nmm - 1, [PQ] + tread, [y_])
                                    j += 1
                            for kc in range(8):
                                MM(g_[:], wg[:, kc, ct * 128:(ct + 1) * 128], hT[:, kc, t0:t0 + 256], kc == 0, kc == 7, hts(t0, 256) + [WB1], [g_])
                            ACT(tht[:], g_[:], AF.Tanh, [g_], [tht], scale=0.5)
                            STT(sgt[:], tht[:], 1.0, g_[:], ALU.add, ALU.mult, [tht, g_], [sgt])
                            STT(yb_[:, ct, :], sgt[:], 0.5, y_[:], ALU.mult, ALU.mult, [sgt, y_], [yb_])
                        DMA("sp", ybr_d.t[b, 3, :, :, t0:t0 + 256].rearrange("c p n -> p c n"), yb_[:], [yb_], [ybr_tok[b][3]])
                    k.barrier()
                k.cur = st

                with ExitStack() as ph:
                    k.cur = ph
                    ybS = k.sb([128, 4, 2, T], BF16)
                    accT = k.sb([128, 8, T], BF16)
                    wu = [k.sb([128, 4, 2, 128], BF16) for _ in range(2)]
                    wm_tok = [Buf(), Buf()]
                    tht = [k.sb([128, 512], F32) for _ in range(2)]
                    tmpm = [k.sb([128, 512], F32) for _ in range(2)]
                    accf = [k.sb([128, 512], F32) for _ in range(2)]
                    xt = [k.sb([128, D], F32) for _ in range(2)]
                    xo = [k.sb([128, D], F32) for _ in range(2)]
                    tmo = [k.sb([128, 512], F32) for _ in range(2)]
                    pM = [k.ps([128, 512], F32) for _ in range(2)]
                    pU = [k.ps([128, 512], F32) for _ in range(2)]
                    pO = [k.ps([128, 512], F32) for _ in range(2)]
                    for i in range(4):
                        DMA("sp", ybS[:, i, :, :], ybr_d.t[b, i].rearrange("c p n -> p c n"), [ybr_tok[b][i]], [ybS])
                    DMA("pool", WB0[:], wout_d.t[l].rearrange("(kc p) n -> p kc n", p=128), [cin], [WB0])
                    mblks = BLKS[:4] if last else BLKS
                    im = 0
                    for ft in range(8):
                        fs_ = slice(ft * 128, (ft + 1) * 128)
                        wmv = WB1[:, (ft % 2) * 4096:(ft % 2 + 1) * 4096].rearrange("p (i kc n) -> p i kc n", i=4, kc=8)
                        wmt = wm_tok[ft % 2]
                        wu_ = wu[ft % 2]
                        DMA("pool", wmv, wmg_d.t[l, :, :, fs_].rearrange("i (kc p) n -> p i kc n", p=128), [cin], [wmt])
                        DMA("pool", wu_[:], wup_d.t[l, :, :, fs_].rearrange("i (fc p) n -> p i fc n", p=128), [cin], [wu_])
                        for bi_, (t0, n) in enumerate(mblks):
                            af = accf[bi_ % 2]
                            for i in range(4):
                                m_ = pM[im % 2]
                                u_ = pU[im % 2]
                                th_ = tht[im % 2]
                                tm_ = tmpm[im % 2]
                                im += 1
                                for kc in range(8):
                                    MM(m_[:, 0:n], wmv[:, i, kc, :], hT[:, kc, t0:t0 + n], kc == 0, kc == 7, hts(t0, n) + [wmt], [m_])
                                for fc in range(2):
                                    MM(u_[:, 0:n], wu_[:, i, fc, :], ybS[:, i, fc, t0:t0 + n], fc == 0, fc == 1, [wu_, ybS], [u_])
                                ACT(th_[:, 0:n], m_[:, 0:n], AF.Tanh, [m_], [th_], scale=0.5)
                                if i == 0:
                                    STT(af[:, 0:n], th_[:, 0:n], 1.0, u_[:, 0:n], ALU.add, ALU.mult, [th_, u_], [af])
                                else:
                                    STT(tm_[:, 0:n], th_[:, 0:n], 1.0, u_[:, 0:n], ALU.add, ALU.mult, [th_, u_], [tm_])
                                    TT(af[:, 0:n], af[:, 0:n], tm_[:, 0:n], ALU.add, [af, tm_], [af], eng="pool")
                            ACT(accT[:, ft, t0:t0 + n], af[:, 0:n], AF.Copy, [af], [accT], scale=0.5)
                    ntile = 16 if last else NT
                    io = 0
                    for t in range(ntile):
                        v = b if t < 16 else 2
                        x_ = xt[t % 2]
                        o_ = xo[t % 2]
                        DMA("sp", x_[:], xs_d.t[b, t * 128:(t + 1) * 128, :], [xs_tok[b][t]], [x_])
                        for hf in range(2):
                            p_ = pO[io % 2]
                            tm_ = tmo[io % 2]
                            io += 1
                            cs_ = slice(hf * 512, (hf + 1) * 512)
                            for kc in range(8):
                                MM(p_[:], accT[:, kc, t * 128:(t + 1) * 128], WB0[:, kc, cs_], kc == 0, kc == 7, [accT, WB0], [p_])
                            TT(tm_[:], p_[:], gate_bc[v][:, cs_], ALU.mult, [p_, gate_bc[v]], [tm_])
                            TT(o_[:, cs_], tm_[:], x_[:, cs_], ALU.add, [tm_, x_], [o_], eng="pool")
                        if last:
                            DMA("sp", out_d.t[b, t * 128:(t + 1) * 128, :], o_[:], [o_], [xs_tok[b][t]])
                        else:
                            DMA("sp", xs_d.t[b, t * 128:(t + 1) * 128, :], o_[:], [o_], [xs_tok[b][t]])
                    k.barrier()
                k.cur = st
        k.emit()
    return nc


def _consts():
    bf = ml_dtypes.bfloat16
    c = {}
    c["identb"] = np.eye(128, dtype=np.float32).astype(bf)
    c["identf"] = np.eye(128, dtype=np.float32)
    s = np.arange(64)[:, None]
    t = np.arange(64)[None, :]
    mf = (s <= t).astype(np.float32)
    mb = (s >= t).astype(np.float32)
    cm = np.stack([np.stack([mf, mf], 0), np.stack([mb, mb], 0)], 0)
    c["cmask"] = np.ascontiguousarray(cm.transpose(2, 0, 1, 3).reshape(64, 256))
    n = np.arange(LAT)
    row = (n // 64).astype(np.float32)
    col = (n % 64).astype(np.float32)
    inv = (10000.0 ** (-np.arange(0, 16, 2, dtype=np.float32) / 16)).astype(np.float32)
    ang = np.concatenate([row[:, None] * inv, col[:, None] * inv], -1).astype(np.float32)
    rp = np.concatenate([np.cos(ang), np.sin(ang)], -1).astype(np.float32)
    c["rope"] = np.ascontiguousarray(rp.reshape(16, 128, 32).transpose(1, 0, 2))
    e = np.arange(64)
    a64 = 2 * np.pi * np.outer(e, e) / 64
    C64 = np.cos(a64) / 8.0
    S64 = np.sin(a64) / 8.0
    cs = np.zeros((128, 256), np.float64)
    for g in range(2):
        cs[g * 64:(g + 1) * 64, g * 64:(g + 1) * 64] = C64
        cs[g * 64:(g + 1) * 64, 128 + g * 64:128 + (g + 1) * 64] = S64
    c["cs64"] = cs.astype(np.float32).astype(bf)

    def dft(N):
        tt = np.arange(N)
        a = 2 * np.pi * ((np.outer(tt, tt)) % N) / N
        Cn = np.cos(a) / np.sqrt(N)
        Sn = -np.sin(a) / np.sqrt(N)
        tab = np.stack([Cn, Sn], 1)
        return np.ascontiguousarray(tab.reshape(N // 128, 128, 2, N).transpose(1, 0, 2, 3)).astype(np.float32).astype(bf)

    c["dftL"] = dft(LAT)
    c["dftC"] = dft(CTX)
    return c


def _nbias_index():
    rows, W, wh, ww = 32, 64, 8, 16
    idx = np.full((5, 5, 128, 128, 2), -1, np.int64)
    for typ, qt in enumerate([0, 1, 5, 14, 15]):
        cs = min(max(qt - 2, 0), 11)
        for i in range(5):
            kc = cs + i
            for kk in range(128):
                kr, kcol = 2 * kc + kk // 64, kk % 64
                for q in range(128):
                    r, cq = 2 * qt + q // 64, q % 64
                    r0 = min(max(r - wh // 2, 0), rows - wh)
                    c0 = min(max(cq - ww // 2, 0), W - ww)
                    if r0 <= kr < r0 + wh and c0 <= kcol < c0 + ww:
                        idx[typ, i, kk, q, 0] = kr - r + wh - 1
                        idx[typ, i, kk, q, 1] = min(max(kcol - cq, 1 - ww), ww - 1) + ww - 1
    return idx


_CACHE = {}


def kernel(x, c, ctx, c_ctx, norm_gain, w_mod, b_mod, w_in, da_qk_gain, da_lambda, da_subln_gain,
           na_qk_gain, na_rpb, hg_lb_logits, hg_norm_gain, w_up, w_merge, w_out):
    f = lambda a: np.ascontiguousarray(np.asarray(a, dtype=np.float32))
    x, c, ctx, c_ctx = f(x), f(c), f(ctx), f(c_ctx)
    if "nc" not in _CACHE:
        _CACHE["nc"] = build()
        _CACHE["consts"] = _consts()
        _CACHE["nbidx"] = _nbias_index()
    nc = _CACHE["nc"]
    shared = dict(_CACHE["consts"])
    shared["w_mod"] = f(w_mod)
    shared["w_in"] = f(w_in)
    shared["w_up"] = f(w_up)
    shared["w_merge"] = f(w_merge)
    shared["w_out"] = f(w_out)
    shared["b_modT"] = np.ascontiguousarray(f(b_mod).reshape(NL, 24, 128).transpose(2, 0, 1))
    shared["gainT"] = np.ascontiguousarray(f(norm_gain).reshape(NL, 8, 128).transpose(2, 0, 1))
    smallv = np.concatenate([f(da_qk_gain).reshape(NL, 64), f(da_lambda).reshape(NL, 128), f(da_subln_gain).reshape(NL, 64),
                             f(na_qk_gain).reshape(NL, 128), f(hg_norm_gain).reshape(NL, 64)], -1)
    shared["smallv"] = np.ascontiguousarray(np.broadcast_to(smallv[None], (128, NL, 448)))
    shared["lbT"] = np.ascontiguousarray(f(hg_lb_logits).reshape(2, NL, 2, 128).transpose(3, 0, 2, 1))
    idx = _CACHE["nbidx"]
    rpb = f(na_rpb)
    inw = idx[..., 0] >= 0
    dr = np.where(inw, idx[..., 0], 0)
    dc = np.where(inw, idx[..., 1], 0)
    gath = rpb[:, :, dr, dc]
    gath = np.where(inw[None, None], gath, np.float32(NEG)).astype(np.float32)
    shared["nbias"] = np.ascontiguousarray(gath.transpose(0, 2, 3, 1, 4, 5).reshape(NL, 100, 128, 128))
    in_maps = []
    for i in range(NCORE):
        m = dict(shared)
        m["x"] = np.ascontiguousarray(x[2 * i:2 * i + 2])
        m["ctx"] = np.ascontiguousarray(ctx[2 * i:2 * i + 2])
        cvec = np.stack([c[2 * i], c[2 * i + 1], c_ctx, np.zeros_like(c_ctx)], 0)
        m["cv"] = np.ascontiguousarray(cvec.reshape(4, 8, 128).transpose(2, 1, 0))
        in_maps.append(m)
    res = run_bass_kernel_spmd(nc, in_maps, core_ids=list(range(NCORE)))
    out = np.concatenate([np.asarray(r["out"], dtype=np.float32) for r in res.results], axis=0)
    return out
```

```python
import math
from contextlib import ExitStack

import numpy as np
import ml_dtypes

import concourse.bass as bass
import concourse.mybir as mybir
from concourse.bass_utils import run_bass_kernel_spmd

F32 = mybir.dt.float32
BF16 = mybir.dt.bfloat16
ALU = mybir.AluOpType
AF = mybir.ActivationFunctionType
AX = mybir.AxisListType

NL = 4
D = 1024
NCORE = 8
LAT = 2048
CTX = 256
T = LAT + CTX
NT = T // 128
NCH = T // 64
EPS = 1e-6
BLKS = [(0, 512), (512, 512), (1024, 512), (1536, 512), (2048, 256)]
NEG = -30000.0


class Buf:
    def __init__(self, t=None, name=""):
        self.t = t
        self.name = name
        self.last_w = None
        self.readers = {}

    def __getitem__(self, k):
        return self.t[k]


class K:
    ENG = ("pe", "act", "dve", "pool", "sp")

    def __init__(self, nc, stack, n_dma_sems=48):
        self.nc = nc
        self.stack = stack
        self.cur = stack
        self.streams = {e: [] for e in self.ENG}
        self.sem = {}
        self.cnt = {}
        for e in self.ENG:
            self.sem[e] = stack.enter_context(nc.semaphore("s_" + e))
            self.cnt[e] = 0
        self.ndma = n_dma_sems
        for i in range(n_dma_sems):
            self.sem[("d", i)] = stack.enter_context(nc.semaphore("d%d" % i))
            self.cnt[("d", i)] = 0
        self.dma_rr = 0
        self.known = {e: {} for e in self.ENG}
        self.nbuf = 0

    def sb(self, shape, dtype, name=None):
        self.nbuf += 1
        name = "%s_%d" % (name or "sb", self.nbuf)
        t = self.cur.enter_context(self.nc.sbuf_tensor(name, list(shape), dtype))
        return Buf(t, name)

    def ps(self, shape, dtype, name=None):
        self.nbuf += 1
        name = "%s_%d" % (name or "ps", self.nbuf)
        t = self.cur.enter_context(self.nc.psum_tensor(name, list(shape), dtype))
        return Buf(t, name)

    def dram(self, name, shape, dtype, kind="Internal"):
        t = self.nc.dram_tensor(name, list(shape), dtype, kind=kind)
        return Buf(t.ap(), name)

    def _need(self, eng, reads, writes):
        need = {}

        def add(dep):
            if dep is None:
                return
            k, v = dep
            if need.get(k, 0) < v:
                need[k] = v

        for b in reads:
            add(b.last_w)
        for b in writes:
            add(b.last_w)
            for k, v in b.readers.items():
                add((k, v))
        waits = []
        kn = self.known[eng]
        for k, v in need.items():
            if k == "pe" and eng == "pe":
                continue
            if kn.get(k, 0) >= v:
                continue
            kn[k] = v
            waits.append((k, v))
        return waits

    def op(self, eng, reads, writes, fn, inc=True):
        waits = self._need(eng, reads, writes)
        if inc:
            self.cnt[eng] += 1
            v = self.cnt[eng]
            self.streams[eng].append((waits, fn, (eng, 1)))
        else:
            v = self.cnt[eng] + 1
            self.streams[eng].append((waits, fn, None))
        for b in reads:
            if b.readers.get(eng, 0) < v:
                b.readers[eng] = v
        for b in writes:
            b.last_w = (eng, v)
            b.readers = {}
        return v

    def dma(self, q, reads, writes, fn):
        i = self.dma_rr
        self.dma_rr = (self.dma_rr + 1) % self.ndma
        key = ("d", i)
        waits = self._need(q, reads, writes)
        prev = self.cnt[key]
        if prev > 0 and self.known[q].get(key, 0) < prev:
            self.known[q][key] = prev
            waits.append((key, prev))
        self.cnt[key] += 16
        v = self.cnt[key]
        self.streams[q].append((waits, fn, (key, 16)))
        for b in reads:
            b.readers[key] = v
        for b in writes:
            b.last_w = (key, v)
            b.readers = {}
        return key, v

    def barrier(self):
        for e in self.ENG:
            waits = []
            for k, c in self.cnt.items():
                if c > 0 and (k != e or e in ("act", "dve", "pool")) and self.known[e].get(k, 0) < c:
                    self.known[e][k] = c
                    waits.append((k, c))
            if waits:
                self.streams[e].append((waits, None, None))

    def emit(self):
        nc = self.nc
        waits = [(k, c) for k, c in self.cnt.items() if c > 0 and k != "sp"]
        self.streams["sp"].append((waits, None, None))
        with nc.Block() as block:
            def mk(ename):
                def body(engine):
                    for ws, fn, inc in self.streams[ename]:
                        for kk, v in ws:
                            engine.wait_ge(self.sem[kk], v)
                        if fn is not None:
                            ins = fn(engine)
                            if inc is not None:
                                ins.then_inc(self.sem[inc[0]], inc[1])
                return body
            block.tensor(mk("pe"))
            block.scalar(mk("act"))
            block.vector(mk("dve"))
            block.gpsimd(mk("pool"))
            block.sync(mk("sp"))


def build(nlayers=NL, dbg=False, PIPE_A=True, PIPE_B=True, PIPE_PREP=True, PREFETCH_M=True):
    nc = bass.Bass("TRN2", target_bir_lowering=False)
    st = ExitStack()
    with st:
        k = K(nc, st)

        def MM(out, lhsT, rhs, start, stop, reads, writes, tp=None):
            if tp is None:
                k.op("pe", reads, writes, lambda e: e.matmul(out, lhsT=lhsT, rhs=rhs, start=start, stop=stop), inc=bool(stop))
            else:
                k.op("pe", reads, writes, lambda e: e.matmul(out, lhsT=lhsT, rhs=rhs, start=start, stop=stop, tile_position=tp), inc=bool(stop))

        def RSQ(buf, ap, npart, n, inv_n):
            TS(ap, ap, inv_n, ALU.mult, [buf], [buf], s2=EPS, op1=ALU.add, eng="pool")
            TT(ap, ap, mhalf[0:npart, 0:n], ALU.pow, [buf, mhalf], [buf], eng="pool")

        def TR(out, in_, idn, reads, writes):
            k.op("pe", reads, writes, lambda e: e.transpose(out=out, in_=in_, identity=idn))

        def ACT(out, in_, func, reads, writes, scale=None, bias=None, accum=None):
            kw = {}
            if scale is not None:
                kw["scale"] = scale
            if bias is not None:
                kw["bias"] = bias
            if accum is not None:
                kw["accum_out"] = accum
            k.op("act", reads, writes, lambda e: e.activation(out=out, in_=in_, func=func, **kw))

        def TT(out, in0, in1, op, reads, writes, eng="dve"):
            k.op(eng, reads, writes, lambda e: e.tensor_tensor(out=out, in0=in0, in1=in1, op=op))

        def TS(out, in0, s1, op0, reads, writes, s2=None, op1=None, eng="dve"):
            if op1 is None:
                k.op(eng, reads, writes, lambda e: e.tensor_scalar(out=out, in0=in0, scalar1=s1, scalar2=None, op0=op0))
            else:
                k.op(eng, reads, writes, lambda e: e.tensor_scalar(out=out, in0=in0, scalar1=s1, scalar2=s2, op0=op0, op1=op1))

        def STT(out, in0, scalar, in1, op0, op1, reads, writes):
            k.op("dve", reads, writes, lambda e: e.scalar_tensor_tensor(out=out, in0=in0, scalar=scalar, in1=in1, op0=op0, op1=op1))

        def CP(out, in_, reads, writes, eng="dve"):
            if eng == "act":
                k.op("act", reads, writes, lambda e: e.activation(out=out, in_=in_, func=AF.Copy))
            else:
                k.op(eng, reads, writes, lambda e: e.tensor_copy(out=out, in_=in_))

        def RED(out, in_, reads, writes):
            k.op("dve", reads, writes, lambda e: e.tensor_reduce(out=out, in_=in_, axis=AX.X, op=ALU.add))

        def RECIP(out, in_, reads, writes):
            k.op("dve", reads, writes, lambda e: e.reciprocal(out=out, in_=in_))

        def MSET(ap, val, writes, eng="pool"):
            k.op(eng, [], writes, lambda e: e.memset(ap, val))

        def DMA(q, out, in_, reads, writes):
            k.dma(q, reads, writes, lambda e: e.dma_start(out=out, in_=in_))

        EI = "ExternalInput"
        x_d = k.dram("x", [2, LAT, D], F32, EI)
        ctx_d = k.dram("ctx", [2, CTX, D], F32, EI)
        cv_d = k.dram("cv", [128, 8, 4], F32, EI)
        wmod_d = k.dram("w_mod", [NL, D, 3 * D], F32, EI)
        bmod_d = k.dram("b_modT", [128, NL, 24], F32, EI)
        gain_d = k.dram("gainT", [128, NL, 8], F32, EI)
        win_d = k.dram("w_in", [NL, D, 3840], F32, EI)
        wup_d = k.dram("w_up", [NL, 4, 256, D], F32, EI)
        wmg_d = k.dram("w_merge", [NL, 4, D, D], F32, EI)
        wout_d = k.dram("w_out", [NL, D, D], F32, EI)
        sv_d = k.dram("smallv", [128, NL, 448], F32, EI)
        lb_d = k.dram("lbT", [128, 2, 2, 4], F32, EI)
        nb_d = k.dram("nbias", [NL, 100, 128, 128], F32, EI)
        rope_d = k.dram("rope", [128, 16, 32], F32, EI)
        cs64_d = k.dram("cs64", [128, 256], BF16, EI)
        dftl_d = k.dram("dftL", [128, 16, 2, LAT], BF16, EI)
        dftc_d = k.dram("dftC", [128, 2, 2, CTX], BF16, EI)
        idb_d = k.dram("identb", [128, 128], BF16, EI)
        idf_d = k.dram("identf", [128, 128], F32, EI)
        cm_d = k.dram("cmask", [64, 256], F32, EI)
        out_d = k.dram("out", [2, LAT, D], F32, "ExternalOutput")
        xs_d = k.dram("xs", [2, T, D], F32, "ExternalOutput" if dbg else "Internal")
        ybr_d = k.dram("ybr", [2, 4, 2, 128, T], BF16, "ExternalOutput" if dbg else "Internal")
        xs_tok = [[Buf() for _ in range(NT)] for _ in range(2)]
        ybr_tok = [[Buf() for _ in range(4)] for _ in range(2)]
        cin = Buf()

        identb = k.sb([128, 128], BF16, "identb")
        identf = k.sb([128, 128], F32, "identf")
        onesf = k.sb([128, 128], F32, "onesf")
        cmask = k.sb([64, 256], F32, "cmask")
        rope = k.sb([128, 16, 32], F32, "rope")
        cs64 = k.sb([128, 256], BF16, "cs64")
        dftc = k.sb([128, 2, 2, CTX], BF16, "dftc")
        sv = k.sb([128, NL, 448], F32, "sv")
        bmod = k.sb([128, NL, 24], F32, "bmod")
        gainT = k.sb([128, NL, 8], F32, "gainT")
        sc = k.sb([128, 8, 4], F32, "sc")
        segm = k.sb([128, 512], F32, "segm")
        mhalf = k.sb([128, 16], F32, "mhalf")
        lb_a = k.sb([128, 2, 2, 4], F32, "lb_a")
        lb_c = k.sb([128, 2, 2, 4], F32, "lb_c")
        lb_na = k.sb([128, 2, 2, 4], F32, "lb_na")
        mod = k.sb([128, 24, 4], F32, "mod")
        A1 = k.sb([128, 8, 4], F32, "A1")
        gate_bc = [k.sb([128, D], F32, "gatebc") for _ in range(3)]
        neglam = k.sb([128, 1], F32, "neglam")
        gA = k.sb([128, 2, 32], F32, "gA")
        gS = k.sb([128, 64], F32, "gS")
        gB = k.sb([128, 2, 64], F32, "gB")
        hT = k.sb([128, 8, T], BF16, "hT")
        hT_tok = [Buf() for _ in range(NT)]
        WB0 = k.sb([128, 8, 1024], BF16, "WB0")
        WB1 = k.sb([128, 8192], BF16, "WB1")

        def hts(t0, n):
            return hT_tok[t0 // 128:(t0 + n + 127) // 128]

        DMA("sp", identb[:], idb_d[:], [cin], [identb])
        DMA("sp", identf[:], idf_d[:], [cin], [identf])
        DMA("sp", cmask[:], cm_d[:], [cin], [cmask])
        DMA("sp", rope[:], rope_d[:], [cin], [rope])
        DMA("sp", cs64[:], cs64_d[:], [cin], [cs64])
        DMA("sp", dftc[:], dftc_d[:], [cin], [dftc])
        DMA("sp", sv[:], sv_d[:], [cin], [sv])
        DMA("sp", bmod[:], bmod_d[:], [cin], [bmod])
        DMA("sp", gainT[:], gain_d[:], [cin], [gainT])
        MSET(onesf[:], 1.0, [onesf])
        MSET(mhalf[:], -0.5, [mhalf])
        MSET(segm[:], 1.0, [segm])
        MSET(segm[:].rearrange("p (c s) -> p c s", s=64)[:, :, 0:1], 0.0, [segm])
        for b in range(2):
            DMA("sp", xs_d.t[b, 0:LAT, :], x_d.t[b], [cin], xs_tok[b][0:16])
            DMA("sp", xs_d.t[b, LAT:T, :], ctx_d.t[b], [cin], xs_tok[b][16:18])

        with ExitStack() as ph:
            k.cur = ph
            cvt = k.sb([128, 8, 4], F32)
            th0 = k.sb([128, 8, 4], F32)
            lbl = k.sb([128, 2, 2, 4], F32)
            lbe = k.sb([128, 2, 2, 4], F32)
            lbs = k.sb([128, 2, 2], F32)
            lbv = k.sb([128, 2, 2, 4], F32)
            lbm = k.sb([128, 2, 2, 4], F32)
            DMA("sp", cvt[:], cv_d[:], [cin], [cvt])
            DMA("sp", lbl[:], lb_d[:], [cin], [lbl])
            ACT(th0[:], cvt[:], AF.Tanh, [cvt], [th0], scale=0.5)
            TS(th0[:], th0[:], 0.5, ALU.mult, [th0], [th0], s2=0.5, op1=ALU.add)
            TT(sc[:], th0[:], cvt[:], ALU.mult, [th0, cvt], [sc])
            ACT(lbe[:], lbl[:], AF.Exp, [lbl], [lbe])
            RED(lbs[:], lbe[:], [lbe], [lbs])
            RECIP(lbs[:], lbs[:], [lbs], [lbs])
            TT(lbe[:], lbe[:], lbs[:].unsqueeze(3).broadcast_to([128, 2, 2, 4]), ALU.mult, [lbe, lbs], [lbe])
            MSET(lbv[:], 0.0, [lbv])
            for l in range(1, 4):
                TT(lbv[:, :, :, l:l + 1], lbv[:, :, :, l - 1:l], lbe[:, :, :, l:l + 1], ALU.add, [lbv, lbe], [lbv])
            TS(lb_a[:], lbv[:], 1e-20, ALU.max, [lbv], [lb_a])
            TS(lb_na[:], lbv[:], -1.0, ALU.mult, [lbv], [lb_na], s2=1.0, op1=ALU.add)
            TT(lb_c[:], lb_a[:], lb_na[:], ALU.add, [lb_a, lb_na], [lb_c])
            k.barrier()
        k.cur = st

        for l in range(nlayers):
            last = (l == NL - 1)
            lam_init = 0.8 - 0.6 * math.exp(-0.3 * l)
            nq = 16 if last else NT
            nck = 32 if last else NCH

            with ExitStack() as ph:
                k.cur = ph
                pm = k.ps([128, 24, 4], F32)
                pg = k.ps([128, 1024], F32)
                wm32 = [k.sb([128, 8, 512], F32) for _ in range(2)]
                diag = [k.sb([128, 128], F32) for _ in range(2)]
                lt = k.sb([128, 2, 32], F32)
                le = k.sb([128, 2], F32)
                for ch in range(6):
                    wb_ = wm32[ch % 2]
                    DMA("sp", wb_[:], wmod_d.t[l, :, ch * 512:(ch + 1) * 512].rearrange("(kc p) n -> p kc n", p=128), [cin], [wb_])
                    for jj in range(4):
                        for kc in range(8):
                            MM(pm[:, ch * 4 + jj, :], wb_[:, kc, jj * 128:(jj + 1) * 128], sc[:, kc, :], kc == 0, kc == 7, [wb_, sc], [pm])
                TT(mod[:], pm[:], bmod[:, l, :].unsqueeze(2).broadcast_to([128, 24, 4]), ALU.add, [pm, bmod], [mod])
                TS(A1[:], mod[:, 8:16, :], 1.0, ALU.add, [mod], [A1])
                TT(A1[:], A1[:], gainT[:, l, :].unsqueeze(2).broadcast_to([128, 8, 4]), ALU.mult, [A1, gainT], [A1])
                for v in range(3):
                    for t in range(8):
                        dg = diag[t % 2]
                        TS(dg[:], identf[:], mod[:, 16 + t, v:v + 1], ALU.mult, [identf, mod], [dg])
                        MM(pg[:, t * 128:(t + 1) * 128], onesf[:], dg[:], True, True, [onesf, dg], [pg])
                    CP(gate_bc[v][:, 0:512], pg[:, 0:512], [pg], [gate_bc[v]])
                    CP(gate_bc[v][:, 512:1024], pg[:, 512:1024], [pg], [gate_bc[v]], eng="act")
                lv = sv[:, l, 64:192].rearrange("p (a b d) -> p a b d", a=2, b=2)
                TT(lt[:], lv[:, :, 0, :], lv[:, :, 1, :], ALU.mult, [sv], [lt])
                RED(le[:], lt[:], [lt], [le])
                ACT(le[:], le[:], AF.Exp, [le], [le])
                TT(neglam[:], le[:, 1:2], le[:, 0:1], ALU.subtract, [le], [neglam])
                TS(neglam[:], neglam[:], -lam_init, ALU.add, [neglam], [neglam])
                CP(gA[:], sv[:, l, 0:64].rearrange("p (a d) -> p a d", a=2), [sv], [gA])
                TS(gA[:, 0, :], gA[:, 0, :], 32.0 ** -0.5, ALU.mult, [gA], [gA])
                TS(gS[:], sv[:, l, 192:256], 1.0 - lam_init, ALU.mult, [sv], [gS])
                CP(gB[:], sv[:, l, 256:384].rearrange("p (a d) -> p a d", a=2), [sv], [gB])
                TS(gB[:, 0, :], gB[:, 0, :], 0.125, ALU.mult, [gB], [gB])
                k.barrier()
            k.cur = st

            for b in range(2):
                with ExitStack() as ph:
                    k.cur = ph
                    xt = [k.sb([128, D], F32) for _ in range(2)]
                    sq = k.sb([128, D], F32)
                    ssq = [k.sb([128, 1], F32) for _ in range(2)]
                    xn = [k.sb([128, D], BF16) for _ in range(2)]
                    tmp = [k.sb([128, 8, 128], F32) for _ in range(2)]
                    pT = [k.ps([128, 8, 128], BF16) for _ in range(2)]
                    for t in range(NT):
                        v = b if t < 16 else 2
                        x_, s_, n_, p_, m_ = xt[t % 2], ssq[t % 2], xn[t % 2], pT[t % 2], tmp[t % 2]
                        DMA("sp", x_[:], xs_d.t[b, t * 128:(t + 1) * 128, :], [xs_tok[b][t]], [x_])
                        ACT(sq[:], x_[:], AF.Square, [x_], [sq, s_], accum=s_[:])
                        RSQ(s_, s_[:], 128, 1, 1.0 / D)
                        TS(n_[:], x_[:], s_[:, 0:1], ALU.mult, [x_, s_], [n_])
                        for kc in range(8):
                            TR(p_[:, kc, :], n_[:, kc * 128:(kc + 1) * 128], identb[:], [n_, identb], [p_])
                        TT(m_[:], p_[:], A1[:, :, v:v + 1].broadcast_to([128, 8, 128]), ALU.mult, [p_, A1], [m_])
                        TT(hT[:, :, t * 128:(t + 1) * 128], m_[:], mod[:, 0:8, v:v + 1].broadcast_to([128, 8, 128]), ALU.add,
                           [m_, mod], [hT_tok[t]], eng="pool")
                    k.barrier()
                k.cur = st

                for br in range(2):
                    with ExitStack() as ph:
                        k.cur = ph
                        isA = (br == 0)
                        dh = 32 if isA else 64
                        ng = 512 // dh
                        nh = ng // 2
                        nmap = 8 if isA else 4
                        c0 = 0 if isA else 768
                        gq = gA if isA else gB
                        qkT = k.sb([128, 4, T], BF16)
                        vaug = k.sb([128, NT, 4, 65], BF16)
                        sqt = [k.sb([128, 512], F32) for _ in range(2)]
                        ssg = [k.sb([128, 16], F32) for _ in range(2)]
                        qkn = [k.sb([128, 512], F32) for _ in range(2)]
                        qkb = [k.sb([128, 512], BF16) for _ in range(2)]
                        ta = k.sb([128, 256], F32)
                        tb = k.sb([128, 256], F32)
                        tc_ = k.sb([128, 256], F32)
                        td = k.sb([128, 256], F32)
                        Pt = [k.sb([128, 512], BF16) for _ in range(3)]
                        Ot2 = [k.sb([128, 8, 65], F32) for _ in range(2)]
                        rsum = k.sb([128, 8], F32)
                        On = k.sb([128, 8, 64], F32)
                        ot = k.sb([128, 256], F32)
                        o2 = k.sb([128, 256], F32)
                        rs4 = k.sb([128, 4], F32)
                        tht = k.sb([128, 256], F32)
                        sgt = k.sb([128, 256], F32)
                        ygb = k.sb([128, 256], BF16)
                        ybt = [k.sb([128, 2, 128], BF16) for _ in range(2)]
                        if isA:
                            OaT = k.sb([66, 8, 512], F32)
                            MSET(OaT[:], 0.0, [OaT])
                        else:
                            nbt = k.sb([128, 100, 128], BF16)
                            DMA("pool", nbt[:], nb_d.t[l].rearrange("n k q -> k n q"), [cin], [nbt])
                        DMA("pool", WB0[:, :, 0:768], win_d.t[l, :, c0:c0 + 768].rearrange("(kc p) n -> p kc n", p=128), [cin], [WB0])
                        g0 = 2816 + 256 * br
                        wg = WB1[:, 0:2048].rearrange("p (kc n) -> p kc n", kc=8)
                        DMA("pool", wg, win_d.t[l, :, g0:g0 + 256].rearrange("(kc p) n -> p kc n", p=128), [cin], [WB1])
                        MSET(vaug[:, :, :, 64:65], 1.0, [vaug])
                        with ExitStack() as ph2:
                            k.cur = ph2
                            pz0 = [k.ps([128, 512], F32) for _ in range(2)]
                            pz1 = [k.ps([128, 512], F32) for _ in range(2)]
                            ptrp = [k.ps([128, 4, 128], BF16) for _ in range(2)]
                            pend = None
                            for t in range(NT):
                                ts_ = slice(t * 128, (t + 1) * 128)
                                z0, z1, pt_ = pz0[t % 2], pz1[t % 2], ptrp[t % 2]
                                sq_, sg_, qn_, qb = sqt[t % 2], ssg[t % 2], qkn[t % 2], qkb[t % 2]
                                for kc in range(8):
                                    MM(z0[:], hT[:, kc, ts_], WB0[:, kc, 0:512], kc == 0, kc == 7, [hT_tok[t], WB0], [z0])
                                for kc in range(8):
                                    MM(z1[:, 0:256], hT[:, kc, ts_], WB0[:, kc, 512:768], kc == 0, kc == 7, [hT_tok[t], WB0], [z1])
                                ACT(sq_[:], z0[:], AF.Square, [z0], [sq_])
                                RED(sg_[:, 0:ng], sq_[:].rearrange("p (g d) -> p g d", d=dh), [sq_], [sg_])
                                RSQ(sg_, sg_[:, 0:ng], 128, ng, 1.0 / dh)
                                TT(qn_[:].rearrange("p (g d) -> p g d", d=dh), z0[:].rearrange("p (g d) -> p g d", d=dh),
                                   sg_[:, 0:ng].unsqueeze(2).broadcast_to([128, ng, dh]), ALU.mult, [z0, sg_], [qn_])
                                if isA and t < 16:
                                    TT(qn_[:].rearrange("p (a h d) -> p a h d", a=2, d=dh), qn_[:].rearrange("p (a h d) -> p a h d", a=2, d=dh),
                                       gq[:].unsqueeze(2).broadcast_to([128, 2, nh, dh]), ALU.mult, [qn_, gq], [qn_], eng="pool")
                                    qv = qn_[:].rearrange("p (g two d) -> p g two d", two=2, d=16)
                                    ov = qb[:].rearrange("p (g two d) -> p g two d", two=2, d=16)
                                    cosb = rope[:, t, 0:16].unsqueeze(1).broadcast_to([128, 16, 16])
                                    sinb = rope[:, t, 16:32].unsqueeze(1).broadcast_to([128, 16, 16])
                                    t3 = lambda buf: buf[:].rearrange("p (g d) -> p g d", d=16)
                                    TT(t3(ta), qv[:, :, 0, :], cosb, ALU.mult, [qn_, rope], [ta])
                                    TT(t3(tb), qv[:, :, 1, :], sinb, ALU.mult, [qn_, rope], [tb])
                                    TT(ov[:, :, 0, :], t3(ta), t3(tb), ALU.subtract, [ta, tb], [qb])
                                    TT(t3(tc_), qv[:, :, 0, :], sinb, ALU.mult, [qn_, rope], [tc_], eng="pool")
                                    TT(t3(td), qv[:, :, 1, :], cosb, ALU.mult, [qn_, rope], [td], eng="pool")
                                    TT(ov[:, :, 1, :], t3(tc_), t3(td), ALU.add, [tc_, td], [qb], eng="pool")
                                else:
                                    TT(qb[:].rearrange("p (a h d) -> p a h d", a=2, d=dh), qn_[:].rearrange("p (a h d) -> p a h d", a=2, d=dh),
                                       gq[:].unsqueeze(2).broadcast_to([128, 2, nh, dh]), ALU.mult, [qn_, gq], [qb], eng="pool")
                                CP(vaug[:, t, :, 0:64], z1[:, 0:256].rearrange("p (h d) -> p h d", d=64), [z1], [vaug], eng="act")

                                def trs(t=t, pt_=pt_, qb=qb, ts_=ts_):
                                    for i in range(4):
                                        TR(pt_[:, i, :], qb[:, i * 128:(i + 1) * 128], identb[:], [qb, identb], [pt_])
                                    CP(qkT[:, :, ts_], pt_[:], [pt_], [qkT], eng="act")
                                if not PIPE_PREP:
                                    trs()
                                    continue
                                if pend is not None:
                                    pend()
                                pend = trs
                            if PIPE_PREP:
                                pend()
                            k.barrier()
                        k.cur = ph
                        Sp = [k.ps([128, 512], F32) for _ in range(2)]
                        if isA:
                            accT = [k.ps([65, 512], F32) for _ in range(2)]
                            pOa = k.ps([128, 4, 66], F32)
                            pOb = k.ps([128, 4, 66], F32)
                        else:
                            acc = [k.ps([128, 4, 65], F32) for _ in range(2)]
                        pgz = k.ps([128, 256], F32)
                        ptr = k.ps([128, 4, 128], BF16)

                        def finish(qt, Ot):
                            qs_ = slice(qt * 128, (qt + 1) * 128)
                            RECIP(rsum[:, 0:nmap], Ot[:, 0:nmap, 64], [Ot], [rsum])
                            TT(On[:, 0:nmap, :], Ot[:, 0:nmap, 0:64], rsum[:, 0:nmap].unsqueeze(2).broadcast_to([128, nmap, 64]), ALU.mult,
                               [Ot, rsum], [On])
                            if isA:
                                Onv = On[:].rearrange("p (h two) d -> p h two d", two=2)
                                STT(ot[:].rearrange("p (h d) -> p h d", d=64), Onv[:, :, 1, :], neglam[:, 0:1], Onv[:, :, 0, :],
                                    ALU.mult, ALU.add, [On, neglam], [ot])
                                TT(o2[:], ot[:], ot[:], ALU.mult, [ot], [o2], eng="pool")
                                RED(rs4[:], o2[:].rearrange("p (h d) -> p h d", d=64), [o2], [rs4])
                                RSQ(rs4, rs4[:], 128, 4, 1.0 / 64)
                                TT(o2[:].rearrange("p (h d) -> p h d", d=64), ot[:].rearrange("p (h d) -> p h d", d=64),
                                   rs4[:].unsqueeze(2).broadcast_to([128, 4, 64]), ALU.mult, [ot, rs4], [o2])
                                TT(ot[:].rearrange("p (h d) -> p h d", d=64), o2[:].rearrange("p (h d) -> p h d", d=64),
                                   gS[:].unsqueeze(1).broadcast_to([128, 4, 64]), ALU.mult, [o2, gS], [ot], eng="pool")
                                ysrc = ot[:]
                                ybuf = ot
                            else:
                                ysrc = On[:, 0:4, :].rearrange("p h d -> p (h d)")
                                ybuf = On
                            for kc in range(8):
                                MM(pgz[:], hT[:, kc, qs_], wg[:, kc, :], kc == 0, kc == 7, [hT_tok[qt], WB1], [pgz])
                            ACT(tht[:], pgz[:], AF.Tanh, [pgz], [tht], scale=0.5)
                            STT(sgt[:], tht[:], 1.0, pgz[:], ALU.add, ALU.mult, [tht, pgz], [sgt])
                            STT(ygb[:], sgt[:], 0.5, ysrc, ALU.mult, ALU.mult, [sgt, ybuf], [ygb])
                            yb_ = ybt[qt % 2]
                            for i in range(2):
                                TR(ptr[:, i, :], ygb[:, i * 128:(i + 1) * 128], identb[:], [ygb, identb], [ptr])
                            CP(yb_[:], ptr[:, 0:2, :], [ptr], [yb_])
                            DMA("sp", ybr_d.t[b, br, :, :, qs_].rearrange("c p n -> p c n"), yb_[:], [yb_], [ybr_tok[b][br]])

                        gi = 0
                        if isA:
                            qblocks = [(0, 512), (512, 512), (1024, 512), (1536, 512)] + ([] if last else [(2048, 256)])
                            for (q0, qn) in qblocks:
                                chunks = list(range(NT)) if q0 < LAT else [16, 17]
                                nci = len(chunks)
                                its = [(m, ci, c) for m in range(8) for ci, c in enumerate(chunks)]

                                def qk(i):
                                    m, ci, c = its[i]
                                    slot, pb = m // 4, 32 * (m % 4)
                                    S_ = Sp[(gi + i) % 2]
                                    P_ = Pt[(gi + i) % 3]
                                    MM(S_[:, 0:qn], qkT[pb:pb + 32, 2 + slot, c * 128:(c + 1) * 128], qkT[pb:pb + 32, slot, q0:q0 + qn],
                                       True, True, [qkT], [S_], tp=(pb, 0))
                                    ACT(P_[:, 0:qn], S_[:, 0:qn], AF.Exp, [S_], [P_])

                                def pv(i):
                                    m, ci, c = its[i]
                                    a_ = accT[m % 2]
                                    P_ = Pt[(gi + i) % 3]
                                    MM(a_[:, 0:qn], vaug[:, c, m // 2, :], P_[:, 0:qn], ci == 0, ci == nci - 1, [vaug, P_], [a_])
                                    if ci == nci - 1:
                                        CP(OaT[0:65, m, 0:qn], a_[:, 0:qn], [a_], [OaT])

                                if PIPE_A:
                                    qk(0)
                                for i in range(len(its)):
                                    if PIPE_A:
                                        if i + 1 < len(its):
                                            qk(i + 1)
                                    else:
                                        qk(i)
                                    pv(i)
                                gi += len(its)
                                for j in range(qn // 128):
                                    qt = q0 // 128 + j
                                    Ot = Ot2[qt % 2]
                                    for m in range(8):
                                        pO_ = pOa if m < 4 else pOb
                                        TR(pO_[:, m % 4, :], OaT[:, m, j * 128:(j + 1) * 128], identf[0:66, 0:66], [OaT, identf], [pO_])
                                    CP(Ot[:, 0:4, :], pOa[:, :, 0:65], [pOa], [Ot])
                                    CP(Ot[:, 4:8, :], pOb[:, :, 0:65], [pOb], [Ot])
                                    finish(qt, Ot)
                        else:
                            work = []
                            for qt in range(nq):
                                if qt >= 16:
                                    chunks = [(16, None), (17, None)]
                                else:
                                    cs = min(max(qt - 2, 0), 11)
                                    typ = 0 if qt == 0 else 1 if qt == 1 else 3 if qt == 14 else 4 if qt == 15 else 2
                                    chunks = [(cs + i, typ * 5 + i) for i in range(5)] + [(16, None), (17, None)]
                                groups = [chunks[i:i + 4] for i in range(0, len(chunks), 4)]
                                for m in range(4):
                                    for gidx, g in enumerate(groups):
                                        work.append((qt, m, g, gidx == 0, gidx == len(groups) - 1))

                            def scb(i):
                                qt, m, g, fg, lg = work[i]
                                slot, pb = m // 2, 64 * (m % 2)
                                Sb = Sp[i % 2]
                                Pb = Pt[i % 3]
                                S_ = Sb[:].rearrange("p (c q) -> p c q", q=128)
                                P_ = Pb[:].rearrange("p (c q) -> p c q", q=128)
                                for j, (c, bi) in enumerate(g):
                                    kT_ = qkT[pb:pb + 64, 2 + slot, c * 128:(c + 1) * 128]
                                    qT_ = qkT[pb:pb + 64, slot, qt * 128:(qt + 1) * 128]
                                    if bi is not None:
                                        MM(S_[:, j, :], identb[:], nbt[:, bi * 4 + m, :], True, False, [identb, nbt], [Sb])
                                        MM(S_[:, j, :], kT_, qT_, False, True, [qkT], [Sb], tp=(pb, 0))
                                    else:
                                        MM(S_[:, j, :], kT_, qT_, True, True, [qkT], [Sb], tp=(pb, 0))
                                ACT(P_[:, 0:len(g), :], S_[:, 0:len(g), :], AF.Exp, [Sb], [Pb])

                            def pvb(i):
                                qt, m, g, fg, lg = work[i]
                                Pb = Pt[i % 3]
                                P_ = Pb[:].rearrange("p (c q) -> p c q", q=128)
                                ab = acc[qt % 2]
                                for j, (c, bi) in enumerate(g):
                                    MM(ab[:, m, :], P_[:, j, :], vaug[:, c, m, :], fg and j == 0, lg and j == len(g) - 1, [Pb, vaug], [ab])
                                if lg and m == 3:
                                    Ot = Ot2[qt % 2]
                                    CP(Ot[:, 0:4, :], ab[:], [ab], [Ot])
                                    finish(qt, Ot)

                            if PIPE_B:
                                scb(0)
                            for i in range(len(work)):
                                if PIPE_B:
                                    if i + 1 < len(work):
                                        scb(i + 1)
                                else:
                                    scb(i)
                                pvb(i)
                        k.barrier()
                    k.cur = st

                DMA("pool", WB0[:], win_d.t[l, :, 1536:2560].rearrange("(kc p) n -> p kc n", p=128), [cin], [WB0])
                wg = WB1[:, 0:2048].rearrange("p (kc n) -> p kc n", kc=8)
                DMA("pool", wg, win_d.t[l, :, 3328:3584].rearrange("(kc p) n -> p kc n", p=128), [cin], [WB1])
                for ct in range(2):
                    with ExitStack() as ph:
                        k.cur = ph
                        KT = [k.sb([128, T], BF16) for _ in range(2)]
                        QBD = [k.sb([128, NCH, 128], BF16) for _ in range(2)]
                        Ktm = k.sb([64, NCH, 128], BF16)
                        Vc = k.sb([64, NCH, 128], BF16)
                        Spr = [k.sb([128, NCH, 64], BF16) for _ in range(2)]
                        er = [k.sb([128, NCH], F32) for _ in range(2)]
                        Sst = k.sb([128, 64], F32)
                        Sp32 = k.sb([128, 64], F32)
                        tht = k.sb([128, 512], F32)
                        qh = k.sb([128, 512], F32)
                        gg = k.sb([128, 512], F32)
                        kk = k.sb([128, 512], F32)
                        uu = k.sb([128, 512], F32)
                        Bc = k.sb([128, 512], F32)
                        tmpb = k.sb([128, 512], F32)
                        tot = k.sb([128, 8], F32)
                        E1 = k.sb([128, 512], F32)
                        E2 = k.sb([128, 512], F32)
                        Am = k.sb([64, 2, 128], BF16)
                        of_ = k.sb([64, 512], F32)
                        o2 = k.sb([64, 512], F32)
                        rs8 = k.sb([64, 8], F32)
                        th2 = k.sb([64, 512], F32)
                        sg2 = k.sb([64, 512], F32)
                        yg2 = k.sb([64, 512], BF16)
                        ybt = [k.sb([128, 256], BF16) for _ in range(2)]
                        pq = [k.ps([128, 512], F32) for _ in range(2)]
                        pv = k.ps([64, 4, 128], F32)
                        pbk = k.ps([128, 1024], BF16)
                        ptk = pbk[0:64, :].rearrange("p (a b) -> p a b", b=128)
                        pTy = pbk[:, 0:256]
                        pu = k.ps([128, 128], F32)
                        pa = k.ps([64, 2, 128], F32)
                        po = k.ps([64, 4, 128], F32)
                        pgz = k.ps([64, 4, 128], F32)
                        for d in range(2):
                            MSET(QBD[d][:], 0.0, [QBD[d]])
                        for c4 in range(0, NCH, 4):
                            for i in range(4):
                                c = c4 + i
                                for kc in range(8):
                                    MM(pv[:, i, :], hT[:, kc, c * 64:(c + 1) * 64], WB0[:, kc, 768 + ct * 128:768 + (ct + 1) * 128],
                                       kc == 0, kc == 7, [hT_tok[c // 2], WB0], [pv])
                            CP(Vc[:, c4:c4 + 4, :], pv[:], [pv], [Vc], eng="act")
                        pi = 0
                        for (t0, n) in BLKS:
                            nc_ = n // 64
                            cb = t0 // 64
                            bs_ = slice(t0, t0 + n)
                            p_ = pq[pi % 2]
                            pi += 1
                            for kc in range(8):
                                MM(p_[:, 0:n], WB0[:, kc, ct * 128:(ct + 1) * 128], hT[:, kc, bs_], kc == 0, kc == 7, hts(t0, n) + [WB0], [p_])
                            ACT(tht[:, 0:n], p_[:, 0:n], AF.Tanh, [p_], [tht], scale=0.5)
                            STT(qh[:, 0:n], tht[:, 0:n], 1.0, p_[:, 0:n], ALU.add, ALU.mult, [tht, p_], [qh])
                            for d in range(2):
                                p_ = pq[pi % 2]
                                pi += 1
                                cc = 256 + d * 256 + ct * 128
                                for kc in range(8):
                                    MM(p_[:, 0:n], WB0[:, kc, cc:cc + 128], hT[:, kc, bs_], kc == 0, kc == 7, hts(t0, n) + [WB0], [p_])
                                ACT(uu[:, 0:n], p_[:, 0:n], AF.Exp, [p_], [uu], scale=-1.0)
                                ACT(E2[:, 0:n], uu[:, 0:n], AF.Ln, [uu], [E2], bias=1.0)
                                ACT(E1[:, 0:n], uu[:, 0:n], AF.Ln, [uu, lb_a, lb_c], [E1], scale=lb_a[:, d, ct, l:l + 1],
                                    bias=lb_c[:, d, ct, l:l + 1])
                                TT(gg[:, 0:n], E1[:, 0:n], E2[:, 0:n], ALU.subtract, [E1, E2], [gg], eng="pool")
                                TT(tht[:, 0:n], p_[:, 0:n], E2[:, 0:n], ALU.add, [p_, E2], [tht])
                                ACT(kk[:, 0:n], tht[:, 0:n], AF.Exp, [tht], [kk], scale=-1.0)
                                k.op("dve", [segm, gg], [Bc], lambda e, n=n: e.tensor_tensor_scan(
                                    out=Bc[:, 0:n], data0=segm[:, 0:n], data1=gg[:, 0:n], initial=0.0, op0=ALU.mult, op1=ALU.add))
                                B3 = Bc[:, 0:n].rearrange("p (c s) -> p c s", s=64)
                                CP(tot[:, 0:nc_], B3[:, :, 63], [Bc], [tot])
                                if d == 1:
                                    TT(tmpb[:, 0:n], gg[:, 0:n], Bc[:, 0:n], ALU.subtract, [gg, Bc], [tmpb], eng="pool")
                                    TT(B3, tmpb[:, 0:n].rearrange("p (c s) -> p c s", s=64),
                                       tot[:, 0:nc_].unsqueeze(2).broadcast_to([128, nc_, 64]), ALU.add, [tmpb, tot], [Bc])
                                ACT(er[d][:, cb:cb + nc_], tot[:, 0:nc_], AF.Exp, [tot], [er[d]], scale=0.5)
                                STT(tmpb[:, 0:n].rearrange("p (c s) -> p c s", s=64), tot[:, 0:nc_].unsqueeze(2).broadcast_to([128, nc_, 64]),
                                    -0.5, B3, ALU.mult, ALU.add, [tot, Bc], [tmpb])
                                TS(tmpb[:, 0:n], tmpb[:, 0:n], 43.0, ALU.min, [tmpb], [tmpb], s2=-43.0, op1=ALU.max, eng="pool")
                                ACT(E1[:, 0:n], tmpb[:, 0:n], AF.Exp, [tmpb], [E1])
                                ACT(E2[:, 0:n], tmpb[:, 0:n], AF.Exp, [tmpb], [E2], scale=-1.0)
                                for hh in range(2):
                                    ps_ = slice(64 * hh, 64 * hh + 64)
                                    STT(QBD[d][ps_, cb:cb + nc_, 64 * hh:64 * hh + 64], qh[ps_, 0:n].rearrange("p (c s) -> p c s", s=64), 0.5,
                                        E1[ps_, 0:n].rearrange("p (c s) -> p c s", s=64), ALU.mult, ALU.mult, [qh, E1], [QBD[d]])
                                STT(KT[d][:, bs_], kk[:, 0:n], lb_na[:, d, ct, l:l + 1], E2[:, 0:n], ALU.mult, ALU.mult, [kk, E2, lb_na], [KT[d]])
                        for d in range(2):
                            for c8 in range(0, NCH, 4):
                                for i in range(4):
                                    c = c8 + i
                                    TR(ptk[:, i, :], KT[d][:, c * 64:(c + 1) * 64], identb[:], [KT[d], identb], [pbk])
                                CP(Ktm[:, c8:c8 + 4, :], ptk[:, 0:4, :], [pbk], [Ktm], eng="act")
                            MSET(Sst[:], 0.0, [Sst], eng="dve")
                            order = ([32, 33, 34, 35] + list(range(32))) if d == 0 else ([35, 34, 33, 32] + list(range(31, -1, -1)))
                            for c in order:
                                MM(pu[:], Ktm[:, c, :], Vc[:, c, :], True, True, [Ktm, Vc], [pu])
                                e_ = er[d][:, c:c + 1]
                                TS(Sp32[:], Sst[:], e_, ALU.mult, [Sst, er[d]], [Sp32])
                                CP(Spr[d][:, c, :], Sp32[:], [Sp32], [Spr[d]], eng="pool")
                                for hh in range(2):
                                    ps_ = slice(64 * hh, 64 * hh + 64)
                                    TT(Sst[ps_, :], Sp32[ps_, :], pu[ps_, 64 * hh:64 * hh + 64], ALU.add, [Sp32, pu], [Sst])
                                TS(Sst[:], Sst[:], e_, ALU.mult, [Sst, er[d]], [Sst])
                        cm4 = cmask[:].rearrange("p (d h t) -> p d h t", d=2, h=2)
                        for c4 in range(0, nck, 4):
                            for i in range(4):
                                c = c4 + i
                                for d in range(2):
                                    MM(pa[:, d, :], KT[d][:, c * 64:(c + 1) * 64], QBD[d][:, c, :], True, True, [KT[d], QBD[d]], [pa])
                                TT(Am[:].rearrange("p d (h t) -> p d h t", h=2), pa[:].rearrange("p d (h t) -> p d h t", h=2), cm4, ALU.mult,
                                   [pa, cmask], [Am])
                                for hh in range(2):
                                    hs_ = slice(64 * hh, 64 * hh + 64)
                                    for d in range(2):
                                        MM(po[:, i, hs_], Am[:, d, hs_], Vc[:, c, hs_], d == 0, False, [Am, Vc], [po])
                                        MM(po[:, i, hs_], QBD[d][:, c, hs_], Spr[d][:, c, :], False, d == 1, [QBD[d], Spr[d]], [po])
                                for kc in range(8):
                                    MM(pgz[:, i, :], hT[:, kc, c * 64:(c + 1) * 64], wg[:, kc, ct * 128:(ct + 1) * 128], kc == 0, kc == 7,
                                       [hT_tok[c // 2], WB1], [pgz])
                            pof = po[:].rearrange("p c n -> p (c n)")
                            pgf = pgz[:].rearrange("p c n -> p (c n)")
                            CP(of_[:], pof, [po], [of_], eng="act")
                            ACT(o2[:], pof, AF.Square, [po], [o2])
                            RED(rs8[:], o2[:].rearrange("p (g d) -> p g d", d=64), [o2], [rs8])
                            RSQ(rs8, rs8[:], 64, 8, 1.0 / 64)
                            TT(o2[:].rearrange("p (g d) -> p g d", d=64), of_[:].rearrange("p (g d) -> p g d", d=64),
                               rs8[:].unsqueeze(2).broadcast_to([64, 8, 64]), ALU.mult, [of_, rs8], [o2])
                            TT(of_[:].rearrange("p (g d) -> p g d", d=64), o2[:].rearrange("p (g d) -> p g d", d=64),
                               sv[0:64, l, 384:448].unsqueeze(1).broadcast_to([64, 8, 64]), ALU.mult, [o2, sv], [of_], eng="pool")
                            ACT(th2[:], pgf, AF.Tanh, [pgz], [th2], scale=0.5)
                            STT(sg2[:], th2[:], 1.0, pgf, ALU.add, ALU.mult, [th2, pgz], [sg2])
                            STT(yg2[:], sg2[:], 0.5, of_[:], ALU.mult, ALU.mult, [sg2, of_], [yg2])
                            yb_ = ybt[(c4 // 4) % 2]
                            for i in range(4):
                                TR(pTy[:, i * 64:(i + 1) * 64], yg2[:, i * 128:(i + 1) * 128], identb[0:64, 0:64], [yg2, identb], [pbk])
                            CP(yb_[:], pTy, [pbk], [yb_], eng="act")
                            DMA("sp", ybr_d.t[b, 2, ct, :, c4 * 64:c4 * 64 + 256], yb_[:], [yb_], [ybr_tok[b][2]])
                        k.barrier()
                    k.cur = st

                with ExitStack() as ph:
                    k.cur = ph
                    uT = k.sb([128, 2, T], BF16)
                    PQ = k.sb([128, NT, 2, 256], BF16)
                    tab = [k.sb([128, 16, 2, 256], BF16) for _ in range(2)]
                    tht = k.sb([128, 256], F32)
                    sgt = k.sb([128, 256], F32)
                    ybt = [k.sb([128, 2, 256], BF16) for _ in range(2)]
                    pq = [k.ps([128, 512], F32) for _ in range(2)]
                    pp = [k.ps([128, 2, 256], F32) for _ in range(2)]
                    py = [k.ps([128, 256], F32) for _ in range(2)]
                    pgz = [k.ps([128, 256], F32) for _ in range(2)]
                    DMA("pool", WB0[:, :, 0:256], win_d.t[l, :, 2560:2816].rearrange("(kc p) n -> p kc n", p=128), [cin], [WB0])
                    wg = WB1[:, 0:2048].rearrange("p (kc n) -> p kc n", kc=8)
                    DMA("pool", wg, win_d.t[l, :, 3584:3840].rearrange("(kc p) n -> p kc n", p=128), [cin], [WB1])
                    pi = 0
                    for ct in range(2):
                        for (t0, n) in BLKS:
                            p_ = pq[pi % 2]
                            pi += 1
                            for kc in range(8):
                                MM(p_[:, 0:n], WB0[:, kc, ct * 128:(ct + 1) * 128], hT[:, kc, t0:t0 + n], kc == 0, kc == 7, hts(t0, n) + [WB0], [p_])
                            CP(uT[:, ct, t0:t0 + n], p_[:, 0:n], [p_], [uT], eng="act")
                    for t in range(NT):
                        p_ = pp[t % 2]
                        for ct in range(2):
                            MM(p_[:, ct, :], uT[:, ct, t * 128:(t + 1) * 128], cs64[:], True, True, [uT, cs64], [p_])
                        CP(PQ[:, t, :, :], p_[:], [p_], [PQ], eng=("act" if t % 2 else "dve"))
                    it = 0
                    nblk = 8 if last else 9
                    for nb in range(nblk):
                        if nb < 8:
                            tb_ = tab[nb % 2]
                            DMA("sp", tb_[:], dftl_d.t[:, :, :, nb * 256:(nb + 1) * 256], [cin], [tb_])
                            tcs = list(range(16))
                            tsrc = lambda tc, cs: tb_[:, tc, cs, :]
                            tread = [tb_]
                            t0 = nb * 256
                        else:
                            tcs = [16, 17]
                            tsrc = lambda tc, cs: dftc[:, tc - 16, cs, :]
                            tread = [dftc]
                            t0 = LAT
                        yb_ = ybt[nb % 2]
                        for ct in range(2):
                            y_ = py[it % 2]
                            g_ = pgz[it % 2]
                            it += 1
                            nmm = len(tcs) * 2
                            j = 0
                            for tc in tcs:
                                for cs in range(2):
                                    MM(y_[:], PQ[:, tc, ct, cs * 128:(cs + 1) * 128], tsrc(tc, cs), j == 0, j == nmm - 1, [PQ] + tread, [y_])
                                    j += 1
                            for kc in range(8):
                                MM(g_[:], wg[:, kc, ct * 128:(ct + 1) * 128], hT[:, kc, t0:t0 + 256], kc == 0, kc == 7, hts(t0, 256) + [WB1], [g_])
                            ACT(tht[:], g_[:], AF.Tanh, [g_], [tht], scale=0.5)
                            STT(sgt[:], tht[:], 1.0, g_[:], ALU.add, ALU.mult, [tht, g_], [sgt])
                            STT(yb_[:, ct, :], sgt[:], 0.5, y_[:], ALU.mult, ALU.mult, [sgt, y_], [yb_])
                        DMA("sp", ybr_d.t[b, 3, :, :, t0:t0 + 256].rearrange("c p n -> p c n"), yb_[:], [yb_], [ybr_tok[b][3]])
                    k.barrier()
                k.cur = st

                with ExitStack() as ph:
                    k.cur = ph
                    ybS = k.sb([128, 4, 2, T], BF16)
                    accT = k.sb([128, 8, T], BF16)
                    wu = [k.sb([128, 4, 2, 128], BF16) for _ in range(2)]
                    wm_tok = [Buf(), Buf()]
                    tht = [k.sb([128, 512], F32) for _ in range(2)]
                    tmpm = [k.sb([128, 512], F32) for _ in range(2)]
                    accf = [k.sb([128, 512], F32) for _ in range(2)]
                    xt = [k.sb([128, D], F32) for _ in range(2)]
                    xo = [k.sb([128, D], F32) for _ in range(2)]
                    tmo = [k.sb([128, 512], F32) for _ in range(2)]
                    pM = [k.ps([128, 512], F32) for _ in range(2)]
                    pU = [k.ps([128, 512], F32) for _ in range(2)]
                    pO = [k.ps([128, 512], F32) for _ in range(2)]
                    for i in range(4):
                        DMA("sp", ybS[:, i, :, :], ybr_d.t[b, i].rearrange("c p n -> p c n"), [ybr_tok[b][i]], [ybS])
                    DMA("pool", WB0[:], wout_d.t[l].rearrange("(kc p) n -> p kc n", p=128), [cin], [WB0])
                    mblks = BLKS[:4] if last else BLKS
                    im = 0
                    def wload(ft):
                        fs_ = slice(ft * 128, (ft + 1) * 128)
                        wmv = WB1[:, (ft % 2) * 4096:(ft % 2 + 1) * 4096].rearrange("p (i kc n) -> p i kc n", i=4, kc=8)
                        DMA("pool", wmv, wmg_d.t[l, :, :, fs_].rearrange("i (kc p) n -> p i kc n", p=128), [cin], [wm_tok[ft % 2]])
                        DMA("pool", wu[ft % 2][:], wup_d.t[l, :, :, fs_].rearrange("i (fc p) n -> p i fc n", p=128), [cin], [wu[ft % 2]])

                    if PREFETCH_M:
                        wload(0)
                    for ft in range(8):
                        fs_ = slice(ft * 128, (ft + 1) * 128)
                        wmv = WB1[:, (ft % 2) * 4096:(ft % 2 + 1) * 4096].rearrange("p (i kc n) -> p i kc n", i=4, kc=8)
                        wmt = wm_tok[ft % 2]
                        wu_ = wu[ft % 2]
                        if not PREFETCH_M:
                            wload(ft)
                        elif ft + 1 < 8:
                            wload(ft + 1)
                        for bi_, (t0, n) in enumerate(mblks):
                            af = accf[bi_ % 2]
                            for i in range(4):
                                m_ = pM[im % 2]
                                u_ = pU[im % 2]
                                th_ = tht[im % 2]
                                tm_ = tmpm[im % 2]
                                im += 1
                                for kc in range(8):
                                    MM(m_[:, 0:n], wmv[:, i, kc, :], hT[:, kc, t0:t0 + n], kc == 0, kc == 7, hts(t0, n) + [wmt], [m_])
                                for fc in range(2):
                                    MM(u_[:, 0:n], wu_[:, i, fc, :], ybS[:, i, fc, t0:t0 + n], fc == 0, fc == 1, [wu_, ybS], [u_])
                                ACT(th_[:, 0:n], m_[:, 0:n], AF.Tanh, [m_], [th_], scale=0.5)
                                if i == 0:
                                    STT(af[:, 0:n], th_[:, 0:n], 1.0, u_[:, 0:n], ALU.add, ALU.mult, [th_, u_], [af])
                                else:
                                    STT(tm_[:, 0:n], th_[:, 0:n], 1.0, u_[:, 0:n], ALU.add, ALU.mult, [th_, u_], [tm_])
                                    TT(af[:, 0:n], af[:, 0:n], tm_[:, 0:n], ALU.add, [af, tm_], [af], eng="pool")
                            ACT(accT[:, ft, t0:t0 + n], af[:, 0:n], AF.Copy, [af], [accT], scale=0.5)
                    ntile = 16 if last else NT
                    io = 0
                    for t in range(ntile):
                        v = b if t < 16 else 2
                        x_ = xt[t % 2]
                        o_ = xo[t % 2]
                        DMA("sp", x_[:], xs_d.t[b, t * 128:(t + 1) * 128, :], [xs_tok[b][t]], [x_])
                        for hf in range(2):
                            p_ = pO[io % 2]
                            tm_ = tmo[io % 2]
                            io += 1
                            cs_ = slice(hf * 512, (hf + 1) * 512)
                            for kc in range(8):
                                MM(p_[:], accT[:, kc, t * 128:(t + 1) * 128], WB0[:, kc, cs_], kc == 0, kc == 7, [accT, WB0], [p_])
                            TT(tm_[:], p_[:], gate_bc[v][:, cs_], ALU.mult, [p_, gate_bc[v]], [tm_])
                            TT(o_[:, cs_], tm_[:], x_[:, cs_], ALU.add, [tm_, x_], [o_], eng="pool")
                        if last:
                            DMA("sp", out_d.t[b, t * 128:(t + 1) * 128, :], o_[:], [o_], [xs_tok[b][t]])
                        else:
                            DMA("sp", xs_d.t[b, t * 128:(t + 1) * 128, :], o_[:], [o_], [xs_tok[b][t]])
                    k.barrier()
                k.cur = st
        k.emit()
    return nc


def _consts():
    bf = ml_dtypes.bfloat16
    c = {}
    c["identb"] = np.eye(128, dtype=np.float32).astype(bf)
    c["identf"] = np.eye(128, dtype=np.float32)
    s = np.arange(64)[:, None]
    t = np.arange(64)[None, :]
    mf = (s <= t).astype(np.float32)
    mb = (s >= t).astype(np.float32)
    cm = np.stack([np.stack([mf, mf], 0), np.stack([mb, mb], 0)], 0)
    c["cmask"] = np.ascontiguousarray(cm.transpose(2, 0, 1, 3).reshape(64, 256))
    n = np.arange(LAT)
    row = (n // 64).astype(np.float32)
    col = (n % 64).astype(np.float32)
    inv = (10000.0 ** (-np.arange(0, 16, 2, dtype=np.float32) / 16)).astype(np.float32)
    ang = np.concatenate([row[:, None] * inv, col[:, None] * inv], -1).astype(np.float32)
    rp = np.concatenate([np.cos(ang), np.sin(ang)], -1).astype(np.float32)
    c["rope"] = np.ascontiguousarray(rp.reshape(16, 128, 32).transpose(1, 0, 2))
    e = np.arange(64)
    a64 = 2 * np.pi * np.outer(e, e) / 64
    C64 = np.cos(a64) / 8.0
    S64 = np.sin(a64) / 8.0
    cs = np.zeros((128, 256), np.float64)
    for g in range(2):
        cs[g * 64:(g + 1) * 64, g * 64:(g + 1) * 64] = C64
        cs[g * 64:(g + 1) * 64, 128 + g * 64:128 + (g + 1) * 64] = S64
    c["cs64"] = cs.astype(np.float32).astype(bf)

    def dft(N):
        tt = np.arange(N)
        a = 2 * np.pi * ((np.outer(tt, tt)) % N) / N
        Cn = np.cos(a) / np.sqrt(N)
        Sn = -np.sin(a) / np.sqrt(N)
        tab = np.stack([Cn, Sn], 1)
        return np.ascontiguousarray(tab.reshape(N // 128, 128, 2, N).transpose(1, 0, 2, 3)).astype(np.float32).astype(bf)

    c["dftL"] = dft(LAT)
    c["dftC"] = dft(CTX)
    return c


def _nbias_index():
    rows, W, wh, ww = 32, 64, 8, 16
    idx = np.full((5, 5, 128, 128, 2), -1, np.int64)
    for typ, qt in enumerate([0, 1, 5, 14, 15]):
        cs = min(max(qt - 2, 0), 11)
        for i in range(5):
            kc = cs + i
            for kk in range(128):
                kr, kcol = 2 * kc + kk // 64, kk % 64
                for q in range(128):
                    r, cq = 2 * qt + q // 64, q % 64
                    r0 = min(max(r - wh // 2, 0), rows - wh)
                    c0 = min(max(cq - ww // 2, 0), W - ww)
                    if r0 <= kr < r0 + wh and c0 <= kcol < c0 + ww:
                        idx[typ, i, kk, q, 0] = kr - r + wh - 1
                        idx[typ, i, kk, q, 1] = min(max(kcol - cq, 1 - ww), ww - 1) + ww - 1
    return idx


_CACHE = {}


def kernel(x, c, ctx, c_ctx, norm_gain, w_mod, b_mod, w_in, da_qk_gain, da_lambda, da_subln_gain,
           na_qk_gain, na_rpb, hg_lb_logits, hg_norm_gain, w_up, w_merge, w_out):
    f = lambda a: np.ascontiguousarray(np.asarray(a, dtype=np.float32))
    x, c, ctx, c_ctx = f(x), f(c), f(ctx), f(c_ctx)
    if "nc" not in _CACHE:
        _CACHE["nc"] = build()
        _CACHE["consts"] = _consts()
        _CACHE["nbidx"] = _nbias_index()
    nc = _CACHE["nc"]
    shared = dict(_CACHE["consts"])
    shared["w_mod"] = f(w_mod)
    shared["w_in"] = f(w_in)
    shared["w_up"] = f(w_up)
    shared["w_merge"] = f(w_merge)
    shared["w_out"] = f(w_out)
    shared["b_modT"] = np.ascontiguousarray(f(b_mod).reshape(NL, 24, 128).transpose(2, 0, 1))
    shared["gainT"] = np.ascontiguousarray(f(norm_gain).reshape(NL, 8, 128).transpose(2, 0, 1))
    smallv = np.concatenate([f(da_qk_gain).reshape(NL, 64), f(da_lambda).reshape(NL, 128), f(da_subln_gain).reshape(NL, 64),
                             f(na_qk_gain).reshape(NL, 128), f(hg_norm_gain).reshape(NL, 64)], -1)
    shared["smallv"] = np.ascontiguousarray(np.broadcast_to(smallv[None], (128, NL, 448)))
    shared["lbT"] = np.ascontiguousarray(f(hg_lb_logits).reshape(2, NL, 2, 128).transpose(3, 0, 2, 1))
    idx = _CACHE["nbidx"]
    rpb = f(na_rpb)
    inw = idx[..., 0] >= 0
    dr = np.where(inw, idx[..., 0], 0)
    dc = np.where(inw, idx[..., 1], 0)
    gath = rpb[:, :, dr, dc]
    gath = np.where(inw[None, None], gath, np.float32(NEG)).astype(np.float32)
    shared["nbias"] = np.ascontiguousarray(gath.transpose(0, 2, 3, 1, 4, 5).reshape(NL, 100, 128, 128))
    in_maps = []
    for i in range(NCORE):
        m = dict(shared)
        m["x"] = np.ascontiguousarray(x[2 * i:2 * i + 2])
        m["ctx"] = np.ascontiguousarray(ctx[2 * i:2 * i + 2])
        cvec = np.stack([c[2 * i], c[2 * i + 1], c_ctx, np.zeros_like(c_ctx)], 0)
        m["cv"] = np.ascontiguousarray(cvec.reshape(4, 8, 128).transpose(2, 1, 0))
        in_maps.append(m)
    res = run_bass_kernel_spmd(nc, in_maps, core_ids=list(range(NCORE)))
    out = np.concatenate([np.asarray(r["out"], dtype=np.float32) for r in res.results], axis=0)
    return out
```

```python
import math
from contextlib import ExitStack

import numpy as np
import ml_dtypes

import concourse.bass as bass
import concourse.mybir as mybir
from concourse.bass_utils import run_bass_kernel_spmd

F32 = mybir.dt.float32
BF16 = mybir.dt.bfloat16
ALU = mybir.AluOpType
AF = mybir.ActivationFunctionType
AX = mybir.AxisListType

NL = 4
D = 1024
NCORE = 8
LAT = 2048
CTX = 256
T = LAT + CTX
NT = T // 128
NCH = T // 64
EPS = 1e-6
BLKS = [(0, 512), (512, 512), (1024, 512), (1536, 512), (2048, 256)]
NEG = -30000.0


class Buf:
    def __init__(self, t=None, name=""):
        self.t = t
        self.name = name
        self.last_w = None
        self.readers = {}

    def __getitem__(self, k):
        return self.t[k]


class K:
    ENG = ("pe", "act", "dve", "pool", "sp")

    def __init__(self, nc, stack, n_dma_sems=48):
        self.nc = nc
        self.stack = stack
        self.cur = stack
        self.streams = {e: [] for e in self.ENG}
        self.sem = {}
        self.cnt = {}
        for e in self.ENG:
            self.sem[e] = stack.enter_context(nc.semaphore("s_" + e))
            self.cnt[e] = 0
        self.ndma = n_dma_sems
        for i in range(n_dma_sems):
            self.sem[("d", i)] = stack.enter_context(nc.semaphore("d%d" % i))
            self.cnt[("d", i)] = 0
        self.dma_rr = 0
        self.known = {e: {} for e in self.ENG}
        self.nbuf = 0

    def sb(self, shape, dtype, name=None):
        self.nbuf += 1
        name = "%s_%d" % (name or "sb", self.nbuf)
        t = self.cur.enter_context(self.nc.sbuf_tensor(name, list(shape), dtype))
        return Buf(t, name)

    def ps(self, shape, dtype, name=None):
        self.nbuf += 1
        name = "%s_%d" % (name or "ps", self.nbuf)
        t = self.cur.enter_context(self.nc.psum_tensor(name, list(shape), dtype))
        return Buf(t, name)

    def dram(self, name, shape, dtype, kind="Internal"):
        t = self.nc.dram_tensor(name, list(shape), dtype, kind=kind)
        return Buf(t.ap(), name)

    def _need(self, eng, reads, writes):
        need = {}

        def add(dep):
            if dep is None:
                return
            k, v = dep
            if need.get(k, 0) < v:
                need[k] = v

        for b in reads:
            add(b.last_w)
        for b in writes:
            add(b.last_w)
            for k, v in b.readers.items():
                add((k, v))
        waits = []
        kn = self.known[eng]
        for k, v in need.items():
            if k == "pe" and eng == "pe":
                continue
            if kn.get(k, 0) >= v:
                continue
            kn[k] = v
            waits.append((k, v))
        return waits

    def op(self, eng, reads, writes, fn, inc=True):
        waits = self._need(eng, reads, writes)
        if inc:
            self.cnt[eng] += 1
            v = self.cnt[eng]
            self.streams[eng].append((waits, fn, (eng, 1)))
        else:
            v = self.cnt[eng] + 1
            self.streams[eng].append((waits, fn, None))
        for b in reads:
            if b.readers.get(eng, 0) < v:
                b.readers[eng] = v
        for b in writes:
            b.last_w = (eng, v)
            b.readers = {}
        return v

    def dma(self, q, reads, writes, fn):
        i = self.dma_rr
        self.dma_rr = (self.dma_rr + 1) % self.ndma
        key = ("d", i)
        waits = self._need(q, reads, writes)
        prev = self.cnt[key]
        if prev > 0 and self.known[q].get(key, 0) < prev:
            self.known[q][key] = prev
            waits.append((key, prev))
        self.cnt[key] += 16
        v = self.cnt[key]
        self.streams[q].append((waits, fn, (key, 16)))
        for b in reads:
            b.readers[key] = v
        for b in writes:
            b.last_w = (key, v)
            b.readers = {}
        return key, v

    def barrier(self):
        for e in self.ENG:
            waits = []
            for k, c in self.cnt.items():
                if c > 0 and (k != e or e in ("act", "dve", "pool")) and self.known[e].get(k, 0) < c:
                    self.known[e][k] = c
                    waits.append((k, c))
            if waits:
                self.streams[e].append((waits, None, None))

    def emit(self):
        nc = self.nc
        waits = [(k, c) for k, c in self.cnt.items() if c > 0 and k != "sp"]
        self.streams["sp"].append((waits, None, None))
        with nc.Block() as block:
            def mk(ename):
                def body(engine):
                    for ws, fn, inc in self.streams[ename]:
                        for kk, v in ws:
                            engine.wait_ge(self.sem[kk], v)
                        if fn is not None:
                            ins = fn(engine)
                            if inc is not None:
                                ins.then_inc(self.sem[inc[0]], inc[1])
                return body
            block.tensor(mk("pe"))
            block.scalar(mk("act"))
            block.vector(mk("dve"))
            block.gpsimd(mk("pool"))
            block.sync(mk("sp"))


def build(nlayers=NL, dbg=False, PIPE_A=True, PIPE_B=True, PIPE_PREP=True, PREFETCH_M=True):
    nc = bass.Bass("TRN2", target_bir_lowering=False)
    st = ExitStack()
    with st:
        k = K(nc, st)

        def MM(out, lhsT, rhs, start, stop, reads, writes, tp=None):
            if tp is None:
                k.op("pe", reads, writes, lambda e: e.matmul(out, lhsT=lhsT, rhs=rhs, start=start, stop=stop), inc=bool(stop))
            else:
                k.op("pe", reads, writes, lambda e: e.matmul(out, lhsT=lhsT, rhs=rhs, start=start, stop=stop, tile_position=tp), inc=bool(stop))

        def RSQ(buf, ap, npart, n, inv_n):
            TS(ap, ap, inv_n, ALU.mult, [buf], [buf], s2=EPS, op1=ALU.add, eng="pool")
            TT(ap, ap, mhalf[0:npart, 0:n], ALU.pow, [buf, mhalf], [buf], eng="pool")

        def TR(out, in_, idn, reads, writes):
            k.op("pe", reads, writes, lambda e: e.transpose(out=out, in_=in_, identity=idn))

        def ACT(out, in_, func, reads, writes, scale=None, bias=None, accum=None):
            kw = {}
            if scale is not None:
                kw["scale"] = scale
            if bias is not None:
                kw["bias"] = bias
            if accum is not None:
                kw["accum_out"] = accum
            k.op("act", reads, writes, lambda e: e.activation(out=out, in_=in_, func=func, **kw))

        def TT(out, in0, in1, op, reads, writes, eng="dve"):
            k.op(eng, reads, writes, lambda e: e.tensor_tensor(out=out, in0=in0, in1=in1, op=op))

        def TS(out, in0, s1, op0, reads, writes, s2=None, op1=None, eng="dve"):
            if op1 is None:
                k.op(eng, reads, writes, lambda e: e.tensor_scalar(out=out, in0=in0, scalar1=s1, scalar2=None, op0=op0))
            else:
                k.op(eng, reads, writes, lambda e: e.tensor_scalar(out=out, in0=in0, scalar1=s1, scalar2=s2, op0=op0, op1=op1))

        def STT(out, in0, scalar, in1, op0, op1, reads, writes):
            k.op("dve", reads, writes, lambda e: e.scalar_tensor_tensor(out=out, in0=in0, scalar=scalar, in1=in1, op0=op0, op1=op1))

        def CP(out, in_, reads, writes, eng="dve"):
            if eng == "act":
                k.op("act", reads, writes, lambda e: e.activation(out=out, in_=in_, func=AF.Copy))
            else:
                k.op(eng, reads, writes, lambda e: e.tensor_copy(out=out, in_=in_))

        def RED(out, in_, reads, writes):
            k.op("dve", reads, writes, lambda e: e.tensor_reduce(out=out, in_=in_, axis=AX.X, op=ALU.add))

        def RECIP(out, in_, reads, writes):
            k.op("dve", reads, writes, lambda e: e.reciprocal(out=out, in_=in_))

        def MSET(ap, val, writes, eng="pool"):
            k.op(eng, [], writes, lambda e: e.memset(ap, val))

        def DMA(q, out, in_, reads, writes):
            k.dma(q, reads, writes, lambda e: e.dma_start(out=out, in_=in_))

        EI = "ExternalInput"
        x_d = k.dram("x", [2, LAT, D], F32, EI)
        ctx_d = k.dram("ctx", [2, CTX, D], F32, EI)
        cv_d = k.dram("cv", [128, 8, 4], F32, EI)
        wmod_d = k.dram("w_mod", [NL, D, 3 * D], F32, EI)
        bmod_d = k.dram("b_modT", [128, NL, 24], F32, EI)
        gain_d = k.dram("gainT", [128, NL, 8], F32, EI)
        win_d = k.dram("w_in", [NL, D, 3840], F32, EI)
        wup_d = k.dram("w_up", [NL, 4, 256, D], F32, EI)
        wmg_d = k.dram("w_merge", [NL, 4, D, D], F32, EI)
        wout_d = k.dram("w_out", [NL, D, D], F32, EI)
        sv_d = k.dram("smallv", [128, NL, 448], F32, EI)
        lb_d = k.dram("lbT", [128, 2, 2, 4], F32, EI)
        nb_d = k.dram("nbias", [NL, 100, 128, 128], F32, EI)
        rope_d = k.dram("rope", [128, 16, 32], F32, EI)
        cs64_d = k.dram("cs64", [128, 256], BF16, EI)
        dftl_d = k.dram("dftL", [128, 16, 2, LAT], BF16, EI)
        dftc_d = k.dram("dftC", [128, 2, 2, CTX], BF16, EI)
        idb_d = k.dram("identb", [128, 128], BF16, EI)
        idf_d = k.dram("identf", [128, 128], F32, EI)
        cm_d = k.dram("cmask", [64, 256], F32, EI)
        out_d = k.dram("out", [2, LAT, D], F32, "ExternalOutput")
        xs_d = k.dram("xs", [2, T, D], F32, "ExternalOutput" if dbg else "Internal")
        ybr_d = k.dram("ybr", [2, 4, 2, 128, T], BF16, "ExternalOutput" if dbg else "Internal")
        xs_tok = [[Buf() for _ in range(NT)] for _ in range(2)]
        ybr_tok = [[Buf() for _ in range(4)] for _ in range(2)]
        cin = Buf()

        identb = k.sb([128, 128], BF16, "identb")
        identf = k.sb([128, 128], F32, "identf")
        onesf = k.sb([128, 128], F32, "onesf")
        cmask = k.sb([64, 256], F32, "cmask")
        rope = k.sb([128, 16, 32], F32, "rope")
        cs64 = k.sb([128, 256], BF16, "cs64")
        dftc = k.sb([128, 2, 2, CTX], BF16, "dftc")
        sv = k.sb([128, NL, 448], F32, "sv")
        bmod = k.sb([128, NL, 24], F32, "bmod")
        gainT = k.sb([128, NL, 8], F32, "gainT")
        sc = k.sb([128, 8, 4], F32, "sc")
        segm = k.sb([128, 512], F32, "segm")
        mhalf = k.sb([128, 16], F32, "mhalf")
        lb_a = k.sb([128, 2, 2, 4], F32, "lb_a")
        lb_c = k.sb([128, 2, 2, 4], F32, "lb_c")
        lb_na = k.sb([128, 2, 2, 4], F32, "lb_na")
        mod = k.sb([128, 24, 4], F32, "mod")
        A1 = k.sb([128, 8, 4], F32, "A1")
        gate_bc = [k.sb([128, D], F32, "gatebc") for _ in range(3)]
        neglam = k.sb([128, 1], F32, "neglam")
        gA = k.sb([128, 2, 32], F32, "gA")
        gS = k.sb([128, 64], F32, "gS")
        gB = k.sb([128, 2, 64], F32, "gB")
        hT = k.sb([128, 8, T], BF16, "hT")
        hT_tok = [Buf() for _ in range(NT)]
        WB0 = k.sb([128, 8, 1024], BF16, "WB0")
        WB1 = k.sb([128, 8192], BF16, "WB1")

        def hts(t0, n):
            return hT_tok[t0 // 128:(t0 + n + 127) // 128]

        DMA("sp", identb[:], idb_d[:], [cin], [identb])
        DMA("sp", identf[:], idf_d[:], [cin], [identf])
        DMA("sp", cmask[:], cm_d[:], [cin], [cmask])
        DMA("sp", rope[:], rope_d[:], [cin], [rope])
        DMA("sp", cs64[:], cs64_d[:], [cin], [cs64])
        DMA("sp", dftc[:], dftc_d[:], [cin], [dftc])
        DMA("sp", sv[:], sv_d[:], [cin], [sv])
        DMA("sp", bmod[:], bmod_d[:], [cin], [bmod])
        DMA("sp", gainT[:], gain_d[:], [cin], [gainT])
        MSET(onesf[:], 1.0, [onesf])
        MSET(mhalf[:], -0.5, [mhalf])
        MSET(segm[:], 1.0, [segm])
        MSET(segm[:].rearrange("p (c s) -> p c s", s=64)[:, :, 0:1], 0.0, [segm])
        for b in range(2):
            DMA("sp", xs_d.t[b, 0:LAT, :], x_d.t[b], [cin], xs_tok[b][0:16])
            DMA("sp", xs_d.t[b, LAT:T, :], ctx_d.t[b], [cin], xs_tok[b][16:18])

        with ExitStack() as ph:
            k.cur = ph
            cvt = k.sb([128, 8, 4], F32)
            th0 = k.sb([128, 8, 4], F32)
            lbl = k.sb([128, 2, 2, 4], F32)
            lbe = k.sb([128, 2, 2, 4], F32)
            lbs = k.sb([128, 2, 2], F32)
            lbv = k.sb([128, 2, 2, 4], F32)
            lbm = k.sb([128, 2, 2, 4], F32)
            DMA("sp", cvt[:], cv_d[:], [cin], [cvt])
            DMA("sp", lbl[:], lb_d[:], [cin], [lbl])
            ACT(th0[:], cvt[:], AF.Tanh, [cvt], [th0], scale=0.5)
            TS(th0[:], th0[:], 0.5, ALU.mult, [th0], [th0], s2=0.5, op1=ALU.add)
            TT(sc[:], th0[:], cvt[:], ALU.mult, [th0, cvt], [sc])
            ACT(lbe[:], lbl[:], AF.Exp, [lbl], [lbe])
            RED(lbs[:], lbe[:], [lbe], [lbs])
            RECIP(lbs[:], lbs[:], [lbs], [lbs])
            TT(lbe[:], lbe[:], lbs[:].unsqueeze(3).broadcast_to([128, 2, 2, 4]), ALU.mult, [lbe, lbs], [lbe])
            MSET(lbv[:], 0.0, [lbv])
            for l in range(1, 4):
                TT(lbv[:, :, :, l:l + 1], lbv[:, :, :, l - 1:l], lbe[:, :, :, l:l + 1], ALU.add, [lbv, lbe], [lbv])
            TS(lb_a[:], lbv[:], 1e-20, ALU.max, [lbv], [lb_a])
            TS(lb_na[:], lbv[:], -1.0, ALU.mult, [lbv], [lb_na], s2=1.0, op1=ALU.add)
            TT(lb_c[:], lb_a[:], lb_na[:], ALU.add, [lb_a, lb_na], [lb_c])
            k.barrier()
        k.cur = st

        for l in range(nlayers):
            last = (l == NL - 1)
            lam_init = 0.8 - 0.6 * math.exp(-0.3 * l)
            nq = 16 if last else NT
            nck = 32 if last else NCH

            with ExitStack() as ph:
                k.cur = ph
                pm = k.ps([128, 24, 4], F32)
                pg = k.ps([128, 1024], F32)
                wm32 = [k.sb([128, 8, 512], F32) for _ in range(2)]
                diag = [k.sb([128, 128], F32) for _ in range(2)]
                lt = k.sb([128, 2, 32], F32)
                le = k.sb([128, 2], F32)
                for ch in range(6):
                    wb_ = wm32[ch % 2]
                    DMA("sp", wb_[:], wmod_d.t[l, :, ch * 512:(ch + 1) * 512].rearrange("(kc p) n -> p kc n", p=128), [cin], [wb_])
                    for jj in range(4):
                        for kc in range(8):
                            MM(pm[:, ch * 4 + jj, :], wb_[:, kc, jj * 128:(jj + 1) * 128], sc[:, kc, :], kc == 0, kc == 7, [wb_, sc], [pm])
                TT(mod[:], pm[:], bmod[:, l, :].unsqueeze(2).broadcast_to([128, 24, 4]), ALU.add, [pm, bmod], [mod])
                TS(A1[:], mod[:, 8:16, :], 1.0, ALU.add, [mod], [A1])
                TT(A1[:], A1[:], gainT[:, l, :].unsqueeze(2).broadcast_to([128, 8, 4]), ALU.mult, [A1, gainT], [A1])
                for v in range(3):
                    for t in range(8):
                        dg = diag[t % 2]
                        TS(dg[:], identf[:], mod[:, 16 + t, v:v + 1], ALU.mult, [identf, mod], [dg])
                        MM(pg[:, t * 128:(t + 1) * 128], onesf[:], dg[:], True, True, [onesf, dg], [pg])
                    CP(gate_bc[v][:, 0:512], pg[:, 0:512], [pg], [gate_bc[v]])
                    CP(gate_bc[v][:, 512:1024], pg[:, 512:1024], [pg], [gate_bc[v]], eng="act")
                lv = sv[:, l, 64:192].rearrange("p (a b d) -> p a b d", a=2, b=2)
                TT(lt[:], lv[:, :, 0, :], lv[:, :, 1, :], ALU.mult, [sv], [lt])
                RED(le[:], lt[:], [lt], [le])
                ACT(le[:], le[:], AF.Exp, [le], [le])
                TT(neglam[:], le[:, 1:2], le[:, 0:1], ALU.subtract, [le], [neglam])
                TS(neglam[:], neglam[:], -lam_init, ALU.add, [neglam], [neglam])
                CP(gA[:], sv[:, l, 0:64].rearrange("p (a d) -> p a d", a=2), [sv], [gA])
                TS(gA[:, 0, :], gA[:, 0, :], 32.0 ** -0.5, ALU.mult, [gA], [gA])
                TS(gS[:], sv[:, l, 192:256], 1.0 - lam_init, ALU.mult, [sv], [gS])
                CP(gB[:], sv[:, l, 256:384].rearrange("p (a d) -> p a d", a=2), [sv], [gB])
                TS(gB[:, 0, :], gB[:, 0, :], 0.125, ALU.mult, [gB], [gB])
                k.barrier()
            k.cur = st

            for b in range(2):
                with ExitStack() as ph:
                    k.cur = ph
                    xt = [k.sb([128, D], F32) for _ in range(2)]
                    sq = k.sb([128, D], F32)
                    ssq = [k.sb([128, 1], F32) for _ in range(2)]
                    xn = [k.sb([128, D], BF16) for _ in range(2)]
                    tmp = [k.sb([128, 8, 128], F32) for _ in range(2)]
                    pT = [k.ps([128, 8, 128], BF16) for _ in range(2)]
                    for t in range(NT):
                        v = b if t < 16 else 2
                        x_, s_, n_, p_, m_ = xt[t % 2], ssq[t % 2], xn[t % 2], pT[t % 2], tmp[t % 2]
                        DMA("sp", x_[:], xs_d.t[b, t * 128:(t + 1) * 128, :], [xs_tok[b][t]], [x_])
                        ACT(sq[:], x_[:], AF.Square, [x_], [sq, s_], accum=s_[:])
                        RSQ(s_, s_[:], 128, 1, 1.0 / D)
                        TS(n_[:], x_[:], s_[:, 0:1], ALU.mult, [x_, s_], [n_])
                        for kc in range(8):
                            TR(p_[:, kc, :], n_[:, kc * 128:(kc + 1) * 128], identb[:], [n_, identb], [p_])
                        TT(m_[:], p_[:], A1[:, :, v:v + 1].broadcast_to([128, 8, 128]), ALU.mult, [p_, A1], [m_])
                        TT(hT[:, :, t * 128:(t + 1) * 128], m_[:], mod[:, 0:8, v:v + 1].broadcast_to([128, 8, 128]), ALU.add,
                           [m_, mod], [hT_tok[t]], eng="pool")
                    k.barrier()
                k.cur = st

                for br in range(2):
                    with ExitStack() as ph:
                        k.cur = ph
                        isA = (br == 0)
                        dh = 32 if isA else 64
                        ng = 512 // dh
                        nh = ng // 2
                        nmap = 8 if isA else 4
                        c0 = 0 if isA else 768
                        gq = gA if isA else gB
                        qkT = k.sb([128, 4, T], BF16)
                        vaug = k.sb([128, NT, 4, 65], BF16)
                        sqt = [k.sb([128, 512], F32) for _ in range(2)]
                        ssg = [k.sb([128, 16], F32) for _ in range(2)]
                        qkn = [k.sb([128, 512], F32) for _ in range(2)]
                        qkb = [k.sb([128, 512], BF16) for _ in range(2)]
                        ta = k.sb([128, 256], F32)
                        tb = k.sb([128, 256], F32)
                        tc_ = k.sb([128, 256], F32)
                        td = k.sb([128, 256], F32)
                        Pt = [k.sb([128, 512], BF16) for _ in range(3)]
                        Ot2 = [k.sb([128, 8, 65], F32) for _ in range(2)]
                        rsum = k.sb([128, 8], F32)
                        On = k.sb([128, 8, 64], F32)
                        ot = k.sb([128, 256], F32)
                        o2 = k.sb([128, 256], F32)
                        rs4 = k.sb([128, 4], F32)
                        tht = k.sb([128, 256], F32)
                        sgt4 = [k.sb([128, 256], F32) for _ in range(4)]
                        ygb4 = [k.sb([128, 256], BF16) for _ in range(4)]
                        ybt = [k.sb([128, 2, 128], BF16) for _ in range(4)]
                        if isA:
                            OaT = k.sb([66, 8, 512], F32)
                            MSET(OaT[:], 0.0, [OaT])
                        else:
                            nbt = k.sb([128, 100, 128], BF16)
                            DMA("pool", nbt[:], nb_d.t[l].rearrange("n k q -> k n q"), [cin], [nbt])
                        DMA("pool", WB0[:, :, 0:768], win_d.t[l, :, c0:c0 + 768].rearrange("(kc p) n -> p kc n", p=128), [cin], [WB0])
                        g0 = 2816 + 256 * br
                        wg = WB1[:, 0:2048].rearrange("p (kc n) -> p kc n", kc=8)
                        DMA("pool", wg, win_d.t[l, :, g0:g0 + 256].rearrange("(kc p) n -> p kc n", p=128), [cin], [WB1])
                        MSET(vaug[:, :, :, 64:65], 1.0, [vaug])
                        with ExitStack() as ph2:
                            k.cur = ph2
                            pz0 = [k.ps([128, 512], F32) for _ in range(2)]
                            pz1 = [k.ps([128, 512], F32) for _ in range(2)]
                            ptrp = [k.ps([128, 4, 128], BF16) for _ in range(2)]
                            pend = None
                            for t in range(NT):
                                ts_ = slice(t * 128, (t + 1) * 128)
                                z0, z1, pt_ = pz0[t % 2], pz1[t % 2], ptrp[t % 2]
                                sq_, sg_, qn_, qb = sqt[t % 2], ssg[t % 2], qkn[t % 2], qkb[t % 2]
                                for kc in range(8):
                                    MM(z0[:], hT[:, kc, ts_], WB0[:, kc, 0:512], kc == 0, kc == 7, [hT_tok[t], WB0], [z0])
                                for kc in range(8):
                                    MM(z1[:, 0:256], hT[:, kc, ts_], WB0[:, kc, 512:768], kc == 0, kc == 7, [hT_tok[t], WB0], [z1])
                                ACT(sq_[:], z0[:], AF.Square, [z0], [sq_])
                                RED(sg_[:, 0:ng], sq_[:].rearrange("p (g d) -> p g d", d=dh), [sq_], [sg_])
                                RSQ(sg_, sg_[:, 0:ng], 128, ng, 1.0 / dh)
                                TT(qn_[:].rearrange("p (g d) -> p g d", d=dh), z0[:].rearrange("p (g d) -> p g d", d=dh),
                                   sg_[:, 0:ng].unsqueeze(2).broadcast_to([128, ng, dh]), ALU.mult, [z0, sg_], [qn_])
                                if isA and t < 16:
                                    TT(qn_[:].rearrange("p (a h d) -> p a h d", a=2, d=dh), qn_[:].rearrange("p (a h d) -> p a h d", a=2, d=dh),
                                       gq[:].unsqueeze(2).broadcast_to([128, 2, nh, dh]), ALU.mult, [qn_, gq], [qn_], eng="pool")
                                    qv = qn_[:].rearrange("p (g two d) -> p g two d", two=2, d=16)
                                    ov = qb[:].rearrange("p (g two d) -> p g two d", two=2, d=16)
                                    cosb = rope[:, t, 0:16].unsqueeze(1).broadcast_to([128, 16, 16])
                                    sinb = rope[:, t, 16:32].unsqueeze(1).broadcast_to([128, 16, 16])
                                    t3 = lambda buf: buf[:].rearrange("p (g d) -> p g d", d=16)
                                    TT(t3(ta), qv[:, :, 0, :], cosb, ALU.mult, [qn_, rope], [ta])
                                    TT(t3(tb), qv[:, :, 1, :], sinb, ALU.mult, [qn_, rope], [tb])
                                    TT(ov[:, :, 0, :], t3(ta), t3(tb), ALU.subtract, [ta, tb], [qb])
                                    TT(t3(tc_), qv[:, :, 0, :], sinb, ALU.mult, [qn_, rope], [tc_], eng="pool")
                                    TT(t3(td), qv[:, :, 1, :], cosb, ALU.mult, [qn_, rope], [td], eng="pool")
                                    TT(ov[:, :, 1, :], t3(tc_), t3(td), ALU.add, [tc_, td], [qb], eng="pool")
                                else:
                                    TT(qb[:].rearrange("p (a h d) -> p a h d", a=2, d=dh), qn_[:].rearrange("p (a h d) -> p a h d", a=2, d=dh),
                                       gq[:].unsqueeze(2).broadcast_to([128, 2, nh, dh]), ALU.mult, [qn_, gq], [qb], eng="pool")
                                CP(vaug[:, t, :, 0:64], z1[:, 0:256].rearrange("p (h d) -> p h d", d=64), [z1], [vaug], eng="act")

                                def trs(t=t, pt_=pt_, qb=qb, ts_=ts_):
                                    for i in range(4):
                                        TR(pt_[:, i, :], qb[:, i * 128:(i + 1) * 128], identb[:], [qb, identb], [pt_])
                                    CP(qkT[:, :, ts_], pt_[:], [pt_], [qkT], eng="act")
                                if not PIPE_PREP:
                                    trs()
                                    continue
                                if pend is not None:
                                    pend()
                                pend = trs
                            if PIPE_PREP:
                                pend()
                            k.barrier()
                        k.cur = ph
                        Sp = [k.ps([128, 512], F32) for _ in range(2)]
                        if isA:
                            accT = [k.ps([65, 512], F32) for _ in range(2)]
                            pOa = k.ps([128, 4, 66], F32)
                            pOb = k.ps([128, 4, 66], F32)
                        else:
                            acc = [k.ps([128, 4, 65], F32) for _ in range(2)]
                        pgz = k.ps([128, 256], F32)
                        ptr = k.ps([128, 4, 128], BF16)

                        pendq = []

                        def finish(qt, Ot):
                            qs_ = slice(qt * 128, (qt + 1) * 128)
                            sg_ = sgt4[qt % 4]
                            yg_ = ygb4[qt % 4]
                            for kc in range(8):
                                MM(pgz[:], hT[:, kc, qs_], wg[:, kc, :], kc == 0, kc == 7, [hT_tok[qt], WB1], [pgz])
                            ACT(tht[:], pgz[:], AF.Tanh, [pgz], [tht], scale=0.5)
                            STT(sg_[:], tht[:], 1.0, pgz[:], ALU.add, ALU.mult, [tht, pgz], [sg_])
                            RECIP(rsum[:, 0:nmap], Ot[:, 0:nmap, 64], [Ot], [rsum])
                            TT(On[:, 0:nmap, :], Ot[:, 0:nmap, 0:64], rsum[:, 0:nmap].unsqueeze(2).broadcast_to([128, nmap, 64]), ALU.mult,
                               [Ot, rsum], [On])
                            if isA:
                                Onv = On[:].rearrange("p (h two) d -> p h two d", two=2)
                                STT(ot[:].rearrange("p (h d) -> p h d", d=64), Onv[:, :, 1, :], neglam[:, 0:1], Onv[:, :, 0, :],
                                    ALU.mult, ALU.add, [On, neglam], [ot])
                                TT(o2[:], ot[:], ot[:], ALU.mult, [ot], [o2], eng="pool")
                                RED(rs4[:], o2[:].rearrange("p (h d) -> p h d", d=64), [o2], [rs4])
                                RSQ(rs4, rs4[:], 128, 4, 1.0 / 64)
                                TT(o2[:].rearrange("p (h d) -> p h d", d=64), ot[:].rearrange("p (h d) -> p h d", d=64),
                                   rs4[:].unsqueeze(2).broadcast_to([128, 4, 64]), ALU.mult, [ot, rs4], [o2])
                                TT(ot[:].rearrange("p (h d) -> p h d", d=64), o2[:].rearrange("p (h d) -> p h d", d=64),
                                   gS[:].unsqueeze(1).broadcast_to([128, 4, 64]), ALU.mult, [o2, gS], [ot], eng="pool")
                                ysrc = ot[:]
                                ybuf = ot
                            else:
                                ysrc = On[:, 0:4, :].rearrange("p h d -> p (h d)")
                                ybuf = On
                            STT(yg_[:], sg_[:], 0.5, ysrc, ALU.mult, ALU.mult, [sg_, ybuf], [yg_])
                            pendq.append(qt)

                        def flush(keep=0):
                            while len(pendq) > keep:
                                qt = pendq.pop(0)
                                qs_ = slice(qt * 128, (qt + 1) * 128)
                                yg_ = ygb4[qt % 4]
                                yb_ = ybt[qt % 4]
                                for i in range(2):
                                    TR(ptr[:, i, :], yg_[:, i * 128:(i + 1) * 128], identb[:], [yg_, identb], [ptr])
                                CP(yb_[:], ptr[:, 0:2, :], [ptr], [yb_])
                                DMA("sp", ybr_d.t[b, br, :, :, qs_].rearrange("c p n -> p c n"), yb_[:], [yb_], [ybr_tok[b][br]])

                        gi = 0
                        if isA:
                            qblocks = [(0, 512), (512, 512), (1024, 512), (1536, 512)] + ([] if last else [(2048, 256)])
                            for (q0, qn) in qblocks:
                                chunks = list(range(NT)) if q0 < LAT else [16, 17]
                                nci = len(chunks)
                                its = [(m, ci, c) for m in range(8) for ci, c in enumerate(chunks)]

                                def qk(i):
                                    m, ci, c = its[i]
                                    slot, pb = m // 4, 32 * (m % 4)
                                    S_ = Sp[(gi + i) % 2]
                                    P_ = Pt[(gi + i) % 3]
                                    MM(S_[:, 0:qn], qkT[pb:pb + 32, 2 + slot, c * 128:(c + 1) * 128], qkT[pb:pb + 32, slot, q0:q0 + qn],
                                       True, True, [qkT], [S_], tp=(pb, 0))
                                    ACT(P_[:, 0:qn], S_[:, 0:qn], AF.Exp, [S_], [P_])

                                def pv(i):
                                    m, ci, c = its[i]
                                    a_ = accT[m % 2]
                                    P_ = Pt[(gi + i) % 3]
                                    MM(a_[:, 0:qn], vaug[:, c, m // 2, :], P_[:, 0:qn], ci == 0, ci == nci - 1, [vaug, P_], [a_])
                                    if ci == nci - 1:
                                        CP(OaT[0:65, m, 0:qn], a_[:, 0:qn], [a_], [OaT])

                                if PIPE_A:
                                    qk(0)
                                for i in range(len(its)):
                                    if PIPE_A:
                                        if i + 1 < len(its):
                                            qk(i + 1)
                                    else:
                                        qk(i)
                                    pv(i)
                                    if i == 8:
                                        flush()
                                gi += len(its)
                                for j in range(qn // 128):
                                    qt = q0 // 128 + j
                                    Ot = Ot2[qt % 2]
                                    for m in range(8):
                                        pO_ = pOa if m < 4 else pOb
                                        TR(pO_[:, m % 4, :], OaT[:, m, j * 128:(j + 1) * 128], identf[0:66, 0:66], [OaT, identf], [pO_])
                                    CP(Ot[:, 0:4, :], pOa[:, :, 0:65], [pOa], [Ot])
                                    CP(Ot[:, 4:8, :], pOb[:, :, 0:65], [pOb], [Ot])
                                    finish(qt, Ot)
                            flush()
                        else:
                            work = []
                            for qt in range(nq):
                                if qt >= 16:
                                    chunks = [(16, None), (17, None)]
                                else:
                                    cs = min(max(qt - 2, 0), 11)
                                    typ = 0 if qt == 0 else 1 if qt == 1 else 3 if qt == 14 else 4 if qt == 15 else 2
                                    chunks = [(cs + i, typ * 5 + i) for i in range(5)] + [(16, None), (17, None)]
                                groups = [chunks[i:i + 4] for i in range(0, len(chunks), 4)]
                                for m in range(4):
                                    for gidx, g in enumerate(groups):
                                        work.append((qt, m, g, gidx == 0, gidx == len(groups) - 1))

                            def scb(i):
                                qt, m, g, fg, lg = work[i]
                                slot, pb = m // 2, 64 * (m % 2)
                                Sb = Sp[i % 2]
                                Pb = Pt[i % 3]
                                S_ = Sb[:].rearrange("p (c q) -> p c q", q=128)
                                P_ = Pb[:].rearrange("p (c q) -> p c q", q=128)
                                for j, (c, bi) in enumerate(g):
                                    kT_ = qkT[pb:pb + 64, 2 + slot, c * 128:(c + 1) * 128]
                                    qT_ = qkT[pb:pb + 64, slot, qt * 128:(qt + 1) * 128]
                                    if bi is not None:
                                        MM(S_[:, j, :], identb[:], nbt[:, bi * 4 + m, :], True, False, [identb, nbt], [Sb])
                                        MM(S_[:, j, :], kT_, qT_, False, True, [qkT], [Sb], tp=(pb, 0))
                                    else:
                                        MM(S_[:, j, :], kT_, qT_, True, True, [qkT], [Sb], tp=(pb, 0))
                                ACT(P_[:, 0:len(g), :], S_[:, 0:len(g), :], AF.Exp, [Sb], [Pb])

                            def pvb(i):
                                qt, m, g, fg, lg = work[i]
                                Pb = Pt[i % 3]
                                P_ = Pb[:].rearrange("p (c q) -> p c q", q=128)
                                ab = acc[qt % 2]
                                for j, (c, bi) in enumerate(g):
                                    MM(ab[:, m, :], P_[:, j, :], vaug[:, c, m, :], fg and j == 0, lg and j == len(g) - 1, [Pb, vaug], [ab])
                                if lg and m == 3:
                                    Ot = Ot2[qt % 2]
                                    CP(Ot[:, 0:4, :], ab[:], [ab], [Ot])
                                    finish(qt, Ot)
                                if lg and m == 1:
                                    flush()

                            if PIPE_B:
                                scb(0)
                            for i in range(len(work)):
                                if PIPE_B:
                                    if i + 1 < len(work):
                                        scb(i + 1)
                                else:
                                    scb(i)
                                pvb(i)
                            flush()
                        k.barrier()
                    k.cur = st

                DMA("pool", WB0[:], win_d.t[l, :, 1536:2560].rearrange("(kc p) n -> p kc n", p=128), [cin], [WB0])
                wg = WB1[:, 0:2048].rearrange("p (kc n) -> p kc n", kc=8)
                DMA("pool", wg, win_d.t[l, :, 3328:3584].rearrange("(kc p) n -> p kc n", p=128), [cin], [WB1])
                for ct in range(2):
                    with ExitStack() as ph:
                        k.cur = ph
                        KT = [k.sb([128, T], BF16) for _ in range(2)]
                        QBD = [k.sb([128, NCH, 128], BF16) for _ in range(2)]
                        Ktm = k.sb([64, NCH, 128], BF16)
                        Vc = k.sb([64, NCH, 128], BF16)
                        Spr = [k.sb([128, NCH, 64], BF16) for _ in range(2)]
                        er = [k.sb([128, NCH], F32) for _ in range(2)]
                        Sst = k.sb([128, 64], F32)
                        Stm = k.sb([128, 64], F32)
                        for d in range(2):
                            MSET(QBD[d][:], 0.0, [QBD[d]])
                        with ExitStack() as ph2:
                            k.cur = ph2
                            qh = k.sb([128, T], F32)
                            thq = [k.sb([128, 512], F32) for _ in range(2)]
                            NS = 2
                            tht = [k.sb([128, 512], F32) for _ in range(NS)]
                            gg = [k.sb([128, 512], F32) for _ in range(NS)]
                            kk = [k.sb([128, 512], F32) for _ in range(NS)]
                            uu = [k.sb([128, 512], F32) for _ in range(NS)]
                            Bc = [k.sb([128, 512], F32) for _ in range(NS)]
                            tmpb = [k.sb([128, 512], F32) for _ in range(NS)]
                            tot = [k.sb([128, 8], F32) for _ in range(NS)]
                            E1 = [k.sb([128, 512], F32) for _ in range(NS)]
                            E2 = [k.sb([128, 512], F32) for _ in range(NS)]
                            pq = [k.ps([128, 512], F32) for _ in range(4)]
                            pv = [k.ps([64, 4, 128], F32) for _ in range(2)]
                            for c4 in range(0, NCH, 4):
                                pv_ = pv[(c4 // 4) % 2]
                                for i in range(4):
                                    c = c4 + i
                                    for kc in range(8):
                                        MM(pv_[:, i, :], hT[:, kc, c * 64:(c + 1) * 64], WB0[:, kc, 768 + ct * 128:768 + (ct + 1) * 128],
                                           kc == 0, kc == 7, [hT_tok[c // 2], WB0], [pv_])
                                CP(Vc[:, c4:c4 + 4, :], pv_[:], [pv_], [Vc], eng="act")
                            pi = 0
                            for bi_, (t0, n) in enumerate(BLKS):
                                p_ = pq[pi % 4]
                                pi += 1
                                tq = thq[bi_ % 2]
                                for kc in range(8):
                                    MM(p_[:, 0:n], WB0[:, kc, ct * 128:(ct + 1) * 128], hT[:, kc, t0:t0 + n], kc == 0, kc == 7, hts(t0, n) + [WB0], [p_])
                                ACT(tq[:, 0:n], p_[:, 0:n], AF.Tanh, [p_], [tq], scale=0.5)
                                STT(qh[:, t0:t0 + n], tq[:, 0:n], 1.0, p_[:, 0:n], ALU.add, ALU.mult, [tq, p_], [qh])
                            it = 0
                            for (t0, n) in BLKS:
                                nc_ = n // 64
                                cb = t0 // 64
                                bs_ = slice(t0, t0 + n)
                                for d in range(2):
                                    j = it % NS
                                    it += 1
                                    p_ = pq[pi % 4]
                                    pi += 1
                                    cc = 256 + d * 256 + ct * 128
                                    for kc in range(8):
                                        MM(p_[:, 0:n], WB0[:, kc, cc:cc + 128], hT[:, kc, bs_], kc == 0, kc == 7, hts(t0, n) + [WB0], [p_])
                                    ACT(uu[j][:, 0:n], p_[:, 0:n], AF.Exp, [p_], [uu[j]], scale=-1.0)
                                    ACT(E2[j][:, 0:n], uu[j][:, 0:n], AF.Ln, [uu[j]], [E2[j]], bias=1.0)
                                    ACT(E1[j][:, 0:n], uu[j][:, 0:n], AF.Ln, [uu[j], lb_a, lb_c], [E1[j]], scale=lb_a[:, d, ct, l:l + 1],
                                        bias=lb_c[:, d, ct, l:l + 1])
                                    TT(gg[j][:, 0:n], E1[j][:, 0:n], E2[j][:, 0:n], ALU.subtract, [E1[j], E2[j]], [gg[j]], eng="pool")
                                    TT(tht[j][:, 0:n], p_[:, 0:n], E2[j][:, 0:n], ALU.add, [p_, E2[j]], [tht[j]])
                                    ACT(kk[j][:, 0:n], tht[j][:, 0:n], AF.Exp, [tht[j]], [kk[j]], scale=-1.0)
                                    k.op("dve", [segm, gg[j]], [Bc[j]], lambda e, n=n, j=j: e.tensor_tensor_scan(
                                        out=Bc[j][:, 0:n], data0=segm[:, 0:n], data1=gg[j][:, 0:n], initial=0.0, op0=ALU.mult, op1=ALU.add))
                                    B3 = Bc[j][:, 0:n].rearrange("p (c s) -> p c s", s=64)
                                    CP(tot[j][:, 0:nc_], B3[:, :, 63], [Bc[j]], [tot[j]])
                                    if d == 1:
                                        TT(tmpb[j][:, 0:n], gg[j][:, 0:n], Bc[j][:, 0:n], ALU.subtract, [gg[j], Bc[j]], [tmpb[j]])
                                        TT(B3, tmpb[j][:, 0:n].rearrange("p (c s) -> p c s", s=64),
                                           tot[j][:, 0:nc_].unsqueeze(2).broadcast_to([128, nc_, 64]), ALU.add, [tmpb[j], tot[j]], [Bc[j]])
                                    ACT(er[d][:, cb:cb + nc_], tot[j][:, 0:nc_], AF.Exp, [tot[j]], [er[d]], scale=0.5)
                                    STT(tmpb[j][:, 0:n].rearrange("p (c s) -> p c s", s=64), tot[j][:, 0:nc_].unsqueeze(2).broadcast_to([128, nc_, 64]),
                                        -0.5, B3, ALU.mult, ALU.add, [tot[j], Bc[j]], [tmpb[j]])
                                    TS(tmpb[j][:, 0:n], tmpb[j][:, 0:n], 43.0, ALU.min, [tmpb[j]], [tmpb[j]], s2=-43.0, op1=ALU.max, eng="pool")
                                    ACT(E1[j][:, 0:n], tmpb[j][:, 0:n], AF.Exp, [tmpb[j]], [E1[j]])
                                    ACT(E2[j][:, 0:n], tmpb[j][:, 0:n], AF.Exp, [tmpb[j]], [E2[j]], scale=-1.0)
                                    for hh in range(2):
                                        ps_ = slice(64 * hh, 64 * hh + 64)
                                        STT(QBD[d][ps_, cb:cb + nc_, 64 * hh:64 * hh + 64], qh[ps_, bs_].rearrange("p (c s) -> p c s", s=64), 0.5,
                                            E1[j][ps_, 0:n].rearrange("p (c s) -> p c s", s=64), ALU.mult, ALU.mult, [qh, E1[j]], [QBD[d]])
                                    STT(KT[d][:, bs_], kk[j][:, 0:n], lb_na[:, d, ct, l:l + 1], E2[j][:, 0:n], ALU.mult, ALU.mult,
                                        [kk[j], E2[j], lb_na], [KT[d]])
                            k.barrier()
                        k.cur = ph
                        Am = [k.sb([64, 2, 128], BF16) for _ in range(2)]
                        of_ = k.sb([64, 512], F32)
                        o2 = k.sb([64, 512], F32)
                        rs8 = k.sb([64, 8], F32)
                        th2 = k.sb([64, 512], F32)
                        sg2 = k.sb([64, 512], F32)
                        yg2 = [k.sb([64, 512], BF16) for _ in range(2)]
                        ybt = [k.sb([128, 256], BF16) for _ in range(2)]
                        pbk = k.ps([128, 1024], BF16)
                        ptk = pbk[0:64, :].rearrange("p (a b) -> p a b", b=128)
                        pTy = pbk[:, 0:256]
                        pu = [k.ps([128, 128], F32) for _ in range(2)]
                        pa = [k.ps([64, 2, 128], F32) for _ in range(2)]
                        po = k.ps([64, 4, 128], F32)
                        pgz = k.ps([64, 4, 128], F32)
                        for d in range(2):
                            for c8 in range(0, NCH, 4):
                                for i in range(4):
                                    c = c8 + i
                                    TR(ptk[:, i, :], KT[d][:, c * 64:(c + 1) * 64], identb[:], [KT[d], identb], [pbk])
                                CP(Ktm[:, c8:c8 + 4, :], ptk[:, 0:4, :], [pbk], [Ktm], eng="act")
                            MSET(Sst[:], 0.0, [Sst], eng="dve")
                            order = ([32, 33, 34, 35] + list(range(32))) if d == 0 else ([35, 34, 33, 32] + list(range(31, -1, -1)))
                            for oi, c in enumerate(order):
                                pu_ = pu[oi % 2]
                                MM(pu_[:], Ktm[:, c, :], Vc[:, c, :], True, True, [Ktm, Vc], [pu_])
                                e_ = er[d][:, c:c + 1]
                                ACT(Spr[d][:, c, :], Sst[:], AF.Copy, [Sst, er[d]], [Spr[d]], scale=e_)
                                for hh in range(2):
                                    ps_ = slice(64 * hh, 64 * hh + 64)
                                    STT(Stm[ps_, :], Sst[ps_, :], er[d][ps_, c:c + 1], pu_[ps_, 64 * hh:64 * hh + 64], ALU.mult, ALU.add,
                                        [Sst, er[d], pu_], [Stm])
                                TS(Sst[:], Stm[:], e_, ALU.mult, [Stm, er[d]], [Sst])
                        cm4 = cmask[:].rearrange("p (d h t) -> p d h t", d=2, h=2)

                        def scm(c):
                            pa_ = pa[c % 2]
                            am_ = Am[c % 2]
                            for d in range(2):
                                MM(pa_[:, d, :], KT[d][:, c * 64:(c + 1) * 64], QBD[d][:, c, :], True, True, [KT[d], QBD[d]], [pa_])
                            TT(am_[:].rearrange("p d (h t) -> p d h t", h=2), pa_[:].rearrange("p d (h t) -> p d h t", h=2), cm4, ALU.mult,
                               [pa_, cmask], [am_])

                        ptail = []

                        def tail():
                            while ptail:
                                c4 = ptail.pop(0)
                                yg_ = yg2[(c4 // 4) % 2]
                                yb_ = ybt[(c4 // 4) % 2]
                                for i in range(4):
                                    TR(pTy[:, i * 64:(i + 1) * 64], yg_[:, i * 128:(i + 1) * 128], identb[0:64, 0:64], [yg_, identb], [pbk])
                                CP(yb_[:], pTy, [pbk], [yb_], eng="act")
                                DMA("sp", ybr_d.t[b, 2, ct, :, c4 * 64:c4 * 64 + 256], yb_[:], [yb_], [ybr_tok[b][2]])

                        scm(0)
                        for c4 in range(0, nck, 4):
                            for i in range(4):
                                c = c4 + i
                                if c + 1 < nck:
                                    scm(c + 1)
                                for kc in range(8):
                                    MM(pgz[:, i, :], hT[:, kc, c * 64:(c + 1) * 64], wg[:, kc, ct * 128:(ct + 1) * 128], kc == 0, kc == 7,
                                       [hT_tok[c // 2], WB1], [pgz])
                                am_ = Am[c % 2]
                                for hh in range(2):
                                    hs_ = slice(64 * hh, 64 * hh + 64)
                                    for d in range(2):
                                        MM(po[:, i, hs_], am_[:, d, hs_], Vc[:, c, hs_], d == 0, False, [am_, Vc], [po])
                                        MM(po[:, i, hs_], QBD[d][:, c, hs_], Spr[d][:, c, :], False, d == 1, [QBD[d], Spr[d]], [po])
                            tail()
                            pof = po[:].rearrange("p c n -> p (c n)")
                            pgf = pgz[:].rearrange("p c n -> p (c n)")
                            yg_ = yg2[(c4 // 4) % 2]
                            ACT(th2[:], pgf, AF.Tanh, [pgz], [th2], scale=0.5)
                            STT(sg2[:], th2[:], 1.0, pgf, ALU.add, ALU.mult, [th2, pgz], [sg2])
                            CP(of_[:], pof, [po], [of_], eng="act")
                            ACT(o2[:], pof, AF.Square, [po], [o2])
                            RED(rs8[:], o2[:].rearrange("p (g d) -> p g d", d=64), [o2], [rs8])
                            RSQ(rs8, rs8[:], 64, 8, 1.0 / 64)
                            TT(o2[:].rearrange("p (g d) -> p g d", d=64), of_[:].rearrange("p (g d) -> p g d", d=64),
                               rs8[:].unsqueeze(2).broadcast_to([64, 8, 64]), ALU.mult, [of_, rs8], [o2])
                            TT(of_[:].rearrange("p (g d) -> p g d", d=64), o2[:].rearrange("p (g d) -> p g d", d=64),
                               sv[0:64, l, 384:448].unsqueeze(1).broadcast_to([64, 8, 64]), ALU.mult, [o2, sv], [of_], eng="pool")
                            STT(yg_[:], sg2[:], 0.5, of_[:], ALU.mult, ALU.mult, [sg2, of_], [yg_])
                            ptail.append(c4)
                        tail()
                        k.barrier()
                    k.cur = st

                with ExitStack() as ph:
                    k.cur = ph
                    uT = k.sb([128, 2, T], BF16)
                    PQ = k.sb([128, NT, 2, 256], BF16)
                    tab = [k.sb([128, 16, 2, 256], BF16) for _ in range(2)]
                    tht = k.sb([128, 256], F32)
                    sgt = k.sb([128, 256], F32)
                    ybt = [k.sb([128, 2, 256], BF16) for _ in range(2)]
                    pq = [k.ps([128, 512], F32) for _ in range(2)]
                    pp = [k.ps([128, 2, 256], F32) for _ in range(2)]
                    py = [k.ps([128, 256], F32) for _ in range(2)]
                    pgz = [k.ps([128, 256], F32) for _ in range(2)]
                    DMA("pool", WB0[:, :, 0:256], win_d.t[l, :, 2560:2816].rearrange("(kc p) n -> p kc n", p=128), [cin], [WB0])
                    wg = WB1[:, 0:2048].rearrange("p (kc n) -> p kc n", kc=8)
                    DMA("pool", wg, win_d.t[l, :, 3584:3840].rearrange("(kc p) n -> p kc n", p=128), [cin], [WB1])
                    pi = 0
                    for ct in range(2):
                        for (t0, n) in BLKS:
                            p_ = pq[pi % 2]
                            pi += 1
                            for kc in range(8):
                                MM(p_[:, 0:n], WB0[:, kc, ct * 128:(ct + 1) * 128], hT[:, kc, t0:t0 + n], kc == 0, kc == 7, hts(t0, n) + [WB0], [p_])
                            CP(uT[:, ct, t0:t0 + n], p_[:, 0:n], [p_], [uT], eng="act")
                    for t in range(NT):
                        p_ = pp[t % 2]
                        for ct in range(2):
                            MM(p_[:, ct, :], uT[:, ct, t * 128:(t + 1) * 128], cs64[:], True, True, [uT, cs64], [p_])
                        CP(PQ[:, t, :, :], p_[:], [p_], [PQ], eng=("act" if t % 2 else "dve"))
                    it = 0
                    nblk = 8 if last else 9
                    for nb in range(nblk):
                        if nb < 8:
                            tb_ = tab[nb % 2]
                            DMA("sp", tb_[:], dftl_d.t[:, :, :, nb * 256:(nb + 1) * 256], [cin], [tb_])
                            tcs = list(range(16))
                            tsrc = lambda tc, cs: tb_[:, tc, cs, :]
                            tread = [tb_]
                            t0 = nb * 256
                        else:
                            tcs = [16, 17]
                            tsrc = lambda tc, cs: dftc[:, tc - 16, cs, :]
                            tread = [dftc]
                            t0 = LAT
                        yb_ = ybt[nb % 2]
                        for ct in range(2):
                            y_ = py[it % 2]
                            g_ = pgz[it % 2]
                            it += 1
                            nmm = len(tcs) * 2
                            j = 0
                            for tc in tcs:
                                for cs in range(2):
                                    MM(y_[:], PQ[:, tc, ct, cs * 128:(cs + 1) * 128], tsrc(tc, cs), j == 0, j == nmm - 1, [PQ] + tread, [y_])
                                    j += 1
                            for kc in range(8):
                                MM(g_[:], wg[:, kc, ct * 128:(ct + 1) * 128], hT[:, kc, t0:t0 + 256], kc == 0, kc == 7, hts(t0, 256) + [WB1], [g_])
                            ACT(tht[:], g_[:], AF.Tanh, [g_], [tht], scale=0.5)
                            STT(sgt[:], tht[:], 1.0, g_[:], ALU.add, ALU.mult, [tht, g_], [sgt])
                            STT(yb_[:, ct, :], sgt[:], 0.5, y_[:], ALU.mult, ALU.mult, [sgt, y_], [yb_])
                        DMA("sp", ybr_d.t[b, 3, :, :, t0:t0 + 256].rearrange("c p n -> p c n"), yb_[:], [yb_], [ybr_tok[b][3]])
                    k.barrier()
                k.cur = st

                with ExitStack() as ph:
                    k.cur = ph
                    ybS = k.sb([128, 4, 2, T], BF16)
                    accT = k.sb([128, 8, T], BF16)
                    wu = [k.sb([128, 4, 2, 128], BF16) for _ in range(2)]
                    wm_tok = [Buf(), Buf()]
                    tht = [k.sb([128, 512], F32) for _ in range(2)]
                    tmpm = [k.sb([128, 512], F32) for _ in range(2)]
                    accf = [k.sb([128, 512], F32) for _ in range(2)]
                    xt = [k.sb([128, D], F32) for _ in range(2)]
                    xo = [k.sb([128, D], F32) for _ in range(2)]
                    tmo = [k.sb([128, 512], F32) for _ in range(2)]
                    pM = [k.ps([128, 512], F32) for _ in range(2)]
                    pU = [k.ps([128, 512], F32) for _ in range(2)]
                    pO = [k.ps([128, 512], F32) for _ in range(2)]
                    for i in range(4):
                        DMA("sp", ybS[:, i, :, :], ybr_d.t[b, i].rearrange("c p n -> p c n"), [ybr_tok[b][i]], [ybS])
                    DMA("pool", WB0[:], wout_d.t[l].rearrange("(kc p) n -> p kc n", p=128), [cin], [WB0])
                    mblks = BLKS[:4] if last else BLKS
                    im = 0
                    def wload(ft):
                        fs_ = slice(ft * 128, (ft + 1) * 128)
                        wmv = WB1[:, (ft % 2) * 4096:(ft % 2 + 1) * 4096].rearrange("p (i kc n) -> p i kc n", i=4, kc=8)
                        DMA("pool", wmv, wmg_d.t[l, :, :, fs_].rearrange("i (kc p) n -> p i kc n", p=128), [cin], [wm_tok[ft % 2]])
                        DMA("pool", wu[ft % 2][:], wup_d.t[l, :, :, fs_].rearrange("i (fc p) n -> p i fc n", p=128), [cin], [wu[ft % 2]])

                    if PREFETCH_M:
                        wload(0)
                    for ft in range(8):
                        fs_ = slice(ft * 128, (ft + 1) * 128)
                        wmv = WB1[:, (ft % 2) * 4096:(ft % 2 + 1) * 4096].rearrange("p (i kc n) -> p i kc n", i=4, kc=8)
                        wmt = wm_tok[ft % 2]
                        wu_ = wu[ft % 2]
                        if not PREFETCH_M:
                            wload(ft)
                        elif ft + 1 < 8:
                            wload(ft + 1)
                        for bi_, (t0, n) in enumerate(mblks):
                            af = accf[bi_ % 2]
                            for i in range(4):
                                m_ = pM[im % 2]
                                u_ = pU[im % 2]
                                th_ = tht[im % 2]
                                tm_ = tmpm[im % 2]
                                im += 1
                                for kc in range(8):
                                    MM(m_[:, 0:n], wmv[:, i, kc, :], hT[:, kc, t0:t0 + n], kc == 0, kc == 7, hts(t0, n) + [wmt], [m_])
                                for fc in range(2):
                                    MM(u_[:, 0:n], wu_[:, i, fc, :], ybS[:, i, fc, t0:t0 + n], fc == 0, fc == 1, [wu_, ybS], [u_])
                                ACT(th_[:, 0:n], m_[:, 0:n], AF.Tanh, [m_], [th_], scale=0.5)
                                if i == 0:
                                    STT(af[:, 0:n], th_[:, 0:n], 1.0, u_[:, 0:n], ALU.add, ALU.mult, [th_, u_], [af])
                                else:
                                    STT(tm_[:, 0:n], th_[:, 0:n], 1.0, u_[:, 0:n], ALU.add, ALU.mult, [th_, u_], [tm_])
                                    TT(af[:, 0:n], af[:, 0:n], tm_[:, 0:n], ALU.add, [af, tm_], [af], eng="pool")
                            ACT(accT[:, ft, t0:t0 + n], af[:, 0:n], AF.Copy, [af], [accT], scale=0.5)
                    ntile = 16 if last else NT
                    io = 0
                    for t in range(ntile):
                        v = b if t < 16 else 2
                        x_ = xt[t % 2]
                        o_ = xo[t % 2]
                        DMA("sp", x_[:], xs_d.t[b, t * 128:(t + 1) * 128, :], [xs_tok[b][t]], [x_])
                        for hf in range(2):
                            p_ = pO[io % 2]
                            tm_ = tmo[io % 2]
                            io += 1
                            cs_ = slice(hf * 512, (hf + 1) * 512)
                            for kc in range(8):
                                MM(p_[:], accT[:, kc, t * 128:(t + 1) * 128], WB0[:, kc, cs_], kc == 0, kc == 7, [accT, WB0], [p_])
                            TT(tm_[:], p_[:], gate_bc[v][:, cs_], ALU.mult, [p_, gate_bc[v]], [tm_])
                            TT(o_[:, cs_], tm_[:], x_[:, cs_], ALU.add, [tm_, x_], [o_], eng="pool")
                        if last:
                            DMA("sp", out_d.t[b, t * 128:(t + 1) * 128, :], o_[:], [o_], [xs_tok[b][t]])
                        else:
                            DMA("sp", xs_d.t[b, t * 128:(t + 1) * 128, :], o_[:], [o_], [xs_tok[b][t]])
                    k.barrier()
                k.cur = st
        k.emit()
    return nc


def _consts():
    bf = ml_dtypes.bfloat16
    c = {}
    c["identb"] = np.eye(128, dtype=np.float32).astype(bf)
    c["identf"] = np.eye(128, dtype=np.float32)
    s = np.arange(64)[:, None]
    t = np.arange(64)[None, :]
    mf = (s <= t).astype(np.float32)
    mb = (s >= t).astype(np.float32)
    cm = np.stack([np.stack([mf, mf], 0), np.stack([mb, mb], 0)], 0)
    c["cmask"] = np.ascontiguousarray(cm.transpose(2, 0, 1, 3).reshape(64, 256))
    n = np.arange(LAT)
    row = (n // 64).astype(np.float32)
    col = (n % 64).astype(np.float32)
    inv = (10000.0 ** (-np.arange(0, 16, 2, dtype=np.float32) / 16)).astype(np.float32)
    ang = np.concatenate([row[:, None] * inv, col[:, None] * inv], -1).astype(np.float32)
    rp = np.concatenate([np.cos(ang), np.sin(ang)], -1).astype(np.float32)
    c["rope"] = np.ascontiguousarray(rp.reshape(16, 128, 32).transpose(1, 0, 2))
    e = np.arange(64)
    a64 = 2 * np.pi * np.outer(e, e) / 64
    C64 = np.cos(a64) / 8.0
    S64 = np.sin(a64) / 8.0
    cs = np.zeros((128, 256), np.float64)
    for g in range(2):
        cs[g * 64:(g + 1) * 64, g * 64:(g + 1) * 64] = C64
        cs[g * 64:(g + 1) * 64, 128 + g * 64:128 + (g + 1) * 64] = S64
    c["cs64"] = cs.astype(np.float32).astype(bf)

    def dft(N):
        tt = np.arange(N)
        a = 2 * np.pi * ((np.outer(tt, tt)) % N) / N
        Cn = np.cos(a) / np.sqrt(N)
        Sn = -np.sin(a) / np.sqrt(N)
        tab = np.stack([Cn, Sn], 1)
        return np.ascontiguousarray(tab.reshape(N // 128, 128, 2, N).transpose(1, 0, 2, 3)).astype(np.float32).astype(bf)

    c["dftL"] = dft(LAT)
    c["dftC"] = dft(CTX)
    return c


def _nbias_index():
    rows, W, wh, ww = 32, 64, 8, 16
    idx = np.full((5, 5, 128, 128, 2), -1, np.int64)
    for typ, qt in enumerate([0, 1, 5, 14, 15]):
        cs = min(max(qt - 2, 0), 11)
        for i in range(5):
            kc = cs + i
            for kk in range(128):
                kr, kcol = 2 * kc + kk // 64, kk % 64
                for q in range(128):
                    r, cq = 2 * qt + q // 64, q % 64
                    r0 = min(max(r - wh // 2, 0), rows - wh)
                    c0 = min(max(cq - ww // 2, 0), W - ww)
                    if r0 <= kr < r0 + wh and c0 <= kcol < c0 + ww:
                        idx[typ, i, kk, q, 0] = kr - r + wh - 1
                        idx[typ, i, kk, q, 1] = min(max(kcol - cq, 1 - ww), ww - 1) + ww - 1
    return idx


_CACHE = {}


def kernel(x, c, ctx, c_ctx, norm_gain, w_mod, b_mod, w_in, da_qk_gain, da_lambda, da_subln_gain,
           na_qk_gain, na_rpb, hg_lb_logits, hg_norm_gain, w_up, w_merge, w_out):
    f = lambda a: np.ascontiguousarray(np.asarray(a, dtype=np.float32))
    x, c, ctx, c_ctx = f(x), f(c), f(ctx), f(c_ctx)
    if "nc" not in _CACHE:
        _CACHE["nc"] = build()
        _CACHE["consts"] = _consts()
        _CACHE["nbidx"] = _nbias_index()
    nc = _CACHE["nc"]
    shared = dict(_CACHE["consts"])
    shared["w_mod"] = f(w_mod)
    shared["w_in"] = f(w_in)
    shared["w_up"] = f(w_up)
    shared["w_merge"] = f(w_merge)
    shared["w_out"] = f(w_out)
    shared["b_modT"] = np.ascontiguousarray(f(b_mod).reshape(NL, 24, 128).transpose(2, 0, 1))
    shared["gainT"] = np.ascontiguousarray(f(norm_gain).reshape(NL, 8, 128).transpose(2, 0, 1))
    smallv = np.concatenate([f(da_qk_gain).reshape(NL, 64), f(da_lambda).reshape(NL, 128), f(da_subln_gain).reshape(NL, 64),
                             f(na_qk_gain).reshape(NL, 128), f(hg_norm_gain).reshape(NL, 64)], -1)
    shared["smallv"] = np.ascontiguousarray(np.broadcast_to(smallv[None], (128, NL, 448)))
    shared["lbT"] = np.ascontiguousarray(f(hg_lb_logits).reshape(2, NL, 2, 128).transpose(3, 0, 2, 1))
    idx = _CACHE["nbidx"]
    rpb = f(na_rpb)
    inw = idx[..., 0] >= 0
    dr = np.where(inw, idx[..., 0], 0)
    dc = np.where(inw, idx[..., 1], 0)
    gath = rpb[:, :, dr, dc]
    gath = np.where(inw[None, None], gath, np.float32(NEG)).astype(np.float32)
    shared["nbias"] = np.ascontiguousarray(gath.transpose(0, 2, 3, 1, 4, 5).reshape(NL, 100, 128, 128))
    in_maps = []
    for i in range(NCORE):
        m = dict(shared)
        m["x"] = np.ascontiguousarray(x[2 * i:2 * i + 2])
        m["ctx"] = np.ascontiguousarray(ctx[2 * i:2 * i + 2])
        cvec = np.stack([c[2 * i], c[2 * i + 1], c_ctx, np.zeros_like(c_ctx)], 0)
        m["cv"] = np.ascontiguousarray(cvec.reshape(4, 8, 128).transpose(2, 1, 0))
        in_maps.append(m)
    res = run_bass_kernel_spmd(nc, in_maps, core_ids=list(range(NCORE)))
    out = np.concatenate([np.asarray(r["out"], dtype=np.float32) for r in res.results], axis=0)
    return out
```

```python
import math
from contextlib import ExitStack

import numpy as np
import ml_dtypes

import concourse.bass as bass
import concourse.mybir as mybir
from concourse.bass_utils import run_bass_kernel_spmd

F32 = mybir.dt.float32
BF16 = mybir.dt.bfloat16
ALU = mybir.AluOpType
AF = mybir.ActivationFunctionType
AX = mybir.AxisListType

NL = 4
D = 1024
NCORE = 8
LAT = 2048
CTX = 256
T = LAT + CTX
NT = T // 128
NCH = T // 64
EPS = 1e-6
BLKS = [(0, 512), (512, 512), (1024, 512), (1536, 512), (2048, 256)]
NEG = -30000.0


class Buf:
    def __init__(self, t=None, name=""):
        self.t = t
        self.name = name
        self.last_w = None
        self.readers = {}

    def __getitem__(self, k):
        return self.t[k]


class K:
    ENG = ("pe", "act", "dve", "pool", "sp")

    def __init__(self, nc, stack, n_dma_sems=48):
        self.nc = nc
        self.stack = stack
        self.cur = stack
        self.streams = {e: [] for e in self.ENG}
        self.sem = {}
        self.cnt = {}
        for e in self.ENG:
            self.sem[e] = stack.enter_context(nc.semaphore("s_" + e))
            self.cnt[e] = 0
        self.ndma = n_dma_sems
        for i in range(n_dma_sems):
            self.sem[("d", i)] = stack.enter_context(nc.semaphore("d%d" % i))
            self.cnt[("d", i)] = 0
        self.dma_rr = 0
        self.known = {e: {} for e in self.ENG}
        self.nbuf = 0

    def sb(self, shape, dtype, name=None):
        self.nbuf += 1
        name = "%s_%d" % (name or "sb", self.nbuf)
        t = self.cur.enter_context(self.nc.sbuf_tensor(name, list(shape), dtype))
        return Buf(t, name)

    def ps(self, shape, dtype, name=None):
        self.nbuf += 1
        name = "%s_%d" % (name or "ps", self.nbuf)
        t = self.cur.enter_context(self.nc.psum_tensor(name, list(shape), dtype))
        return Buf(t, name)

    def dram(self, name, shape, dtype, kind="Internal"):
        t = self.nc.dram_tensor(name, list(shape), dtype, kind=kind)
        return Buf(t.ap(), name)

    def _need(self, eng, reads, writes):
        need = {}

        def add(dep):
            if dep is None:
                return
            k, v = dep
            if need.get(k, 0) < v:
                need[k] = v

        for b in reads:
            add(b.last_w)
        for b in writes:
            add(b.last_w)
            for k, v in b.readers.items():
                add((k, v))
        waits = []
        kn = self.known[eng]
        for k, v in need.items():
            if k == "pe" and eng == "pe":
                continue
            if kn.get(k, 0) >= v:
                continue
            kn[k] = v
            waits.append((k, v))
        return waits

    def op(self, eng, reads, writes, fn, inc=True):
        waits = self._need(eng, reads, writes)
        if inc:
            self.cnt[eng] += 1
            v = self.cnt[eng]
            self.streams[eng].append((waits, fn, (eng, 1)))
        else:
            v = self.cnt[eng] + 1
            self.streams[eng].append((waits, fn, None))
        for b in reads:
            if b.readers.get(eng, 0) < v:
                b.readers[eng] = v
        for b in writes:
            b.last_w = (eng, v)
            b.readers = {}
        return v

    def dma(self, q, reads, writes, fn):
        i = self.dma_rr
        self.dma_rr = (self.dma_rr + 1) % self.ndma
        key = ("d", i)
        waits = self._need(q, reads, writes)
        prev = self.cnt[key]
        if prev > 0 and self.known[q].get(key, 0) < prev:
            self.known[q][key] = prev
            waits.append((key, prev))
        self.cnt[key] += 16
        v = self.cnt[key]
        self.streams[q].append((waits, fn, (key, 16)))
        for b in reads:
            b.readers[key] = v
        for b in writes:
            b.last_w = (key, v)
            b.readers = {}
        return key, v

    def barrier(self):
        for e in self.ENG:
            waits = []
            for k, c in self.cnt.items():
                if c > 0 and (k != e or e in ("act", "dve", "pool")) and self.known[e].get(k, 0) < c:
                    self.known[e][k] = c
                    waits.append((k, c))
            if waits:
                self.streams[e].append((waits, None, None))

    def emit(self):
        nc = self.nc
        waits = [(k, c) for k, c in self.cnt.items() if c > 0 and k != "sp"]
        self.streams["sp"].append((waits, None, None))
        with nc.Block() as block:
            def mk(ename):
                def body(engine):
                    for ws, fn, inc in self.streams[ename]:
                        for kk, v in ws:
                            engine.wait_ge(self.sem[kk], v)
                        if fn is not None:
                            ins = fn(engine)
                            if inc is not None:
                                ins.then_inc(self.sem[inc[0]], inc[1])
                return body
            block.tensor(mk("pe"))
            block.scalar(mk("act"))
            block.vector(mk("dve"))
            block.gpsimd(mk("pool"))
            block.sync(mk("sp"))


def build(nlayers=NL, dbg=False, PIPE_A=True, PIPE_B=True, PIPE_PREP=True, PREFETCH_M=True):
    nc = bass.Bass("TRN2", target_bir_lowering=False)
    st = ExitStack()
    with st:
        k = K(nc, st)

        def MM(out, lhsT, rhs, start, stop, reads, writes, tp=None):
            if tp is None:
                k.op("pe", reads, writes, lambda e: e.matmul(out, lhsT=lhsT, rhs=rhs, start=start, stop=stop), inc=bool(stop))
            else:
                k.op("pe", reads, writes, lambda e: e.matmul(out, lhsT=lhsT, rhs=rhs, start=start, stop=stop, tile_position=tp), inc=bool(stop))

        def RSQ(buf, ap, npart, n, inv_n):
            TS(ap, ap, inv_n, ALU.mult, [buf], [buf], s2=EPS, op1=ALU.add, eng="pool")
            TT(ap, ap, mhalf[0:npart, 0:n], ALU.pow, [buf, mhalf], [buf], eng="pool")

        def TR(out, in_, idn, reads, writes):
            k.op("pe", reads, writes, lambda e: e.transpose(out=out, in_=in_, identity=idn))

        def ACT(out, in_, func, reads, writes, scale=None, bias=None, accum=None):
            kw = {}
            if scale is not None:
                kw["scale"] = scale
            if bias is not None:
                kw["bias"] = bias
            if accum is not None:
                kw["accum_out"] = accum
            k.op("act", reads, writes, lambda e: e.activation(out=out, in_=in_, func=func, **kw))

        def TT(out, in0, in1, op, reads, writes, eng="dve"):
            k.op(eng, reads, writes, lambda e: e.tensor_tensor(out=out, in0=in0, in1=in1, op=op))

        def TS(out, in0, s1, op0, reads, writes, s2=None, op1=None, eng="dve"):
            if op1 is None:
                k.op(eng, reads, writes, lambda e: e.tensor_scalar(out=out, in0=in0, scalar1=s1, scalar2=None, op0=op0))
            else:
                k.op(eng, reads, writes, lambda e: e.tensor_scalar(out=out, in0=in0, scalar1=s1, scalar2=s2, op0=op0, op1=op1))

        def STT(out, in0, scalar, in1, op0, op1, reads, writes):
            k.op("dve", reads, writes, lambda e: e.scalar_tensor_tensor(out=out, in0=in0, scalar=scalar, in1=in1, op0=op0, op1=op1))

        def CP(out, in_, reads, writes, eng="dve"):
            if eng == "act":
                k.op("act", reads, writes, lambda e: e.activation(out=out, in_=in_, func=AF.Copy))
            else:
                k.op(eng, reads, writes, lambda e: e.tensor_copy(out=out, in_=in_))

        def RED(out, in_, reads, writes):
            k.op("dve", reads, writes, lambda e: e.tensor_reduce(out=out, in_=in_, axis=AX.X, op=ALU.add))

        def RECIP(out, in_, reads, writes):
            k.op("dve", reads, writes, lambda e: e.reciprocal(out=out, in_=in_))

        def MSET(ap, val, writes, eng="pool"):
            k.op(eng, [], writes, lambda e: e.memset(ap, val))

        def DMA(q, out, in_, reads, writes):
            k.dma(q, reads, writes, lambda e: e.dma_start(out=out, in_=in_))

        EI = "ExternalInput"
        x_d = k.dram("x", [2, LAT, D], F32, EI)
        ctx_d = k.dram("ctx", [2, CTX, D], F32, EI)
        cv_d = k.dram("cv", [128, 8, 4], F32, EI)
        wmod_d = k.dram("w_mod", [NL, D, 3 * D], F32, EI)
        bmod_d = k.dram("b_modT", [128, NL, 24], F32, EI)
        gain_d = k.dram("gainT", [128, NL, 8], F32, EI)
        win_d = k.dram("w_in", [NL, D, 3840], F32, EI)
        wup_d = k.dram("w_up", [NL, 4, 256, D], F32, EI)
        wmg_d = k.dram("w_merge", [NL, 4, D, D], F32, EI)
        wout_d = k.dram("w_out", [NL, D, D], F32, EI)
        sv_d = k.dram("smallv", [128, NL, 448], F32, EI)
        lb_d = k.dram("lbT", [128, 2, 2, 4], F32, EI)
        nb_d = k.dram("nbias", [NL, 100, 128, 128], F32, EI)
        rope_d = k.dram("rope", [128, 16, 32], F32, EI)
        cs64_d = k.dram("cs64", [128, 256], BF16, EI)
        dftl_d = k.dram("dftL", [128, 16, 2, LAT], BF16, EI)
        dftc_d = k.dram("dftC", [128, 2, 2, CTX], BF16, EI)
        idb_d = k.dram("identb", [128, 128], BF16, EI)
        idf_d = k.dram("identf", [128, 128], F32, EI)
        cm_d = k.dram("cmask", [64, 256], F32, EI)
        out_d = k.dram("out", [2, LAT, D], F32, "ExternalOutput")
        xs_d = k.dram("xs", [2, T, D], F32, "ExternalOutput" if dbg else "Internal")
        ybr_d = k.dram("ybr", [2, 4, 2, 128, T], BF16, "ExternalOutput" if dbg else "Internal")
        xs_tok = [[Buf() for _ in range(NT)] for _ in range(2)]
        ybr_tok = [[Buf() for _ in range(4)] for _ in range(2)]
        cin = Buf()

        identb = k.sb([128, 128], BF16, "identb")
        identf = k.sb([128, 128], F32, "identf")
        onesf = k.sb([128, 128], F32, "onesf")
        cmask = k.sb([64, 256], F32, "cmask")
        rope = k.sb([128, 16, 32], F32, "rope")
        cs64 = k.sb([128, 256], BF16, "cs64")
        dftc = k.sb([128, 2, 2, CTX], BF16, "dftc")
        sv = k.sb([128, NL, 448], F32, "sv")
        bmod = k.sb([128, NL, 24], F32, "bmod")
        gainT = k.sb([128, NL, 8], F32, "gainT")
        sc = k.sb([128, 8, 4], F32, "sc")
        segm = k.sb([128, 512], F32, "segm")
        mhalf = k.sb([128, 16], F32, "mhalf")
        lb_a = k.sb([128, 2, 2, 4], F32, "lb_a")
        lb_c = k.sb([128, 2, 2, 4], F32, "lb_c")
        lb_na = k.sb([128, 2, 2, 4], F32, "lb_na")
        mod = k.sb([128, 24, 4], F32, "mod")
        A1 = k.sb([128, 8, 4], F32, "A1")
        gate_bc = [k.sb([128, D], F32, "gatebc") for _ in range(3)]
        neglam = k.sb([128, 1], F32, "neglam")
        gA = k.sb([128, 2, 32], F32, "gA")
        gS = k.sb([128, 64], F32, "gS")
        gB = k.sb([128, 2, 64], F32, "gB")
        hT = k.sb([128, 8, T], BF16, "hT")
        hT_tok = [Buf() for _ in range(NT)]
        WB0 = k.sb([128, 8, 1024], BF16, "WB0")
        WB1 = k.sb([128, 8192], BF16, "WB1")

        def hts(t0, n):
            return hT_tok[t0 // 128:(t0 + n + 127) // 128]

        DMA("sp", identb[:], idb_d[:], [cin], [identb])
        DMA("sp", identf[:], idf_d[:], [cin], [identf])
        DMA("sp", cmask[:], cm_d[:], [cin], [cmask])
        DMA("sp", rope[:], rope_d[:], [cin], [rope])
        DMA("sp", cs64[:], cs64_d[:], [cin], [cs64])
        DMA("sp", dftc[:], dftc_d[:], [cin], [dftc])
        DMA("sp", sv[:], sv_d[:], [cin], [sv])
        DMA("sp", bmod[:], bmod_d[:], [cin], [bmod])
        DMA("sp", gainT[:], gain_d[:], [cin], [gainT])
        MSET(onesf[:], 1.0, [onesf])
        MSET(mhalf[:], -0.5, [mhalf])
        MSET(segm[:], 1.0, [segm])
        MSET(segm[:].rearrange("p (c s) -> p c s", s=64)[:, :, 0:1], 0.0, [segm])
        for b in range(2):
            DMA("sp", xs_d.t[b, 0:LAT, :], x_d.t[b], [cin], xs_tok[b][0:16])
            DMA("sp", xs_d.t[b, LAT:T, :], ctx_d.t[b], [cin], xs_tok[b][16:18])

        with ExitStack() as ph:
            k.cur = ph
            cvt = k.sb([128, 8, 4], F32)
            th0 = k.sb([128, 8, 4], F32)
            lbl = k.sb([128, 2, 2, 4], F32)
            lbe = k.sb([128, 2, 2, 4], F32)
            lbs = k.sb([128, 2, 2], F32)
            lbv = k.sb([128, 2, 2, 4], F32)
            lbm = k.sb([128, 2, 2, 4], F32)
            DMA("sp", cvt[:], cv_d[:], [cin], [cvt])
            DMA("sp", lbl[:], lb_d[:], [cin], [lbl])
            ACT(th0[:], cvt[:], AF.Tanh, [cvt], [th0], scale=0.5)
            TS(th0[:], th0[:], 0.5, ALU.mult, [th0], [th0], s2=0.5, op1=ALU.add)
            TT(sc[:], th0[:], cvt[:], ALU.mult, [th0, cvt], [sc])
            ACT(lbe[:], lbl[:], AF.Exp, [lbl], [lbe])
            RED(lbs[:], lbe[:], [lbe], [lbs])
            RECIP(lbs[:], lbs[:], [lbs], [lbs])
            TT(lbe[:], lbe[:], lbs[:].unsqueeze(3).broadcast_to([128, 2, 2, 4]), ALU.mult, [lbe, lbs], [lbe])
            MSET(lbv[:], 0.0, [lbv])
            for l in range(1, 4):
                TT(lbv[:, :, :, l:l + 1], lbv[:, :, :, l - 1:l], lbe[:, :, :, l:l + 1], ALU.add, [lbv, lbe], [lbv])
            TS(lb_a[:], lbv[:], 1e-20, ALU.max, [lbv], [lb_a])
            TS(lb_na[:], lbv[:], -1.0, ALU.mult, [lbv], [lb_na], s2=1.0, op1=ALU.add)
            TT(lb_c[:], lb_a[:], lb_na[:], ALU.add, [lb_a, lb_na], [lb_c])
            k.barrier()
        k.cur = st

        for l in range(nlayers):
            last = (l == NL - 1)
            lam_init = 0.8 - 0.6 * math.exp(-0.3 * l)
            nq = 16 if last else NT
            nck = 32 if last else NCH

            with ExitStack() as ph:
                k.cur = ph
                pm = k.ps([128, 24, 4], F32)
                pg = k.ps([128, 1024], F32)
                wm32 = [k.sb([128, 8, 512], F32) for _ in range(2)]
                diag = [k.sb([128, 128], F32) for _ in range(2)]
                lt = k.sb([128, 2, 32], F32)
                le = k.sb([128, 2], F32)
                for ch in range(6):
                    wb_ = wm32[ch % 2]
                    DMA("sp", wb_[:], wmod_d.t[l, :, ch * 512:(ch + 1) * 512].rearrange("(kc p) n -> p kc n", p=128), [cin], [wb_])
                    for jj in range(4):
                        for kc in range(8):
                            MM(pm[:, ch * 4 + jj, :], wb_[:, kc, jj * 128:(jj + 1) * 128], sc[:, kc, :], kc == 0, kc == 7, [wb_, sc], [pm])
                TT(mod[:], pm[:], bmod[:, l, :].unsqueeze(2).broadcast_to([128, 24, 4]), ALU.add, [pm, bmod], [mod])
                TS(A1[:], mod[:, 8:16, :], 1.0, ALU.add, [mod], [A1])
                TT(A1[:], A1[:], gainT[:, l, :].unsqueeze(2).broadcast_to([128, 8, 4]), ALU.mult, [A1, gainT], [A1])
                for v in range(3):
                    for t in range(8):
                        dg = diag[t % 2]
                        TS(dg[:], identf[:], mod[:, 16 + t, v:v + 1], ALU.mult, [identf, mod], [dg])
                        MM(pg[:, t * 128:(t + 1) * 128], onesf[:], dg[:], True, True, [onesf, dg], [pg])
                    CP(gate_bc[v][:, 0:512], pg[:, 0:512], [pg], [gate_bc[v]])
                    CP(gate_bc[v][:, 512:1024], pg[:, 512:1024], [pg], [gate_bc[v]], eng="act")
                lv = sv[:, l, 64:192].rearrange("p (a b d) -> p a b d", a=2, b=2)
                TT(lt[:], lv[:, :, 0, :], lv[:, :, 1, :], ALU.mult, [sv], [lt])
                RED(le[:], lt[:], [lt], [le])
                ACT(le[:], le[:], AF.Exp, [le], [le])
                TT(neglam[:], le[:, 1:2], le[:, 0:1], ALU.subtract, [le], [neglam])
                TS(neglam[:], neglam[:], -lam_init, ALU.add, [neglam], [neglam])
                CP(gA[:], sv[:, l, 0:64].rearrange("p (a d) -> p a d", a=2), [sv], [gA])
                TS(gA[:, 0, :], gA[:, 0, :], 32.0 ** -0.5, ALU.mult, [gA], [gA])
                TS(gS[:], sv[:, l, 192:256], 1.0 - lam_init, ALU.mult, [sv], [gS])
                CP(gB[:], sv[:, l, 256:384].rearrange("p (a d) -> p a d", a=2), [sv], [gB])
                TS(gB[:, 0, :], gB[:, 0, :], 0.125, ALU.mult, [gB], [gB])
                k.barrier()
            k.cur = st

            for b in range(2):
                with ExitStack() as ph:
                    k.cur = ph
                    xt = [k.sb([128, D], F32) for _ in range(2)]
                    sq = k.sb([128, D], F32)
                    ssq = [k.sb([128, 1], F32) for _ in range(2)]
                    xn = [k.sb([128, D], BF16) for _ in range(2)]
                    tmp = [k.sb([128, 8, 128], F32) for _ in range(2)]
                    pT = [k.ps([128, 8, 128], BF16) for _ in range(2)]
                    for t in range(NT):
                        v = b if t < 16 else 2
                        x_, s_, n_, p_, m_ = xt[t % 2], ssq[t % 2], xn[t % 2], pT[t % 2], tmp[t % 2]
                        DMA("sp", x_[:], xs_d.t[b, t * 128:(t + 1) * 128, :], [xs_tok[b][t]], [x_])
                        ACT(sq[:], x_[:], AF.Square, [x_], [sq, s_], accum=s_[:])
                        RSQ(s_, s_[:], 128, 1, 1.0 / D)
                        TS(n_[:], x_[:], s_[:, 0:1], ALU.mult, [x_, s_], [n_])
                        for kc in range(8):
                            TR(p_[:, kc, :], n_[:, kc * 128:(kc + 1) * 128], identb[:], [n_, identb], [p_])
                        TT(m_[:], p_[:], A1[:, :, v:v + 1].broadcast_to([128, 8, 128]), ALU.mult, [p_, A1], [m_])
                        TT(hT[:, :, t * 128:(t + 1) * 128], m_[:], mod[:, 0:8, v:v + 1].broadcast_to([128, 8, 128]), ALU.add,
                           [m_, mod], [hT_tok[t]], eng="pool")
                    k.barrier()
                k.cur = st

                for br in range(2):
                    with ExitStack() as ph:
                        k.cur = ph
                        isA = (br == 0)
                        dh = 32 if isA else 64
                        ng = 512 // dh
                        nh = ng // 2
                        nmap = 8 if isA else 4
                        c0 = 0 if isA else 768
                        gq = gA if isA else gB
                        qkT = k.sb([128, 4, T], BF16)
                        vaug = k.sb([128, NT, 4, 65], BF16)
                        sqt = [k.sb([128, 512], F32) for _ in range(2)]
                        ssg = [k.sb([128, 16], F32) for _ in range(2)]
                        qkn = [k.sb([128, 512], F32) for _ in range(2)]
                        qkb = [k.sb([128, 512], BF16) for _ in range(2)]
                        ta2 = [k.sb([128, 256], F32) for _ in range(2)]
                        tb2 = [k.sb([128, 256], F32) for _ in range(2)]
                        tc2 = [k.sb([128, 256], F32) for _ in range(2)]
                        td2 = [k.sb([128, 256], F32) for _ in range(2)]
                        Pt = [k.sb([128, 512], BF16) for _ in range(3)]
                        Ot2 = [k.sb([128, 8, 65], F32) for _ in range(2)]
                        rsum = k.sb([128, 8], F32)
                        On = k.sb([128, 8, 64], F32)
                        ot = k.sb([128, 256], F32)
                        o2 = k.sb([128, 256], F32)
                        rs4 = k.sb([128, 4], F32)
                        tht = k.sb([128, 256], F32)
                        sgt4 = [k.sb([128, 256], F32) for _ in range(4)]
                        ygb4 = [k.sb([128, 256], BF16) for _ in range(4)]
                        ybt = [k.sb([128, 2, 128], BF16) for _ in range(4)]
                        if isA:
                            OaT = k.sb([66, 8, 512], F32)
                            MSET(OaT[:], 0.0, [OaT])
                        else:
                            nbt = k.sb([128, 100, 128], BF16)
                            DMA("pool", nbt[:], nb_d.t[l].rearrange("n k q -> k n q"), [cin], [nbt])
                        DMA("pool", WB0[:, :, 0:768], win_d.t[l, :, c0:c0 + 768].rearrange("(kc p) n -> p kc n", p=128), [cin], [WB0])
                        g0 = 2816 + 256 * br
                        wg = WB1[:, 0:2048].rearrange("p (kc n) -> p kc n", kc=8)
                        DMA("pool", wg, win_d.t[l, :, g0:g0 + 256].rearrange("(kc p) n -> p kc n", p=128), [cin], [WB1])
                        MSET(vaug[:, :, :, 64:65], 1.0, [vaug])
                        with ExitStack() as ph2:
                            k.cur = ph2
                            pz0 = [k.ps([128, 512], F32) for _ in range(2)]
                            pz1 = [k.ps([128, 512], F32) for _ in range(2)]
                            ptrp = [k.ps([128, 4, 128], BF16) for _ in range(2)]
                            pend = None
                            for t in range(NT):
                                ts_ = slice(t * 128, (t + 1) * 128)
                                z0, z1, pt_ = pz0[t % 2], pz1[t % 2], ptrp[t % 2]
                                sq_, sg_, qn_, qb = sqt[t % 2], ssg[t % 2], qkn[t % 2], qkb[t % 2]
                                for kc in range(8):
                                    MM(z0[:], hT[:, kc, ts_], WB0[:, kc, 0:512], kc == 0, kc == 7, [hT_tok[t], WB0], [z0])
                                for kc in range(8):
                                    MM(z1[:, 0:256], hT[:, kc, ts_], WB0[:, kc, 512:768], kc == 0, kc == 7, [hT_tok[t], WB0], [z1])
                                ACT(sq_[:], z0[:], AF.Square, [z0], [sq_])
                                RED(sg_[:, 0:ng], sq_[:].rearrange("p (g d) -> p g d", d=dh), [sq_], [sg_])
                                RSQ(sg_, sg_[:, 0:ng], 128, ng, 1.0 / dh)
                                TT(qn_[:].rearrange("p (g d) -> p g d", d=dh), z0[:].rearrange("p (g d) -> p g d", d=dh),
                                   sg_[:, 0:ng].unsqueeze(2).broadcast_to([128, ng, dh]), ALU.mult, [z0, sg_], [qn_])
                                if isA and t < 16:
                                    TT(qn_[:].rearrange("p (a h d) -> p a h d", a=2, d=dh), qn_[:].rearrange("p (a h d) -> p a h d", a=2, d=dh),
                                       gq[:].unsqueeze(2).broadcast_to([128, 2, nh, dh]), ALU.mult, [qn_, gq], [qn_], eng="pool")
                                    qv = qn_[:].rearrange("p (g two d) -> p g two d", two=2, d=16)
                                    ov = qb[:].rearrange("p (g two d) -> p g two d", two=2, d=16)
                                    cosb = rope[:, t, 0:16].unsqueeze(1).broadcast_to([128, 16, 16])
                                    sinb = rope[:, t, 16:32].unsqueeze(1).broadcast_to([128, 16, 16])
                                    t3 = lambda buf: buf[:].rearrange("p (g d) -> p g d", d=16)
                                    ta, tb, tc_, td = ta2[t % 2], tb2[t % 2], tc2[t % 2], td2[t % 2]
                                    TT(t3(ta), qv[:, :, 0, :], cosb, ALU.mult, [qn_, rope], [ta])
                                    TT(t3(tb), qv[:, :, 1, :], sinb, ALU.mult, [qn_, rope], [tb])
                                    TT(ov[:, :, 0, :], t3(ta), t3(tb), ALU.subtract, [ta, tb], [qb])
                                    TT(t3(tc_), qv[:, :, 0, :], sinb, ALU.mult, [qn_, rope], [tc_])
                                    TT(t3(td), qv[:, :, 1, :], cosb, ALU.mult, [qn_, rope], [td], eng="pool")
                                    TT(ov[:, :, 1, :], t3(tc_), t3(td), ALU.add, [tc_, td], [qb], eng="pool")
                                else:
                                    TT(qb[:].rearrange("p (a h d) -> p a h d", a=2, d=dh), qn_[:].rearrange("p (a h d) -> p a h d", a=2, d=dh),
                                       gq[:].unsqueeze(2).broadcast_to([128, 2, nh, dh]), ALU.mult, [qn_, gq], [qb], eng="pool")
                                CP(vaug[:, t, :, 0:64], z1[:, 0:256].rearrange("p (h d) -> p h d", d=64), [z1], [vaug], eng="act")

                                def trs(t=t, pt_=pt_, qb=qb, ts_=ts_):
                                    for i in range(4):
                                        TR(pt_[:, i, :], qb[:, i * 128:(i + 1) * 128], identb[:], [qb, identb], [pt_])
                                    CP(qkT[:, :, ts_], pt_[:], [pt_], [qkT], eng="act")
                                if not PIPE_PREP:
                                    trs()
                                    continue
                                if pend is not None:
                                    pend()
                                pend = trs
                            if PIPE_PREP:
                                pend()
                            k.barrier()
                        k.cur = ph
                        Sp = [k.ps([128, 512], F32) for _ in range(2)]
                        if isA:
                            accT = [k.ps([65, 512], F32) for _ in range(2)]
                            pOa = k.ps([128, 4, 66], F32)
                            pOb = k.ps([128, 4, 66], F32)
                        else:
                            acc = [k.ps([128, 4, 65], F32) for _ in range(2)]
                        pgz = k.ps([128, 256], F32)
                        ptr = k.ps([128, 4, 128], BF16)

                        pendq = []

                        def finish(qt, Ot):
                            qs_ = slice(qt * 128, (qt + 1) * 128)
                            sg_ = sgt4[qt % 4]
                            yg_ = ygb4[qt % 4]
                            for kc in range(8):
                                MM(pgz[:], hT[:, kc, qs_], wg[:, kc, :], kc == 0, kc == 7, [hT_tok[qt], WB1], [pgz])
                            ACT(tht[:], pgz[:], AF.Tanh, [pgz], [tht], scale=0.5)
                            STT(sg_[:], tht[:], 1.0, pgz[:], ALU.add, ALU.mult, [tht, pgz], [sg_])
                            RECIP(rsum[:, 0:nmap], Ot[:, 0:nmap, 64], [Ot], [rsum])
                            TT(On[:, 0:nmap, :], Ot[:, 0:nmap, 0:64], rsum[:, 0:nmap].unsqueeze(2).broadcast_to([128, nmap, 64]), ALU.mult,
                               [Ot, rsum], [On])
                            if isA:
                                Onv = On[:].rearrange("p (h two) d -> p h two d", two=2)
                                STT(ot[:].rearrange("p (h d) -> p h d", d=64), Onv[:, :, 1, :], neglam[:, 0:1], Onv[:, :, 0, :],
                                    ALU.mult, ALU.add, [On, neglam], [ot])
                                TT(o2[:], ot[:], ot[:], ALU.mult, [ot], [o2], eng="pool")
                                RED(rs4[:], o2[:].rearrange("p (h d) -> p h d", d=64), [o2], [rs4])
                                RSQ(rs4, rs4[:], 128, 4, 1.0 / 64)
                                TT(o2[:].rearrange("p (h d) -> p h d", d=64), ot[:].rearrange("p (h d) -> p h d", d=64),
                                   rs4[:].unsqueeze(2).broadcast_to([128, 4, 64]), ALU.mult, [ot, rs4], [o2])
                                TT(ot[:].rearrange("p (h d) -> p h d", d=64), o2[:].rearrange("p (h d) -> p h d", d=64),
                                   gS[:].unsqueeze(1).broadcast_to([128, 4, 64]), ALU.mult, [o2, gS], [ot], eng="pool")
                                ysrc = ot[:]
                                ybuf = ot
                            else:
                                ysrc = On[:, 0:4, :].rearrange("p h d -> p (h d)")
                                ybuf = On
                            STT(yg_[:], sg_[:], 0.5, ysrc, ALU.mult, ALU.mult, [sg_, ybuf], [yg_])
                            pendq.append(qt)

                        def flush(keep=0):
                            while len(pendq) > keep:
                                qt = pendq.pop(0)
                                qs_ = slice(qt * 128, (qt + 1) * 128)
                                yg_ = ygb4[qt % 4]
                                yb_ = ybt[qt % 4]
                                for i in range(2):
                                    TR(ptr[:, i, :], yg_[:, i * 128:(i + 1) * 128], identb[:], [yg_, identb], [ptr])
                                CP(yb_[:], ptr[:, 0:2, :], [ptr], [yb_])
                                DMA("sp", ybr_d.t[b, br, :, :, qs_].rearrange("c p n -> p c n"), yb_[:], [yb_], [ybr_tok[b][br]])

                        gi = 0
                        if isA:
                            qblocks = [(0, 512), (512, 512), (1024, 512), (1536, 512)] + ([] if last else [(2048, 256)])
                            for (q0, qn) in qblocks:
                                chunks = list(range(NT)) if q0 < LAT else [16, 17]
                                nci = len(chunks)
                                its = [(m, ci, c) for m in range(8) for ci, c in enumerate(chunks)]

                                def qk(i):
                                    m, ci, c = its[i]
                                    slot, pb = m // 4, 32 * (m % 4)
                                    S_ = Sp[(gi + i) % 2]
                                    P_ = Pt[(gi + i) % 3]
                                    MM(S_[:, 0:qn], qkT[pb:pb + 32, 2 + slot, c * 128:(c + 1) * 128], qkT[pb:pb + 32, slot, q0:q0 + qn],
                                       True, True, [qkT], [S_], tp=(pb, 0))
                                    ACT(P_[:, 0:qn], S_[:, 0:qn], AF.Exp, [S_], [P_])

                                def pv(i):
                                    m, ci, c = its[i]
                                    a_ = accT[m % 2]
                                    P_ = Pt[(gi + i) % 3]
                                    MM(a_[:, 0:qn], vaug[:, c, m // 2, :], P_[:, 0:qn], ci == 0, ci == nci - 1, [vaug, P_], [a_])
                                    if ci == nci - 1:
                                        CP(OaT[0:65, m, 0:qn], a_[:, 0:qn], [a_], [OaT])

                                if PIPE_A:
                                    qk(0)
                                for i in range(len(its)):
                                    if PIPE_A:
                                        if i + 1 < len(its):
                                            qk(i + 1)
                                    else:
                                        qk(i)
                                    pv(i)
                                    if i == 8:
                                        flush()
                                gi += len(its)
                                for j in range(qn // 128):
                                    qt = q0 // 128 + j
                                    Ot = Ot2[qt % 2]
                                    for m in range(8):
                                        pO_ = pOa if m < 4 else pOb
                                        TR(pO_[:, m % 4, :], OaT[:, m, j * 128:(j + 1) * 128], identf[0:66, 0:66], [OaT, identf], [pO_])
                                    CP(Ot[:, 0:4, :], pOa[:, :, 0:65], [pOa], [Ot])
                                    CP(Ot[:, 4:8, :], pOb[:, :, 0:65], [pOb], [Ot])
                                    finish(qt, Ot)
                            flush()
                        else:
                            work = []
                            for qt in range(nq):
                                if qt >= 16:
                                    chunks = [(16, None), (17, None)]
                                else:
                                    cs = min(max(qt - 2, 0), 11)
                                    typ = 0 if qt == 0 else 1 if qt == 1 else 3 if qt == 14 else 4 if qt == 15 else 2
                                    chunks = [(cs + i, typ * 5 + i) for i in range(5)] + [(16, None), (17, None)]
                                groups = [chunks[i:i + 4] for i in range(0, len(chunks), 4)]
                                for m in range(4):
                                    for gidx, g in enumerate(groups):
                                        work.append((qt, m, g, gidx == 0, gidx == len(groups) - 1))

                            def scb(i):
                                qt, m, g, fg, lg = work[i]
                                slot, pb = m // 2, 64 * (m % 2)
                                Sb = Sp[i % 2]
                                Pb = Pt[i % 3]
                                S_ = Sb[:].rearrange("p (c q) -> p c q", q=128)
                                P_ = Pb[:].rearrange("p (c q) -> p c q", q=128)
                                for j, (c, bi) in enumerate(g):
                                    kT_ = qkT[pb:pb + 64, 2 + slot, c * 128:(c + 1) * 128]
                                    qT_ = qkT[pb:pb + 64, slot, qt * 128:(qt + 1) * 128]
                                    if bi is not None:
                                        MM(S_[:, j, :], identb[:], nbt[:, bi * 4 + m, :], True, False, [identb, nbt], [Sb])
                                        MM(S_[:, j, :], kT_, qT_, False, True, [qkT], [Sb], tp=(pb, 0))
                                    else:
                                        MM(S_[:, j, :], kT_, qT_, True, True, [qkT], [Sb], tp=(pb, 0))
                                ACT(P_[:, 0:len(g), :], S_[:, 0:len(g), :], AF.Exp, [Sb], [Pb])

                            def pvb(i):
                                qt, m, g, fg, lg = work[i]
                                Pb = Pt[i % 3]
                                P_ = Pb[:].rearrange("p (c q) -> p c q", q=128)
                                ab = acc[qt % 2]
                                for j, (c, bi) in enumerate(g):
                                    MM(ab[:, m, :], P_[:, j, :], vaug[:, c, m, :], fg and j == 0, lg and j == len(g) - 1, [Pb, vaug], [ab])
                                if lg and m == 3:
                                    Ot = Ot2[qt % 2]
                                    CP(Ot[:, 0:4, :], ab[:], [ab], [Ot])
                                    finish(qt, Ot)
                                if lg and m == 1:
                                    flush()

                            if PIPE_B:
                                scb(0)
                            for i in range(len(work)):
                                if PIPE_B:
                                    if i + 1 < len(work):
                                        scb(i + 1)
                                else:
                                    scb(i)
                                pvb(i)
                            flush()
                        k.barrier()
                    k.cur = st

                DMA("pool", WB0[:], win_d.t[l, :, 1536:2560].rearrange("(kc p) n -> p kc n", p=128), [cin], [WB0])
                wg = WB1[:, 0:2048].rearrange("p (kc n) -> p kc n", kc=8)
                DMA("pool", wg, win_d.t[l, :, 3328:3584].rearrange("(kc p) n -> p kc n", p=128), [cin], [WB1])
                for ct in range(2):
                    with ExitStack() as ph:
                        k.cur = ph
                        KT = [k.sb([128, T], BF16) for _ in range(2)]
                        QBD = [k.sb([128, NCH, 128], BF16) for _ in range(2)]
                        Ktm2 = [k.sb([64, NCH, 128], BF16) for _ in range(2)]
                        Vc = k.sb([64, NCH, 128], BF16)
                        Spr = [k.sb([128, NCH, 64], BF16) for _ in range(2)]
                        er = [k.sb([128, NCH], F32) for _ in range(2)]
                        Sst2 = [k.sb([128, 64], F32) for _ in range(2)]
                        Stm2 = [k.sb([128, 64], F32) for _ in range(2)]
                        for d in range(2):
                            MSET(QBD[d][:], 0.0, [QBD[d]])
                        with ExitStack() as ph2:
                            k.cur = ph2
                            qh = k.sb([128, T], F32)
                            thq = [k.sb([128, 512], F32) for _ in range(2)]
                            NS = 2
                            tht = [k.sb([128, 512], F32) for _ in range(NS)]
                            gg = [k.sb([128, 512], F32) for _ in range(NS)]
                            kk = [k.sb([128, 512], F32) for _ in range(NS)]
                            uu = [k.sb([128, 512], F32) for _ in range(NS)]
                            Bc = [k.sb([128, 512], F32) for _ in range(NS)]
                            tmpb = [k.sb([128, 512], F32) for _ in range(NS)]
                            tot = [k.sb([128, 8], F32) for _ in range(NS)]
                            E1 = [k.sb([128, 512], F32) for _ in range(NS)]
                            E2 = [k.sb([128, 512], F32) for _ in range(NS)]
                            pq = [k.ps([128, 512], F32) for _ in range(4)]
                            pv = [k.ps([64, 4, 128], F32) for _ in range(2)]
                            for c4 in range(0, NCH, 4):
                                pv_ = pv[(c4 // 4) % 2]
                                for i in range(4):
                                    c = c4 + i
                                    for kc in range(8):
                                        MM(pv_[:, i, :], hT[:, kc, c * 64:(c + 1) * 64], WB0[:, kc, 768 + ct * 128:768 + (ct + 1) * 128],
                                           kc == 0, kc == 7, [hT_tok[c // 2], WB0], [pv_])
                                CP(Vc[:, c4:c4 + 4, :], pv_[:], [pv_], [Vc], eng="act")
                            pi = 0
                            for bi_, (t0, n) in enumerate(BLKS):
                                p_ = pq[pi % 4]
                                pi += 1
                                tq = thq[bi_ % 2]
                                for kc in range(8):
                                    MM(p_[:, 0:n], WB0[:, kc, ct * 128:(ct + 1) * 128], hT[:, kc, t0:t0 + n], kc == 0, kc == 7, hts(t0, n) + [WB0], [p_])
                                ACT(tq[:, 0:n], p_[:, 0:n], AF.Tanh, [p_], [tq], scale=0.5)
                                STT(qh[:, t0:t0 + n], tq[:, 0:n], 1.0, p_[:, 0:n], ALU.add, ALU.mult, [tq, p_], [qh])
                            it = 0
                            for (t0, n) in BLKS:
                                nc_ = n // 64
                                cb = t0 // 64
                                bs_ = slice(t0, t0 + n)
                                for d in range(2):
                                    j = it % NS
                                    it += 1
                                    p_ = pq[pi % 4]
                                    pi += 1
                                    cc = 256 + d * 256 + ct * 128
                                    for kc in range(8):
                                        MM(p_[:, 0:n], WB0[:, kc, cc:cc + 128], hT[:, kc, bs_], kc == 0, kc == 7, hts(t0, n) + [WB0], [p_])
                                    ACT(uu[j][:, 0:n], p_[:, 0:n], AF.Exp, [p_], [uu[j]], scale=-1.0)
                                    ACT(E2[j][:, 0:n], uu[j][:, 0:n], AF.Ln, [uu[j]], [E2[j]], bias=1.0)
                                    ACT(E1[j][:, 0:n], uu[j][:, 0:n], AF.Ln, [uu[j], lb_a, lb_c], [E1[j]], scale=lb_a[:, d, ct, l:l + 1],
                                        bias=lb_c[:, d, ct, l:l + 1])
                                    TT(gg[j][:, 0:n], E1[j][:, 0:n], E2[j][:, 0:n], ALU.subtract, [E1[j], E2[j]], [gg[j]], eng="pool")
                                    TT(tht[j][:, 0:n], p_[:, 0:n], E2[j][:, 0:n], ALU.add, [p_, E2[j]], [tht[j]])
                                    ACT(kk[j][:, 0:n], tht[j][:, 0:n], AF.Exp, [tht[j]], [kk[j]], scale=-1.0)
                                    k.op("dve", [segm, gg[j]], [Bc[j]], lambda e, n=n, j=j: e.tensor_tensor_scan(
                                        out=Bc[j][:, 0:n], data0=segm[:, 0:n], data1=gg[j][:, 0:n], initial=0.0, op0=ALU.mult, op1=ALU.add))
                                    B3 = Bc[j][:, 0:n].rearrange("p (c s) -> p c s", s=64)
                                    CP(tot[j][:, 0:nc_], B3[:, :, 63], [Bc[j]], [tot[j]])
                                    if d == 1:
                                        TT(tmpb[j][:, 0:n], gg[j][:, 0:n], Bc[j][:, 0:n], ALU.subtract, [gg[j], Bc[j]], [tmpb[j]])
                                        TT(B3, tmpb[j][:, 0:n].rearrange("p (c s) -> p c s", s=64),
                                           tot[j][:, 0:nc_].unsqueeze(2).broadcast_to([128, nc_, 64]), ALU.add, [tmpb[j], tot[j]], [Bc[j]])
                                    ACT(er[d][:, cb:cb + nc_], tot[j][:, 0:nc_], AF.Exp, [tot[j]], [er[d]], scale=0.5)
                                    STT(tmpb[j][:, 0:n].rearrange("p (c s) -> p c s", s=64), tot[j][:, 0:nc_].unsqueeze(2).broadcast_to([128, nc_, 64]),
                                        -0.5, B3, ALU.mult, ALU.add, [tot[j], Bc[j]], [tmpb[j]])
                                    TS(tmpb[j][:, 0:n], tmpb[j][:, 0:n], 43.0, ALU.min, [tmpb[j]], [tmpb[j]], s2=-43.0, op1=ALU.max, eng="pool")
                                    ACT(E1[j][:, 0:n], tmpb[j][:, 0:n], AF.Exp, [tmpb[j]], [E1[j]])
                                    ACT(E2[j][:, 0:n], tmpb[j][:, 0:n], AF.Exp, [tmpb[j]], [E2[j]], scale=-1.0)
                                    for hh in range(2):
                                        ps_ = slice(64 * hh, 64 * hh + 64)
                                        STT(QBD[d][ps_, cb:cb + nc_, 64 * hh:64 * hh + 64], qh[ps_, bs_].rearrange("p (c s) -> p c s", s=64), 0.5,
                                            E1[j][ps_, 0:n].rearrange("p (c s) -> p c s", s=64), ALU.mult, ALU.mult, [qh, E1[j]], [QBD[d]])
                                    STT(KT[d][:, bs_], kk[j][:, 0:n], lb_na[:, d, ct, l:l + 1], E2[j][:, 0:n], ALU.mult, ALU.mult,
                                        [kk[j], E2[j], lb_na], [KT[d]])
                            k.barrier()
                        k.cur = ph
                        Am = [k.sb([64, 2, 128], BF16) for _ in range(2)]
                        of_ = k.sb([64, 512], F32)
                        o2 = k.sb([64, 512], F32)
                        rs8 = k.sb([64, 8], F32)
                        th2 = k.sb([64, 512], F32)
                        sg2 = k.sb([64, 512], F32)
                        yg2 = [k.sb([64, 512], BF16) for _ in range(2)]
                        ybt = [k.sb([128, 256], BF16) for _ in range(2)]
                        pbk = k.ps([128, 1024], BF16)
                        ptk = pbk[0:64, :].rearrange("p (a b) -> p a b", b=128)
                        pTy = pbk[:, 0:256]
                        pu = [k.ps([128, 128], F32) for _ in range(2)]
                        pa = [k.ps([64, 2, 128], F32) for _ in range(2)]
                        po = k.ps([64, 4, 128], F32)
                        pgz = k.ps([64, 4, 128], F32)
                        for d in range(2):
                            for c8 in range(0, NCH, 4):
                                for i in range(4):
                                    c = c8 + i
                                    TR(ptk[:, i, :], KT[d][:, c * 64:(c + 1) * 64], identb[:], [KT[d], identb], [pbk])
                                CP(Ktm2[d][:, c8:c8 + 4, :], ptk[:, 0:4, :], [pbk], [Ktm2[d]], eng="act")
                            MSET(Sst2[d][:], 0.0, [Sst2[d]], eng="dve")
                        orders = [[32, 33, 34, 35] + list(range(32)), [35, 34, 33, 32] + list(range(31, -1, -1))]
                        for oi in range(NCH):
                            for d in range(2):
                                c = orders[d][oi]
                                pu_ = pu[d]
                                Sst, Stm = Sst2[d], Stm2[d]
                                MM(pu_[:], Ktm2[d][:, c, :], Vc[:, c, :], True, True, [Ktm2[d], Vc], [pu_])
                                e_ = er[d][:, c:c + 1]
                                ACT(Spr[d][:, c, :], Sst[:], AF.Copy, [Sst, er[d]], [Spr[d]], scale=e_)
                                for hh in range(2):
                                    ps_ = slice(64 * hh, 64 * hh + 64)
                                    STT(Stm[ps_, :], Sst[ps_, :], er[d][ps_, c:c + 1], pu_[ps_, 64 * hh:64 * hh + 64], ALU.mult, ALU.add,
                                        [Sst, er[d], pu_], [Stm])
                                TS(Sst[:], Stm[:], e_, ALU.mult, [Stm, er[d]], [Sst])
                        cm4 = cmask[:].rearrange("p (d h t) -> p d h t", d=2, h=2)

                        def scm(c):
                            pa_ = pa[c % 2]
                            am_ = Am[c % 2]
                            for d in range(2):
                                MM(pa_[:, d, :], KT[d][:, c * 64:(c + 1) * 64], QBD[d][:, c, :], True, True, [KT[d], QBD[d]], [pa_])
                            TT(am_[:].rearrange("p d (h t) -> p d h t", h=2), pa_[:].rearrange("p d (h t) -> p d h t", h=2), cm4, ALU.mult,
                               [pa_, cmask], [am_])

                        ptail = []

                        def tail():
                            while ptail:
                                c4 = ptail.pop(0)
                                yg_ = yg2[(c4 // 4) % 2]
                                yb_ = ybt[(c4 // 4) % 2]
                                for i in range(4):
                                    TR(pTy[:, i * 64:(i + 1) * 64], yg_[:, i * 128:(i + 1) * 128], identb[0:64, 0:64], [yg_, identb], [pbk])
                                CP(yb_[:], pTy, [pbk], [yb_], eng="act")
                                DMA("sp", ybr_d.t[b, 2, ct, :, c4 * 64:c4 * 64 + 256], yb_[:], [yb_], [ybr_tok[b][2]])

                        scm(0)
                        for c4 in range(0, nck, 4):
                            for i in range(4):
                                c = c4 + i
                                if c + 1 < nck:
                                    scm(c + 1)
                                for kc in range(8):
                                    MM(pgz[:, i, :], hT[:, kc, c * 64:(c + 1) * 64], wg[:, kc, ct * 128:(ct + 1) * 128], kc == 0, kc == 7,
                                       [hT_tok[c // 2], WB1], [pgz])
                                am_ = Am[c % 2]
                                for hh in range(2):
                                    hs_ = slice(64 * hh, 64 * hh + 64)
                                    for d in range(2):
                                        MM(po[:, i, hs_], am_[:, d, hs_], Vc[:, c, hs_], d == 0, False, [am_, Vc], [po])
                                        MM(po[:, i, hs_], QBD[d][:, c, hs_], Spr[d][:, c, :], False, d == 1, [QBD[d], Spr[d]], [po])
                            tail()
                            pof = po[:].rearrange("p c n -> p (c n)")
                            pgf = pgz[:].rearrange("p c n -> p (c n)")
                            yg_ = yg2[(c4 // 4) % 2]
                            ACT(th2[:], pgf, AF.Tanh, [pgz], [th2], scale=0.5)
                            STT(sg2[:], th2[:], 1.0, pgf, ALU.add, ALU.mult, [th2, pgz], [sg2])
                            CP(of_[:], pof, [po], [of_], eng="act")
                            ACT(o2[:], pof, AF.Square, [po], [o2])
                            RED(rs8[:], o2[:].rearrange("p (g d) -> p g d", d=64), [o2], [rs8])
                            RSQ(rs8, rs8[:], 64, 8, 1.0 / 64)
                            TT(o2[:].rearrange("p (g d) -> p g d", d=64), of_[:].rearrange("p (g d) -> p g d", d=64),
                               rs8[:].unsqueeze(2).broadcast_to([64, 8, 64]), ALU.mult, [of_, rs8], [o2])
                            TT(of_[:].rearrange("p (g d) -> p g d", d=64), o2[:].rearrange("p (g d) -> p g d", d=64),
                               sv[0:64, l, 384:448].unsqueeze(1).broadcast_to([64, 8, 64]), ALU.mult, [o2, sv], [of_], eng="pool")
                            STT(yg_[:], sg2[:], 0.5, of_[:], ALU.mult, ALU.mult, [sg2, of_], [yg_])
                            ptail.append(c4)
                        tail()
                        k.barrier()
                    k.cur = st

                with ExitStack() as ph:
                    k.cur = ph
                    uT = k.sb([128, 2, T], BF16)
                    PQ = k.sb([128, NT, 2, 256], BF16)
                    tab = [k.sb([128, 16, 2, 256], BF16) for _ in range(2)]
                    tht = k.sb([128, 256], F32)
                    sgt = k.sb([128, 256], F32)
                    ybt = [k.sb([128, 2, 256], BF16) for _ in range(2)]
                    pq = [k.ps([128, 512], F32) for _ in range(2)]
                    pp = [k.ps([128, 2, 256], F32) for _ in range(2)]
                    py = [k.ps([128, 256], F32) for _ in range(2)]
                    pgz = [k.ps([128, 256], F32) for _ in range(2)]
                    DMA("pool", WB0[:, :, 0:256], win_d.t[l, :, 2560:2816].rearrange("(kc p) n -> p kc n", p=128), [cin], [WB0])
                    wg = WB1[:, 0:2048].rearrange("p (kc n) -> p kc n", kc=8)
                    DMA("pool", wg, win_d.t[l, :, 3584:3840].rearrange("(kc p) n -> p kc n", p=128), [cin], [WB1])
                    pi = 0
                    for ct in range(2):
                        for (t0, n) in BLKS:
                            p_ = pq[pi % 2]
                            pi += 1
                            for kc in range(8):
                                MM(p_[:, 0:n], WB0[:, kc, ct * 128:(ct + 1) * 128], hT[:, kc, t0:t0 + n], kc == 0, kc == 7, hts(t0, n) + [WB0], [p_])
                            CP(uT[:, ct, t0:t0 + n], p_[:, 0:n], [p_], [uT], eng="act")
                    for t in range(NT):
                        p_ = pp[t % 2]
                        for ct in range(2):
                            MM(p_[:, ct, :], uT[:, ct, t * 128:(t + 1) * 128], cs64[:], True, True, [uT, cs64], [p_])
                        CP(PQ[:, t, :, :], p_[:], [p_], [PQ], eng=("act" if t % 2 else "dve"))
                    it = 0
                    nblk = 8 if last else 9
                    for nb in range(nblk):
                        if nb < 8:
                            tb_ = tab[nb % 2]
                            DMA("sp", tb_[:], dftl_d.t[:, :, :, nb * 256:(nb + 1) * 256], [cin], [tb_])
                            tcs = list(range(16))
                            tsrc = lambda tc, cs: tb_[:, tc, cs, :]
                            tread = [tb_]
                            t0 = nb * 256
                        else:
                            tcs = [16, 17]
                            tsrc = lambda tc, cs: dftc[:, tc - 16, cs, :]
                            tread = [dftc]
                            t0 = LAT
                        yb_ = ybt[nb % 2]
                        for ct in range(2):
                            y_ = py[it % 2]
                            g_ = pgz[it % 2]
                            it += 1
                            nmm = len(tcs) * 2
                            j = 0
                            for tc in tcs:
                                for cs in range(2):
                                    MM(y_[:], PQ[:, tc, ct, cs * 128:(cs + 1) * 128], tsrc(tc, cs), j == 0, j == nmm - 1, [PQ] + tread, [y_])
                                    j += 1
                            for kc in range(8):
                                MM(g_[:], wg[:, kc, ct * 128:(ct + 1) * 128], hT[:, kc, t0:t0 + 256], kc == 0, kc == 7, hts(t0, 256) + [WB1], [g_])
                            ACT(tht[:], g_[:], AF.Tanh, [g_], [tht], scale=0.5)
                            STT(sgt[:], tht[:], 1.0, g_[:], ALU.add, ALU.mult, [tht, g_], [sgt])
                            STT(yb_[:, ct, :], sgt[:], 0.5, y_[:], ALU.mult, ALU.mult, [sgt, y_], [yb_])
                        DMA("sp", ybr_d.t[b, 3, :, :, t0:t0 + 256].rearrange("c p n -> p c n"), yb_[:], [yb_], [ybr_tok[b][3]])
                    k.barrier()
                k.cur = st

                with ExitStack() as ph:
                    k.cur = ph
                    ybS = k.sb([128, 4, 2, T], BF16)
                    accT = k.sb([128, 8, T], BF16)
                    wu = [k.sb([128, 4, 2, 128], BF16) for _ in range(2)]
                    wm_tok = [Buf(), Buf()]
                    tht = [k.sb([128, 512], F32) for _ in range(2)]
                    tmpm = [k.sb([128, 512], F32) for _ in range(2)]
                    accf = [k.sb([128, 512], F32) for _ in range(2)]
                    xt = [k.sb([128, D], F32) for _ in range(2)]
                    xo = [k.sb([128, D], F32) for _ in range(2)]
                    tmo = [k.sb([128, 512], F32) for _ in range(2)]
                    pM = [k.ps([128, 512], F32) for _ in range(2)]
                    pU = [k.ps([128, 512], F32) for _ in range(2)]
                    pO = [k.ps([128, 512], F32) for _ in range(2)]
                    for i in range(4):
                        DMA("sp", ybS[:, i, :, :], ybr_d.t[b, i].rearrange("c p n -> p c n"), [ybr_tok[b][i]], [ybS])
                    DMA("pool", WB0[:], wout_d.t[l].rearrange("(kc p) n -> p kc n", p=128), [cin], [WB0])
                    mblks = BLKS[:4] if last else BLKS
                    im = 0
                    def wload(ft):
                        fs_ = slice(ft * 128, (ft + 1) * 128)
                        wmv = WB1[:, (ft % 2) * 4096:(ft % 2 + 1) * 4096].rearrange("p (i kc n) -> p i kc n", i=4, kc=8)
                        DMA("pool", wmv, wmg_d.t[l, :, :, fs_].rearrange("i (kc p) n -> p i kc n", p=128), [cin], [wm_tok[ft % 2]])
                        DMA("pool", wu[ft % 2][:], wup_d.t[l, :, :, fs_].rearrange("i (fc p) n -> p i fc n", p=128), [cin], [wu[ft % 2]])

                    if PREFETCH_M:
                        wload(0)
                    for ft in range(8):
                        fs_ = slice(ft * 128, (ft + 1) * 128)
                        wmv = WB1[:, (ft % 2) * 4096:(ft % 2 + 1) * 4096].rearrange("p (i kc n) -> p i kc n", i=4, kc=8)
                        wmt = wm_tok[ft % 2]
                        wu_ = wu[ft % 2]
                        if not PREFETCH_M:
                            wload(ft)
                        elif ft + 1 < 8:
                            wload(ft + 1)
                        for bi_, (t0, n) in enumerate(mblks):
                            af = accf[bi_ % 2]
                            for i in range(4):
                                m_ = pM[im % 2]
                                u_ = pU[im % 2]
                                th_ = tht[im % 2]
                                tm_ = tmpm[im % 2]
                                im += 1
                                for kc in range(8):
                                    MM(m_[:, 0:n], wmv[:, i, kc, :], hT[:, kc, t0:t0 + n], kc == 0, kc == 7, hts(t0, n) + [wmt], [m_])
                                for fc in range(2):
                                    MM(u_[:, 0:n], wu_[:, i, fc, :], ybS[:, i, fc, t0:t0 + n], fc == 0, fc == 1, [wu_, ybS], [u_])
                                ACT(th_[:, 0:n], m_[:, 0:n], AF.Tanh, [m_], [th_], scale=0.5)
                                if i == 0:
                                    STT(af[:, 0:n], th_[:, 0:n], 1.0, u_[:, 0:n], ALU.add, ALU.mult, [th_, u_], [af])
                                else:
                                    STT(tm_[:, 0:n], th_[:, 0:n], 1.0, u_[:, 0:n], ALU.add, ALU.mult, [th_, u_], [tm_])
                                    TT(af[:, 0:n], af[:, 0:n], tm_[:, 0:n], ALU.add, [af, tm_], [af], eng="pool")
                            ACT(accT[:, ft, t0:t0 + n], af[:, 0:n], AF.Copy, [af], [accT], scale=0.5)
                    ntile = 16 if last else NT
                    io = 0
                    for t in range(ntile):
                        v = b if t < 16 else 2
                        x_ = xt[t % 2]
                        o_ = xo[t % 2]
                        DMA("sp", x_[:], xs_d.t[b, t * 128:(t + 1) * 128, :], [xs_tok[b][t]], [x_])
                        for hf in range(2):
                            p_ = pO[io % 2]
                            tm_ = tmo[io % 2]
                            io += 1
                            cs_ = slice(hf * 512, (hf + 1) * 512)
                            for kc in range(8):
                                MM(p_[:], accT[:, kc, t * 128:(t + 1) * 128], WB0[:, kc, cs_], kc == 0, kc == 7, [accT, WB0], [p_])
                            TT(tm_[:], p_[:], gate_bc[v][:, cs_], ALU.mult, [p_, gate_bc[v]], [tm_])
                            TT(o_[:, cs_], tm_[:], x_[:, cs_], ALU.add, [tm_, x_], [o_], eng="pool")
                        if last:
                            DMA("sp", out_d.t[b, t * 128:(t + 1) * 128, :], o_[:], [o_], [xs_tok[b][t]])
                        else:
                            DMA("sp", xs_d.t[b, t * 128:(t + 1) * 128, :], o_[:], [o_], [xs_tok[b][t]])
                    k.barrier()
                k.cur = st
        k.emit()
    return nc


def _consts():
    bf = ml_dtypes.bfloat16
    c = {}
    c["identb"] = np.eye(128, dtype=np.float32).astype(bf)
    c["identf"] = np.eye(128, dtype=np.float32)
    s = np.arange(64)[:, None]
    t = np.arange(64)[None, :]
    mf = (s <= t).astype(np.float32)
    mb = (s >= t).astype(np.float32)
    cm = np.stack([np.stack([mf, mf], 0), np.stack([mb, mb], 0)], 0)
    c["cmask"] = np.ascontiguousarray(cm.transpose(2, 0, 1, 3).reshape(64, 256))
    n = np.arange(LAT)
    row = (n // 64).astype(np.float32)
    col = (n % 64).astype(np.float32)
    inv = (10000.0 ** (-np.arange(0, 16, 2, dtype=np.float32) / 16)).astype(np.float32)
    ang = np.concatenate([row[:, None] * inv, col[:, None] * inv], -1).astype(np.float32)
    rp = np.concatenate([np.cos(ang), np.sin(ang)], -1).astype(np.float32)
    c["rope"] = np.ascontiguousarray(rp.reshape(16, 128, 32).transpose(1, 0, 2))
    e = np.arange(64)
    a64 = 2 * np.pi * np.outer(e, e) / 64
    C64 = np.cos(a64) / 8.0
    S64 = np.sin(a64) / 8.0
    cs = np.zeros((128, 256), np.float64)
    for g in range(2):
        cs[g * 64:(g + 1) * 64, g * 64:(g + 1) * 64] = C64
        cs[g * 64:(g + 1) * 64, 128 + g * 64:128 + (g + 1) * 64] = S64
    c["cs64"] = cs.astype(np.float32).astype(bf)

    def dft(N):
        tt = np.arange(N)
        a = 2 * np.pi * ((np.outer(tt, tt)) % N) / N
        Cn = np.cos(a) / np.sqrt(N)
        Sn = -np.sin(a) / np.sqrt(N)
        tab = np.stack([Cn, Sn], 1)
        return np.ascontiguousarray(tab.reshape(N // 128, 128, 2, N).transpose(1, 0, 2, 3)).astype(np.float32).astype(bf)

    c["dftL"] = dft(LAT)
    c["dftC"] = dft(CTX)
    return c


def _nbias_index():
    rows, W, wh, ww = 32, 64, 8, 16
    idx = np.full((5, 5, 128, 128, 2), -1, np.int64)
    for typ, qt in enumerate([0, 1, 5, 14, 15]):
        cs = min(max(qt - 2, 0), 11)
        for i in range(5):
            kc = cs + i
            for kk in range(128):
                kr, kcol = 2 * kc + kk // 64, kk % 64
                for q in range(128):
                    r, cq = 2 * qt + q // 64, q % 64
                    r0 = min(max(r - wh // 2, 0), rows - wh)
                    c0 = min(max(cq - ww // 2, 0), W - ww)
                    if r0 <= kr < r0 + wh and c0 <= kcol < c0 + ww:
                        idx[typ, i, kk, q, 0] = kr - r + wh - 1
                        idx[typ, i, kk, q, 1] = min(max(kcol - cq, 1 - ww), ww - 1) + ww - 1
    return idx


_CACHE = {}


def kernel(x, c, ctx, c_ctx, norm_gain, w_mod, b_mod, w_in, da_qk_gain, da_lambda, da_subln_gain,
           na_qk_gain, na_rpb, hg_lb_logits, hg_norm_gain, w_up, w_merge, w_out):
    f = lambda a: np.ascontiguousarray(np.asarray(a, dtype=np.float32))
    x, c, ctx, c_ctx = f(x), f(c), f(ctx), f(c_ctx)
    if "nc" not in _CACHE:
        _CACHE["nc"] = build()
        _CACHE["consts"] = _consts()
        _CACHE["nbidx"] = _nbias_index()
    nc = _CACHE["nc"]
    shared = dict(_CACHE["consts"])
    shared["w_mod"] = f(w_mod)
    shared["w_in"] = f(w_in)
    shared["w_up"] = f(w_up)
    shared["w_merge"] = f(w_merge)
    shared["w_out"] = f(w_out)
    shared["b_modT"] = np.ascontiguousarray(f(b_mod).reshape(NL, 24, 128).transpose(2, 0, 1))
    shared["gainT"] = np.ascontiguousarray(f(norm_gain).reshape(NL, 8, 128).transpose(2, 0, 1))
    smallv = np.concatenate([f(da_qk_gain).reshape(NL, 64), f(da_lambda).reshape(NL, 128), f(da_subln_gain).reshape(NL, 64),
                             f(na_qk_gain).reshape(NL, 128), f(hg_norm_gain).reshape(NL, 64)], -1)
    shared["smallv"] = np.ascontiguousarray(np.broadcast_to(smallv[None], (128, NL, 448)))
    shared["lbT"] = np.ascontiguousarray(f(hg_lb_logits).reshape(2, NL, 2, 128).transpose(3, 0, 2, 1))
    idx = _CACHE["nbidx"]
    rpb = f(na_rpb)
    inw = idx[..., 0] >= 0
    dr = np.where(inw, idx[..., 0], 0)
    dc = np.where(inw, idx[..., 1], 0)
    gath = rpb[:, :, dr, dc]
    gath = np.where(inw[None, None], gath, np.float32(NEG)).astype(np.float32)
    shared["nbias"] = np.ascontiguousarray(gath.transpose(0, 2, 3, 1, 4, 5).reshape(NL, 100, 128, 128))
    in_maps = []
    for i in range(NCORE):
        m = dict(shared)
        m["x"] = np.ascontiguousarray(x[2 * i:2 * i + 2])
        m["ctx"] = np.ascontiguousarray(ctx[2 * i:2 * i + 2])
        cvec = np.stack([c[2 * i], c[2 * i + 1], c_ctx, np.zeros_like(c_ctx)], 0)
        m["cv"] = np.ascontiguousarray(cvec.reshape(4, 8, 128).transpose(2, 1, 0))
        in_maps.append(m)
    res = run_bass_kernel_spmd(nc, in_maps, core_ids=list(range(NCORE)))
    out = np.concatenate([np.asarray(r["out"], dtype=np.float32) for r in res.results], axis=0)
    return out
```

```python
import math
from contextlib import ExitStack

import numpy as np
import ml_dtypes

import concourse.bass as bass
import concourse.mybir as mybir
from concourse.bass_utils import run_bass_kernel_spmd

F32 = mybir.dt.float32
BF16 = mybir.dt.bfloat16
ALU = mybir.AluOpType
AF = mybir.ActivationFunctionType
AX = mybir.AxisListType

NL = 4
D = 1024
NCORE = 8
LAT = 2048
CTX = 256
T = LAT + CTX
NT = T // 128
NCH = T // 64
EPS = 1e-6
BLKS = [(0, 512), (512, 512), (1024, 512), (1536, 512), (2048, 256)]
NEG = -30000.0


class Buf:
    def __init__(self, t=None, name=""):
        self.t = t
        self.name = name
        self.last_w = None
        self.readers = {}

    def __getitem__(self, k):
        return self.t[k]


class K:
    ENG = ("pe", "act", "dve", "pool", "sp")

    def __init__(self, nc, stack, n_dma_sems=48):
        self.nc = nc
        self.stack = stack
        self.cur = stack
        self.streams = {e: [] for e in self.ENG}
        self.sem = {}
        self.cnt = {}
        for e in self.ENG:
            self.sem[e] = stack.enter_context(nc.semaphore("s_" + e))
            self.cnt[e] = 0
        self.ndma = n_dma_sems
        for i in range(n_dma_sems):
            self.sem[("d", i)] = stack.enter_context(nc.semaphore("d%d" % i))
            self.cnt[("d", i)] = 0
        self.dma_rr = 0
        self.known = {e: {} for e in self.ENG}
        self.nbuf = 0

    def sb(self, shape, dtype, name=None):
        self.nbuf += 1
        name = "%s_%d" % (name or "sb", self.nbuf)
        t = self.cur.enter_context(self.nc.sbuf_tensor(name, list(shape), dtype))
        return Buf(t, name)

    def ps(self, shape, dtype, name=None):
        self.nbuf += 1
        name = "%s_%d" % (name or "ps", self.nbuf)
        t = self.cur.enter_context(self.nc.psum_tensor(name, list(shape), dtype))
        return Buf(t, name)

    def dram(self, name, shape, dtype, kind="Internal"):
        t = self.nc.dram_tensor(name, list(shape), dtype, kind=kind)
        return Buf(t.ap(), name)

    def _need(self, eng, reads, writes):
        need = {}

        def add(dep):
            if dep is None:
                return
            k, v = dep
            if need.get(k, 0) < v:
                need[k] = v

        for b in reads:
            add(b.last_w)
        for b in writes:
            add(b.last_w)
            for k, v in b.readers.items():
                add((k, v))
        waits = []
        kn = self.known[eng]
        for k, v in need.items():
            if k == "pe" and eng == "pe":
                continue
            if kn.get(k, 0) >= v:
                continue
            kn[k] = v
            waits.append((k, v))
        return waits

    def op(self, eng, reads, writes, fn, inc=True):
        waits = self._need(eng, reads, writes)
        if inc:
            self.cnt[eng] += 1
            v = self.cnt[eng]
            self.streams[eng].append((waits, fn, (eng, 1)))
        else:
            v = self.cnt[eng] + 1
            self.streams[eng].append((waits, fn, None))
        for b in reads:
            if b.readers.get(eng, 0) < v:
                b.readers[eng] = v
        for b in writes:
            b.last_w = (eng, v)
            b.readers = {}
        return v

    def dma(self, q, reads, writes, fn):
        i = self.dma_rr
        self.dma_rr = (self.dma_rr + 1) % self.ndma
        key = ("d", i)
        waits = self._need(q, reads, writes)
        prev = self.cnt[key]
        if prev > 0 and self.known[q].get(key, 0) < prev:
            self.known[q][key] = prev
            waits.append((key, prev))
        self.cnt[key] += 16
        v = self.cnt[key]
        self.streams[q].append((waits, fn, (key, 16)))
        for b in reads:
            b.readers[key] = v
        for b in writes:
            b.last_w = (key, v)
            b.readers = {}
        return key, v

    def barrier(self):
        for e in self.ENG:
            waits = []
            for k, c in self.cnt.items():
                if c > 0 and (k != e or e in ("act", "dve", "pool")) and self.known[e].get(k, 0) < c:
                    self.known[e][k] = c
                    waits.append((k, c))
            if waits:
                self.streams[e].append((waits, None, None))

    def emit(self):
        nc = self.nc
        waits = [(k, c) for k, c in self.cnt.items() if c > 0 and k != "sp"]
        self.streams["sp"].append((waits, None, None))
        with nc.Block() as block:
            def mk(ename):
                def body(engine):
                    for ws, fn, inc in self.streams[ename]:
                        for kk, v in ws:
                            engine.wait_ge(self.sem[kk], v)
                        if fn is not None:
                            ins = fn(engine)
                            if inc is not None:
                                ins.then_inc(self.sem[inc[0]], inc[1])
                return body
            block.tensor(mk("pe"))
            block.scalar(mk("act"))
            block.vector(mk("dve"))
            block.gpsimd(mk("pool"))
            block.sync(mk("sp"))


def build(nlayers=NL, dbg=False, PIPE_A=True, PIPE_B=True, PIPE_PREP=True, PREFETCH_M=True):
    nc = bass.Bass("TRN2", target_bir_lowering=False)
    st = ExitStack()
    with st:
        k = K(nc, st)

        def MM(out, lhsT, rhs, start, stop, reads, writes, tp=None):
            if tp is None:
                k.op("pe", reads, writes, lambda e: e.matmul(out, lhsT=lhsT, rhs=rhs, start=start, stop=stop), inc=bool(stop))
            else:
                k.op("pe", reads, writes, lambda e: e.matmul(out, lhsT=lhsT, rhs=rhs, start=start, stop=stop, tile_position=tp), inc=bool(stop))

        def RSQ(buf, ap, npart, n, inv_n):
            TS(ap, ap, inv_n, ALU.mult, [buf], [buf], s2=EPS, op1=ALU.add, eng="pool")
            TT(ap, ap, mhalf[0:npart, 0:n], ALU.pow, [buf, mhalf], [buf], eng="pool")

        def TR(out, in_, idn, reads, writes):
            k.op("pe", reads, writes, lambda e: e.transpose(out=out, in_=in_, identity=idn))

        def ACT(out, in_, func, reads, writes, scale=None, bias=None, accum=None):
            kw = {}
            if scale is not None:
                kw["scale"] = scale
            if bias is not None:
                kw["bias"] = bias
            if accum is not None:
                kw["accum_out"] = accum
            k.op("act", reads, writes, lambda e: e.activation(out=out, in_=in_, func=func, **kw))

        def TT(out, in0, in1, op, reads, writes, eng="dve"):
            k.op(eng, reads, writes, lambda e: e.tensor_tensor(out=out, in0=in0, in1=in1, op=op))

        def TS(out, in0, s1, op0, reads, writes, s2=None, op1=None, eng="dve"):
            if op1 is None:
                k.op(eng, reads, writes, lambda e: e.tensor_scalar(out=out, in0=in0, scalar1=s1, scalar2=None, op0=op0))
            else:
                k.op(eng, reads, writes, lambda e: e.tensor_scalar(out=out, in0=in0, scalar1=s1, scalar2=s2, op0=op0, op1=op1))

        def STT(out, in0, scalar, in1, op0, op1, reads, writes):
            k.op("dve", reads, writes, lambda e: e.scalar_tensor_tensor(out=out, in0=in0, scalar=scalar, in1=in1, op0=op0, op1=op1))

        def CP(out, in_, reads, writes, eng="dve"):
            if eng == "act":
                k.op("act", reads, writes, lambda e: e.activation(out=out, in_=in_, func=AF.Copy))
            else:
                k.op(eng, reads, writes, lambda e: e.tensor_copy(out=out, in_=in_))

        def RED(out, in_, reads, writes):
            k.op("dve", reads, writes, lambda e: e.tensor_reduce(out=out, in_=in_, axis=AX.X, op=ALU.add))

        def RECIP(out, in_, reads, writes):
            k.op("dve", reads, writes, lambda e: e.reciprocal(out=out, in_=in_))

        def MSET(ap, val, writes, eng="pool"):
            k.op(eng, [], writes, lambda e: e.memset(ap, val))

        def DMA(q, out, in_, reads, writes):
            k.dma(q, reads, writes, lambda e: e.dma_start(out=out, in_=in_))

        EI = "ExternalInput"
        x_d = k.dram("x", [2, LAT, D], F32, EI)
        ctx_d = k.dram("ctx", [2, CTX, D], F32, EI)
        cv_d = k.dram("cv", [128, 8, 4], F32, EI)
        wmod_d = k.dram("w_mod", [NL, D, 3 * D], F32, EI)
        bmod_d = k.dram("b_modT", [128, NL, 24], F32, EI)
        gain_d = k.dram("gainT", [128, NL, 8], F32, EI)
        win_d = k.dram("w_in", [NL, D, 3840], F32, EI)
        wup_d = k.dram("w_up", [NL, 4, 256, D], F32, EI)
        wmg_d = k.dram("w_merge", [NL, 4, D, D], F32, EI)
        wout_d = k.dram("w_out", [NL, D, D], F32, EI)
        sv_d = k.dram("smallv", [128, NL, 448], F32, EI)
        lb_d = k.dram("lbT", [128, 2, 2, 4], F32, EI)
        nb_d = k.dram("nbias", [NL, 100, 128, 128], F32, EI)
        rope_d = k.dram("rope", [128, 16, 32], F32, EI)
        cs64_d = k.dram("cs64", [128, 256], BF16, EI)
        dftl_d = k.dram("dftL", [128, 16, 2, LAT], BF16, EI)
        dftc_d = k.dram("dftC", [128, 2, 2, CTX], BF16, EI)
        idb_d = k.dram("identb", [128, 128], BF16, EI)
        idf_d = k.dram("identf", [128, 128], F32, EI)
        cm_d = k.dram("cmask", [64, 256], F32, EI)
        out_d = k.dram("out", [2, LAT, D], F32, "ExternalOutput")
        xs_d = k.dram("xs", [2, T, D], F32, "ExternalOutput" if dbg else "Internal")
        ybr_d = k.dram("ybr", [2, 4, 2, 128, T], BF16, "ExternalOutput" if dbg else "Internal")
        xs_tok = [[Buf() for _ in range(NT)] for _ in range(2)]
        ybr_tok = [[Buf() for _ in range(4)] for _ in range(2)]
        cin = Buf()

        identb = k.sb([128, 128], BF16, "identb")
        identf = k.sb([128, 128], F32, "identf")
        onesf = k.sb([128, 128], F32, "onesf")
        cmask = k.sb([64, 256], F32, "cmask")
        rope = k.sb([128, 16, 32], F32, "rope")
        cs64 = k.sb([128, 256], BF16, "cs64")
        dftc = k.sb([128, 2, 2, CTX], BF16, "dftc")
        sv = k.sb([128, NL, 448], F32, "sv")
        bmod = k.sb([128, NL, 24], F32, "bmod")
        gainT = k.sb([128, NL, 8], F32, "gainT")
        sc = k.sb([128, 8, 4], F32, "sc")
        segm = k.sb([128, 512], F32, "segm")
        mhalf = k.sb([128, 16], F32, "mhalf")
        rmask32 = k.sb([128, 4], F32, "rmask32")
        rmask64 = k.sb([128, 2], F32, "rmask64")
        lb_a = k.sb([128, 2, 2, 4], F32, "lb_a")
        lb_c = k.sb([128, 2, 2, 4], F32, "lb_c")
        lb_na = k.sb([128, 2, 2, 4], F32, "lb_na")
        mod = k.sb([128, 24, 4], F32, "mod")
        A1 = k.sb([128, 8, 4], F32, "A1")
        gate_bc = [k.sb([128, D], F32, "gatebc") for _ in range(3)]
        neglam = k.sb([128, 1], F32, "neglam")
        gA = k.sb([128, 2, 32], F32, "gA")
        gS = k.sb([128, 64], F32, "gS")
        gB = k.sb([128, 2, 64], F32, "gB")
        hT = k.sb([128, 8, T], BF16, "hT")
        hT_tok = [Buf() for _ in range(NT)]
        WB0 = k.sb([128, 8, 1024], BF16, "WB0")
        WB1 = k.sb([128, 8192], BF16, "WB1")

        def hts(t0, n):
            return hT_tok[t0 // 128:(t0 + n + 127) // 128]

        DMA("sp", identb[:], idb_d[:], [cin], [identb])
        DMA("sp", identf[:], idf_d[:], [cin], [identf])
        DMA("sp", cmask[:], cm_d[:], [cin], [cmask])
        DMA("sp", rope[:], rope_d[:], [cin], [rope])
        DMA("sp", cs64[:], cs64_d[:], [cin], [cs64])
        DMA("sp", dftc[:], dftc_d[:], [cin], [dftc])
        DMA("sp", sv[:], sv_d[:], [cin], [sv])
        DMA("sp", bmod[:], bmod_d[:], [cin], [bmod])
        DMA("sp", gainT[:], gain_d[:], [cin], [gainT])
        MSET(onesf[:], 1.0, [onesf])
        MSET(mhalf[:], -0.5, [mhalf])
        RED(rmask32[:], identf[:].rearrange("p (j c) -> p j c", c=32), [identf], [rmask32])
        RED(rmask64[:], identf[:].rearrange("p (j c) -> p j c", c=64), [identf], [rmask64])
        MSET(segm[:], 1.0, [segm])
        MSET(segm[:].rearrange("p (c s) -> p c s", s=64)[:, :, 0:1], 0.0, [segm])
        for b in range(2):
            DMA("sp", xs_d.t[b, 0:LAT, :], x_d.t[b], [cin], xs_tok[b][0:16])
            DMA("sp", xs_d.t[b, LAT:T, :], ctx_d.t[b], [cin], xs_tok[b][16:18])

        with ExitStack() as ph:
            k.cur = ph
            cvt = k.sb([128, 8, 4], F32)
            th0 = k.sb([128, 8, 4], F32)
            lbl = k.sb([128, 2, 2, 4], F32)
            lbe = k.sb([128, 2, 2, 4], F32)
            lbs = k.sb([128, 2, 2], F32)
            lbv = k.sb([128, 2, 2, 4], F32)
            lbm = k.sb([128, 2, 2, 4], F32)
            DMA("sp", cvt[:], cv_d[:], [cin], [cvt])
            DMA("sp", lbl[:], lb_d[:], [cin], [lbl])
            ACT(th0[:], cvt[:], AF.Tanh, [cvt], [th0], scale=0.5)
            TS(th0[:], th0[:], 0.5, ALU.mult, [th0], [th0], s2=0.5, op1=ALU.add)
            TT(sc[:], th0[:], cvt[:], ALU.mult, [th0, cvt], [sc])
            ACT(lbe[:], lbl[:], AF.Exp, [lbl], [lbe])
            RED(lbs[:], lbe[:], [lbe], [lbs])
            RECIP(lbs[:], lbs[:], [lbs], [lbs])
            TT(lbe[:], lbe[:], lbs[:].unsqueeze(3).broadcast_to([128, 2, 2, 4]), ALU.mult, [lbe, lbs], [lbe])
            MSET(lbv[:], 0.0, [lbv])
            for l in range(1, 4):
                TT(lbv[:, :, :, l:l + 1], lbv[:, :, :, l - 1:l], lbe[:, :, :, l:l + 1], ALU.add, [lbv, lbe], [lbv])
            TS(lb_a[:], lbv[:], 1e-20, ALU.max, [lbv], [lb_a])
            TS(lb_na[:], lbv[:], -1.0, ALU.mult, [lbv], [lb_na], s2=1.0, op1=ALU.add)
            TT(lb_c[:], lb_a[:], lb_na[:], ALU.add, [lb_a, lb_na], [lb_c])
            k.barrier()
        k.cur = st

        for l in range(nlayers):
            last = (l == NL - 1)
            lam_init = 0.8 - 0.6 * math.exp(-0.3 * l)
            nq = 16 if last else NT
            nck = 32 if last else NCH

            with ExitStack() as ph:
                k.cur = ph
                pm = k.ps([128, 24, 4], F32)
                pg = k.ps([128, 1024], F32)
                wm32 = [k.sb([128, 8, 512], F32) for _ in range(2)]
                diag = [k.sb([128, 128], F32) for _ in range(2)]
                lt = k.sb([128, 2, 32], F32)
                le = k.sb([128, 2], F32)
                for ch in range(6):
                    wb_ = wm32[ch % 2]
                    DMA("sp", wb_[:], wmod_d.t[l, :, ch * 512:(ch + 1) * 512].rearrange("(kc p) n -> p kc n", p=128), [cin], [wb_])
                    for jj in range(4):
                        for kc in range(8):
                            MM(pm[:, ch * 4 + jj, :], wb_[:, kc, jj * 128:(jj + 1) * 128], sc[:, kc, :], kc == 0, kc == 7, [wb_, sc], [pm])
                TT(mod[:], pm[:], bmod[:, l, :].unsqueeze(2).broadcast_to([128, 24, 4]), ALU.add, [pm, bmod], [mod])
                TS(A1[:], mod[:, 8:16, :], 1.0, ALU.add, [mod], [A1])
                TT(A1[:], A1[:], gainT[:, l, :].unsqueeze(2).broadcast_to([128, 8, 4]), ALU.mult, [A1, gainT], [A1])
                for v in range(3):
                    for t in range(8):
                        dg = diag[t % 2]
                        TS(dg[:], identf[:], mod[:, 16 + t, v:v + 1], ALU.mult, [identf, mod], [dg])
                        MM(pg[:, t * 128:(t + 1) * 128], onesf[:], dg[:], True, True, [onesf, dg], [pg])
                    CP(gate_bc[v][:, 0:512], pg[:, 0:512], [pg], [gate_bc[v]])
                    CP(gate_bc[v][:, 512:1024], pg[:, 512:1024], [pg], [gate_bc[v]], eng="act")
                lv = sv[:, l, 64:192].rearrange("p (a b d) -> p a b d", a=2, b=2)
                TT(lt[:], lv[:, :, 0, :], lv[:, :, 1, :], ALU.mult, [sv], [lt])
                RED(le[:], lt[:], [lt], [le])
                ACT(le[:], le[:], AF.Exp, [le], [le])
                TT(neglam[:], le[:, 1:2], le[:, 0:1], ALU.subtract, [le], [neglam])
                TS(neglam[:], neglam[:], -lam_init, ALU.add, [neglam], [neglam])
                CP(gA[:], sv[:, l, 0:64].rearrange("p (a d) -> p a d", a=2), [sv], [gA])
                TS(gA[:, 0, :], gA[:, 0, :], 32.0 ** -0.5, ALU.mult, [gA], [gA])
                TS(gS[:], sv[:, l, 192:256], 1.0 - lam_init, ALU.mult, [sv], [gS])
                CP(gB[:], sv[:, l, 256:384].rearrange("p (a d) -> p a d", a=2), [sv], [gB])
                TS(gB[:, 0, :], gB[:, 0, :], 0.125, ALU.mult, [gB], [gB])
                k.barrier()
            k.cur = st

            for b in range(2):
                with ExitStack() as ph:
                    k.cur = ph
                    xt = [k.sb([128, D], F32) for _ in range(2)]
                    sq = k.sb([128, D], F32)
                    ssq = [k.sb([128, 1], F32) for _ in range(2)]
                    xn = [k.sb([128, D], BF16) for _ in range(2)]
                    tmp = [k.sb([128, 8, 128], F32) for _ in range(2)]
                    pT = [k.ps([128, 8, 128], BF16) for _ in range(2)]
                    for t in range(NT):
                        v = b if t < 16 else 2
                        x_, s_, n_, p_, m_ = xt[t % 2], ssq[t % 2], xn[t % 2], pT[t % 2], tmp[t % 2]
                        DMA("sp", x_[:], xs_d.t[b, t * 128:(t + 1) * 128, :], [xs_tok[b][t]], [x_])
                        ACT(sq[:], x_[:], AF.Square, [x_], [sq, s_], accum=s_[:])
                        RSQ(s_, s_[:], 128, 1, 1.0 / D)
                        TS(n_[:], x_[:], s_[:, 0:1], ALU.mult, [x_, s_], [n_])
                        for kc in range(8):
                            TR(p_[:, kc, :], n_[:, kc * 128:(kc + 1) * 128], identb[:], [n_, identb], [p_])
                        TT(m_[:], p_[:], A1[:, :, v:v + 1].broadcast_to([128, 8, 128]), ALU.mult, [p_, A1], [m_])
                        TT(hT[:, :, t * 128:(t + 1) * 128], m_[:], mod[:, 0:8, v:v + 1].broadcast_to([128, 8, 128]), ALU.add,
                           [m_, mod], [hT_tok[t]], eng="pool")
                    k.barrier()
                k.cur = st

                for br in range(2):
                    with ExitStack() as ph:
                        k.cur = ph
                        isA = (br == 0)
                        dh = 32 if isA else 64
                        ng = 512 // dh
                        nh = ng // 2
                        nmap = 8 if isA else 4
                        c0 = 0 if isA else 768
                        gq = gA if isA else gB
                        kT = k.sb([128, 2, T], BF16)
                        QP = [k.sb([128, T], BF16) for _ in range(nmap)]
                        mps = nmap // 2
                        rmask = rmask32 if isA else rmask64
                        vaug = k.sb([128, NT, 4, 65], BF16)
                        sqt = [k.sb([128, 512], F32) for _ in range(2)]
                        ssg = [k.sb([128, 16], F32) for _ in range(2)]
                        qkn = [k.sb([128, 512], F32) for _ in range(2)]
                        qkb = [k.sb([128, 512], BF16) for _ in range(2)]
                        ta2 = [k.sb([128, 256], F32) for _ in range(2)]
                        tb2 = [k.sb([128, 256], F32) for _ in range(2)]
                        tc2 = [k.sb([128, 256], F32) for _ in range(2)]
                        td2 = [k.sb([128, 256], F32) for _ in range(2)]
                        Pt = [k.sb([128, 512], BF16) for _ in range(3)]
                        Ot2 = [k.sb([128, 8, 65], F32) for _ in range(2)]
                        rsum = k.sb([128, 8], F32)
                        On = k.sb([128, 8, 64], F32)
                        ot = k.sb([128, 256], F32)
                        o2 = k.sb([128, 256], F32)
                        rs4 = k.sb([128, 4], F32)
                        tht = k.sb([128, 256], F32)
                        sgt4 = [k.sb([128, 256], F32) for _ in range(4)]
                        ygb4 = [k.sb([128, 256], BF16) for _ in range(4)]
                        ybt = [k.sb([128, 2, 128], BF16) for _ in range(4)]
                        if isA:
                            OaT = k.sb([66, 8, 512], F32)
                            MSET(OaT[:], 0.0, [OaT])
                        else:
                            nbt = k.sb([128, 100, 128], BF16)
                            DMA("pool", nbt[:], nb_d.t[l].rearrange("n k q -> k n q"), [cin], [nbt])
                        DMA("pool", WB0[:, :, 0:768], win_d.t[l, :, c0:c0 + 768].rearrange("(kc p) n -> p kc n", p=128), [cin], [WB0])
                        g0 = 2816 + 256 * br
                        wg = WB1[:, 0:2048].rearrange("p (kc n) -> p kc n", kc=8)
                        DMA("pool", wg, win_d.t[l, :, g0:g0 + 256].rearrange("(kc p) n -> p kc n", p=128), [cin], [WB1])
                        MSET(vaug[:, :, :, 64:65], 1.0, [vaug])
                        with ExitStack() as ph2:
                            k.cur = ph2
                            pz0 = [k.ps([128, 512], F32) for _ in range(2)]
                            pz1 = [k.ps([128, 512], F32) for _ in range(2)]
                            ptrp = [k.ps([128, 4, 128], BF16) for _ in range(2)]
                            pend = None
                            for t in range(NT):
                                ts_ = slice(t * 128, (t + 1) * 128)
                                z0, z1, pt_ = pz0[t % 2], pz1[t % 2], ptrp[t % 2]
                                sq_, sg_, qn_, qb = sqt[t % 2], ssg[t % 2], qkn[t % 2], qkb[t % 2]
                                for kc in range(8):
                                    MM(z0[:], hT[:, kc, ts_], WB0[:, kc, 0:512], kc == 0, kc == 7, [hT_tok[t], WB0], [z0])
                                for kc in range(8):
                                    MM(z1[:, 0:256], hT[:, kc, ts_], WB0[:, kc, 512:768], kc == 0, kc == 7, [hT_tok[t], WB0], [z1])
                                ACT(sq_[:], z0[:], AF.Square, [z0], [sq_])
                                RED(sg_[:, 0:ng], sq_[:].rearrange("p (g d) -> p g d", d=dh), [sq_], [sg_])
                                RSQ(sg_, sg_[:, 0:ng], 128, ng, 1.0 / dh)
                                TT(qn_[:].rearrange("p (g d) -> p g d", d=dh), z0[:].rearrange("p (g d) -> p g d", d=dh),
                                   sg_[:, 0:ng].unsqueeze(2).broadcast_to([128, ng, dh]), ALU.mult, [z0, sg_], [qn_])
                                if isA and t < 16:
                                    TT(qn_[:].rearrange("p (a h d) -> p a h d", a=2, d=dh), qn_[:].rearrange("p (a h d) -> p a h d", a=2, d=dh),
                                       gq[:].unsqueeze(2).broadcast_to([128, 2, nh, dh]), ALU.mult, [qn_, gq], [qn_], eng="pool")
                                    qv = qn_[:].rearrange("p (g two d) -> p g two d", two=2, d=16)
                                    ov = qb[:].rearrange("p (g two d) -> p g two d", two=2, d=16)
                                    cosb = rope[:, t, 0:16].unsqueeze(1).broadcast_to([128, 16, 16])
                                    sinb = rope[:, t, 16:32].unsqueeze(1).broadcast_to([128, 16, 16])
                                    t3 = lambda buf: buf[:].rearrange("p (g d) -> p g d", d=16)
                                    ta, tb, tc_, td = ta2[t % 2], tb2[t % 2], tc2[t % 2], td2[t % 2]
                                    TT(t3(ta), qv[:, :, 0, :], cosb, ALU.mult, [qn_, rope], [ta])
                                    TT(t3(tb), qv[:, :, 1, :], sinb, ALU.mult, [qn_, rope], [tb])
                                    TT(ov[:, :, 0, :], t3(ta), t3(tb), ALU.subtract, [ta, tb], [qb])
                                    TT(t3(tc_), qv[:, :, 0, :], sinb, ALU.mult, [qn_, rope], [tc_])
                                    TT(t3(td), qv[:, :, 1, :], cosb, ALU.mult, [qn_, rope], [td], eng="pool")
                                    TT(ov[:, :, 1, :], t3(tc_), t3(td), ALU.add, [tc_, td], [qb], eng="pool")
                                else:
                                    TT(qb[:].rearrange("p (a h d) -> p a h d", a=2, d=dh), qn_[:].rearrange("p (a h d) -> p a h d", a=2, d=dh),
                                       gq[:].unsqueeze(2).broadcast_to([128, 2, nh, dh]), ALU.mult, [qn_, gq], [qb], eng="pool")
                                CP(vaug[:, t, :, 0:64], z1[:, 0:256].rearrange("p (h d) -> p h d", d=64), [z1], [vaug], eng="act")

                                def trs(t=t, pt_=pt_, qb=qb, ts_=ts_):
                                    for i in range(4):
                                        TR(pt_[:, i, :], qb[:, i * 128:(i + 1) * 128], identb[:], [qb, identb], [pt_])
                                    CP(kT[:, :, ts_], pt_[:, 2:4, :], [pt_], [kT], eng="act")
                                    for m in range(nmap):
                                        ACT(QP[m][:, ts_], pt_[:, m // mps, :], AF.Copy, [pt_, rmask], [QP[m]],
                                            scale=rmask[:, (m % mps):(m % mps) + 1])
                                if not PIPE_PREP:
                                    trs()
                                    continue
                                if pend is not None:
                                    pend()
                                pend = trs
                            if PIPE_PREP:
                                pend()
                            k.barrier()
                        k.cur = ph
                        Sp = [k.ps([128, 512], F32) for _ in range(2)]
                        if isA:
                            accT = [k.ps([65, 512], F32) for _ in range(2)]
                            pOa = k.ps([128, 4, 66], F32)
                            pOb = k.ps([128, 4, 66], F32)
                        else:
                            acc = [k.ps([128, 4, 65], F32) for _ in range(2)]
                        pgz = k.ps([128, 256], F32)
                        ptr = k.ps([128, 4, 128], BF16)

                        pendq = []

                        def finish(qt, Ot):
                            qs_ = slice(qt * 128, (qt + 1) * 128)
                            sg_ = sgt4[qt % 4]
                            yg_ = ygb4[qt % 4]
                            for kc in range(8):
                                MM(pgz[:], hT[:, kc, qs_], wg[:, kc, :], kc == 0, kc == 7, [hT_tok[qt], WB1], [pgz])
                            ACT(tht[:], pgz[:], AF.Tanh, [pgz], [tht], scale=0.5)
                            STT(sg_[:], tht[:], 1.0, pgz[:], ALU.add, ALU.mult, [tht, pgz], [sg_])
                            RECIP(rsum[:, 0:nmap], Ot[:, 0:nmap, 64], [Ot], [rsum])
                            TT(On[:, 0:nmap, :], Ot[:, 0:nmap, 0:64], rsum[:, 0:nmap].unsqueeze(2).broadcast_to([128, nmap, 64]), ALU.mult,
                               [Ot, rsum], [On])
                            if isA:
                                Onv = On[:].rearrange("p (h two) d -> p h two d", two=2)
                                STT(ot[:].rearrange("p (h d) -> p h d", d=64), Onv[:, :, 1, :], neglam[:, 0:1], Onv[:, :, 0, :],
                                    ALU.mult, ALU.add, [On, neglam], [ot])
                                TT(o2[:], ot[:], ot[:], ALU.mult, [ot], [o2], eng="pool")
                                RED(rs4[:], o2[:].rearrange("p (h d) -> p h d", d=64), [o2], [rs4])
                                RSQ(rs4, rs4[:], 128, 4, 1.0 / 64)
                                TT(o2[:].rearrange("p (h d) -> p h d", d=64), ot[:].rearrange("p (h d) -> p h d", d=64),
                                   rs4[:].unsqueeze(2).broadcast_to([128, 4, 64]), ALU.mult, [ot, rs4], [o2])
                                TT(ot[:].rearrange("p (h d) -> p h d", d=64), o2[:].rearrange("p (h d) -> p h d", d=64),
                                   gS[:].unsqueeze(1).broadcast_to([128, 4, 64]), ALU.mult, [o2, gS], [ot], eng="pool")
                                ysrc = ot[:]
                                ybuf = ot
                            else:
                                ysrc = On[:, 0:4, :].rearrange("p h d -> p (h d)")
                                ybuf = On
                            STT(yg_[:], sg_[:], 0.5, ysrc, ALU.mult, ALU.mult, [sg_, ybuf], [yg_])
                            pendq.append(qt)

                        def flush(keep=0):
                            while len(pendq) > keep:
                                qt = pendq.pop(0)
                                qs_ = slice(qt * 128, (qt + 1) * 128)
                                yg_ = ygb4[qt % 4]
                                yb_ = ybt[qt % 4]
                                for i in range(2):
                                    TR(ptr[:, i, :], yg_[:, i * 128:(i + 1) * 128], identb[:], [yg_, identb], [ptr])
                                CP(yb_[:], ptr[:, 0:2, :], [ptr], [yb_])
                                DMA("sp", ybr_d.t[b, br, :, :, qs_].rearrange("c p n -> p c n"), yb_[:], [yb_], [ybr_tok[b][br]])

                        gi = 0
                        if isA:
                            qblocks = [(0, 512), (512, 512), (1024, 512), (1536, 512)] + ([] if last else [(2048, 256)])
                            for (q0, qn) in qblocks:
                                chunks = list(range(NT)) if q0 < LAT else [16, 17]
                                nci = len(chunks)
                                its = [(m, ci, c) for m in range(8) for ci, c in enumerate(chunks)]

                                def qk(i):
                                    m, ci, c = its[i]
                                    slot, pb = m // 4, 32 * (m % 4)
                                    S_ = Sp[(gi + i) % 2]
                                    P_ = Pt[(gi + i) % 3]
                                    MM(S_[:, 0:qn], kT[:, slot, c * 128:(c + 1) * 128], QP[m][:, q0:q0 + qn], True, True, [kT, QP[m]], [S_])
                                    ACT(P_[:, 0:qn], S_[:, 0:qn], AF.Exp, [S_], [P_])

                                def pv(i):
                                    m, ci, c = its[i]
                                    a_ = accT[m % 2]
                                    P_ = Pt[(gi + i) % 3]
                                    MM(a_[:, 0:qn], vaug[:, c, m // 2, :], P_[:, 0:qn], ci == 0, ci == nci - 1, [vaug, P_], [a_])
                                    if ci == nci - 1:
                                        CP(OaT[0:65, m, 0:qn], a_[:, 0:qn], [a_], [OaT])

                                if PIPE_A:
                                    qk(0)
                                for i in range(len(its)):
                                    if PIPE_A:
                                        if i + 1 < len(its):
                                            qk(i + 1)
                                    else:
                                        qk(i)
                                    pv(i)
                                    if i == 8:
                                        flush()
                                gi += len(its)
                                for j in range(qn // 128):
                                    qt = q0 // 128 + j
                                    Ot = Ot2[qt % 2]
                                    for m in range(8):
                                        pO_ = pOa if m < 4 else pOb
                                        TR(pO_[:, m % 4, :], OaT[:, m, j * 128:(j + 1) * 128], identf[0:66, 0:66], [OaT, identf], [pO_])
                                    CP(Ot[:, 0:4, :], pOa[:, :, 0:65], [pOa], [Ot])
                                    CP(Ot[:, 4:8, :], pOb[:, :, 0:65], [pOb], [Ot])
                                    finish(qt, Ot)
                            flush()
                        else:
                            work = []
                            for qt in range(nq):
                                if qt >= 16:
                                    chunks = [(16, None), (17, None)]
                                else:
                                    cs = min(max(qt - 2, 0), 11)
                                    typ = 0 if qt == 0 else 1 if qt == 1 else 3 if qt == 14 else 4 if qt == 15 else 2
                                    chunks = [(cs + i, typ * 5 + i) for i in range(5)] + [(16, None), (17, None)]
                                groups = [chunks[i:i + 4] for i in range(0, len(chunks), 4)]
                                for m in range(4):
                                    for gidx, g in enumerate(groups):
                                        work.append((qt, m, g, gidx == 0, gidx == len(groups) - 1))

                            def scb(i):
                                qt, m, g, fg, lg = work[i]
                                slot, pb = m // 2, 64 * (m % 2)
                                Sb = Sp[i % 2]
                                Pb = Pt[i % 3]
                                S_ = Sb[:].rearrange("p (c q) -> p c q", q=128)
                                P_ = Pb[:].rearrange("p (c q) -> p c q", q=128)
                                for j, (c, bi) in enumerate(g):
                                    kT_ = kT[:, slot, c * 128:(c + 1) * 128]
                                    qT_ = QP[m][:, qt * 128:(qt + 1) * 128]
                                    if bi is not None:
                                        MM(S_[:, j, :], identb[:], nbt[:, bi * 4 + m, :], True, False, [identb, nbt], [Sb])
                                        MM(S_[:, j, :], kT_, qT_, False, True, [kT, QP[m]], [Sb])
                                    else:
                                        MM(S_[:, j, :], kT_, qT_, True, True, [kT, QP[m]], [Sb])
                                ACT(P_[:, 0:len(g), :], S_[:, 0:len(g), :], AF.Exp, [Sb], [Pb])

                            def pvb(i):
                                qt, m, g, fg, lg = work[i]
                                Pb = Pt[i % 3]
                                P_ = Pb[:].rearrange("p (c q) -> p c q", q=128)
                                ab = acc[qt % 2]
                                for j, (c, bi) in enumerate(g):
                                    MM(ab[:, m, :], P_[:, j, :], vaug[:, c, m, :], fg and j == 0, lg and j == len(g) - 1, [Pb, vaug], [ab])
                                if lg and m == 3:
                                    Ot = Ot2[qt % 2]
                                    CP(Ot[:, 0:4, :], ab[:], [ab], [Ot])
                                    finish(qt, Ot)
                                if lg and m == 1:
                                    flush()

                            if PIPE_B:
                                scb(0)
                            for i in range(len(work)):
                                if PIPE_B:
                                    if i + 1 < len(work):
                                        scb(i + 1)
                                else:
                                    scb(i)
                                pvb(i)
                            flush()
                        k.barrier()
                    k.cur = st

                DMA("pool", WB0[:], win_d.t[l, :, 1536:2560].rearrange("(kc p) n -> p kc n", p=128), [cin], [WB0])
                wg = WB1[:, 0:2048].rearrange("p (kc n) -> p kc n", kc=8)
                DMA("pool", wg, win_d.t[l, :, 3328:3584].rearrange("(kc p) n -> p kc n", p=128), [cin], [WB1])
                for ct in range(2):
                    with ExitStack() as ph:
                        k.cur = ph
                        KT = [k.sb([128, T], BF16) for _ in range(2)]
                        QBD = [k.sb([128, NCH, 128], BF16) for _ in range(2)]
                        Ktm2 = [k.sb([64, NCH, 128], BF16) for _ in range(2)]
                        Vc = k.sb([64, NCH, 128], BF16)
                        Spr = [k.sb([128, NCH, 64], BF16) for _ in range(2)]
                        er = [k.sb([128, NCH], F32) for _ in range(2)]
                        Sst2 = [k.sb([128, 64], F32) for _ in range(2)]
                        Stm2 = [k.sb([128, 64], F32) for _ in range(2)]
                        for d in range(2):
                            MSET(QBD[d][:], 0.0, [QBD[d]])
                        with ExitStack() as ph2:
                            k.cur = ph2
                            qh = k.sb([128, T], F32)
                            thq = [k.sb([128, 512], F32) for _ in range(2)]
                            NS = 2
                            tht = [k.sb([128, 512], F32) for _ in range(NS)]
                            gg = [k.sb([128, 512], F32) for _ in range(NS)]
                            kk = [k.sb([128, 512], F32) for _ in range(NS)]
                            uu = [k.sb([128, 512], F32) for _ in range(NS)]
                            Bc = [k.sb([128, 512], F32) for _ in range(NS)]
                            tmpb = [k.sb([128, 512], F32) for _ in range(NS)]
                            tot = [k.sb([128, 8], F32) for _ in range(NS)]
                            E1 = [k.sb([128, 512], F32) for _ in range(NS)]
                            E2 = [k.sb([128, 512], F32) for _ in range(NS)]
                            pq = [k.ps([128, 512], F32) for _ in range(4)]
                            pv = [k.ps([64, 4, 128], F32) for _ in range(2)]
                            for c4 in range(0, NCH, 4):
                                pv_ = pv[(c4 // 4) % 2]
                                for i in range(4):
                                    c = c4 + i
                                    for kc in range(8):
                                        MM(pv_[:, i, :], hT[:, kc, c * 64:(c + 1) * 64], WB0[:, kc, 768 + ct * 128:768 + (ct + 1) * 128],
                                           kc == 0, kc == 7, [hT_tok[c // 2], WB0], [pv_])
                                CP(Vc[:, c4:c4 + 4, :], pv_[:], [pv_], [Vc], eng="act")
                            pi = 0
                            for bi_, (t0, n) in enumerate(BLKS):
                                p_ = pq[pi % 4]
                                pi += 1
                                tq = thq[bi_ % 2]
                                for kc in range(8):
                                    MM(p_[:, 0:n], WB0[:, kc, ct * 128:(ct + 1) * 128], hT[:, kc, t0:t0 + n], kc == 0, kc == 7, hts(t0, n) + [WB0], [p_])
                                ACT(tq[:, 0:n], p_[:, 0:n], AF.Tanh, [p_], [tq], scale=0.5)
                                STT(qh[:, t0:t0 + n], tq[:, 0:n], 1.0, p_[:, 0:n], ALU.add, ALU.mult, [tq, p_], [qh])
                            it = 0
                            for (t0, n) in BLKS:
                                nc_ = n // 64
                                cb = t0 // 64
                                bs_ = slice(t0, t0 + n)
                                for d in range(2):
                                    j = it % NS
                                    it += 1
                                    p_ = pq[pi % 4]
                                    pi += 1
                                    cc = 256 + d * 256 + ct * 128
                                    for kc in range(8):
                                        MM(p_[:, 0:n], WB0[:, kc, cc:cc + 128], hT[:, kc, bs_], kc == 0, kc == 7, hts(t0, n) + [WB0], [p_])
                                    ACT(uu[j][:, 0:n], p_[:, 0:n], AF.Exp, [p_], [uu[j]], scale=-1.0)
                                    ACT(E2[j][:, 0:n], uu[j][:, 0:n], AF.Ln, [uu[j]], [E2[j]], bias=1.0)
                                    ACT(E1[j][:, 0:n], uu[j][:, 0:n], AF.Ln, [uu[j], lb_a, lb_c], [E1[j]], scale=lb_a[:, d, ct, l:l + 1],
                                        bias=lb_c[:, d, ct, l:l + 1])
                                    TT(gg[j][:, 0:n], E1[j][:, 0:n], E2[j][:, 0:n], ALU.subtract, [E1[j], E2[j]], [gg[j]], eng="pool")
                                    TT(tht[j][:, 0:n], p_[:, 0:n], E2[j][:, 0:n], ALU.add, [p_, E2[j]], [tht[j]])
                                    ACT(kk[j][:, 0:n], tht[j][:, 0:n], AF.Exp, [tht[j]], [kk[j]], scale=-1.0)
                                    k.op("dve", [segm, gg[j]], [Bc[j]], lambda e, n=n, j=j: e.tensor_tensor_scan(
                                        out=Bc[j][:, 0:n], data0=segm[:, 0:n], data1=gg[j][:, 0:n], initial=0.0, op0=ALU.mult, op1=ALU.add))
                                    B3 = Bc[j][:, 0:n].rearrange("p (c s) -> p c s", s=64)
                                    CP(tot[j][:, 0:nc_], B3[:, :, 63], [Bc[j]], [tot[j]])
                                    if d == 1:
                                        TT(tmpb[j][:, 0:n], gg[j][:, 0:n], Bc[j][:, 0:n], ALU.subtract, [gg[j], Bc[j]], [tmpb[j]])
                                        TT(B3, tmpb[j][:, 0:n].rearrange("p (c s) -> p c s", s=64),
                                           tot[j][:, 0:nc_].unsqueeze(2).broadcast_to([128, nc_, 64]), ALU.add, [tmpb[j], tot[j]], [Bc[j]])
                                    ACT(er[d][:, cb:cb + nc_], tot[j][:, 0:nc_], AF.Exp, [tot[j]], [er[d]], scale=0.5)
                                    STT(tmpb[j][:, 0:n].rearrange("p (c s) -> p c s", s=64), tot[j][:, 0:nc_].unsqueeze(2).broadcast_to([128, nc_, 64]),
                                        -0.5, B3, ALU.mult, ALU.add, [tot[j], Bc[j]], [tmpb[j]])
                                    TS(tmpb[j][:, 0:n], tmpb[j][:, 0:n], 43.0, ALU.min, [tmpb[j]], [tmpb[j]], s2=-43.0, op1=ALU.max, eng="pool")
                                    ACT(E1[j][:, 0:n], tmpb[j][:, 0:n], AF.Exp, [tmpb[j]], [E1[j]])
                                    ACT(E2[j][:, 0:n], tmpb[j][:, 0:n], AF.Exp, [tmpb[j]], [E2[j]], scale=-1.0)
                                    for hh in range(2):
                                        ps_ = slice(64 * hh, 64 * hh + 64)
                                        STT(QBD[d][ps_, cb:cb + nc_, 64 * hh:64 * hh + 64], qh[ps_, bs_].rearrange("p (c s) -> p c s", s=64), 0.5,
                                            E1[j][ps_, 0:n].rearrange("p (c s) -> p c s", s=64), ALU.mult, ALU.mult, [qh, E1[j]], [QBD[d]])
                                    STT(KT[d][:, bs_], kk[j][:, 0:n], lb_na[:, d, ct, l:l + 1], E2[j][:, 0:n], ALU.mult, ALU.mult,
                                        [kk[j], E2[j], lb_na], [KT[d]])
                            k.barrier()
                        k.cur = ph
                        Am = [k.sb([64, 2, 128], BF16) for _ in range(2)]
                        of_ = k.sb([64, 512], F32)
                        o2 = k.sb([64, 512], F32)
                        rs8 = k.sb([64, 8], F32)
                        th2 = k.sb([64, 512], F32)
                        sg2 = k.sb([64, 512], F32)
                        yg2 = [k.sb([64, 512], BF16) for _ in range(2)]
                        ybt = [k.sb([128, 256], BF16) for _ in range(2)]
                        pbk = k.ps([128, 1024], BF16)
                        ptk = pbk[0:64, :].rearrange("p (a b) -> p a b", b=128)
                        pTy = pbk[:, 0:256]
                        pu = [k.ps([128, 128], F32) for _ in range(2)]
                        pa = [k.ps([64, 2, 128], F32) for _ in range(2)]
                        po = k.ps([64, 4, 128], F32)
                        pgz = k.ps([64, 4, 128], F32)
                        for d in range(2):
                            for c8 in range(0, NCH, 4):
                                for i in range(4):
                                    c = c8 + i
                                    TR(ptk[:, i, :], KT[d][:, c * 64:(c + 1) * 64], identb[:], [KT[d], identb], [pbk])
                                CP(Ktm2[d][:, c8:c8 + 4, :], ptk[:, 0:4, :], [pbk], [Ktm2[d]], eng="act")
                            MSET(Sst2[d][:], 0.0, [Sst2[d]], eng="dve")
                        orders = [[32, 33, 34, 35] + list(range(32)), [35, 34, 33, 32] + list(range(31, -1, -1))]
                        for oi in range(NCH):
                            for d in range(2):
                                c = orders[d][oi]
                                pu_ = pu[d]
                                Sst, Stm = Sst2[d], Stm2[d]
                                MM(pu_[:], Ktm2[d][:, c, :], Vc[:, c, :], True, True, [Ktm2[d], Vc], [pu_])
                                e_ = er[d][:, c:c + 1]
                                ACT(Spr[d][:, c, :], Sst[:], AF.Copy, [Sst, er[d]], [Spr[d]], scale=e_)
                                for hh in range(2):
                                    ps_ = slice(64 * hh, 64 * hh + 64)
                                    STT(Stm[ps_, :], Sst[ps_, :], er[d][ps_, c:c + 1], pu_[ps_, 64 * hh:64 * hh + 64], ALU.mult, ALU.add,
                                        [Sst, er[d], pu_], [Stm])
                                TS(Sst[:], Stm[:], e_, ALU.mult, [Stm, er[d]], [Sst])
                        cm4 = cmask[:].rearrange("p (d h t) -> p d h t", d=2, h=2)

                        def scm(c):
                            pa_ = pa[c % 2]
                            am_ = Am[c % 2]
                            for d in range(2):
                                MM(pa_[:, d, :], KT[d][:, c * 64:(c + 1) * 64], QBD[d][:, c, :], True, True, [KT[d], QBD[d]], [pa_])
                            TT(am_[:].rearrange("p d (h t) -> p d h t", h=2), pa_[:].rearrange("p d (h t) -> p d h t", h=2), cm4, ALU.mult,
                               [pa_, cmask], [am_])

                        ptail = []

                        def tail():
                            while ptail:
                                c4 = ptail.pop(0)
                                yg_ = yg2[(c4 // 4) % 2]
                                yb_ = ybt[(c4 // 4) % 2]
                                for i in range(4):
                                    TR(pTy[:, i * 64:(i + 1) * 64], yg_[:, i * 128:(i + 1) * 128], identb[0:64, 0:64], [yg_, identb], [pbk])
                                CP(yb_[:], pTy, [pbk], [yb_], eng="act")
                                DMA("sp", ybr_d.t[b, 2, ct, :, c4 * 64:c4 * 64 + 256], yb_[:], [yb_], [ybr_tok[b][2]])

                        scm(0)
                        for c4 in range(0, nck, 4):
                            for i in range(4):
                                c = c4 + i
                                if c + 1 < nck:
                                    scm(c + 1)
                                for kc in range(8):
                                    MM(pgz[:, i, :], hT[:, kc, c * 64:(c + 1) * 64], wg[:, kc, ct * 128:(ct + 1) * 128], kc == 0, kc == 7,
                                       [hT_tok[c // 2], WB1], [pgz])
                                am_ = Am[c % 2]
                                for hh in range(2):
                                    hs_ = slice(64 * hh, 64 * hh + 64)
                                    for d in range(2):
                                        MM(po[:, i, hs_], am_[:, d, hs_], Vc[:, c, hs_], d == 0, False, [am_, Vc], [po])
                                        MM(po[:, i, hs_], QBD[d][:, c, hs_], Spr[d][:, c, :], False, d == 1, [QBD[d], Spr[d]], [po])
                            tail()
                            pof = po[:].rearrange("p c n -> p (c n)")
                            pgf = pgz[:].rearrange("p c n -> p (c n)")
                            yg_ = yg2[(c4 // 4) % 2]
                            ACT(th2[:], pgf, AF.Tanh, [pgz], [th2], scale=0.5)
                            STT(sg2[:], th2[:], 1.0, pgf, ALU.add, ALU.mult, [th2, pgz], [sg2])
                            CP(of_[:], pof, [po], [of_], eng="act")
                            ACT(o2[:], pof, AF.Square, [po], [o2])
                            RED(rs8[:], o2[:].rearrange("p (g d) -> p g d", d=64), [o2], [rs8])
                            RSQ(rs8, rs8[:], 64, 8, 1.0 / 64)
                            TT(o2[:].rearrange("p (g d) -> p g d", d=64), of_[:].rearrange("p (g d) -> p g d", d=64),
                               rs8[:].unsqueeze(2).broadcast_to([64, 8, 64]), ALU.mult, [of_, rs8], [o2])
                            TT(of_[:].rearrange("p (g d) -> p g d", d=64), o2[:].rearrange("p (g d) -> p g d", d=64),
                               sv[0:64, l, 384:448].unsqueeze(1).broadcast_to([64, 8, 64]), ALU.mult, [o2, sv], [of_], eng="pool")
                            STT(yg_[:], sg2[:], 0.5, of_[:], ALU.mult, ALU.mult, [sg2, of_], [yg_])
                            ptail.append(c4)
                        tail()
                        k.barrier()
                    k.cur = st

                with ExitStack() as ph:
                    k.cur = ph
                    uT = k.sb([128, 2, T], BF16)
                    PQ = k.sb([128, NT, 2, 256], BF16)
                    tab = [k.sb([128, 16, 2, 256], BF16) for _ in range(2)]
                    tht = k.sb([128, 256], F32)
                    sgt = k.sb([128, 256], F32)
                    ybt = [k.sb([128, 2, 256], BF16) for _ in range(2)]
                    pq = [k.ps([128, 512], F32) for _ in range(2)]
                    pp = [k.ps([128, 2, 256], F32) for _ in range(2)]
                    py = [k.ps([128, 256], F32) for _ in range(2)]
                    pgz = [k.ps([128, 256], F32) for _ in range(2)]
                    DMA("pool", WB0[:, :, 0:256], win_d.t[l, :, 2560:2816].rearrange("(kc p) n -> p kc n", p=128), [cin], [WB0])
                    wg = WB1[:, 0:2048].rearrange("p (kc n) -> p kc n", kc=8)
                    DMA("pool", wg, win_d.t[l, :, 3584:3840].rearrange("(kc p) n -> p kc n", p=128), [cin], [WB1])
                    pi = 0
                    for ct in range(2):
                        for (t0, n) in BLKS:
                            p_ = pq[pi % 2]
                            pi += 1
                            for kc in range(8):
                                MM(p_[:, 0:n], WB0[:, kc, ct * 128:(ct + 1) * 128], hT[:, kc, t0:t0 + n], kc == 0, kc == 7, hts(t0, n) + [WB0], [p_])
                            CP(uT[:, ct, t0:t0 + n], p_[:, 0:n], [p_], [uT], eng="act")
                    for t in range(NT):
                        p_ = pp[t % 2]
                        for ct in range(2):
                            MM(p_[:, ct, :], uT[:, ct, t * 128:(t + 1) * 128], cs64[:], True, True, [uT, cs64], [p_])
                        CP(PQ[:, t, :, :], p_[:], [p_], [PQ], eng=("act" if t % 2 else "dve"))
                    it = 0
                    nblk = 8 if last else 9
                    for nb in range(nblk):
                        if nb < 8:
                            tb_ = tab[nb % 2]
                            DMA("sp", tb_[:], dftl_d.t[:, :, :, nb * 256:(nb + 1) * 256], [cin], [tb_])
                            tcs = list(range(16))
                            tsrc = lambda tc, cs: tb_[:, tc, cs, :]
                            tread = [tb_]
                            t0 = nb * 256
                        else:
                            tcs = [16, 17]
                            tsrc = lambda tc, cs: dftc[:, tc - 16, cs, :]
                            tread = [dftc]
                            t0 = LAT
                        yb_ = ybt[nb % 2]
                        for ct in range(2):
                            y_ = py[it % 2]
                            g_ = pgz[it % 2]
                            it += 1
                            nmm = len(tcs) * 2
                            j = 0
                            for tc in tcs:
                                for cs in range(2):
                                    MM(y_[:], PQ[:, tc, ct, cs * 128:(cs + 1) * 128], tsrc(tc, cs), j == 0, j == nmm - 1, [PQ] + tread, [y_])
                                    j += 1
                            for kc in range(8):
                                MM(g_[:], wg[:, kc, ct * 128:(ct + 1) * 128], hT[:, kc, t0:t0 + 256], kc == 0, kc == 7, hts(t0, 256) + [WB1], [g_])
                            ACT(tht[:], g_[:], AF.Tanh, [g_], [tht], scale=0.5)
                            STT(sgt[:], tht[:], 1.0, g_[:], ALU.add, ALU.mult, [tht, g_], [sgt])
                            STT(yb_[:, ct, :], sgt[:], 0.5, y_[:], ALU.mult, ALU.mult, [sgt, y_], [yb_])
                        DMA("sp", ybr_d.t[b, 3, :, :, t0:t0 + 256].rearrange("c p n -> p c n"), yb_[:], [yb_], [ybr_tok[b][3]])
                    k.barrier()
                k.cur = st

                with ExitStack() as ph:
                    k.cur = ph
                    ybS = k.sb([128, 4, 2, T], BF16)
                    accT = k.sb([128, 8, T], BF16)
                    wu = [k.sb([128, 4, 2, 128], BF16) for _ in range(2)]
                    wm_tok = [Buf(), Buf()]
                    tht = [k.sb([128, 512], F32) for _ in range(2)]
                    tmpm = [k.sb([128, 512], F32) for _ in range(2)]
                    accf = [k.sb([128, 512], F32) for _ in range(2)]
                    xt = [k.sb([128, D], F32) for _ in range(2)]
                    xo = [k.sb([128, D], F32) for _ in range(2)]
                    tmo = [k.sb([128, 512], F32) for _ in range(2)]
                    pM = [k.ps([128, 512], F32) for _ in range(2)]
                    pU = [k.ps([128, 512], F32) for _ in range(2)]
                    pO = [k.ps([128, 512], F32) for _ in range(2)]
                    for i in range(4):
                        DMA("sp", ybS[:, i, :, :], ybr_d.t[b, i].rearrange("c p n -> p c n"), [ybr_tok[b][i]], [ybS])
                    DMA("pool", WB0[:], wout_d.t[l].rearrange("(kc p) n -> p kc n", p=128), [cin], [WB0])
                    mblks = BLKS[:4] if last else BLKS
                    im = 0
                    def wload(ft):
                        fs_ = slice(ft * 128, (ft + 1) * 128)
                        wmv = WB1[:, (ft % 2) * 4096:(ft % 2 + 1) * 4096].rearrange("p (i kc n) -> p i kc n", i=4, kc=8)
                        DMA("pool", wmv, wmg_d.t[l, :, :, fs_].rearrange("i (kc p) n -> p i kc n", p=128), [cin], [wm_tok[ft % 2]])
                        DMA("pool", wu[ft % 2][:], wup_d.t[l, :, :, fs_].rearrange("i (fc p) n -> p i fc n", p=128), [cin], [wu[ft % 2]])

                    if PREFETCH_M:
                        wload(0)
                    for ft in range(8):
                        fs_ = slice(ft * 128, (ft + 1) * 128)
                        wmv = WB1[:, (ft % 2) * 4096:(ft % 2 + 1) * 4096].rearrange("p (i kc n) -> p i kc n", i=4, kc=8)
                        wmt = wm_tok[ft % 2]
                        wu_ = wu[ft % 2]
                        if not PREFETCH_M:
                            wload(ft)
                        elif ft + 1 < 8:
                            wload(ft + 1)
                        for bi_, (t0, n) in enumerate(mblks):
                            af = accf[bi_ % 2]
                            for i in range(4):
                                m_ = pM[im % 2]
                                u_ = pU[im % 2]
                                th_ = tht[im % 2]
                                tm_ = tmpm[im % 2]
                                im += 1
                                for kc in range(8):
                                    MM(m_[:, 0:n], wmv[:, i, kc, :], hT[:, kc, t0:t0 + n], kc == 0, kc == 7, hts(t0, n) + [wmt], [m_])
                                for fc in range(2):
                                    MM(u_[:, 0:n], wu_[:, i, fc, :], ybS[:, i, fc, t0:t0 + n], fc == 0, fc == 1, [wu_, ybS], [u_])
                                ACT(th_[:, 0:n], m_[:, 0:n], AF.Tanh, [m_], [th_], scale=0.5)
                                if i == 0:
                                    STT(af[:, 0:n], th_[:, 0:n], 1.0, u_[:, 0:n], ALU.add, ALU.mult, [th_, u_], [af])
                                else:
                                    STT(tm_[:, 0:n], th_[:, 0:n], 1.0, u_[:, 0:n], ALU.add, ALU.mult, [th_, u_], [tm_])
                                    TT(af[:, 0:n], af[:, 0:n], tm_[:, 0:n], ALU.add, [af, tm_], [af], eng="pool")
                            ACT(accT[:, ft, t0:t0 + n], af[:, 0:n], AF.Copy, [af], [accT], scale=0.5)
                    ntile = 16 if last else NT
                    io = 0
                    for t in range(ntile):
                        v = b if t < 16 else 2
                        x_ = xt[t % 2]
                        o_ = xo[t % 2]
                        DMA("sp", x_[:], xs_d.t[b, t * 128:(t + 1) * 128, :], [xs_tok[b][t]], [x_])
                        for hf in range(2):
                            p_ = pO[io % 2]
                            tm_ = tmo[io % 2]
                            io += 1
                            cs_ = slice(hf * 512, (hf + 1) * 512)
                            for kc in range(8):
                                MM(p_[:], accT[:, kc, t * 128:(t + 1) * 128], WB0[:, kc, cs_], kc == 0, kc == 7, [accT, WB0], [p_])
                            TT(tm_[:], p_[:], gate_bc[v][:, cs_], ALU.mult, [p_, gate_bc[v]], [tm_])
                            TT(o_[:, cs_], tm_[:], x_[:, cs_], ALU.add, [tm_, x_], [o_], eng="pool")
                        if last:
                            DMA("sp", out_d.t[b, t * 128:(t + 1) * 128, :], o_[:], [o_], [xs_tok[b][t]])
                        else:
                            DMA("sp", xs_d.t[b, t * 128:(t + 1) * 128, :], o_[:], [o_], [xs_tok[b][t]])
                    k.barrier()
                k.cur = st
        k.emit()
    return nc


def _consts():
    bf = ml_dtypes.bfloat16
    c = {}
    c["identb"] = np.eye(128, dtype=np.float32).astype(bf)
    c["identf"] = np.eye(128, dtype=np.float32)
    s = np.arange(64)[:, None]
    t = np.arange(64)[None, :]
    mf = (s <= t).astype(np.float32)
    mb = (s >= t).astype(np.float32)
    cm = np.stack([np.stack([mf, mf], 0), np.stack([mb, mb], 0)], 0)
    c["cmask"] = np.ascontiguousarray(cm.transpose(2, 0, 1, 3).reshape(64, 256))
    n = np.arange(LAT)
    row = (n // 64).astype(np.float32)
    col = (n % 64).astype(np.float32)
    inv = (10000.0 ** (-np.arange(0, 16, 2, dtype=np.float32) / 16)).astype(np.float32)
    ang = np.concatenate([row[:, None] * inv, col[:, None] * inv], -1).astype(np.float32)
    rp = np.concatenate([np.cos(ang), np.sin(ang)], -1).astype(np.float32)
    c["rope"] = np.ascontiguousarray(rp.reshape(16, 128, 32).transpose(1, 0, 2))
    e = np.arange(64)
    a64 = 2 * np.pi * np.outer(e, e) / 64
    C64 = np.cos(a64) / 8.0
    S64 = np.sin(a64) / 8.0
    cs = np.zeros((128, 256), np.float64)
    for g in range(2):
        cs[g * 64:(g + 1) * 64, g * 64:(g + 1) * 64] = C64
        cs[g * 64:(g + 1) * 64, 128 + g * 64:128 + (g + 1) * 64] = S64
    c["cs64"] = cs.astype(np.float32).astype(bf)

    def dft(N):
        tt = np.arange(N)
        a = 2 * np.pi * ((np.outer(tt, tt)) % N) / N
        Cn = np.cos(a) / np.sqrt(N)
        Sn = -np.sin(a) / np.sqrt(N)
        tab = np.stack([Cn, Sn], 1)
        return np.ascontiguousarray(tab.reshape(N // 128, 128, 2, N).transpose(1, 0, 2, 3)).astype(np.float32).astype(bf)

    c["dftL"] = dft(LAT)
    c["dftC"] = dft(CTX)
    return c


def _nbias_index():
    rows, W, wh, ww = 32, 64, 8, 16
    idx = np.full((5, 5, 128, 128, 2), -1, np.int64)
    for typ, qt in enumerate([0, 1, 5, 14, 15]):
        cs = min(max(qt - 2, 0), 11)
        for i in range(5):
            kc = cs + i
            for kk in range(128):
                kr, kcol = 2 * kc + kk // 64, kk % 64
                for q in range(128):
                    r, cq = 2 * qt + q // 64, q % 64
                    r0 = min(max(r - wh // 2, 0), rows - wh)
                    c0 = min(max(cq - ww // 2, 0), W - ww)
                    if r0 <= kr < r0 + wh and c0 <= kcol < c0 + ww:
                        idx[typ, i, kk, q, 0] = kr - r + wh - 1
                        idx[typ, i, kk, q, 1] = min(max(kcol - cq, 1 - ww), ww - 1) + ww - 1
    return idx


_CACHE = {}


def kernel(x, c, ctx, c_ctx, norm_gain, w_mod, b_mod, w_in, da_qk_gain, da_lambda, da_subln_gain,
           na_qk_gain, na_rpb, hg_lb_logits, hg_norm_gain, w_up, w_merge, w_out):
    f = lambda a: np.ascontiguousarray(np.asarray(a, dtype=np.float32))
    x, c, ctx, c_ctx = f(x), f(c), f(ctx), f(c_ctx)
    if "nc" not in _CACHE:
        _CACHE["nc"] = build()
        _CACHE["consts"] = _consts()
        _CACHE["nbidx"] = _nbias_index()
    nc = _CACHE["nc"]
    shared = dict(_CACHE["consts"])
    shared["w_mod"] = f(w_mod)
    shared["w_in"] = f(w_in)
    shared["w_up"] = f(w_up)
    shared["w_merge"] = f(w_merge)
    shared["w_out"] = f(w_out)
    shared["b_modT"] = np.ascontiguousarray(f(b_mod).reshape(NL, 24, 128).transpose(2, 0, 1))
    shared["gainT"] = np.ascontiguousarray(f(norm_gain).reshape(NL, 8, 128).transpose(2, 0, 1))
    smallv = np.concatenate([f(da_qk_gain).reshape(NL, 64), f(da_lambda).reshape(NL, 128), f(da_subln_gain).reshape(NL, 64),
                             f(na_qk_gain).reshape(NL, 128), f(hg_norm_gain).reshape(NL, 64)], -1)
    shared["smallv"] = np.ascontiguousarray(np.broadcast_to(smallv[None], (128, NL, 448)))
    shared["lbT"] = np.ascontiguousarray(f(hg_lb_logits).reshape(2, NL, 2, 128).transpose(3, 0, 2, 1))
    idx = _CACHE["nbidx"]
    rpb = f(na_rpb)
    inw = idx[..., 0] >= 0
    dr = np.where(inw, idx[..., 0], 0)
    dc = np.where(inw, idx[..., 1], 0)
    gath = rpb[:, :, dr, dc]
    gath = np.where(inw[None, None], gath, np.float32(NEG)).astype(np.float32)
    shared["nbias"] = np.ascontiguousarray(gath.transpose(0, 2, 3, 1, 4, 5).reshape(NL, 100, 128, 128))
    in_maps = []
    for i in range(NCORE):
        m = dict(shared)
        m["x"] = np.ascontiguousarray(x[2 * i:2 * i + 2])
        m["ctx"] = np.ascontiguousarray(ctx[2 * i:2 * i + 2])
        cvec = np.stack([c[2 * i], c[2 * i + 1], c_ctx, np.zeros_like(c_ctx)], 0)
        m["cv"] = np.ascontiguousarray(cvec.reshape(4, 8, 128).transpose(2, 1, 0))
        in_maps.append(m)
    res = run_bass_kernel_spmd(nc, in_maps, core_ids=list(range(NCORE)))
    out = np.concatenate([np.asarray(r["out"], dtype=np.float32) for r in res.results], axis=0)
    return out
```

```python
import math
from contextlib import ExitStack

import numpy as np
import ml_dtypes

import concourse.bass as bass
import concourse.mybir as mybir
from concourse.bass_utils import run_bass_kernel_spmd

F32 = mybir.dt.float32
BF16 = mybir.dt.bfloat16
ALU = mybir.AluOpType
AF = mybir.ActivationFunctionType
AX = mybir.AxisListType

NL = 4
D = 1024
NCORE = 8
LAT = 2048
CTX = 256
T = LAT + CTX
NT = T // 128
NCH = T // 64
EPS = 1e-6
BLKS = [(0, 512), (512, 512), (1024, 512), (1536, 512), (2048, 256)]
NEG = -30000.0


class Buf:
    def __init__(self, t=None, name=""):
        self.t = t
        self.name = name
        self.last_w = None
        self.readers = {}

    def __getitem__(self, k):
        return self.t[k]


class K:
    ENG = ("pe", "act", "dve", "pool", "sp")

    def __init__(self, nc, stack, n_dma_sems=48):
        self.nc = nc
        self.stack = stack
        self.cur = stack
        self.streams = {e: [] for e in self.ENG}
        self.sem = {}
        self.cnt = {}
        for e in self.ENG:
            self.sem[e] = stack.enter_context(nc.semaphore("s_" + e))
            self.cnt[e] = 0
        self.ndma = n_dma_sems
        for i in range(n_dma_sems):
            self.sem[("d", i)] = stack.enter_context(nc.semaphore("d%d" % i))
            self.cnt[("d", i)] = 0
        self.dma_rr = 0
        self.known = {e: {} for e in self.ENG}
        self.nbuf = 0

    def sb(self, shape, dtype, name=None):
        self.nbuf += 1
        name = "%s_%d" % (name or "sb", self.nbuf)
        t = self.cur.enter_context(self.nc.sbuf_tensor(name, list(shape), dtype))
        return Buf(t, name)

    def ps(self, shape, dtype, name=None):
        self.nbuf += 1
        name = "%s_%d" % (name or "ps", self.nbuf)
        t = self.cur.enter_context(self.nc.psum_tensor(name, list(shape), dtype))
        return Buf(t, name)

    def dram(self, name, shape, dtype, kind="Internal"):
        t = self.nc.dram_tensor(name, list(shape), dtype, kind=kind)
        return Buf(t.ap(), name)

    def _need(self, eng, reads, writes):
        need = {}

        def add(dep):
            if dep is None:
                return
            k, v = dep
            if need.get(k, 0) < v:
                need[k] = v

        for b in reads:
            add(b.last_w)
        for b in writes:
            add(b.last_w)
            for k, v in b.readers.items():
                add((k, v))
        waits = []
        kn = self.known[eng]
        for k, v in need.items():
            if k == "pe" and eng == "pe":
                continue
            if kn.get(k, 0) >= v:
                continue
            kn[k] = v
            waits.append((k, v))
        return waits

    def op(self, eng, reads, writes, fn, inc=True):
        waits = self._need(eng, reads, writes)
        if inc:
            self.cnt[eng] += 1
            v = self.cnt[eng]
            self.streams[eng].append((waits, fn, (eng, 1)))
        else:
            v = self.cnt[eng] + 1
            self.streams[eng].append((waits, fn, None))
        for b in reads:
            if b.readers.get(eng, 0) < v:
                b.readers[eng] = v
        for b in writes:
            b.last_w = (eng, v)
            b.readers = {}
        return v

    def dma(self, q, reads, writes, fn):
        i = self.dma_rr
        self.dma_rr = (self.dma_rr + 1) % self.ndma
        key = ("d", i)
        waits = self._need(q, reads, writes)
        prev = self.cnt[key]
        if prev > 0 and self.known[q].get(key, 0) < prev:
            self.known[q][key] = prev
            waits.append((key, prev))
        self.cnt[key] += 16
        v = self.cnt[key]
        self.streams[q].append((waits, fn, (key, 16)))
        for b in reads:
            b.readers[key] = v
        for b in writes:
            b.last_w = (key, v)
            b.readers = {}
        return key, v

    def barrier(self):
        for e in self.ENG:
            waits = []
            for k, c in self.cnt.items():
                if c > 0 and (k != e or e in ("act", "dve", "pool")) and self.known[e].get(k, 0) < c:
                    self.known[e][k] = c
                    waits.append((k, c))
            if waits:
                self.streams[e].append((waits, None, None))

    def emit(self):
        nc = self.nc
        waits = [(k, c) for k, c in self.cnt.items() if c > 0 and k != "sp"]
        self.streams["sp"].append((waits, None, None))
        with nc.Block() as block:
            def mk(ename):
                def body(engine):
                    for ws, fn, inc in self.streams[ename]:
                        for kk, v in ws:
                            engine.wait_ge(self.sem[kk], v)
                        if fn is not None:
                            ins = fn(engine)
                            if inc is not None:
                                ins.then_inc(self.sem[inc[0]], inc[1])
                return body
            block.tensor(mk("pe"))
            block.scalar(mk("act"))
            block.vector(mk("dve"))
            block.gpsimd(mk("pool"))
            block.sync(mk("sp"))


def build(nlayers=NL, dbg=False, PIPE_A=True, PIPE_B=True, PIPE_PREP=True, PREFETCH_M=True):
    nc = bass.Bass("TRN2", target_bir_lowering=False)
    st = ExitStack()
    with st:
        k = K(nc, st)

        def MM(out, lhsT, rhs, start, stop, reads, writes, tp=None):
            if tp is None:
                k.op("pe", reads, writes, lambda e: e.matmul(out, lhsT=lhsT, rhs=rhs, start=start, stop=stop), inc=bool(stop))
            else:
                k.op("pe", reads, writes, lambda e: e.matmul(out, lhsT=lhsT, rhs=rhs, start=start, stop=stop, tile_position=tp), inc=bool(stop))

        def RSQ(buf, ap, npart, n, inv_n):
            TS(ap, ap, inv_n, ALU.mult, [buf], [buf], s2=EPS, op1=ALU.add, eng="pool")
            TT(ap, ap, mhalf[0:npart, 0:n], ALU.pow, [buf, mhalf], [buf], eng="pool")

        def TR(out, in_, idn, reads, writes):
            k.op("pe", reads, writes, lambda e: e.transpose(out=out, in_=in_, identity=idn))

        def ACT(out, in_, func, reads, writes, scale=None, bias=None, accum=None):
            kw = {}
            if scale is not None:
                kw["scale"] = scale
            if bias is not None:
                kw["bias"] = bias
            if accum is not None:
                kw["accum_out"] = accum
            k.op("act", reads, writes, lambda e: e.activation(out=out, in_=in_, func=func, **kw))

        def TT(out, in0, in1, op, reads, writes, eng="dve"):
            k.op(eng, reads, writes, lambda e: e.tensor_tensor(out=out, in0=in0, in1=in1, op=op))

        def TS(out, in0, s1, op0, reads, writes, s2=None, op1=None, eng="dve"):
            if op1 is None:
                k.op(eng, reads, writes, lambda e: e.tensor_scalar(out=out, in0=in0, scalar1=s1, scalar2=None, op0=op0))
            else:
                k.op(eng, reads, writes, lambda e: e.tensor_scalar(out=out, in0=in0, scalar1=s1, scalar2=s2, op0=op0, op1=op1))

        def STT(out, in0, scalar, in1, op0, op1, reads, writes):
            k.op("dve", reads, writes, lambda e: e.scalar_tensor_tensor(out=out, in0=in0, scalar=scalar, in1=in1, op0=op0, op1=op1))

        def CP(out, in_, reads, writes, eng="dve"):
            if eng == "act":
                k.op("act", reads, writes, lambda e: e.activation(out=out, in_=in_, func=AF.Copy))
            else:
                k.op(eng, reads, writes, lambda e: e.tensor_copy(out=out, in_=in_))

        def RED(out, in_, reads, writes):
            k.op("dve", reads, writes, lambda e: e.tensor_reduce(out=out, in_=in_, axis=AX.X, op=ALU.add))

        def RECIP(out, in_, reads, writes):
            k.op("dve", reads, writes, lambda e: e.reciprocal(out=out, in_=in_))

        def MSET(ap, val, writes, eng="pool"):
            k.op(eng, [], writes, lambda e: e.memset(ap, val))

        def DMA(q, out, in_, reads, writes):
            k.dma(q, reads, writes, lambda e: e.dma_start(out=out, in_=in_))

        EI = "ExternalInput"
        x_d = k.dram("x", [2, LAT, D], F32, EI)
        ctx_d = k.dram("ctx", [2, CTX, D], F32, EI)
        cv_d = k.dram("cv", [128, 8, 4], F32, EI)
        wmod_d = k.dram("w_mod", [NL, D, 3 * D], F32, EI)
        bmod_d = k.dram("b_modT", [128, NL, 24], F32, EI)
        gain_d = k.dram("gainT", [128, NL, 8], F32, EI)
        win_d = k.dram("w_in", [NL, D, 3840], F32, EI)
        wup_d = k.dram("w_up", [NL, 4, 256, D], F32, EI)
        wmg_d = k.dram("w_merge", [NL, 4, D, D], F32, EI)
        wout_d = k.dram("w_out", [NL, D, D], F32, EI)
        sv_d = k.dram("smallv", [128, NL, 448], F32, EI)
        lb_d = k.dram("lbT", [128, 2, 2, 4], F32, EI)
        nb_d = k.dram("nbias", [NL, 100, 128, 128], F32, EI)
        rope_d = k.dram("rope", [128, 16, 32], F32, EI)
        cs64_d = k.dram("cs64", [128, 256], BF16, EI)
        dftl_d = k.dram("dftL", [128, 16, 2, LAT], BF16, EI)
        dftc_d = k.dram("dftC", [128, 2, 2, CTX], BF16, EI)
        idb_d = k.dram("identb", [128, 128], BF16, EI)
        idf_d = k.dram("identf", [128, 128], F32, EI)
        cm_d = k.dram("cmask", [64, 256], F32, EI)
        out_d = k.dram("out", [2, LAT, D], F32, "ExternalOutput")
        xs_d = k.dram("xs", [2, T, D], F32, "ExternalOutput" if dbg else "Internal")
        ybr_d = k.dram("ybr", [2, 4, 2, 128, T], BF16, "ExternalOutput" if dbg else "Internal")
        xs_tok = [[Buf() for _ in range(NT)] for _ in range(2)]
        ybr_tok = [[Buf() for _ in range(4)] for _ in range(2)]
        cin = Buf()

        identb = k.sb([128, 128], BF16, "identb")
        identf = k.sb([128, 128], F32, "identf")
        onesf = k.sb([128, 128], F32, "onesf")
        cmask = k.sb([64, 256], F32, "cmask")
        rope = k.sb([128, 16, 32], F32, "rope")
        cs64 = k.sb([128, 256], BF16, "cs64")
        dftc = k.sb([128, 2, 2, CTX], BF16, "dftc")
        sv = k.sb([128, NL, 448], F32, "sv")
        bmod = k.sb([128, NL, 24], F32, "bmod")
        gainT = k.sb([128, NL, 8], F32, "gainT")
        sc = k.sb([128, 8, 4], F32, "sc")
        segm = k.sb([128, 512], F32, "segm")
        mhalf = k.sb([128, 16], F32, "mhalf")
        rmask32 = k.sb([128, 4], F32, "rmask32")
        rmask64 = k.sb([128, 2], F32, "rmask64")
        lb_a = k.sb([128, 2, 2, 4], F32, "lb_a")
        lb_c = k.sb([128, 2, 2, 4], F32, "lb_c")
        lb_na = k.sb([128, 2, 2, 4], F32, "lb_na")
        mod = k.sb([128, 24, 4], F32, "mod")
        A1 = k.sb([128, 8, 4], F32, "A1")
        gate_bc = [k.sb([128, D], F32, "gatebc") for _ in range(3)]
        neglam = k.sb([128, 1], F32, "neglam")
        gA = k.sb([128, 2, 32], F32, "gA")
        gS = k.sb([128, 64], F32, "gS")
        gB = k.sb([128, 2, 64], F32, "gB")
        hT = k.sb([128, 8, T], BF16, "hT")
        hT_tok = [Buf() for _ in range(NT)]
        WB0 = k.sb([128, 8, 1024], BF16, "WB0")
        WB1 = k.sb([128, 8192], BF16, "WB1")

        def hts(t0, n):
            return hT_tok[t0 // 128:(t0 + n + 127) // 128]

        DMA("sp", identb[:], idb_d[:], [cin], [identb])
        DMA("sp", identf[:], idf_d[:], [cin], [identf])
        DMA("sp", cmask[:], cm_d[:], [cin], [cmask])
        DMA("sp", rope[:], rope_d[:], [cin], [rope])
        DMA("sp", cs64[:], cs64_d[:], [cin], [cs64])
        DMA("sp", dftc[:], dftc_d[:], [cin], [dftc])
        DMA("sp", sv[:], sv_d[:], [cin], [sv])
        DMA("sp", bmod[:], bmod_d[:], [cin], [bmod])
        DMA("sp", gainT[:], gain_d[:], [cin], [gainT])
        MSET(onesf[:], 1.0, [onesf])
        MSET(mhalf[:], -0.5, [mhalf])
        RED(rmask32[:], identf[:].rearrange("p (j c) -> p j c", c=32), [identf], [rmask32])
        RED(rmask64[:], identf[:].rearrange("p (j c) -> p j c", c=64), [identf], [rmask64])
        MSET(segm[:], 1.0, [segm])
        MSET(segm[:].rearrange("p (c s) -> p c s", s=64)[:, :, 0:1], 0.0, [segm])
        for b in range(2):
            DMA("sp", xs_d.t[b, 0:LAT, :], x_d.t[b], [cin], xs_tok[b][0:16])
            DMA("sp", xs_d.t[b, LAT:T, :], ctx_d.t[b], [cin], xs_tok[b][16:18])

        with ExitStack() as ph:
            k.cur = ph
            cvt = k.sb([128, 8, 4], F32)
            th0 = k.sb([128, 8, 4], F32)
            lbl = k.sb([128, 2, 2, 4], F32)
            lbe = k.sb([128, 2, 2, 4], F32)
            lbs = k.sb([128, 2, 2], F32)
            lbv = k.sb([128, 2, 2, 4], F32)
            lbm = k.sb([128, 2, 2, 4], F32)
            DMA("sp", cvt[:], cv_d[:], [cin], [cvt])
            DMA("sp", lbl[:], lb_d[:], [cin], [lbl])
            ACT(th0[:], cvt[:], AF.Tanh, [cvt], [th0], scale=0.5)
            TS(th0[:], th0[:], 0.5, ALU.mult, [th0], [th0], s2=0.5, op1=ALU.add)
            TT(sc[:], th0[:], cvt[:], ALU.mult, [th0, cvt], [sc])
            ACT(lbe[:], lbl[:], AF.Exp, [lbl], [lbe])
            RED(lbs[:], lbe[:], [lbe], [lbs])
            RECIP(lbs[:], lbs[:], [lbs], [lbs])
            TT(lbe[:], lbe[:], lbs[:].unsqueeze(3).broadcast_to([128, 2, 2, 4]), ALU.mult, [lbe, lbs], [lbe])
            MSET(lbv[:], 0.0, [lbv])
            for l in range(1, 4):
                TT(lbv[:, :, :, l:l + 1], lbv[:, :, :, l - 1:l], lbe[:, :, :, l:l + 1], ALU.add, [lbv, lbe], [lbv])
            TS(lb_a[:], lbv[:], 1e-20, ALU.max, [lbv], [lb_a])
            TS(lb_na[:], lbv[:], -1.0, ALU.mult, [lbv], [lb_na], s2=1.0, op1=ALU.add)
            TT(lb_c[:], lb_a[:], lb_na[:], ALU.add, [lb_a, lb_na], [lb_c])
            k.barrier()
        k.cur = st

        for l in range(nlayers):
            last = (l == NL - 1)
            lam_init = 0.8 - 0.6 * math.exp(-0.3 * l)
            nq = 16 if last else NT
            nck = 32 if last else NCH

            with ExitStack() as ph:
                k.cur = ph
                pm = k.ps([128, 24, 4], F32)
                pg = k.ps([128, 1024], F32)
                wm32 = [k.sb([128, 8, 512], F32) for _ in range(2)]
                diag = [k.sb([128, 128], F32) for _ in range(2)]
                lt = k.sb([128, 2, 32], F32)
                le = k.sb([128, 2], F32)
                for ch in range(6):
                    wb_ = wm32[ch % 2]
                    DMA("sp", wb_[:], wmod_d.t[l, :, ch * 512:(ch + 1) * 512].rearrange("(kc p) n -> p kc n", p=128), [cin], [wb_])
                    for jj in range(4):
                        for kc in range(8):
                            MM(pm[:, ch * 4 + jj, :], wb_[:, kc, jj * 128:(jj + 1) * 128], sc[:, kc, :], kc == 0, kc == 7, [wb_, sc], [pm])
                TT(mod[:], pm[:], bmod[:, l, :].unsqueeze(2).broadcast_to([128, 24, 4]), ALU.add, [pm, bmod], [mod])
                TS(A1[:], mod[:, 8:16, :], 1.0, ALU.add, [mod], [A1])
                TT(A1[:], A1[:], gainT[:, l, :].unsqueeze(2).broadcast_to([128, 8, 4]), ALU.mult, [A1, gainT], [A1])
                for v in range(3):
                    for t in range(8):
                        dg = diag[t % 2]
                        TS(dg[:], identf[:], mod[:, 16 + t, v:v + 1], ALU.mult, [identf, mod], [dg])
                        MM(pg[:, t * 128:(t + 1) * 128], onesf[:], dg[:], True, True, [onesf, dg], [pg])
                    CP(gate_bc[v][:, 0:512], pg[:, 0:512], [pg], [gate_bc[v]])
                    CP(gate_bc[v][:, 512:1024], pg[:, 512:1024], [pg], [gate_bc[v]], eng="act")
                lv = sv[:, l, 64:192].rearrange("p (a b d) -> p a b d", a=2, b=2)
                TT(lt[:], lv[:, :, 0, :], lv[:, :, 1, :], ALU.mult, [sv], [lt])
                RED(le[:], lt[:], [lt], [le])
                ACT(le[:], le[:], AF.Exp, [le], [le])
                TT(neglam[:], le[:, 1:2], le[:, 0:1], ALU.subtract, [le], [neglam])
                TS(neglam[:], neglam[:], -lam_init, ALU.add, [neglam], [neglam])
                CP(gA[:], sv[:, l, 0:64].rearrange("p (a d) -> p a d", a=2), [sv], [gA])
                TS(gA[:, 0, :], gA[:, 0, :], 32.0 ** -0.5, ALU.mult, [gA], [gA])
                TS(gS[:], sv[:, l, 192:256], 1.0 - lam_init, ALU.mult, [sv], [gS])
                CP(gB[:], sv[:, l, 256:384].rearrange("p (a d) -> p a d", a=2), [sv], [gB])
                TS(gB[:, 0, :], gB[:, 0, :], 0.125, ALU.mult, [gB], [gB])
                k.barrier()
            k.cur = st

            for b in range(2):
                with ExitStack() as ph:
                    k.cur = ph
                    xt = [k.sb([128, D], F32) for _ in range(2)]
                    sq = k.sb([128, D], F32)
                    ssq = [k.sb([128, 1], F32) for _ in range(2)]
                    xn = [k.sb([128, D], BF16) for _ in range(2)]
                    tmp = [k.sb([128, 8, 128], F32) for _ in range(2)]
                    pT = [k.ps([128, 8, 128], BF16) for _ in range(2)]
                    for t in range(NT):
                        v = b if t < 16 else 2
                        x_, s_, n_, p_, m_ = xt[t % 2], ssq[t % 2], xn[t % 2], pT[t % 2], tmp[t % 2]
                        DMA("sp", x_[:], xs_d.t[b, t * 128:(t + 1) * 128, :], [xs_tok[b][t]], [x_])
                        ACT(sq[:], x_[:], AF.Square, [x_], [sq, s_], accum=s_[:])
                        RSQ(s_, s_[:], 128, 1, 1.0 / D)
                        TS(n_[:], x_[:], s_[:, 0:1], ALU.mult, [x_, s_], [n_])
                        for kc in range(8):
                            TR(p_[:, kc, :], n_[:, kc * 128:(kc + 1) * 128], identb[:], [n_, identb], [p_])
                        TT(m_[:], p_[:], A1[:, :, v:v + 1].broadcast_to([128, 8, 128]), ALU.mult, [p_, A1], [m_])
                        TT(hT[:, :, t * 128:(t + 1) * 128], m_[:], mod[:, 0:8, v:v + 1].broadcast_to([128, 8, 128]), ALU.add,
                           [m_, mod], [hT_tok[t]], eng="pool")
                    k.barrier()
                k.cur = st

                for br in range(2):
                    with ExitStack() as ph:
                        k.cur = ph
                        isA = (br == 0)
                        dh = 32 if isA else 64
                        ng = 512 // dh
                        nh = ng // 2
                        nmap = 8 if isA else 4
                        c0 = 0 if isA else 768
                        gq = gA if isA else gB
                        kT = k.sb([128, 2, T], BF16)
                        QP = [k.sb([128, T], BF16) for _ in range(nmap)]
                        mps = nmap // 2
                        rmask = rmask32 if isA else rmask64
                        vaug = k.sb([128, NT, 4, 65], BF16)
                        sqt = [k.sb([128, 512], F32) for _ in range(2)]
                        ssg = [k.sb([128, 16], F32) for _ in range(2)]
                        qkn = [k.sb([128, 512], F32) for _ in range(2)]
                        qkb = [k.sb([128, 512], BF16) for _ in range(2)]
                        ta2 = [k.sb([128, 256], F32) for _ in range(2)]
                        tb2 = [k.sb([128, 256], F32) for _ in range(2)]
                        tc2 = [k.sb([128, 256], F32) for _ in range(2)]
                        td2 = [k.sb([128, 256], F32) for _ in range(2)]
                        Pt = [k.sb([128, 512], BF16) for _ in range(3)]
                        Ot2 = [k.sb([128, 8, 65], F32) for _ in range(2)]
                        rsum = k.sb([128, 8], F32)
                        On = k.sb([128, 8, 64], F32)
                        ot = k.sb([128, 256], F32)
                        o2 = k.sb([128, 256], F32)
                        rs4 = k.sb([128, 4], F32)
                        tht = k.sb([128, 256], F32)
                        sgt4 = [k.sb([128, 256], F32) for _ in range(4)]
                        ygb4 = [k.sb([128, 256], BF16) for _ in range(4)]
                        ybt = [k.sb([128, 2, 128], BF16) for _ in range(4)]
                        if isA:
                            OaT = k.sb([66, 8, 512], F32)
                            MSET(OaT[:], 0.0, [OaT])
                        else:
                            nbt = k.sb([128, 100, 128], BF16)
                            DMA("pool", nbt[:], nb_d.t[l].rearrange("n k q -> k n q"), [cin], [nbt])
                        DMA("pool", WB0[:, :, 0:768], win_d.t[l, :, c0:c0 + 768].rearrange("(kc p) n -> p kc n", p=128), [cin], [WB0])
                        g0 = 2816 + 256 * br
                        wg = WB1[:, 0:2048].rearrange("p (kc n) -> p kc n", kc=8)
                        DMA("pool", wg, win_d.t[l, :, g0:g0 + 256].rearrange("(kc p) n -> p kc n", p=128), [cin], [WB1])
                        MSET(vaug[:, :, :, 64:65], 1.0, [vaug])
                        with ExitStack() as ph2:
                            k.cur = ph2
                            pz0 = [k.ps([128, 512], F32) for _ in range(2)]
                            pz1 = [k.ps([128, 512], F32) for _ in range(2)]
                            ptrp = [k.ps([128, 4, 128], BF16) for _ in range(2)]
                            pend = None
                            for t in range(NT):
                                ts_ = slice(t * 128, (t + 1) * 128)
                                z0, z1, pt_ = pz0[t % 2], pz1[t % 2], ptrp[t % 2]
                                sq_, sg_, qn_, qb = sqt[t % 2], ssg[t % 2], qkn[t % 2], qkb[t % 2]
                                for kc in range(8):
                                    MM(z0[:], hT[:, kc, ts_], WB0[:, kc, 0:512], kc == 0, kc == 7, [hT_tok[t], WB0], [z0])
                                for kc in range(8):
                                    MM(z1[:, 0:256], hT[:, kc, ts_], WB0[:, kc, 512:768], kc == 0, kc == 7, [hT_tok[t], WB0], [z1])
                                ACT(sq_[:], z0[:], AF.Square, [z0], [sq_])
                                RED(sg_[:, 0:ng], sq_[:].rearrange("p (g d) -> p g d", d=dh), [sq_], [sg_])
                                RSQ(sg_, sg_[:, 0:ng], 128, ng, 1.0 / dh)
                                TT(qn_[:].rearrange("p (g d) -> p g d", d=dh), z0[:].rearrange("p (g d) -> p g d", d=dh),
                                   sg_[:, 0:ng].unsqueeze(2).broadcast_to([128, ng, dh]), ALU.mult, [z0, sg_], [qn_])
                                if isA and t < 16:
                                    TT(qn_[:].rearrange("p (a h d) -> p a h d", a=2, d=dh), qn_[:].rearrange("p (a h d) -> p a h d", a=2, d=dh),
                                       gq[:].unsqueeze(2).broadcast_to([128, 2, nh, dh]), ALU.mult, [qn_, gq], [qn_], eng="pool")
                                    qv = qn_[:].rearrange("p (g two d) -> p g two d", two=2, d=16)
                                    ov = qb[:].rearrange("p (g two d) -> p g two d", two=2, d=16)
                                    cosb = rope[:, t, 0:16].unsqueeze(1).broadcast_to([128, 16, 16])
                                    sinb = rope[:, t, 16:32].unsqueeze(1).broadcast_to([128, 16, 16])
                                    t3 = lambda buf: buf[:].rearrange("p (g d) -> p g d", d=16)
                                    ta, tb, tc_, td = ta2[t % 2], tb2[t % 2], tc2[t % 2], td2[t % 2]
                                    TT(t3(ta), qv[:, :, 0, :], cosb, ALU.mult, [qn_, rope], [ta])
                                    TT(t3(tb), qv[:, :, 1, :], sinb, ALU.mult, [qn_, rope], [tb])
                                    TT(ov[:, :, 0, :], t3(ta), t3(tb), ALU.subtract, [ta, tb], [qb])
                                    TT(t3(tc_), qv[:, :, 0, :], sinb, ALU.mult, [qn_, rope], [tc_])
                                    TT(t3(td), qv[:, :, 1, :], cosb, ALU.mult, [qn_, rope], [td])
                                    TT(ov[:, :, 1, :], t3(tc_), t3(td), ALU.add, [tc_, td], [qb], eng="pool")
                                else:
                                    TT(qb[:].rearrange("p (a h d) -> p a h d", a=2, d=dh), qn_[:].rearrange("p (a h d) -> p a h d", a=2, d=dh),
                                       gq[:].unsqueeze(2).broadcast_to([128, 2, nh, dh]), ALU.mult, [qn_, gq], [qb], eng="pool")
                                CP(vaug[:, t, :, 0:64], z1[:, 0:256].rearrange("p (h d) -> p h d", d=64), [z1], [vaug], eng="act")

                                def trs(t=t, pt_=pt_, qb=qb, ts_=ts_):
                                    for i in range(4):
                                        TR(pt_[:, i, :], qb[:, i * 128:(i + 1) * 128], identb[:], [qb, identb], [pt_])
                                    CP(kT[:, :, ts_], pt_[:, 2:4, :], [pt_], [kT], eng="act")
                                    for m in range(nmap):
                                        ACT(QP[m][:, ts_], pt_[:, m // mps, :], AF.Copy, [pt_, rmask], [QP[m]],
                                            scale=rmask[:, (m % mps):(m % mps) + 1])
                                if not PIPE_PREP:
                                    trs()
                                    continue
                                if pend is not None:
                                    pend()
                                pend = trs
                            if PIPE_PREP:
                                pend()
                            k.barrier()
                        k.cur = ph
                        Sp = [k.ps([128, 512], F32) for _ in range(2)]
                        if isA:
                            accT = [k.ps([65, 512], F32) for _ in range(2)]
                            pOa = k.ps([128, 4, 66], F32)
                            pOb = k.ps([128, 4, 66], F32)
                        else:
                            acc = [k.ps([128, 4, 65], F32) for _ in range(2)]
                        pgz = k.ps([128, 256], F32)
                        ptr = k.ps([128, 4, 128], BF16)

                        pendq = []

                        def finish(qt, Ot):
                            qs_ = slice(qt * 128, (qt + 1) * 128)
                            sg_ = sgt4[qt % 4]
                            yg_ = ygb4[qt % 4]
                            for kc in range(8):
                                MM(pgz[:], hT[:, kc, qs_], wg[:, kc, :], kc == 0, kc == 7, [hT_tok[qt], WB1], [pgz])
                            ACT(tht[:], pgz[:], AF.Tanh, [pgz], [tht], scale=0.5)
                            STT(sg_[:], tht[:], 1.0, pgz[:], ALU.add, ALU.mult, [tht, pgz], [sg_])
                            RECIP(rsum[:, 0:nmap], Ot[:, 0:nmap, 64], [Ot], [rsum])
                            TT(On[:, 0:nmap, :], Ot[:, 0:nmap, 0:64], rsum[:, 0:nmap].unsqueeze(2).broadcast_to([128, nmap, 64]), ALU.mult,
                               [Ot, rsum], [On])
                            if isA:
                                Onv = On[:].rearrange("p (h two) d -> p h two d", two=2)
                                STT(ot[:].rearrange("p (h d) -> p h d", d=64), Onv[:, :, 1, :], neglam[:, 0:1], Onv[:, :, 0, :],
                                    ALU.mult, ALU.add, [On, neglam], [ot])
                                TT(o2[:], ot[:], ot[:], ALU.mult, [ot], [o2], eng="pool")
                                RED(rs4[:], o2[:].rearrange("p (h d) -> p h d", d=64), [o2], [rs4])
                                RSQ(rs4, rs4[:], 128, 4, 1.0 / 64)
                                TT(o2[:].rearrange("p (h d) -> p h d", d=64), ot[:].rearrange("p (h d) -> p h d", d=64),
                                   rs4[:].unsqueeze(2).broadcast_to([128, 4, 64]), ALU.mult, [ot, rs4], [o2])
                                TT(ot[:].rearrange("p (h d) -> p h d", d=64), o2[:].rearrange("p (h d) -> p h d", d=64),
                                   gS[:].unsqueeze(1).broadcast_to([128, 4, 64]), ALU.mult, [o2, gS], [ot], eng="pool")
                                ysrc = ot[:]
                                ybuf = ot
                            else:
                                ysrc = On[:, 0:4, :].rearrange("p h d -> p (h d)")
                                ybuf = On
                            STT(yg_[:], sg_[:], 0.5, ysrc, ALU.mult, ALU.mult, [sg_, ybuf], [yg_])
                            pendq.append(qt)

                        def flush(keep=0):
                            while len(pendq) > keep:
                                qt = pendq.pop(0)
                                qs_ = slice(qt * 128, (qt + 1) * 128)
                                yg_ = ygb4[qt % 4]
                                yb_ = ybt[qt % 4]
                                for i in range(2):
                                    TR(ptr[:, i, :], yg_[:, i * 128:(i + 1) * 128], identb[:], [yg_, identb], [ptr])
                                CP(yb_[:], ptr[:, 0:2, :], [ptr], [yb_])
                                DMA("sp", ybr_d.t[b, br, :, :, qs_].rearrange("c p n -> p c n"), yb_[:], [yb_], [ybr_tok[b][br]])

                        gi = 0
                        if isA:
                            qblocks = [(0, 512), (512, 512), (1024, 512), (1536, 512)] + ([] if last else [(2048, 256)])
                            for (q0, qn) in qblocks:
                                chunks = list(range(NT)) if q0 < LAT else [16, 17]
                                nci = len(chunks)
                                its = [(m, ci, c) for m in range(8) for ci, c in enumerate(chunks)]

                                def qk(i):
                                    m, ci, c = its[i]
                                    slot, pb = m // 4, 32 * (m % 4)
                                    S_ = Sp[(gi + i) % 2]
                                    P_ = Pt[(gi + i) % 3]
                                    MM(S_[:, 0:qn], kT[:, slot, c * 128:(c + 1) * 128], QP[m][:, q0:q0 + qn], True, True, [kT, QP[m]], [S_])
                                    ACT(P_[:, 0:qn], S_[:, 0:qn], AF.Exp, [S_], [P_])

                                def pv(i):
                                    m, ci, c = its[i]
                                    a_ = accT[m % 2]
                                    P_ = Pt[(gi + i) % 3]
                                    MM(a_[:, 0:qn], vaug[:, c, m // 2, :], P_[:, 0:qn], ci == 0, ci == nci - 1, [vaug, P_], [a_])
                                    if ci == nci - 1:
                                        CP(OaT[0:65, m, 0:qn], a_[:, 0:qn], [a_], [OaT])

                                if PIPE_A:
                                    qk(0)
                                for i in range(len(its)):
                                    if PIPE_A:
                                        if i + 1 < len(its):
                                            qk(i + 1)
                                    else:
                                        qk(i)
                                    pv(i)
                                    if i == 8:
                                        flush()
                                gi += len(its)
                                for j in range(qn // 128):
                                    qt = q0 // 128 + j
                                    Ot = Ot2[qt % 2]
                                    for m in range(8):
                                        pO_ = pOa if m < 4 else pOb
                                        TR(pO_[:, m % 4, :], OaT[:, m, j * 128:(j + 1) * 128], identf[0:66, 0:66], [OaT, identf], [pO_])
                                    CP(Ot[:, 0:4, :], pOa[:, :, 0:65], [pOa], [Ot])
                                    CP(Ot[:, 4:8, :], pOb[:, :, 0:65], [pOb], [Ot])
                                    finish(qt, Ot)
                            flush()
                        else:
                            work = []
                            for qt in range(nq):
                                if qt >= 16:
                                    chunks = [(16, None), (17, None)]
                                else:
                                    cs = min(max(qt - 2, 0), 11)
                                    typ = 0 if qt == 0 else 1 if qt == 1 else 3 if qt == 14 else 4 if qt == 15 else 2
                                    chunks = [(cs + i, typ * 5 + i) for i in range(5)] + [(16, None), (17, None)]
                                groups = [chunks[i:i + 4] for i in range(0, len(chunks), 4)]
                                for m in range(4):
                                    for gidx, g in enumerate(groups):
                                        work.append((qt, m, g, gidx == 0, gidx == len(groups) - 1))

                            def scb(i):
                                qt, m, g, fg, lg = work[i]
                                slot, pb = m // 2, 64 * (m % 2)
                                Sb = Sp[i % 2]
                                Pb = Pt[i % 3]
                                S_ = Sb[:].rearrange("p (c q) -> p c q", q=128)
                                P_ = Pb[:].rearrange("p (c q) -> p c q", q=128)
                                for j, (c, bi) in enumerate(g):
                                    kT_ = kT[:, slot, c * 128:(c + 1) * 128]
                                    qT_ = QP[m][:, qt * 128:(qt + 1) * 128]
                                    if bi is not None:
                                        MM(S_[:, j, :], identb[:], nbt[:, bi * 4 + m, :], True, False, [identb, nbt], [Sb])
                                        MM(S_[:, j, :], kT_, qT_, False, True, [kT, QP[m]], [Sb])
                                    else:
                                        MM(S_[:, j, :], kT_, qT_, True, True, [kT, QP[m]], [Sb])
                                ACT(P_[:, 0:len(g), :], S_[:, 0:len(g), :], AF.Exp, [Sb], [Pb])

                            def pvb(i):
                                qt, m, g, fg, lg = work[i]
                                Pb = Pt[i % 3]
                                P_ = Pb[:].rearrange("p (c q) -> p c q", q=128)
                                ab = acc[qt % 2]
                                for j, (c, bi) in enumerate(g):
                                    MM(ab[:, m, :], P_[:, j, :], vaug[:, c, m, :], fg and j == 0, lg and j == len(g) - 1, [Pb, vaug], [ab])
                                if lg and m == 3:
                                    Ot = Ot2[qt % 2]
                                    CP(Ot[:, 0:4, :], ab[:], [ab], [Ot])
                                    finish(qt, Ot)
                                if lg and m == 1:
                                    flush()

                            if PIPE_B:
                                scb(0)
                            for i in range(len(work)):
                                if PIPE_B:
                                    if i + 1 < len(work):
                                        scb(i + 1)
                                else:
                                    scb(i)
                                pvb(i)
                            flush()
                        k.barrier()
                    k.cur = st

                DMA("pool", WB0[:], win_d.t[l, :, 1536:2560].rearrange("(kc p) n -> p kc n", p=128), [cin], [WB0])
                wg = WB1[:, 0:2048].rearrange("p (kc n) -> p kc n", kc=8)
                DMA("pool", wg, win_d.t[l, :, 3328:3584].rearrange("(kc p) n -> p kc n", p=128), [cin], [WB1])
                for ct in range(2):
                    with ExitStack() as ph:
                        k.cur = ph
                        KT = [k.sb([128, T], BF16) for _ in range(2)]
                        QBD = [k.sb([128, NCH, 128], BF16) for _ in range(2)]
                        Ktm2 = [k.sb([64, NCH, 128], BF16) for _ in range(2)]
                        Vc = k.sb([64, NCH, 128], BF16)
                        Spr = [k.sb([128, NCH, 64], BF16) for _ in range(2)]
                        er = [k.sb([128, NCH], F32) for _ in range(2)]
                        Sst2 = [k.sb([128, 64], F32) for _ in range(2)]
                        Stm2 = [k.sb([128, 64], F32) for _ in range(2)]
                        for d in range(2):
                            MSET(QBD[d][:], 0.0, [QBD[d]])
                        with ExitStack() as ph2:
                            k.cur = ph2
                            qh = k.sb([128, T], F32)
                            thq = [k.sb([128, 512], F32) for _ in range(2)]
                            NS = 2
                            tht = [k.sb([128, 512], F32) for _ in range(NS)]
                            gg = [k.sb([128, 512], F32) for _ in range(NS)]
                            kk = [k.sb([128, 512], F32) for _ in range(NS)]
                            uu = [k.sb([128, 512], F32) for _ in range(NS)]
                            Bc = [k.sb([128, 512], F32) for _ in range(NS)]
                            tmpb = [k.sb([128, 512], F32) for _ in range(NS)]
                            tot = [k.sb([128, 8], F32) for _ in range(NS)]
                            E1 = [k.sb([128, 512], F32) for _ in range(NS)]
                            E2 = [k.sb([128, 512], F32) for _ in range(NS)]
                            pq = [k.ps([128, 512], F32) for _ in range(4)]
                            pv = [k.ps([64, 4, 128], F32) for _ in range(2)]
                            for c4 in range(0, NCH, 4):
                                pv_ = pv[(c4 // 4) % 2]
                                for i in range(4):
                                    c = c4 + i
                                    for kc in range(8):
                                        MM(pv_[:, i, :], hT[:, kc, c * 64:(c + 1) * 64], WB0[:, kc, 768 + ct * 128:768 + (ct + 1) * 128],
                                           kc == 0, kc == 7, [hT_tok[c // 2], WB0], [pv_])
                                CP(Vc[:, c4:c4 + 4, :], pv_[:], [pv_], [Vc], eng="act")
                            pi = 0
                            for bi_, (t0, n) in enumerate(BLKS):
                                p_ = pq[pi % 4]
                                pi += 1
                                tq = thq[bi_ % 2]
                                for kc in range(8):
                                    MM(p_[:, 0:n], WB0[:, kc, ct * 128:(ct + 1) * 128], hT[:, kc, t0:t0 + n], kc == 0, kc == 7, hts(t0, n) + [WB0], [p_])
                                ACT(tq[:, 0:n], p_[:, 0:n], AF.Tanh, [p_], [tq], scale=0.5)
                                STT(qh[:, t0:t0 + n], tq[:, 0:n], 1.0, p_[:, 0:n], ALU.add, ALU.mult, [tq, p_], [qh])
                            it = 0
                            for (t0, n) in BLKS:
                                nc_ = n // 64
                                cb = t0 // 64
                                bs_ = slice(t0, t0 + n)
                                for d in range(2):
                                    j = it % NS
                                    it += 1
                                    p_ = pq[pi % 4]
                                    pi += 1
                                    cc = 256 + d * 256 + ct * 128
                                    for kc in range(8):
                                        MM(p_[:, 0:n], WB0[:, kc, cc:cc + 128], hT[:, kc, bs_], kc == 0, kc == 7, hts(t0, n) + [WB0], [p_])
                                    ACT(uu[j][:, 0:n], p_[:, 0:n], AF.Exp, [p_], [uu[j]], scale=-1.0)
                                    ACT(E2[j][:, 0:n], uu[j][:, 0:n], AF.Ln, [uu[j]], [E2[j]], bias=1.0)
                                    ACT(E1[j][:, 0:n], uu[j][:, 0:n], AF.Ln, [uu[j], lb_a, lb_c], [E1[j]], scale=lb_a[:, d, ct, l:l + 1],
                                        bias=lb_c[:, d, ct, l:l + 1])
                                    TT(gg[j][:, 0:n], E1[j][:, 0:n], E2[j][:, 0:n], ALU.subtract, [E1[j], E2[j]], [gg[j]], eng="pool")
                                    TT(tht[j][:, 0:n], p_[:, 0:n], E2[j][:, 0:n], ALU.add, [p_, E2[j]], [tht[j]])
                                    ACT(kk[j][:, 0:n], tht[j][:, 0:n], AF.Exp, [tht[j]], [kk[j]], scale=-1.0)
                                    k.op("dve", [segm, gg[j]], [Bc[j]], lambda e, n=n, j=j: e.tensor_tensor_scan(
                                        out=Bc[j][:, 0:n], data0=segm[:, 0:n], data1=gg[j][:, 0:n], initial=0.0, op0=ALU.mult, op1=ALU.add))
                                    B3 = Bc[j][:, 0:n].rearrange("p (c s) -> p c s", s=64)
                                    CP(tot[j][:, 0:nc_], B3[:, :, 63], [Bc[j]], [tot[j]])
                                    if d == 1:
                                        TT(tmpb[j][:, 0:n], gg[j][:, 0:n], Bc[j][:, 0:n], ALU.subtract, [gg[j], Bc[j]], [tmpb[j]])
                                        TT(B3, tmpb[j][:, 0:n].rearrange("p (c s) -> p c s", s=64),
                                           tot[j][:, 0:nc_].unsqueeze(2).broadcast_to([128, nc_, 64]), ALU.add, [tmpb[j], tot[j]], [Bc[j]])
                                    ACT(er[d][:, cb:cb + nc_], tot[j][:, 0:nc_], AF.Exp, [tot[j]], [er[d]], scale=0.5)
                                    STT(tmpb[j][:, 0:n].rearrange("p (c s) -> p c s", s=64), tot[j][:, 0:nc_].unsqueeze(2).broadcast_to([128, nc_, 64]),
                                        -0.5, B3, ALU.mult, ALU.add, [tot[j], Bc[j]], [tmpb[j]])
                                    TS(tmpb[j][:, 0:n], tmpb[j][:, 0:n], 43.0, ALU.min, [tmpb[j]], [tmpb[j]], s2=-43.0, op1=ALU.max, eng="pool")
                                    ACT(E1[j][:, 0:n], tmpb[j][:, 0:n], AF.Exp, [tmpb[j]], [E1[j]])
                                    ACT(E2[j][:, 0:n], tmpb[j][:, 0:n], AF.Exp, [tmpb[j]], [E2[j]], scale=-1.0)
                                    for hh in range(2):
                                        ps_ = slice(64 * hh, 64 * hh + 64)
                                        STT(QBD[d][ps_, cb:cb + nc_, 64 * hh:64 * hh + 64], qh[ps_, bs_].rearrange("p (c s) -> p c s", s=64), 0.5,
                                            E1[j][ps_, 0:n].rearrange("p (c s) -> p c s", s=64), ALU.mult, ALU.mult, [qh, E1[j]], [QBD[d]])
                                    STT(KT[d][:, bs_], kk[j][:, 0:n], lb_na[:, d, ct, l:l + 1], E2[j][:, 0:n], ALU.mult, ALU.mult,
                                        [kk[j], E2[j], lb_na], [KT[d]])
                            k.barrier()
                        k.cur = ph
                        Am = [k.sb([64, 2, 128], BF16) for _ in range(2)]
                        of_ = k.sb([64, 512], F32)
                        o2 = k.sb([64, 512], F32)
                        rs8 = k.sb([64, 8], F32)
                        th2 = k.sb([64, 512], F32)
                        sg2 = k.sb([64, 512], F32)
                        yg2 = [k.sb([64, 512], BF16) for _ in range(2)]
                        ybt = [k.sb([128, 256], BF16) for _ in range(2)]
                        pbk = k.ps([128, 1024], BF16)
                        ptk = pbk[0:64, :].rearrange("p (a b) -> p a b", b=128)
                        pTy = pbk[:, 0:256]
                        pu = [k.ps([128, 128], F32) for _ in range(2)]
                        pa = [k.ps([64, 2, 128], F32) for _ in range(2)]
                        po = k.ps([64, 4, 128], F32)
                        pgz = k.ps([64, 4, 128], F32)
                        for d in range(2):
                            for c8 in range(0, NCH, 4):
                                for i in range(4):
                                    c = c8 + i
                                    TR(ptk[:, i, :], KT[d][:, c * 64:(c + 1) * 64], identb[:], [KT[d], identb], [pbk])
                                CP(Ktm2[d][:, c8:c8 + 4, :], ptk[:, 0:4, :], [pbk], [Ktm2[d]], eng="act")
                            MSET(Sst2[d][:], 0.0, [Sst2[d]], eng="dve")
                        orders = [[32, 33, 34, 35] + list(range(32)), [35, 34, 33, 32] + list(range(31, -1, -1))]
                        for oi in range(NCH):
                            for d in range(2):
                                c = orders[d][oi]
                                pu_ = pu[d]
                                Sst, Stm = Sst2[d], Stm2[d]
                                MM(pu_[:], Ktm2[d][:, c, :], Vc[:, c, :], True, True, [Ktm2[d], Vc], [pu_])
                                e_ = er[d][:, c:c + 1]
                                ACT(Spr[d][:, c, :], Sst[:], AF.Copy, [Sst, er[d]], [Spr[d]], scale=e_)
                                for hh in range(2):
                                    ps_ = slice(64 * hh, 64 * hh + 64)
                                    STT(Stm[ps_, :], Sst[ps_, :], er[d][ps_, c:c + 1], pu_[ps_, 64 * hh:64 * hh + 64], ALU.mult, ALU.add,
                                        [Sst, er[d], pu_], [Stm])
                                TS(Sst[:], Stm[:], e_, ALU.mult, [Stm, er[d]], [Sst])
                        cm4 = cmask[:].rearrange("p (d h t) -> p d h t", d=2, h=2)

                        def scm(c):
                            pa_ = pa[c % 2]
                            am_ = Am[c % 2]
                            for d in range(2):
                                MM(pa_[:, d, :], KT[d][:, c * 64:(c + 1) * 64], QBD[d][:, c, :], True, True, [KT[d], QBD[d]], [pa_])
                            TT(am_[:].rearrange("p d (h t) -> p d h t", h=2), pa_[:].rearrange("p d (h t) -> p d h t", h=2), cm4, ALU.mult,
                               [pa_, cmask], [am_])

                        ptail = []

                        def tail():
                            while ptail:
                                c4 = ptail.pop(0)
                                yg_ = yg2[(c4 // 4) % 2]
                                yb_ = ybt[(c4 // 4) % 2]
                                for i in range(4):
                                    TR(pTy[:, i * 64:(i + 1) * 64], yg_[:, i * 128:(i + 1) * 128], identb[0:64, 0:64], [yg_, identb], [pbk])
                                CP(yb_[:], pTy, [pbk], [yb_], eng="act")
                                DMA("sp", ybr_d.t[b, 2, ct, :, c4 * 64:c4 * 64 + 256], yb_[:], [yb_], [ybr_tok[b][2]])

                        scm(0)
                        for c4 in range(0, nck, 4):
                            for i in range(4):
                                c = c4 + i
                                if c + 1 < nck:
                                    scm(c + 1)
                                for kc in range(8):
                                    MM(pgz[:, i, :], hT[:, kc, c * 64:(c + 1) * 64], wg[:, kc, ct * 128:(ct + 1) * 128], kc == 0, kc == 7,
                                       [hT_tok[c // 2], WB1], [pgz])
                                am_ = Am[c % 2]
                                for hh in range(2):
                                    hs_ = slice(64 * hh, 64 * hh + 64)
                                    for d in range(2):
                                        MM(po[:, i, hs_], am_[:, d, hs_], Vc[:, c, hs_], d == 0, False, [am_, Vc], [po])
                                        MM(po[:, i, hs_], QBD[d][:, c, hs_], Spr[d][:, c, :], False, d == 1, [QBD[d], Spr[d]], [po])
                            tail()
                            pof = po[:].rearrange("p c n -> p (c n)")
                            pgf = pgz[:].rearrange("p c n -> p (c n)")
                            yg_ = yg2[(c4 // 4) % 2]
                            ACT(th2[:], pgf, AF.Tanh, [pgz], [th2], scale=0.5)
                            STT(sg2[:], th2[:], 1.0, pgf, ALU.add, ALU.mult, [th2, pgz], [sg2])
                            CP(of_[:], pof, [po], [of_], eng="act")
                            ACT(o2[:], pof, AF.Square, [po], [o2])
                            RED(rs8[:], o2[:].rearrange("p (g d) -> p g d", d=64), [o2], [rs8])
                            RSQ(rs8, rs8[:], 64, 8, 1.0 / 64)
                            TT(o2[:].rearrange("p (g d) -> p g d", d=64), of_[:].rearrange("p (g d) -> p g d", d=64),
                               rs8[:].unsqueeze(2).broadcast_to([64, 8, 64]), ALU.mult, [of_, rs8], [o2])
                            TT(of_[:].rearrange("p (g d) -> p g d", d=64), o2[:].rearrange("p (g d) -> p g d", d=64),
                               sv[0:64, l, 384:448].unsqueeze(1).broadcast_to([64, 8, 64]), ALU.mult, [o2, sv], [of_], eng="pool")
                            STT(yg_[:], sg2[:], 0.5, of_[:], ALU.mult, ALU.mult, [sg2, of_], [yg_])
                            ptail.append(c4)
                        tail()
                        k.barrier()
                    k.cur = st

                with ExitStack() as ph:
                    k.cur = ph
                    uT = k.sb([128, 2, T], BF16)
                    PQ = k.sb([128, NT, 2, 256], BF16)
                    tab = [k.sb([128, 16, 2, 256], BF16) for _ in range(2)]
                    tht = k.sb([128, 256], F32)
                    sgt = k.sb([128, 256], F32)
                    ybt = [k.sb([128, 2, 256], BF16) for _ in range(2)]
                    pq = [k.ps([128, 512], F32) for _ in range(2)]
                    pp = [k.ps([128, 2, 256], F32) for _ in range(2)]
                    py = [k.ps([128, 256], F32) for _ in range(2)]
                    pgz = [k.ps([128, 256], F32) for _ in range(2)]
                    DMA("pool", WB0[:, :, 0:256], win_d.t[l, :, 2560:2816].rearrange("(kc p) n -> p kc n", p=128), [cin], [WB0])
                    wg = WB1[:, 0:2048].rearrange("p (kc n) -> p kc n", kc=8)
                    DMA("pool", wg, win_d.t[l, :, 3584:3840].rearrange("(kc p) n -> p kc n", p=128), [cin], [WB1])
                    pi = 0
                    for ct in range(2):
                        for (t0, n) in BLKS:
                            p_ = pq[pi % 2]
                            pi += 1
                            for kc in range(8):
                                MM(p_[:, 0:n], WB0[:, kc, ct * 128:(ct + 1) * 128], hT[:, kc, t0:t0 + n], kc == 0, kc == 7, hts(t0, n) + [WB0], [p_])
                            CP(uT[:, ct, t0:t0 + n], p_[:, 0:n], [p_], [uT], eng="act")
                    for t in range(NT):
                        p_ = pp[t % 2]
                        for ct in range(2):
                            MM(p_[:, ct, :], uT[:, ct, t * 128:(t + 1) * 128], cs64[:], True, True, [uT, cs64], [p_])
                        CP(PQ[:, t, :, :], p_[:], [p_], [PQ], eng=("act" if t % 2 else "dve"))
                    it = 0
                    nblk = 8 if last else 9
                    for nb in range(nblk):
                        if nb < 8:
                            tb_ = tab[nb % 2]
                            DMA("sp", tb_[:], dftl_d.t[:, :, :, nb * 256:(nb + 1) * 256], [cin], [tb_])
                            tcs = list(range(16))
                            tsrc = lambda tc, cs: tb_[:, tc, cs, :]
                            tread = [tb_]
                            t0 = nb * 256
                        else:
                            tcs = [16, 17]
                            tsrc = lambda tc, cs: dftc[:, tc - 16, cs, :]
                            tread = [dftc]
                            t0 = LAT
                        yb_ = ybt[nb % 2]
                        for ct in range(2):
                            y_ = py[it % 2]
                            g_ = pgz[it % 2]
                            it += 1
                            nmm = len(tcs) * 2
                            j = 0
                            for tc in tcs:
                                for cs in range(2):
                                    MM(y_[:], PQ[:, tc, ct, cs * 128:(cs + 1) * 128], tsrc(tc, cs), j == 0, j == nmm - 1, [PQ] + tread, [y_])
                                    j += 1
                            for kc in range(8):
                                MM(g_[:], wg[:, kc, ct * 128:(ct + 1) * 128], hT[:, kc, t0:t0 + 256], kc == 0, kc == 7, hts(t0, 256) + [WB1], [g_])
                            ACT(tht[:], g_[:], AF.Tanh, [g_], [tht], scale=0.5)
                            STT(sgt[:], tht[:], 1.0, g_[:], ALU.add, ALU.mult, [tht, g_], [sgt])
                            STT(yb_[:, ct, :], sgt[:], 0.5, y_[:], ALU.mult, ALU.mult, [sgt, y_], [yb_])
                        DMA("sp", ybr_d.t[b, 3, :, :, t0:t0 + 256].rearrange("c p n -> p c n"), yb_[:], [yb_], [ybr_tok[b][3]])
                    k.barrier()
                k.cur = st

                with ExitStack() as ph:
                    k.cur = ph
                    ybS = k.sb([128, 4, 2, T], BF16)
                    accT = k.sb([128, 8, T], BF16)
                    wu = [k.sb([128, 4, 2, 128], BF16) for _ in range(2)]
                    wm_tok = [Buf(), Buf()]
                    tht = [k.sb([128, 512], F32) for _ in range(2)]
                    tmpm = [k.sb([128, 512], F32) for _ in range(2)]
                    accf = [k.sb([128, 512], F32) for _ in range(2)]
                    xt = [k.sb([128, D], F32) for _ in range(2)]
                    xo = [k.sb([128, D], F32) for _ in range(2)]
                    tmo = [k.sb([128, 512], F32) for _ in range(2)]
                    pM = [k.ps([128, 512], F32) for _ in range(2)]
                    pU = [k.ps([128, 512], F32) for _ in range(2)]
                    pO = [k.ps([128, 512], F32) for _ in range(2)]
                    for i in range(4):
                        DMA("sp", ybS[:, i, :, :], ybr_d.t[b, i].rearrange("c p n -> p c n"), [ybr_tok[b][i]], [ybS])
                    DMA("pool", WB0[:], wout_d.t[l].rearrange("(kc p) n -> p kc n", p=128), [cin], [WB0])
                    mblks = BLKS[:4] if last else BLKS
                    im = 0
                    def wload(ft):
                        fs_ = slice(ft * 128, (ft + 1) * 128)
                        wmv = WB1[:, (ft % 2) * 4096:(ft % 2 + 1) * 4096].rearrange("p (i kc n) -> p i kc n", i=4, kc=8)
                        DMA("pool", wmv, wmg_d.t[l, :, :, fs_].rearrange("i (kc p) n -> p i kc n", p=128), [cin], [wm_tok[ft % 2]])
                        DMA("pool", wu[ft % 2][:], wup_d.t[l, :, :, fs_].rearrange("i (fc p) n -> p i fc n", p=128), [cin], [wu[ft % 2]])

                    if PREFETCH_M:
                        wload(0)
                    for ft in range(8):
                        fs_ = slice(ft * 128, (ft + 1) * 128)
                        wmv = WB1[:, (ft % 2) * 4096:(ft % 2 + 1) * 4096].rearrange("p (i kc n) -> p i kc n", i=4, kc=8)
                        wmt = wm_tok[ft % 2]
                        wu_ = wu[ft % 2]
                        if not PREFETCH_M:
                            wload(ft)
                        elif ft + 1 < 8:
                            wload(ft + 1)
                        for bi_, (t0, n) in enumerate(mblks):
                            af = accf[bi_ % 2]
                            for i in range(4):
                                m_ = pM[im % 2]
                                u_ = pU[im % 2]
                                th_ = tht[im % 2]
                                tm_ = tmpm[im % 2]
                                im += 1
                                for kc in range(8):
                                    MM(m_[:, 0:n], wmv[:, i, kc, :], hT[:, kc, t0:t0 + n], kc == 0, kc == 7, hts(t0, n) + [wmt], [m_])
                                for fc in range(2):
                                    MM(u_[:, 0:n], wu_[:, i, fc, :], ybS[:, i, fc, t0:t0 + n], fc == 0, fc == 1, [wu_, ybS], [u_])
                                ACT(th_[:, 0:n], m_[:, 0:n], AF.Tanh, [m_], [th_], scale=0.5)
                                if i == 0:
                                    STT(af[:, 0:n], th_[:, 0:n], 1.0, u_[:, 0:n], ALU.add, ALU.mult, [th_, u_], [af])
                                else:
                                    STT(tm_[:, 0:n], th_[:, 0:n], 1.0, u_[:, 0:n], ALU.add, ALU.mult, [th_, u_], [tm_])
                                    TT(af[:, 0:n], af[:, 0:n], tm_[:, 0:n], ALU.add, [af, tm_], [af], eng=("dve" if i == 2 else "pool"))
                            ACT(accT[:, ft, t0:t0 + n], af[:, 0:n], AF.Copy, [af], [accT], scale=0.5)
                    ntile = 16 if last else NT
                    io = 0
                    for t in range(ntile):
                        v = b if t < 16 else 2
                        x_ = xt[t % 2]
                        o_ = xo[t % 2]
                        DMA("sp", x_[:], xs_d.t[b, t * 128:(t + 1) * 128, :], [xs_tok[b][t]], [x_])
                        for hf in range(2):
                            p_ = pO[io % 2]
                            tm_ = tmo[io % 2]
                            io += 1
                            cs_ = slice(hf * 512, (hf + 1) * 512)
                            for kc in range(8):
                                MM(p_[:], accT[:, kc, t * 128:(t + 1) * 128], WB0[:, kc, cs_], kc == 0, kc == 7, [accT, WB0], [p_])
                            TT(tm_[:], p_[:], gate_bc[v][:, cs_], ALU.mult, [p_, gate_bc[v]], [tm_])
                            TT(o_[:, cs_], tm_[:], x_[:, cs_], ALU.add, [tm_, x_], [o_], eng="pool")
                        if last:
                            DMA("sp", out_d.t[b, t * 128:(t + 1) * 128, :], o_[:], [o_], [xs_tok[b][t]])
                        else:
                            DMA("sp", xs_d.t[b, t * 128:(t + 1) * 128, :], o_[:], [o_], [xs_tok[b][t]])
                    k.barrier()
                k.cur = st
        k.emit()
    return nc


def _consts():
    bf = ml_dtypes.bfloat16
    c = {}
    c["identb"] = np.eye(128, dtype=np.float32).astype(bf)
    c["identf"] = np.eye(128, dtype=np.float32)
    s = np.arange(64)[:, None]
    t = np.arange(64)[None, :]
    mf = (s <= t).astype(np.float32)
    mb = (s >= t).astype(np.float32)
    cm = np.stack([np.stack([mf, mf], 0), np.stack([mb, mb], 0)], 0)
    c["cmask"] = np.ascontiguousarray(cm.transpose(2, 0, 1, 3).reshape(64, 256))
    n = np.arange(LAT)
    row = (n // 64).astype(np.float32)
    col = (n % 64).astype(np.float32)
    inv = (10000.0 ** (-np.arange(0, 16, 2, dtype=np.float32) / 16)).astype(np.float32)
    ang = np.concatenate([row[:, None] * inv, col[:, None] * inv], -1).astype(np.float32)
    rp = np.concatenate([np.cos(ang), np.sin(ang)], -1).astype(np.float32)
    c["rope"] = np.ascontiguousarray(rp.reshape(16, 128, 32).transpose(1, 0, 2))
    e = np.arange(64)
    a64 = 2 * np.pi * np.outer(e, e) / 64
    C64 = np.cos(a64) / 8.0
    S64 = np.sin(a64) / 8.0
    cs = np.zeros((128, 256), np.float64)
    for g in range(2):
        cs[g * 64:(g + 1) * 64, g * 64:(g + 1) * 64] = C64
        cs[g * 64:(g + 1) * 64, 128 + g * 64:128 + (g + 1) * 64] = S64
    c["cs64"] = cs.astype(np.float32).astype(bf)

    def dft(N):
        tt = np.arange(N)
        a = 2 * np.pi * ((np.outer(tt, tt)) % N) / N
        Cn = np.cos(a) / np.sqrt(N)
        Sn = -np.sin(a) / np.sqrt(N)
        tab = np.stack([Cn, Sn], 1)
        return np.ascontiguousarray(tab.reshape(N // 128, 128, 2, N).transpose(1, 0, 2, 3)).astype(np.float32).astype(bf)

    c["dftL"] = dft(LAT)
    c["dftC"] = dft(CTX)
    return c


def _nbias_index():
    rows, W, wh, ww = 32, 64, 8, 16
    idx = np.full((5, 5, 128, 128, 2), -1, np.int64)
    for typ, qt in enumerate([0, 1, 5, 14, 15]):
        cs = min(max(qt - 2, 0), 11)
        for i in range(5):
            kc = cs + i
            for kk in range(128):
                kr, kcol = 2 * kc + kk // 64, kk % 64
                for q in range(128):
                    r, cq = 2 * qt + q // 64, q % 64
                    r0 = min(max(r - wh // 2, 0), rows - wh)
                    c0 = min(max(cq - ww // 2, 0), W - ww)
                    if r0 <= kr < r0 + wh and c0 <= kcol < c0 + ww:
                        idx[typ, i, kk, q, 0] = kr - r + wh - 1
                        idx[typ, i, kk, q, 1] = min(max(kcol - cq, 1 - ww), ww - 1) + ww - 1
    return idx


_CACHE = {}


def kernel(x, c, ctx, c_ctx, norm_gain, w_mod, b_mod, w_in, da_qk_gain, da_lambda, da_subln_gain,
           na_qk_gain, na_rpb, hg_lb_logits, hg_norm_gain, w_up, w_merge, w_out):
    f = lambda a: np.ascontiguousarray(np.asarray(a, dtype=np.float32))
    x, c, ctx, c_ctx = f(x), f(c), f(ctx), f(c_ctx)
    if "nc" not in _CACHE:
        _CACHE["nc"] = build()
        _CACHE["consts"] = _consts()
        _CACHE["nbidx"] = _nbias_index()
    nc = _CACHE["nc"]
    shared = dict(_CACHE["consts"])
    shared["w_mod"] = f(w_mod)
    shared["w_in"] = f(w_in)
    shared["w_up"] = f(w_up)
    shared["w_merge"] = f(w_merge)
    shared["w_out"] = f(w_out)
    shared["b_modT"] = np.ascontiguousarray(f(b_mod).reshape(NL, 24, 128).transpose(2, 0, 1))
    shared["gainT"] = np.ascontiguousarray(f(norm_gain).reshape(NL, 8, 128).transpose(2, 0, 1))
    smallv = np.concatenate([f(da_qk_gain).reshape(NL, 64), f(da_lambda).reshape(NL, 128), f(da_subln_gain).reshape(NL, 64),
                             f(na_qk_gain).reshape(NL, 128), f(hg_norm_gain).reshape(NL, 64)], -1)
    shared["smallv"] = np.ascontiguousarray(np.broadcast_to(smallv[None], (128, NL, 448)))
    shared["lbT"] = np.ascontiguousarray(f(hg_lb_logits).reshape(2, NL, 2, 128).transpose(3, 0, 2, 1))
    idx = _CACHE["nbidx"]
    rpb = f(na_rpb)
    inw = idx[..., 0] >= 0
    dr = np.where(inw, idx[..., 0], 0)
    dc = np.where(inw, idx[..., 1], 0)
    gath = rpb[:, :, dr, dc]
    gath = np.where(inw[None, None], gath, np.float32(NEG)).astype(np.float32)
    shared["nbias"] = np.ascontiguousarray(gath.transpose(0, 2, 3, 1, 4, 5).reshape(NL, 100, 128, 128))
    in_maps = []
    for i in range(NCORE):
        m = dict(shared)
        m["x"] = np.ascontiguousarray(x[2 * i:2 * i + 2])
        m["ctx"] = np.ascontiguousarray(ctx[2 * i:2 * i + 2])
        cvec = np.stack([c[2 * i], c[2 * i + 1], c_ctx, np.zeros_like(c_ctx)], 0)
        m["cv"] = np.ascontiguousarray(cvec.reshape(4, 8, 128).transpose(2, 1, 0))
        in_maps.append(m)
    res = run_bass_kernel_spmd(nc, in_maps, core_ids=list(range(NCORE)))
    out = np.concatenate([np.asarray(r["out"], dtype=np.float32) for r in res.results], axis=0)
    return out
```

```python
import math
from contextlib import ExitStack

import numpy as np
import ml_dtypes

import concourse.bass as bass
import concourse.mybir as mybir
from concourse.bass_utils import run_bass_kernel_spmd

F32 = mybir.dt.float32
BF16 = mybir.dt.bfloat16
ALU = mybir.AluOpType
AF = mybir.ActivationFunctionType
AX = mybir.AxisListType

NL = 4
D = 1024
NCORE = 8
LAT = 2048
CTX = 256
T = LAT + CTX
NT = T // 128
NCH = T // 64
EPS = 1e-6
BLKS = [(0, 512), (512, 512), (1024, 512), (1536, 512), (2048, 256)]
NEG = -30000.0


class Buf:
    def __init__(self, t=None, name=""):
        self.t = t
        self.name = name
        self.last_w = None
        self.readers = {}

    def __getitem__(self, k):
        return self.t[k]


class K:
    ENG = ("pe", "act", "dve", "pool", "sp")

    def __init__(self, nc, stack, n_dma_sems=48):
        self.nc = nc
        self.stack = stack
        self.cur = stack
        self.streams = {e: [] for e in self.ENG}
        self.sem = {}
        self.cnt = {}
        for e in self.ENG:
            self.sem[e] = stack.enter_context(nc.semaphore("s_" + e))
            self.cnt[e] = 0
        self.ndma = n_dma_sems
        for i in range(n_dma_sems):
            self.sem[("d", i)] = stack.enter_context(nc.semaphore("d%d" % i))
            self.cnt[("d", i)] = 0
        self.dma_rr = 0
        self.known = {e: {} for e in self.ENG}
        self.nbuf = 0

    def sb(self, shape, dtype, name=None):
        self.nbuf += 1
        name = "%s_%d" % (name or "sb", self.nbuf)
        t = self.cur.enter_context(self.nc.sbuf_tensor(name, list(shape), dtype))
        return Buf(t, name)

    def ps(self, shape, dtype, name=None):
        self.nbuf += 1
        name = "%s_%d" % (name or "ps", self.nbuf)
        t = self.cur.enter_context(self.nc.psum_tensor(name, list(shape), dtype))
        return Buf(t, name)

    def dram(self, name, shape, dtype, kind="Internal"):
        t = self.nc.dram_tensor(name, list(shape), dtype, kind=kind)
        return Buf(t.ap(), name)

    def _need(self, eng, reads, writes):
        need = {}

        def add(dep):
            if dep is None:
                return
            k, v = dep
            if need.get(k, 0) < v:
                need[k] = v

        for b in reads:
            add(b.last_w)
        for b in writes:
            add(b.last_w)
            for k, v in b.readers.items():
                add((k, v))
        waits = []
        kn = self.known[eng]
        for k, v in need.items():
            if k == "pe" and eng == "pe":
                continue
            if kn.get(k, 0) >= v:
                continue
            kn[k] = v
            waits.append((k, v))
        return waits

    def op(self, eng, reads, writes, fn, inc=True):
        waits = self._need(eng, reads, writes)
        if inc:
            self.cnt[eng] += 1
            v = self.cnt[eng]
            self.streams[eng].append((waits, fn, (eng, 1)))
        else:
            v = self.cnt[eng] + 1
            self.streams[eng].append((waits, fn, None))
        for b in reads:
            if b.readers.get(eng, 0) < v:
                b.readers[eng] = v
        for b in writes:
            b.last_w = (eng, v)
            b.readers = {}
        return v

    def dma(self, q, reads, writes, fn):
        i = self.dma_rr
        self.dma_rr = (self.dma_rr + 1) % self.ndma
        key = ("d", i)
        waits = self._need(q, reads, writes)
        prev = self.cnt[key]
        if prev > 0 and self.known[q].get(key, 0) < prev:
            self.known[q][key] = prev
            waits.append((key, prev))
        self.cnt[key] += 16
        v = self.cnt[key]
        self.streams[q].append((waits, fn, (key, 16)))
        for b in reads:
            b.readers[key] = v
        for b in writes:
            b.last_w = (key, v)
            b.readers = {}
        return key, v

    def barrier(self):
        for e in self.ENG:
            waits = []
            for k, c in self.cnt.items():
                if c > 0 and (k != e or e in ("act", "dve", "pool")) and self.known[e].get(k, 0) < c:
                    self.known[e][k] = c
                    waits.append((k, c))
            if waits:
                self.streams[e].append((waits, None, None))

    def emit(self):
        nc = self.nc
        waits = [(k, c) for k, c in self.cnt.items() if c > 0 and k != "sp"]
        self.streams["sp"].append((waits, None, None))
        with nc.Block() as block:
            def mk(ename):
                def body(engine):
                    for ws, fn, inc in self.streams[ename]:
                        for kk, v in ws:
                            engine.wait_ge(self.sem[kk], v)
                        if fn is not None:
                            ins = fn(engine)
                            if inc is not None:
                                ins.then_inc(self.sem[inc[0]], inc[1])
                return body
            block.tensor(mk("pe"))
            block.scalar(mk("act"))
            block.vector(mk("dve"))
            block.gpsimd(mk("pool"))
            block.sync(mk("sp"))


def build(nlayers=NL, dbg=False, PIPE_A=True, PIPE_B=True, PIPE_PREP=True, PREFETCH_M=True):
    nc = bass.Bass("TRN2", target_bir_lowering=False)
    st = ExitStack()
    with st:
        k = K(nc, st)

        def MM(out, lhsT, rhs, start, stop, reads, writes, tp=None):
            if tp is None:
                k.op("pe", reads, writes, lambda e: e.matmul(out, lhsT=lhsT, rhs=rhs, start=start, stop=stop), inc=bool(stop))
            else:
                k.op("pe", reads, writes, lambda e: e.matmul(out, lhsT=lhsT, rhs=rhs, start=start, stop=stop, tile_position=tp), inc=bool(stop))

        def RSQ(buf, ap, npart, n, inv_n):
            TS(ap, ap, inv_n, ALU.mult, [buf], [buf], s2=EPS, op1=ALU.add, eng="pool")
            TT(ap, ap, mhalf[0:npart, 0:n], ALU.pow, [buf, mhalf], [buf], eng="pool")

        def TR(out, in_, idn, reads, writes):
            k.op("pe", reads, writes, lambda e: e.transpose(out=out, in_=in_, identity=idn))

        def ACT(out, in_, func, reads, writes, scale=None, bias=None, accum=None):
            kw = {}
            if scale is not None:
                kw["scale"] = scale
            if bias is not None:
                kw["bias"] = bias
            if accum is not None:
                kw["accum_out"] = accum
            k.op("act", reads, writes, lambda e: e.activation(out=out, in_=in_, func=func, **kw))

        def TT(out, in0, in1, op, reads, writes, eng="dve"):
            k.op(eng, reads, writes, lambda e: e.tensor_tensor(out=out, in0=in0, in1=in1, op=op))

        def TS(out, in0, s1, op0, reads, writes, s2=None, op1=None, eng="dve"):
            if op1 is None:
                k.op(eng, reads, writes, lambda e: e.tensor_scalar(out=out, in0=in0, scalar1=s1, scalar2=None, op0=op0))
            else:
                k.op(eng, reads, writes, lambda e: e.tensor_scalar(out=out, in0=in0, scalar1=s1, scalar2=s2, op0=op0, op1=op1))

        def STT(out, in0, scalar, in1, op0, op1, reads, writes):
            k.op("dve", reads, writes, lambda e: e.scalar_tensor_tensor(out=out, in0=in0, scalar=scalar, in1=in1, op0=op0, op1=op1))

        def CP(out, in_, reads, writes, eng="dve"):
            if eng == "act":
                k.op("act", reads, writes, lambda e: e.activation(out=out, in_=in_, func=AF.Copy))
            else:
                k.op(eng, reads, writes, lambda e: e.tensor_copy(out=out, in_=in_))

        def RED(out, in_, reads, writes):
            k.op("dve", reads, writes, lambda e: e.tensor_reduce(out=out, in_=in_, axis=AX.X, op=ALU.add))

        def RECIP(out, in_, reads, writes):
            k.op("dve", reads, writes, lambda e: e.reciprocal(out=out, in_=in_))

        def MSET(ap, val, writes, eng="pool"):
            k.op(eng, [], writes, lambda e: e.memset(ap, val))

        def DMA(q, out, in_, reads, writes):
            k.dma(q, reads, writes, lambda e: e.dma_start(out=out, in_=in_))

        EI = "ExternalInput"
        x_d = k.dram("x", [2, LAT, D], F32, EI)
        ctx_d = k.dram("ctx", [2, CTX, D], F32, EI)
        cv_d = k.dram("cv", [128, 8, 4], F32, EI)
        wmod_d = k.dram("w_mod", [NL, D, 3 * D], F32, EI)
        bmod_d = k.dram("b_modT", [128, NL, 24], F32, EI)
        gain_d = k.dram("gainT", [128, NL, 8], F32, EI)
        win_d = k.dram("w_in", [NL, D, 3840], F32, EI)
        wup_d = k.dram("w_up", [NL, 4, 256, D], F32, EI)
        wmg_d = k.dram("w_merge", [NL, 4, D, D], F32, EI)
        wout_d = k.dram("w_out", [NL, D, D], F32, EI)
        sv_d = k.dram("smallv", [128, NL, 448], F32, EI)
        lb_d = k.dram("lbT", [128, 2, 2, 4], F32, EI)
        nb_d = k.dram("nbias", [NL, 100, 128, 128], F32, EI)
        rope_d = k.dram("rope", [128, 16, 32], F32, EI)
        cs64_d = k.dram("cs64", [128, 256], BF16, EI)
        dftl_d = k.dram("dftL", [128, 16, 2, LAT], BF16, EI)
        dftc_d = k.dram("dftC", [128, 2, 2, CTX], BF16, EI)
        idb_d = k.dram("identb", [128, 128], BF16, EI)
        idf_d = k.dram("identf", [128, 128], F32, EI)
        cm_d = k.dram("cmask", [64, 256], F32, EI)
        out_d = k.dram("out", [2, LAT, D], F32, "ExternalOutput")
        xs_d = k.dram("xs", [2, T, D], F32, "ExternalOutput" if dbg else "Internal")
        ybr_d = k.dram("ybr", [2, 4, 2, 128, T], BF16, "ExternalOutput" if dbg else "Internal")
        xs_tok = [[Buf() for _ in range(NT)] for _ in range(2)]
        ybr_tok = [[Buf() for _ in range(4)] for _ in range(2)]
        cin = Buf()

        identb = k.sb([128, 128], BF16, "identb")
        identf = k.sb([128, 128], F32, "identf")
        onesf = k.sb([128, 128], F32, "onesf")
        cmask = k.sb([64, 256], F32, "cmask")
        rope = k.sb([128, 16, 32], F32, "rope")
        cs64 = k.sb([128, 256], BF16, "cs64")
        dftc = k.sb([128, 2, 2, CTX], BF16, "dftc")
        sv = k.sb([128, NL, 448], F32, "sv")
        bmod = k.sb([128, NL, 24], F32, "bmod")
        gainT = k.sb([128, NL, 8], F32, "gainT")
        sc = k.sb([128, 8, 4], F32, "sc")
        sc16 = k.sb([128, 8, 4], BF16, "sc16")
        segm = k.sb([128, 512], F32, "segm")
        mhalf = k.sb([128, 16], F32, "mhalf")
        rmask32 = k.sb([128, 4], F32, "rmask32")
        rmask64 = k.sb([128, 2], F32, "rmask64")
        lb_a = k.sb([128, 2, 2, 4], F32, "lb_a")
        lb_c = k.sb([128, 2, 2, 4], F32, "lb_c")
        lb_na = k.sb([128, 2, 2, 4], F32, "lb_na")
        mod = k.sb([128, 24, 4], F32, "mod")
        A1 = k.sb([128, 8, 4], F32, "A1")
        gate_bc = [k.sb([128, D], F32, "gatebc") for _ in range(3)]
        neglam = k.sb([128, 1], F32, "neglam")
        gA = k.sb([128, 2, 32], F32, "gA")
        gS = k.sb([128, 64], F32, "gS")
        gB = k.sb([128, 2, 64], F32, "gB")
        hT = k.sb([128, 8, T], BF16, "hT")
        hT_tok = [Buf() for _ in range(NT)]
        WB0 = k.sb([128, 8, 1024], BF16, "WB0")
        WB1 = k.sb([128, 8192], BF16, "WB1")

        def hts(t0, n):
            return hT_tok[t0 // 128:(t0 + n + 127) // 128]

        DMA("sp", identb[:], idb_d[:], [cin], [identb])
        DMA("sp", identf[:], idf_d[:], [cin], [identf])
        DMA("sp", cmask[:], cm_d[:], [cin], [cmask])
        DMA("sp", rope[:], rope_d[:], [cin], [rope])
        DMA("sp", cs64[:], cs64_d[:], [cin], [cs64])
        DMA("sp", dftc[:], dftc_d[:], [cin], [dftc])
        DMA("sp", sv[:], sv_d[:], [cin], [sv])
        DMA("sp", bmod[:], bmod_d[:], [cin], [bmod])
        DMA("sp", gainT[:], gain_d[:], [cin], [gainT])
        MSET(onesf[:], 1.0, [onesf])
        MSET(mhalf[:], -0.5, [mhalf])
        RED(rmask32[:], identf[:].rearrange("p (j c) -> p j c", c=32), [identf], [rmask32])
        RED(rmask64[:], identf[:].rearrange("p (j c) -> p j c", c=64), [identf], [rmask64])
        MSET(segm[:], 1.0, [segm])
        MSET(segm[:].rearrange("p (c s) -> p c s", s=64)[:, :, 0:1], 0.0, [segm])
        for b in range(2):
            DMA("sp", xs_d.t[b, 0:LAT, :], x_d.t[b], [cin], xs_tok[b][0:16])
            DMA("sp", xs_d.t[b, LAT:T, :], ctx_d.t[b], [cin], xs_tok[b][16:18])

        with ExitStack() as ph:
            k.cur = ph
            cvt = k.sb([128, 8, 4], F32)
            th0 = k.sb([128, 8, 4], F32)
            lbl = k.sb([128, 2, 2, 4], F32)
            lbe = k.sb([128, 2, 2, 4], F32)
            lbs = k.sb([128, 2, 2], F32)
            lbv = k.sb([128, 2, 2, 4], F32)
            lbm = k.sb([128, 2, 2, 4], F32)
            DMA("sp", cvt[:], cv_d[:], [cin], [cvt])
            DMA("sp", lbl[:], lb_d[:], [cin], [lbl])
            ACT(th0[:], cvt[:], AF.Tanh, [cvt], [th0], scale=0.5)
            TS(th0[:], th0[:], 0.5, ALU.mult, [th0], [th0], s2=0.5, op1=ALU.add)
            TT(sc[:], th0[:], cvt[:], ALU.mult, [th0, cvt], [sc])
            CP(sc16[:], sc[:], [sc], [sc16])
            ACT(lbe[:], lbl[:], AF.Exp, [lbl], [lbe])
            RED(lbs[:], lbe[:], [lbe], [lbs])
            RECIP(lbs[:], lbs[:], [lbs], [lbs])
            TT(lbe[:], lbe[:], lbs[:].unsqueeze(3).broadcast_to([128, 2, 2, 4]), ALU.mult, [lbe, lbs], [lbe])
            MSET(lbv[:], 0.0, [lbv])
            for l in range(1, 4):
                TT(lbv[:, :, :, l:l + 1], lbv[:, :, :, l - 1:l], lbe[:, :, :, l:l + 1], ALU.add, [lbv, lbe], [lbv])
            TS(lb_a[:], lbv[:], 1e-20, ALU.max, [lbv], [lb_a])
            TS(lb_na[:], lbv[:], -1.0, ALU.mult, [lbv], [lb_na], s2=1.0, op1=ALU.add)
            TT(lb_c[:], lb_a[:], lb_na[:], ALU.add, [lb_a, lb_na], [lb_c])
            k.barrier()
        k.cur = st

        for l in range(nlayers):
            last = (l == NL - 1)
            lam_init = 0.8 - 0.6 * math.exp(-0.3 * l)
            nq = 16 if last else NT
            nck = 32 if last else NCH

            with ExitStack() as ph:
                k.cur = ph
                pm = k.ps([128, 24, 4], F32)
                pg = k.ps([128, 1024], F32)
                wm32 = [k.sb([128, 8, 512], BF16) for _ in range(2)]
                diag = [k.sb([128, 128], F32) for _ in range(2)]
                lt = k.sb([128, 2, 32], F32)
                le = k.sb([128, 2], F32)
                for ch in range(6):
                    wb_ = wm32[ch % 2]
                    DMA("pool", wb_[:], wmod_d.t[l, :, ch * 512:(ch + 1) * 512].rearrange("(kc p) n -> p kc n", p=128), [cin], [wb_])
                    for jj in range(4):
                        for kc in range(8):
                            MM(pm[:, ch * 4 + jj, :], wb_[:, kc, jj * 128:(jj + 1) * 128], sc16[:, kc, :], kc == 0, kc == 7, [wb_, sc16], [pm])
                TT(mod[:], pm[:], bmod[:, l, :].unsqueeze(2).broadcast_to([128, 24, 4]), ALU.add, [pm, bmod], [mod])
                TS(A1[:], mod[:, 8:16, :], 1.0, ALU.add, [mod], [A1])
                TT(A1[:], A1[:], gainT[:, l, :].unsqueeze(2).broadcast_to([128, 8, 4]), ALU.mult, [A1, gainT], [A1])
                for v in range(3):
                    for t in range(8):
                        dg = diag[t % 2]
                        TS(dg[:], identf[:], mod[:, 16 + t, v:v + 1], ALU.mult, [identf, mod], [dg])
                        MM(pg[:, t * 128:(t + 1) * 128], onesf[:], dg[:], True, True, [onesf, dg], [pg])
                    CP(gate_bc[v][:, 0:512], pg[:, 0:512], [pg], [gate_bc[v]])
                    CP(gate_bc[v][:, 512:1024], pg[:, 512:1024], [pg], [gate_bc[v]], eng="act")
                lv = sv[:, l, 64:192].rearrange("p (a b d) -> p a b d", a=2, b=2)
                TT(lt[:], lv[:, :, 0, :], lv[:, :, 1, :], ALU.mult, [sv], [lt])
                RED(le[:], lt[:], [lt], [le])
                ACT(le[:], le[:], AF.Exp, [le], [le])
                TT(neglam[:], le[:, 1:2], le[:, 0:1], ALU.subtract, [le], [neglam])
                TS(neglam[:], neglam[:], -lam_init, ALU.add, [neglam], [neglam])
                CP(gA[:], sv[:, l, 0:64].rearrange("p (a d) -> p a d", a=2), [sv], [gA])
                TS(gA[:, 0, :], gA[:, 0, :], 32.0 ** -0.5, ALU.mult, [gA], [gA])
                TS(gS[:], sv[:, l, 192:256], 1.0 - lam_init, ALU.mult, [sv], [gS])
                CP(gB[:], sv[:, l, 256:384].rearrange("p (a d) -> p a d", a=2), [sv], [gB])
                TS(gB[:, 0, :], gB[:, 0, :], 0.125, ALU.mult, [gB], [gB])
                k.barrier()
            k.cur = st

            for b in range(2):
                with ExitStack() as ph:
                    k.cur = ph
                    xt = [k.sb([128, D], F32) for _ in range(2)]
                    sq = k.sb([128, D], F32)
                    ssq = [k.sb([128, 1], F32) for _ in range(2)]
                    xn = [k.sb([128, D], BF16) for _ in range(2)]
                    tmp = [k.sb([128, 8, 128], F32) for _ in range(2)]
                    pT = [k.ps([128, 8, 128], BF16) for _ in range(2)]
                    for t in range(NT):
                        v = b if t < 16 else 2
                        x_, s_, n_, p_, m_ = xt[t % 2], ssq[t % 2], xn[t % 2], pT[t % 2], tmp[t % 2]
                        DMA("sp", x_[:], xs_d.t[b, t * 128:(t + 1) * 128, :], [xs_tok[b][t]], [x_])
                        ACT(sq[:], x_[:], AF.Square, [x_], [sq, s_], accum=s_[:])
                        RSQ(s_, s_[:], 128, 1, 1.0 / D)
                        TS(n_[:], x_[:], s_[:, 0:1], ALU.mult, [x_, s_], [n_])
                        for kc in range(8):
                            TR(p_[:, kc, :], n_[:, kc * 128:(kc + 1) * 128], identb[:], [n_, identb], [p_])
                        TT(m_[:], p_[:], A1[:, :, v:v + 1].broadcast_to([128, 8, 128]), ALU.mult, [p_, A1], [m_])
                        TT(hT[:, :, t * 128:(t + 1) * 128], m_[:], mod[:, 0:8, v:v + 1].broadcast_to([128, 8, 128]), ALU.add,
                           [m_, mod], [hT_tok[t]], eng="pool")
                    k.barrier()
                k.cur = st

                for br in range(2):
                    with ExitStack() as ph:
                        k.cur = ph
                        isA = (br == 0)
                        dh = 32 if isA else 64
                        ng = 512 // dh
                        nh = ng // 2
                        nmap = 8 if isA else 4
                        c0 = 0 if isA else 768
                        gq = gA if isA else gB
                        kT = k.sb([128, 2, T], BF16)
                        QP = [k.sb([128, T], BF16) for _ in range(nmap)]
                        mps = nmap // 2
                        rmask = rmask32 if isA else rmask64
                        vaug = k.sb([128, NT, 4, 65], BF16)
                        sqt = [k.sb([128, 512], F32) for _ in range(2)]
                        ssg = [k.sb([128, 16], F32) for _ in range(2)]
                        qkn = [k.sb([128, 512], F32) for _ in range(2)]
                        qkb = [k.sb([128, 512], BF16) for _ in range(2)]
                        ta2 = [k.sb([128, 256], F32) for _ in range(2)]
                        tb2 = [k.sb([128, 256], F32) for _ in range(2)]
                        tc2 = [k.sb([128, 256], F32) for _ in range(2)]
                        td2 = [k.sb([128, 256], F32) for _ in range(2)]
                        Pt = [k.sb([128, 512], BF16) for _ in range(3)]
                        Ot2 = [k.sb([128, 8, 65], F32) for _ in range(2)]
                        rsum = k.sb([128, 8], F32)
                        On = k.sb([128, 8, 64], F32)
                        ot = k.sb([128, 256], F32)
                        o2 = k.sb([128, 256], F32)
                        rs4 = k.sb([128, 4], F32)
                        tht = k.sb([128, 256], F32)
                        sgt4 = [k.sb([128, 256], F32) for _ in range(4)]
                        ygb4 = [k.sb([128, 256], BF16) for _ in range(4)]
                        ybt = [k.sb([128, 2, 128], BF16) for _ in range(4)]
                        if isA:
                            OaT = k.sb([66, 8, 512], F32)
                            MSET(OaT[:], 0.0, [OaT])
                        else:
                            nbt = k.sb([128, 100, 128], BF16)
                            DMA("pool", nbt[:], nb_d.t[l].rearrange("n k q -> k n q"), [cin], [nbt])
                        DMA("pool", WB0[:, :, 0:768], win_d.t[l, :, c0:c0 + 768].rearrange("(kc p) n -> p kc n", p=128), [cin], [WB0])
                        g0 = 2816 + 256 * br
                        wg = WB1[:, 0:2048].rearrange("p (kc n) -> p kc n", kc=8)
                        DMA("pool", wg, win_d.t[l, :, g0:g0 + 256].rearrange("(kc p) n -> p kc n", p=128), [cin], [WB1])
                        MSET(vaug[:, :, :, 64:65], 1.0, [vaug])
                        with ExitStack() as ph2:
                            k.cur = ph2
                            pz0 = [k.ps([128, 512], F32) for _ in range(2)]
                            pz1 = [k.ps([128, 512], F32) for _ in range(2)]
                            ptrp = [k.ps([128, 4, 128], BF16) for _ in range(2)]
                            pend = None
                            for t in range(NT):
                                ts_ = slice(t * 128, (t + 1) * 128)
                                z0, z1, pt_ = pz0[t % 2], pz1[t % 2], ptrp[t % 2]
                                sq_, sg_, qn_, qb = sqt[t % 2], ssg[t % 2], qkn[t % 2], qkb[t % 2]
                                for kc in range(8):
                                    MM(z0[:], hT[:, kc, ts_], WB0[:, kc, 0:512], kc == 0, kc == 7, [hT_tok[t], WB0], [z0])
                                for kc in range(8):
                                    MM(z1[:, 0:256], hT[:, kc, ts_], WB0[:, kc, 512:768], kc == 0, kc == 7, [hT_tok[t], WB0], [z1])
                                ACT(sq_[:], z0[:], AF.Square, [z0], [sq_])
                                RED(sg_[:, 0:ng], sq_[:].rearrange("p (g d) -> p g d", d=dh), [sq_], [sg_])
                                RSQ(sg_, sg_[:, 0:ng], 128, ng, 1.0 / dh)
                                TT(qn_[:].rearrange("p (g d) -> p g d", d=dh), z0[:].rearrange("p (g d) -> p g d", d=dh),
                                   sg_[:, 0:ng].unsqueeze(2).broadcast_to([128, ng, dh]), ALU.mult, [z0, sg_], [qn_])
                                if isA and t < 16:
                                    TT(qn_[:].rearrange("p (a h d) -> p a h d", a=2, d=dh), qn_[:].rearrange("p (a h d) -> p a h d", a=2, d=dh),
                                       gq[:].unsqueeze(2).broadcast_to([128, 2, nh, dh]), ALU.mult, [qn_, gq], [qn_], eng="pool")
                                    qv = qn_[:].rearrange("p (g two d) -> p g two d", two=2, d=16)
                                    ov = qb[:].rearrange("p (g two d) -> p g two d", two=2, d=16)
                                    cosb = rope[:, t, 0:16].unsqueeze(1).broadcast_to([128, 16, 16])
                                    sinb = rope[:, t, 16:32].unsqueeze(1).broadcast_to([128, 16, 16])
                                    t3 = lambda buf: buf[:].rearrange("p (g d) -> p g d", d=16)
                                    ta, tb, tc_, td = ta2[t % 2], tb2[t % 2], tc2[t % 2], td2[t % 2]
                                    TT(t3(ta), qv[:, :, 0, :], cosb, ALU.mult, [qn_, rope], [ta])
                                    TT(t3(tb), qv[:, :, 1, :], sinb, ALU.mult, [qn_, rope], [tb])
                                    TT(ov[:, :, 0, :], t3(ta), t3(tb), ALU.subtract, [ta, tb], [qb])
                                    TT(t3(tc_), qv[:, :, 0, :], sinb, ALU.mult, [qn_, rope], [tc_])
                                    TT(t3(td), qv[:, :, 1, :], cosb, ALU.mult, [qn_, rope], [td])
                                    TT(ov[:, :, 1, :], t3(tc_), t3(td), ALU.add, [tc_, td], [qb], eng="pool")
                                else:
                                    TT(qb[:].rearrange("p (a h d) -> p a h d", a=2, d=dh), qn_[:].rearrange("p (a h d) -> p a h d", a=2, d=dh),
                                       gq[:].unsqueeze(2).broadcast_to([128, 2, nh, dh]), ALU.mult, [qn_, gq], [qb], eng="pool")
                                CP(vaug[:, t, :, 0:64], z1[:, 0:256].rearrange("p (h d) -> p h d", d=64), [z1], [vaug], eng="act")

                                def trs(t=t, pt_=pt_, qb=qb, ts_=ts_):
                                    for i in range(4):
                                        TR(pt_[:, i, :], qb[:, i * 128:(i + 1) * 128], identb[:], [qb, identb], [pt_])
                                    CP(kT[:, :, ts_], pt_[:, 2:4, :], [pt_], [kT], eng="act")
                                    for m in range(nmap):
                                        ACT(QP[m][:, ts_], pt_[:, m // mps, :], AF.Copy, [pt_, rmask], [QP[m]],
                                            scale=rmask[:, (m % mps):(m % mps) + 1])
                                if not PIPE_PREP:
                                    trs()
                                    continue
                                if pend is not None:
                                    pend()
                                pend = trs
                            if PIPE_PREP:
                                pend()
                            k.barrier()
                        k.cur = ph
                        Sp = [k.ps([128, 512], F32) for _ in range(2)]
                        if isA:
                            accT = [k.ps([65, 512], F32) for _ in range(2)]
                            pOa = k.ps([128, 4, 66], F32)
                            pOb = k.ps([128, 4, 66], F32)
                        else:
                            acc = [k.ps([128, 4, 65], F32) for _ in range(2)]
                        pgz = k.ps([128, 256], F32)
                        ptr = k.ps([128, 4, 128], BF16)

                        pendq = []

                        def finish(qt, Ot):
                            qs_ = slice(qt * 128, (qt + 1) * 128)
                            sg_ = sgt4[qt % 4]
                            yg_ = ygb4[qt % 4]
                            for kc in range(8):
                                MM(pgz[:], hT[:, kc, qs_], wg[:, kc, :], kc == 0, kc == 7, [hT_tok[qt], WB1], [pgz])
                            ACT(tht[:], pgz[:], AF.Tanh, [pgz], [tht], scale=0.5)
                            STT(sg_[:], tht[:], 1.0, pgz[:], ALU.add, ALU.mult, [tht, pgz], [sg_])
                            RECIP(rsum[:, 0:nmap], Ot[:, 0:nmap, 64], [Ot], [rsum])
                            TT(On[:, 0:nmap, :], Ot[:, 0:nmap, 0:64], rsum[:, 0:nmap].unsqueeze(2).broadcast_to([128, nmap, 64]), ALU.mult,
                               [Ot, rsum], [On])
                            if isA:
                                Onv = On[:].rearrange("p (h two) d -> p h two d", two=2)
                                STT(ot[:].rearrange("p (h d) -> p h d", d=64), Onv[:, :, 1, :], neglam[:, 0:1], Onv[:, :, 0, :],
                                    ALU.mult, ALU.add, [On, neglam], [ot])
                                TT(o2[:], ot[:], ot[:], ALU.mult, [ot], [o2], eng="pool")
                                RED(rs4[:], o2[:].rearrange("p (h d) -> p h d", d=64), [o2], [rs4])
                                RSQ(rs4, rs4[:], 128, 4, 1.0 / 64)
                                TT(o2[:].rearrange("p (h d) -> p h d", d=64), ot[:].rearrange("p (h d) -> p h d", d=64),
                                   rs4[:].unsqueeze(2).broadcast_to([128, 4, 64]), ALU.mult, [ot, rs4], [o2])
                                TT(ot[:].rearrange("p (h d) -> p h d", d=64), o2[:].rearrange("p (h d) -> p h d", d=64),
                                   gS[:].unsqueeze(1).broadcast_to([128, 4, 64]), ALU.mult, [o2, gS], [ot], eng="pool")
                                ysrc = ot[:]
                                ybuf = ot
                            else:
                                ysrc = On[:, 0:4, :].rearrange("p h d -> p (h d)")
                                ybuf = On
                            STT(yg_[:], sg_[:], 0.5, ysrc, ALU.mult, ALU.mult, [sg_, ybuf], [yg_])
                            pendq.append(qt)

                        def flush(keep=0):
                            while len(pendq) > keep:
                                qt = pendq.pop(0)
                                qs_ = slice(qt * 128, (qt + 1) * 128)
                                yg_ = ygb4[qt % 4]
                                yb_ = ybt[qt % 4]
                                for i in range(2):
                                    TR(ptr[:, i, :], yg_[:, i * 128:(i + 1) * 128], identb[:], [yg_, identb], [ptr])
                                CP(yb_[:], ptr[:, 0:2, :], [ptr], [yb_])
                                DMA("sp", ybr_d.t[b, br, :, :, qs_].rearrange("c p n -> p c n"), yb_[:], [yb_], [ybr_tok[b][br]])

                        gi = 0
                        if isA:
                            qblocks = [(0, 512), (512, 512), (1024, 512), (1536, 512)] + ([] if last else [(2048, 256)])
                            for (q0, qn) in qblocks:
                                chunks = list(range(NT)) if q0 < LAT else [16, 17]
                                nci = len(chunks)
                                its = [(m, ci, c) for m in range(8) for ci, c in enumerate(chunks)]

                                def qk(i):
                                    m, ci, c = its[i]
                                    slot, pb = m // 4, 32 * (m % 4)
                                    S_ = Sp[(gi + i) % 2]
                                    P_ = Pt[(gi + i) % 3]
                                    MM(S_[:, 0:qn], kT[:, slot, c * 128:(c + 1) * 128], QP[m][:, q0:q0 + qn], True, True, [kT, QP[m]], [S_])
                                    ACT(P_[:, 0:qn], S_[:, 0:qn], AF.Exp, [S_], [P_])

                                def pv(i):
                                    m, ci, c = its[i]
                                    a_ = accT[m % 2]
                                    P_ = Pt[(gi + i) % 3]
                                    MM(a_[:, 0:qn], vaug[:, c, m // 2, :], P_[:, 0:qn], ci == 0, ci == nci - 1, [vaug, P_], [a_])
                                    if ci == nci - 1:
                                        CP(OaT[0:65, m, 0:qn], a_[:, 0:qn], [a_], [OaT])

                                if PIPE_A:
                                    qk(0)
                                for i in range(len(its)):
                                    if PIPE_A:
                                        if i + 1 < len(its):
                                            qk(i + 1)
                                    else:
                                        qk(i)
                                    pv(i)
                                    if i == 8:
                                        flush()
                                gi += len(its)
                                for j in range(qn // 128):
                                    qt = q0 // 128 + j
                                    Ot = Ot2[qt % 2]
                                    for m in range(8):
                                        pO_ = pOa if m < 4 else pOb
                                        TR(pO_[:, m % 4, :], OaT[:, m, j * 128:(j + 1) * 128], identf[0:66, 0:66], [OaT, identf], [pO_])
                                    CP(Ot[:, 0:4, :], pOa[:, :, 0:65], [pOa], [Ot])
                                    CP(Ot[:, 4:8, :], pOb[:, :, 0:65], [pOb], [Ot])
                                    finish(qt, Ot)
                            flush()
                        else:
                            work = []
                            for qt in range(nq):
                                if qt >= 16:
                                    chunks = [(16, None), (17, None)]
                                else:
                                    cs = min(max(qt - 2, 0), 11)
                                    typ = 0 if qt == 0 else 1 if qt == 1 else 3 if qt == 14 else 4 if qt == 15 else 2
                                    chunks = [(cs + i, typ * 5 + i) for i in range(5)] + [(16, None), (17, None)]
                                groups = [chunks[i:i + 4] for i in range(0, len(chunks), 4)]
                                for m in range(4):
                                    for gidx, g in enumerate(groups):
                                        work.append((qt, m, g, gidx == 0, gidx == len(groups) - 1))

                            def scb(i):
                                qt, m, g, fg, lg = work[i]
                                slot, pb = m // 2, 64 * (m % 2)
                                Sb = Sp[i % 2]
                                Pb = Pt[i % 3]
                                S_ = Sb[:].rearrange("p (c q) -> p c q", q=128)
                                P_ = Pb[:].rearrange("p (c q) -> p c q", q=128)
                                for j, (c, bi) in enumerate(g):
                                    kT_ = kT[:, slot, c * 128:(c + 1) * 128]
                                    qT_ = QP[m][:, qt * 128:(qt + 1) * 128]
                                    if bi is not None:
                                        MM(S_[:, j, :], identb[:], nbt[:, bi * 4 + m, :], True, False, [identb, nbt], [Sb])
                                        MM(S_[:, j, :], kT_, qT_, False, True, [kT, QP[m]], [Sb])
                                    else:
                                        MM(S_[:, j, :], kT_, qT_, True, True, [kT, QP[m]], [Sb])
                                ACT(P_[:, 0:len(g), :], S_[:, 0:len(g), :], AF.Exp, [Sb], [Pb])

                            def pvb(i):
                                qt, m, g, fg, lg = work[i]
                                Pb = Pt[i % 3]
                                P_ = Pb[:].rearrange("p (c q) -> p c q", q=128)
                                ab = acc[qt % 2]
                                for j, (c, bi) in enumerate(g):
                                    MM(ab[:, m, :], P_[:, j, :], vaug[:, c, m, :], fg and j == 0, lg and j == len(g) - 1, [Pb, vaug], [ab])
                                if lg and m == 3:
                                    Ot = Ot2[qt % 2]
                                    CP(Ot[:, 0:4, :], ab[:], [ab], [Ot])
                                    finish(qt, Ot)
                                if lg and m == 1:
                                    flush()

                            if PIPE_B:
                                scb(0)
                            for i in range(len(work)):
                                if PIPE_B:
                                    if i + 1 < len(work):
                                        scb(i + 1)
                                else:
                                    scb(i)
                                pvb(i)
                            flush()
                        k.barrier()
                    k.cur = st

                DMA("pool", WB0[:], win_d.t[l, :, 1536:2560].rearrange("(kc p) n -> p kc n", p=128), [cin], [WB0])
                wg = WB1[:, 0:2048].rearrange("p (kc n) -> p kc n", kc=8)
                DMA("pool", wg, win_d.t[l, :, 3328:3584].rearrange("(kc p) n -> p kc n", p=128), [cin], [WB1])
                for ct in range(2):
                    with ExitStack() as ph:
                        k.cur = ph
                        KT = [k.sb([128, T], BF16) for _ in range(2)]
                        QBD = [k.sb([128, NCH, 128], BF16) for _ in range(2)]
                        Ktm2 = [k.sb([64, NCH, 128], BF16) for _ in range(2)]
                        Vc = k.sb([64, NCH, 128], BF16)
                        Spr = [k.sb([128, NCH, 64], BF16) for _ in range(2)]
                        er = [k.sb([128, NCH], F32) for _ in range(2)]
                        Sst2 = [k.sb([128, 64], F32) for _ in range(2)]
                        Stm2 = [k.sb([128, 64], F32) for _ in range(2)]
                        for d in range(2):
                            MSET(QBD[d][:], 0.0, [QBD[d]])
                        with ExitStack() as ph2:
                            k.cur = ph2
                            qh = k.sb([128, T], F32)
                            thq = [k.sb([128, 512], F32) for _ in range(2)]
                            NS = 2
                            tht = [k.sb([128, 512], F32) for _ in range(NS)]
                            gg = [k.sb([128, 512], F32) for _ in range(NS)]
                            kk = [k.sb([128, 512], F32) for _ in range(NS)]
                            uu = [k.sb([128, 512], F32) for _ in range(NS)]
                            Bc = [k.sb([128, 512], F32) for _ in range(NS)]
                            tmpb = [k.sb([128, 512], F32) for _ in range(NS)]
                            tot = [k.sb([128, 8], F32) for _ in range(NS)]
                            E1 = [k.sb([128, 512], F32) for _ in range(NS)]
                            E2 = [k.sb([128, 512], F32) for _ in range(NS)]
                            pq = [k.ps([128, 512], F32) for _ in range(4)]
                            pv = [k.ps([64, 4, 128], F32) for _ in range(2)]
                            for c4 in range(0, NCH, 4):
                                pv_ = pv[(c4 // 4) % 2]
                                for i in range(4):
                                    c = c4 + i
                                    for kc in range(8):
                                        MM(pv_[:, i, :], hT[:, kc, c * 64:(c + 1) * 64], WB0[:, kc, 768 + ct * 128:768 + (ct + 1) * 128],
                                           kc == 0, kc == 7, [hT_tok[c // 2], WB0], [pv_])
                                CP(Vc[:, c4:c4 + 4, :], pv_[:], [pv_], [Vc], eng="act")
                            pi = 0
                            for bi_, (t0, n) in enumerate(BLKS):
                                p_ = pq[pi % 4]
                                pi += 1
                                tq = thq[bi_ % 2]
                                for kc in range(8):
                                    MM(p_[:, 0:n], WB0[:, kc, ct * 128:(ct + 1) * 128], hT[:, kc, t0:t0 + n], kc == 0, kc == 7, hts(t0, n) + [WB0], [p_])
                                ACT(tq[:, 0:n], p_[:, 0:n], AF.Tanh, [p_], [tq], scale=0.5)
                                STT(qh[:, t0:t0 + n], tq[:, 0:n], 1.0, p_[:, 0:n], ALU.add, ALU.mult, [tq, p_], [qh])
                            it = 0
                            for (t0, n) in BLKS:
                                nc_ = n // 64
                                cb = t0 // 64
                                bs_ = slice(t0, t0 + n)
                                for d in range(2):
                                    j = it % NS
                                    it += 1
                                    p_ = pq[pi % 4]
                                    pi += 1
                                    cc = 256 + d * 256 + ct * 128
                                    for kc in range(8):
                                        MM(p_[:, 0:n], WB0[:, kc, cc:cc + 128], hT[:, kc, bs_], kc == 0, kc == 7, hts(t0, n) + [WB0], [p_])
                                    ACT(uu[j][:, 0:n], p_[:, 0:n], AF.Exp, [p_], [uu[j]], scale=-1.0)
                                    ACT(E2[j][:, 0:n], uu[j][:, 0:n], AF.Ln, [uu[j]], [E2[j]], bias=1.0)
                                    ACT(E1[j][:, 0:n], uu[j][:, 0:n], AF.Ln, [uu[j], lb_a, lb_c], [E1[j]], scale=lb_a[:, d, ct, l:l + 1],
                                        bias=lb_c[:, d, ct, l:l + 1])
                                    TT(gg[j][:, 0:n], E1[j][:, 0:n], E2[j][:, 0:n], ALU.subtract, [E1[j], E2[j]], [gg[j]], eng="pool")
                                    TT(tht[j][:, 0:n], p_[:, 0:n], E2[j][:, 0:n], ALU.add, [p_, E2[j]], [tht[j]])
                                    ACT(kk[j][:, 0:n], tht[j][:, 0:n], AF.Exp, [tht[j]], [kk[j]], scale=-1.0)
                                    k.op("dve", [segm, gg[j]], [Bc[j]], lambda e, n=n, j=j: e.tensor_tensor_scan(
                                        out=Bc[j][:, 0:n], data0=segm[:, 0:n], data1=gg[j][:, 0:n], initial=0.0, op0=ALU.mult, op1=ALU.add))
                                    B3 = Bc[j][:, 0:n].rearrange("p (c s) -> p c s", s=64)
                                    CP(tot[j][:, 0:nc_], B3[:, :, 63], [Bc[j]], [tot[j]])
                                    if d == 1:
                                        TT(tmpb[j][:, 0:n], gg[j][:, 0:n], Bc[j][:, 0:n], ALU.subtract, [gg[j], Bc[j]], [tmpb[j]])
                                        TT(B3, tmpb[j][:, 0:n].rearrange("p (c s) -> p c s", s=64),
                                           tot[j][:, 0:nc_].unsqueeze(2).broadcast_to([128, nc_, 64]), ALU.add, [tmpb[j], tot[j]], [Bc[j]])
                                    ACT(er[d][:, cb:cb + nc_], tot[j][:, 0:nc_], AF.Exp, [tot[j]], [er[d]], scale=0.5)
                                    STT(tmpb[j][:, 0:n].rearrange("p (c s) -> p c s", s=64), tot[j][:, 0:nc_].unsqueeze(2).broadcast_to([128, nc_, 64]),
                                        -0.5, B3, ALU.mult, ALU.add, [tot[j], Bc[j]], [tmpb[j]])
                                    TS(tmpb[j][:, 0:n], tmpb[j][:, 0:n], 43.0, ALU.min, [tmpb[j]], [tmpb[j]], s2=-43.0, op1=ALU.max, eng="pool")
                                    ACT(E1[j][:, 0:n], tmpb[j][:, 0:n], AF.Exp, [tmpb[j]], [E1[j]])
                                    ACT(E2[j][:, 0:n], tmpb[j][:, 0:n], AF.Exp, [tmpb[j]], [E2[j]], scale=-1.0)
                                    for hh in range(2):
                                        ps_ = slice(64 * hh, 64 * hh + 64)
                                        STT(QBD[d][ps_, cb:cb + nc_, 64 * hh:64 * hh + 64], qh[ps_, bs_].rearrange("p (c s) -> p c s", s=64), 0.5,
                                            E1[j][ps_, 0:n].rearrange("p (c s) -> p c s", s=64), ALU.mult, ALU.mult, [qh, E1[j]], [QBD[d]])
                                    STT(KT[d][:, bs_], kk[j][:, 0:n], lb_na[:, d, ct, l:l + 1], E2[j][:, 0:n], ALU.mult, ALU.mult,
                                        [kk[j], E2[j], lb_na], [KT[d]])
                            k.barrier()
                        k.cur = ph
                        Am = [k.sb([64, 2, 128], BF16) for _ in range(2)]
                        of_ = k.sb([64, 512], F32)
                        o2 = k.sb([64, 512], F32)
                        rs8 = k.sb([64, 8], F32)
                        th2 = k.sb([64, 512], F32)
                        sg2 = k.sb([64, 512], F32)
                        yg2 = [k.sb([64, 512], BF16) for _ in range(2)]
                        ybt = [k.sb([128, 256], BF16) for _ in range(2)]
                        pbk = k.ps([128, 1024], BF16)
                        ptk = pbk[0:64, :].rearrange("p (a b) -> p a b", b=128)
                        pTy = pbk[:, 0:256]
                        pu = [k.ps([128, 128], F32) for _ in range(2)]
                        pa = [k.ps([64, 2, 128], F32) for _ in range(2)]
                        po = k.ps([64, 4, 128], F32)
                        pgz = k.ps([64, 4, 128], F32)
                        for d in range(2):
                            for c8 in range(0, NCH, 4):
                                for i in range(4):
                                    c = c8 + i
                                    TR(ptk[:, i, :], KT[d][:, c * 64:(c + 1) * 64], identb[:], [KT[d], identb], [pbk])
                                CP(Ktm2[d][:, c8:c8 + 4, :], ptk[:, 0:4, :], [pbk], [Ktm2[d]], eng="act")
                            MSET(Sst2[d][:], 0.0, [Sst2[d]], eng="dve")
                        orders = [[32, 33, 34, 35] + list(range(32)), [35, 34, 33, 32] + list(range(31, -1, -1))]
                        for oi in range(NCH):
                            for d in range(2):
                                c = orders[d][oi]
                                pu_ = pu[d]
                                Sst, Stm = Sst2[d], Stm2[d]
                                MM(pu_[:], Ktm2[d][:, c, :], Vc[:, c, :], True, True, [Ktm2[d], Vc], [pu_])
                                e_ = er[d][:, c:c + 1]
                                ACT(Spr[d][:, c, :], Sst[:], AF.Copy, [Sst, er[d]], [Spr[d]], scale=e_)
                                for hh in range(2):
                                    ps_ = slice(64 * hh, 64 * hh + 64)
                                    STT(Stm[ps_, :], Sst[ps_, :], er[d][ps_, c:c + 1], pu_[ps_, 64 * hh:64 * hh + 64], ALU.mult, ALU.add,
                                        [Sst, er[d], pu_], [Stm])
                                TS(Sst[:], Stm[:], e_, ALU.mult, [Stm, er[d]], [Sst])
                        cm4 = cmask[:].rearrange("p (d h t) -> p d h t", d=2, h=2)

                        def scm(c):
                            pa_ = pa[c % 2]
                            am_ = Am[c % 2]
                            for d in range(2):
                                MM(pa_[:, d, :], KT[d][:, c * 64:(c + 1) * 64], QBD[d][:, c, :], True, True, [KT[d], QBD[d]], [pa_])
                            TT(am_[:].rearrange("p d (h t) -> p d h t", h=2), pa_[:].rearrange("p d (h t) -> p d h t", h=2), cm4, ALU.mult,
                               [pa_, cmask], [am_])

                        ptail = []

                        def tail():
                            while ptail:
                                c4 = ptail.pop(0)
                                yg_ = yg2[(c4 // 4) % 2]
                                yb_ = ybt[(c4 // 4) % 2]
                                for i in range(4):
                                    TR(pTy[:, i * 64:(i + 1) * 64], yg_[:, i * 128:(i + 1) * 128], identb[0:64, 0:64], [yg_, identb], [pbk])
                                CP(yb_[:], pTy, [pbk], [yb_], eng="act")
                                DMA("sp", ybr_d.t[b, 2, ct, :, c4 * 64:c4 * 64 + 256], yb_[:], [yb_], [ybr_tok[b][2]])

                        scm(0)
                        for c4 in range(0, nck, 4):
                            for i in range(4):
                                c = c4 + i
                                if c + 1 < nck:
                                    scm(c + 1)
                                for kc in range(8):
                                    MM(pgz[:, i, :], hT[:, kc, c * 64:(c + 1) * 64], wg[:, kc, ct * 128:(ct + 1) * 128], kc == 0, kc == 7,
                                       [hT_tok[c // 2], WB1], [pgz])
                                am_ = Am[c % 2]
                                for hh in range(2):
                                    hs_ = slice(64 * hh, 64 * hh + 64)
                                    for d in range(2):
                                        MM(po[:, i, hs_], am_[:, d, hs_], Vc[:, c, hs_], d == 0, False, [am_, Vc], [po])
                                        MM(po[:, i, hs_], QBD[d][:, c, hs_], Spr[d][:, c, :], False, d == 1, [QBD[d], Spr[d]], [po])
                            tail()
                            pof = po[:].rearrange("p c n -> p (c n)")
                            pgf = pgz[:].rearrange("p c n -> p (c n)")
                            yg_ = yg2[(c4 // 4) % 2]
                            ACT(th2[:], pgf, AF.Tanh, [pgz], [th2], scale=0.5)
                            STT(sg2[:], th2[:], 1.0, pgf, ALU.add, ALU.mult, [th2, pgz], [sg2])
                            CP(of_[:], pof, [po], [of_], eng="act")
                            ACT(o2[:], pof, AF.Square, [po], [o2])
                            RED(rs8[:], o2[:].rearrange("p (g d) -> p g d", d=64), [o2], [rs8])
                            RSQ(rs8, rs8[:], 64, 8, 1.0 / 64)
                            TT(o2[:].rearrange("p (g d) -> p g d", d=64), of_[:].rearrange("p (g d) -> p g d", d=64),
                               rs8[:].unsqueeze(2).broadcast_to([64, 8, 64]), ALU.mult, [of_, rs8], [o2])
                            TT(of_[:].rearrange("p (g d) -> p g d", d=64), o2[:].rearrange("p (g d) -> p g d", d=64),
                               sv[0:64, l, 384:448].unsqueeze(1).broadcast_to([64, 8, 64]), ALU.mult, [o2, sv], [of_], eng="pool")
                            STT(yg_[:], sg2[:], 0.5, of_[:], ALU.mult, ALU.mult, [sg2, of_], [yg_])
                            ptail.append(c4)
                        tail()
                        k.barrier()
                    k.cur = st

                with ExitStack() as ph:
                    k.cur = ph
                    uT = k.sb([128, 2, T], BF16)
                    PQ = k.sb([128, NT, 2, 256], BF16)
                    tab = [k.sb([128, 16, 2, 256], BF16) for _ in range(2)]
                    tht = k.sb([128, 256], F32)
                    sgt = k.sb([128, 256], F32)
                    ybt = [k.sb([128, 2, 256], BF16) for _ in range(2)]
                    pq = [k.ps([128, 512], F32) for _ in range(2)]
                    pp = [k.ps([128, 2, 256], F32) for _ in range(2)]
                    py = [k.ps([128, 256], F32) for _ in range(2)]
                    pgz = [k.ps([128, 256], F32) for _ in range(2)]
                    DMA("pool", WB0[:, :, 0:256], win_d.t[l, :, 2560:2816].rearrange("(kc p) n -> p kc n", p=128), [cin], [WB0])
                    wg = WB1[:, 0:2048].rearrange("p (kc n) -> p kc n", kc=8)
                    DMA("pool", wg, win_d.t[l, :, 3584:3840].rearrange("(kc p) n -> p kc n", p=128), [cin], [WB1])
                    pi = 0
                    for ct in range(2):
                        for (t0, n) in BLKS:
                            p_ = pq[pi % 2]
                            pi += 1
                            for kc in range(8):
                                MM(p_[:, 0:n], WB0[:, kc, ct * 128:(ct + 1) * 128], hT[:, kc, t0:t0 + n], kc == 0, kc == 7, hts(t0, n) + [WB0], [p_])
                            CP(uT[:, ct, t0:t0 + n], p_[:, 0:n], [p_], [uT], eng="act")
                    for t in range(NT):
                        p_ = pp[t % 2]
                        for ct in range(2):
                            MM(p_[:, ct, :], uT[:, ct, t * 128:(t + 1) * 128], cs64[:], True, True, [uT, cs64], [p_])
                        CP(PQ[:, t, :, :], p_[:], [p_], [PQ], eng=("act" if t % 2 else "dve"))
                    it = 0
                    nblk = 8 if last else 9
                    for nb in range(nblk):
                        if nb < 8:
                            tb_ = tab[nb % 2]
                            DMA("sp", tb_[:], dftl_d.t[:, :, :, nb * 256:(nb + 1) * 256], [cin], [tb_])
                            tcs = list(range(16))
                            tsrc = lambda tc, cs: tb_[:, tc, cs, :]
                            tread = [tb_]
                            t0 = nb * 256
                        else:
                            tcs = [16, 17]
                            tsrc = lambda tc, cs: dftc[:, tc - 16, cs, :]
                            tread = [dftc]
                            t0 = LAT
                        yb_ = ybt[nb % 2]
                        for ct in range(2):
                            y_ = py[it % 2]
                            g_ = pgz[it % 2]
                            it += 1
                            nmm = len(tcs) * 2
                            j = 0
                            for tc in tcs:
                                for cs in range(2):
                                    MM(y_[:], PQ[:, tc, ct, cs * 128:(cs + 1) * 128], tsrc(tc, cs), j == 0, j == nmm - 1, [PQ] + tread, [y_])
                                    j += 1
                            for kc in range(8):
                                MM(g_[:], wg[:, kc, ct * 128:(ct + 1) * 128], hT[:, kc, t0:t0 + 256], kc == 0, kc == 7, hts(t0, 256) + [WB1], [g_])
                            ACT(tht[:], g_[:], AF.Tanh, [g_], [tht], scale=0.5)
                            STT(sgt[:], tht[:], 1.0, g_[:], ALU.add, ALU.mult, [tht, g_], [sgt])
                            STT(yb_[:, ct, :], sgt[:], 0.5, y_[:], ALU.mult, ALU.mult, [sgt, y_], [yb_])
                        DMA("sp", ybr_d.t[b, 3, :, :, t0:t0 + 256].rearrange("c p n -> p c n"), yb_[:], [yb_], [ybr_tok[b][3]])
                    k.barrier()
                k.cur = st

                with ExitStack() as ph:
                    k.cur = ph
                    ybS = k.sb([128, 4, 2, T], BF16)
                    accT = k.sb([128, 8, T], BF16)
                    wu = [k.sb([128, 4, 2, 128], BF16) for _ in range(2)]
                    wm_tok = [Buf(), Buf()]
                    tht = [k.sb([128, 512], F32) for _ in range(2)]
                    tmpm = [k.sb([128, 512], F32) for _ in range(2)]
                    accf = [k.sb([128, 512], F32) for _ in range(2)]
                    xt = [k.sb([128, D], F32) for _ in range(2)]
                    xo = [k.sb([128, D], F32) for _ in range(2)]
                    tmo = [k.sb([128, 512], F32) for _ in range(2)]
                    pM = [k.ps([128, 512], F32) for _ in range(2)]
                    pU = [k.ps([128, 512], F32) for _ in range(2)]
                    pO = [k.ps([128, 512], F32) for _ in range(2)]
                    for i in range(4):
                        DMA("sp", ybS[:, i, :, :], ybr_d.t[b, i].rearrange("c p n -> p c n"), [ybr_tok[b][i]], [ybS])
                    DMA("pool", WB0[:], wout_d.t[l].rearrange("(kc p) n -> p kc n", p=128), [cin], [WB0])
                    mblks = BLKS[:4] if last else BLKS
                    im = 0
                    def wload(ft):
                        fs_ = slice(ft * 128, (ft + 1) * 128)
                        wmv = WB1[:, (ft % 2) * 4096:(ft % 2 + 1) * 4096].rearrange("p (i kc n) -> p i kc n", i=4, kc=8)
                        DMA("pool", wmv, wmg_d.t[l, :, :, fs_].rearrange("i (kc p) n -> p i kc n", p=128), [cin], [wm_tok[ft % 2]])
                        DMA("pool", wu[ft % 2][:], wup_d.t[l, :, :, fs_].rearrange("i (fc p) n -> p i fc n", p=128), [cin], [wu[ft % 2]])

                    if PREFETCH_M:
                        wload(0)
                    for ft in range(8):
                        fs_ = slice(ft * 128, (ft + 1) * 128)
                        wmv = WB1[:, (ft % 2) * 4096:(ft % 2 + 1) * 4096].rearrange("p (i kc n) -> p i kc n", i=4, kc=8)
                        wmt = wm_tok[ft % 2]
                        wu_ = wu[ft % 2]
                        if not PREFETCH_M:
                            wload(ft)
                        elif ft + 1 < 8:
                            wload(ft + 1)
                        for bi_, (t0, n) in enumerate(mblks):
                            af = accf[bi_ % 2]
                            for i in range(4):
                                m_ = pM[im % 2]
                                u_ = pU[im % 2]
                                th_ = tht[im % 2]
                                tm_ = tmpm[im % 2]
                                im += 1
                                for kc in range(8):
                                    MM(m_[:, 0:n], wmv[:, i, kc, :], hT[:, kc, t0:t0 + n], kc == 0, kc == 7, hts(t0, n) + [wmt], [m_])
                                for fc in range(2):
                                    MM(u_[:, 0:n], wu_[:, i, fc, :], ybS[:, i, fc, t0:t0 + n], fc == 0, fc == 1, [wu_, ybS], [u_])
                                ACT(th_[:, 0:n], m_[:, 0:n], AF.Tanh, [m_], [th_], scale=0.5)
                                if i == 0:
                                    STT(af[:, 0:n], th_[:, 0:n], 1.0, u_[:, 0:n], ALU.add, ALU.mult, [th_, u_], [af])
                                else:
                                    STT(tm_[:, 0:n], th_[:, 0:n], 1.0, u_[:, 0:n], ALU.add, ALU.mult, [th_, u_], [tm_])
                                    TT(af[:, 0:n], af[:, 0:n], tm_[:, 0:n], ALU.add, [af, tm_], [af], eng=("dve" if i == 2 else "pool"))
                            ACT(accT[:, ft, t0:t0 + n], af[:, 0:n], AF.Copy, [af], [accT], scale=0.5)
                    ntile = 16 if last else NT
                    io = 0
                    for t in range(ntile):
                        v = b if t < 16 else 2
                        x_ = xt[t % 2]
                        o_ = xo[t % 2]
                        DMA("sp", x_[:], xs_d.t[b, t * 128:(t + 1) * 128, :], [xs_tok[b][t]], [x_])
                        for hf in range(2):
                            p_ = pO[io % 2]
                            tm_ = tmo[io % 2]
                            io += 1
                            cs_ = slice(hf * 512, (hf + 1) * 512)
                            for kc in range(8):
                                MM(p_[:], accT[:, kc, t * 128:(t + 1) * 128], WB0[:, kc, cs_], kc == 0, kc == 7, [accT, WB0], [p_])
                            TT(tm_[:], p_[:], gate_bc[v][:, cs_], ALU.mult, [p_, gate_bc[v]], [tm_])
                            TT(o_[:, cs_], tm_[:], x_[:, cs_], ALU.add, [tm_, x_], [o_], eng="pool")
                        if last:
                            DMA("sp", out_d.t[b, t * 128:(t + 1) * 128, :], o_[:], [o_], [xs_tok[b][t]])
                        else:
                            DMA("sp", xs_d.t[b, t * 128:(t + 1) * 128, :], o_[:], [o_], [xs_tok[b][t]])
                    k.barrier()
                k.cur = st
        k.emit()
    return nc


def _consts():
    bf = ml_dtypes.bfloat16
    c = {}
    c["identb"] = np.eye(128, dtype=np.float32).astype(bf)
    c["identf"] = np.eye(128, dtype=np.float32)
    s = np.arange(64)[:, None]
    t = np.arange(64)[None, :]
    mf = (s <= t).astype(np.float32)
    mb = (s >= t).astype(np.float32)
    cm = np.stack([np.stack([mf, mf], 0), np.stack([mb, mb], 0)], 0)
    c["cmask"] = np.ascontiguousarray(cm.transpose(2, 0, 1, 3).reshape(64, 256))
    n = np.arange(LAT)
    row = (n // 64).astype(np.float32)
    col = (n % 64).astype(np.float32)
    inv = (10000.0 ** (-np.arange(0, 16, 2, dtype=np.float32) / 16)).astype(np.float32)
    ang = np.concatenate([row[:, None] * inv, col[:, None] * inv], -1).astype(np.float32)
    rp = np.concatenate([np.cos(ang), np.sin(ang)], -1).astype(np.float32)
    c["rope"] = np.ascontiguousarray(rp.reshape(16, 128, 32).transpose(1, 0, 2))
    e = np.arange(64)
    a64 = 2 * np.pi * np.outer(e, e) / 64
    C64 = np.cos(a64) / 8.0
    S64 = np.sin(a64) / 8.0
    cs = np.zeros((128, 256), np.float64)
    for g in range(2):
        cs[g * 64:(g + 1) * 64, g * 64:(g + 1) * 64] = C64
        cs[g * 64:(g + 1) * 64, 128 + g * 64:128 + (g + 1) * 64] = S64
    c["cs64"] = cs.astype(np.float32).astype(bf)

    def dft(N):
        tt = np.arange(N)
        a = 2 * np.pi * ((np.outer(tt, tt)) % N) / N
        Cn = np.cos(a) / np.sqrt(N)
        Sn = -np.sin(a) / np.sqrt(N)
        tab = np.stack([Cn, Sn], 1)
        return np.ascontiguousarray(tab.reshape(N // 128, 128, 2, N).transpose(1, 0, 2, 3)).astype(np.float32).astype(bf)

    c["dftL"] = dft(LAT)
    c["dftC"] = dft(CTX)
    return c


def _nbias_index():
    rows, W, wh, ww = 32, 64, 8, 16
    idx = np.full((5, 5, 128, 128, 2), -1, np.int64)
    for typ, qt in enumerate([0, 1, 5, 14, 15]):
        cs = min(max(qt - 2, 0), 11)
        for i in range(5):
            kc = cs + i
            for kk in range(128):
                kr, kcol = 2 * kc + kk // 64, kk % 64
                for q in range(128):
                    r, cq = 2 * qt + q // 64, q % 64
                    r0 = min(max(r - wh // 2, 0), rows - wh)
                    c0 = min(max(cq - ww // 2, 0), W - ww)
                    if r0 <= kr < r0 + wh and c0 <= kcol < c0 + ww:
                        idx[typ, i, kk, q, 0] = kr - r + wh - 1
                        idx[typ, i, kk, q, 1] = min(max(kcol - cq, 1 - ww), ww - 1) + ww - 1
    return idx


_CACHE = {}


def kernel(x, c, ctx, c_ctx, norm_gain, w_mod, b_mod, w_in, da_qk_gain, da_lambda, da_subln_gain,
           na_qk_gain, na_rpb, hg_lb_logits, hg_norm_gain, w_up, w_merge, w_out):
    f = lambda a: np.ascontiguousarray(np.asarray(a, dtype=np.float32))
    x, c, ctx, c_ctx = f(x), f(c), f(ctx), f(c_ctx)
    if "nc" not in _CACHE:
        _CACHE["nc"] = build()
        _CACHE["consts"] = _consts()
        _CACHE["nbidx"] = _nbias_index()
    nc = _CACHE["nc"]
    shared = dict(_CACHE["consts"])
    shared["w_mod"] = f(w_mod)
    shared["w_in"] = f(w_in)
    shared["w_up"] = f(w_up)
    shared["w_merge"] = f(w_merge)
    shared["w_out"] = f(w_out)
    shared["b_modT"] = np.ascontiguousarray(f(b_mod).reshape(NL, 24, 128).transpose(2, 0, 1))
    shared["gainT"] = np.ascontiguousarray(f(norm_gain).reshape(NL, 8, 128).transpose(2, 0, 1))
    smallv = np.concatenate([f(da_qk_gain).reshape(NL, 64), f(da_lambda).reshape(NL, 128), f(da_subln_gain).reshape(NL, 64),
                             f(na_qk_gain).reshape(NL, 128), f(hg_norm_gain).reshape(NL, 64)], -1)
    shared["smallv"] = np.ascontiguousarray(np.broadcast_to(smallv[None], (128, NL, 448)))
    shared["lbT"] = np.ascontiguousarray(f(hg_lb_logits).reshape(2, NL, 2, 128).transpose(3, 0, 2, 1))
    idx = _CACHE["nbidx"]
    rpb = f(na_rpb)
    inw = idx[..., 0] >= 0
    dr = np.where(inw, idx[..., 0], 0)
    dc = np.where(inw, idx[..., 1], 0)
    gath = rpb[:, :, dr, dc]
    gath = np.where(inw[None, None], gath, np.float32(NEG)).astype(np.float32)
    shared["nbias"] = np.ascontiguousarray(gath.transpose(0, 2, 3, 1, 4, 5).reshape(NL, 100, 128, 128))
    in_maps = []
    for i in range(NCORE):
        m = dict(shared)
        m["x"] = np.ascontiguousarray(x[2 * i:2 * i + 2])
        m["ctx"] = np.ascontiguousarray(ctx[2 * i:2 * i + 2])
        cvec = np.stack([c[2 * i], c[2 * i + 1], c_ctx, np.zeros_like(c_ctx)], 0)
        m["cv"] = np.ascontiguousarray(cvec.reshape(4, 8, 128).transpose(2, 1, 0))
        in_maps.append(m)
    res = run_bass_kernel_spmd(nc, in_maps, core_ids=list(range(NCORE)))
    out = np.concatenate([np.asarray(r["out"], dtype=np.float32) for r in res.results], axis=0)
    return out
```

```python
import math
from contextlib import ExitStack

import numpy as np
import ml_dtypes

import concourse.bass as bass
import concourse.mybir as mybir
from concourse.bass_utils import run_bass_kernel_spmd

F32 = mybir.dt.float32
BF16 = mybir.dt.bfloat16
ALU = mybir.AluOpType
AF = mybir.ActivationFunctionType
AX = mybir.AxisListType

NL = 4
D = 1024
NCORE = 8
LAT = 2048
CTX = 256
T = LAT + CTX
NT = T // 128
NCH = T // 64
EPS = 1e-6
BLKS = [(0, 512), (512, 512), (1024, 512), (1536, 512), (2048, 256)]
NEG = -30000.0


class Buf:
    def __init__(self, t=None, name=""):
        self.t = t
        self.name = name
        self.last_w = None
        self.readers = {}

    def __getitem__(self, k):
        return self.t[k]


class K:
    ENG = ("pe", "act", "dve", "pool", "sp")

    def __init__(self, nc, stack, n_dma_sems=48):
        self.nc = nc
        self.stack = stack
        self.cur = stack
        self.streams = {e: [] for e in self.ENG}
        self.sem = {}
        self.cnt = {}
        for e in self.ENG:
            self.sem[e] = stack.enter_context(nc.semaphore("s_" + e))
            self.cnt[e] = 0
        self.ndma = n_dma_sems
        for i in range(n_dma_sems):
            self.sem[("d", i)] = stack.enter_context(nc.semaphore("d%d" % i))
            self.cnt[("d", i)] = 0
        self.dma_rr = 0
        self.known = {e: {} for e in self.ENG}
        self.nbuf = 0

    def sb(self, shape, dtype, name=None):
        self.nbuf += 1
        name = "%s_%d" % (name or "sb", self.nbuf)
        t = self.cur.enter_context(self.nc.sbuf_tensor(name, list(shape), dtype))
        return Buf(t, name)

    def ps(self, shape, dtype, name=None):
        self.nbuf += 1
        name = "%s_%d" % (name or "ps", self.nbuf)
        t = self.cur.enter_context(self.nc.psum_tensor(name, list(shape), dtype))
        return Buf(t, name)

    def dram(self, name, shape, dtype, kind="Internal"):
        t = self.nc.dram_tensor(name, list(shape), dtype, kind=kind)
        return Buf(t.ap(), name)

    def _need(self, eng, reads, writes):
        need = {}

        def add(dep):
            if dep is None:
                return
            k, v = dep
            if need.get(k, 0) < v:
                need[k] = v

        for b in reads:
            add(b.last_w)
        for b in writes:
            add(b.last_w)
            for k, v in b.readers.items():
                add((k, v))
        waits = []
        kn = self.known[eng]
        for k, v in need.items():
            if k == "pe" and eng == "pe":
                continue
            if kn.get(k, 0) >= v:
                continue
            kn[k] = v
            waits.append((k, v))
        return waits

    def op(self, eng, reads, writes, fn, inc=True):
        waits = self._need(eng, reads, writes)
        if inc:
            self.cnt[eng] += 1
            v = self.cnt[eng]
            self.streams[eng].append((waits, fn, (eng, 1)))
        else:
            v = self.cnt[eng] + 1
            self.streams[eng].append((waits, fn, None))
        for b in reads:
            if b.readers.get(eng, 0) < v:
                b.readers[eng] = v
        for b in writes:
            b.last_w = (eng, v)
            b.readers = {}
        return v

    def dma(self, q, reads, writes, fn):
        i = self.dma_rr
        self.dma_rr = (self.dma_rr + 1) % self.ndma
        key = ("d", i)
        waits = self._need(q, reads, writes)
        prev = self.cnt[key]
        if prev > 0 and self.known[q].get(key, 0) < prev:
            self.known[q][key] = prev
            waits.append((key, prev))
        self.cnt[key] += 16
        v = self.cnt[key]
        self.streams[q].append((waits, fn, (key, 16)))
        for b in reads:
            b.readers[key] = v
        for b in writes:
            b.last_w = (key, v)
            b.readers = {}
        return key, v

    def barrier(self):
        for e in self.ENG:
            waits = []
            for k, c in self.cnt.items():
                if c > 0 and (k != e or e in ("act", "dve", "pool")) and self.known[e].get(k, 0) < c:
                    self.known[e][k] = c
                    waits.append((k, c))
            if waits:
                self.streams[e].append((waits, None, None))

    def emit(self):
        nc = self.nc
        waits = [(k, c) for k, c in self.cnt.items() if c > 0 and k != "sp"]
        self.streams["sp"].append((waits, None, None))
        with nc.Block() as block:
            def mk(ename):
                def body(engine):
                    for ws, fn, inc in self.streams[ename]:
                        for kk, v in ws:
                            engine.wait_ge(self.sem[kk], v)
                        if fn is not None:
                            ins = fn(engine)
                            if inc is not None:
                                ins.then_inc(self.sem[inc[0]], inc[1])
                return body
            block.tensor(mk("pe"))
            block.scalar(mk("act"))
            block.vector(mk("dve"))
            block.gpsimd(mk("pool"))
            block.sync(mk("sp"))


def build(nlayers=NL, dbg=False, PIPE_A=True, PIPE_B=True, PIPE_PREP=True, PREFETCH_M=True):
    nc = bass.Bass("TRN2", target_bir_lowering=False)
    st = ExitStack()
    with st:
        k = K(nc, st)

        def MM(out, lhsT, rhs, start, stop, reads, writes, tp=None):
            if tp is None:
                k.op("pe", reads, writes, lambda e: e.matmul(out, lhsT=lhsT, rhs=rhs, start=start, stop=stop), inc=bool(stop))
            else:
                k.op("pe", reads, writes, lambda e: e.matmul(out, lhsT=lhsT, rhs=rhs, start=start, stop=stop, tile_position=tp), inc=bool(stop))

        def RSQ(buf, ap, npart, n, inv_n):
            TS(ap, ap, inv_n, ALU.mult, [buf], [buf], s2=EPS, op1=ALU.add, eng="pool")
            TT(ap, ap, mhalf[0:npart, 0:n], ALU.pow, [buf, mhalf], [buf], eng="pool")

        def TR(out, in_, idn, reads, writes):
            k.op("pe", reads, writes, lambda e: e.transpose(out=out, in_=in_, identity=idn))

        def ACT(out, in_, func, reads, writes, scale=None, bias=None, accum=None):
            kw = {}
            if scale is not None:
                kw["scale"] = scale
            if bias is not None:
                kw["bias"] = bias
            if accum is not None:
                kw["accum_out"] = accum
            k.op("act", reads, writes, lambda e: e.activation(out=out, in_=in_, func=func, **kw))

        def TT(out, in0, in1, op, reads, writes, eng="dve"):
            k.op(eng, reads, writes, lambda e: e.tensor_tensor(out=out, in0=in0, in1=in1, op=op))

        def TS(out, in0, s1, op0, reads, writes, s2=None, op1=None, eng="dve"):
            if op1 is None:
                k.op(eng, reads, writes, lambda e: e.tensor_scalar(out=out, in0=in0, scalar1=s1, scalar2=None, op0=op0))
            else:
                k.op(eng, reads, writes, lambda e: e.tensor_scalar(out=out, in0=in0, scalar1=s1, scalar2=s2, op0=op0, op1=op1))

        def STT(out, in0, scalar, in1, op0, op1, reads, writes):
            k.op("dve", reads, writes, lambda e: e.scalar_tensor_tensor(out=out, in0=in0, scalar=scalar, in1=in1, op0=op0, op1=op1))

        def CP(out, in_, reads, writes, eng="dve"):
            if eng == "act":
                k.op("act", reads, writes, lambda e: e.activation(out=out, in_=in_, func=AF.Copy))
            else:
                k.op(eng, reads, writes, lambda e: e.tensor_copy(out=out, in_=in_))

        def RED(out, in_, reads, writes):
            k.op("dve", reads, writes, lambda e: e.tensor_reduce(out=out, in_=in_, axis=AX.X, op=ALU.add))

        def RECIP(out, in_, reads, writes):
            k.op("dve", reads, writes, lambda e: e.reciprocal(out=out, in_=in_))

        def MSET(ap, val, writes, eng="pool"):
            k.op(eng, [], writes, lambda e: e.memset(ap, val))

        def DMA(q, out, in_, reads, writes):
            k.dma(q, reads, writes, lambda e: e.dma_start(out=out, in_=in_))

        EI = "ExternalInput"
        x_d = k.dram("x", [2, LAT, D], F32, EI)
        ctx_d = k.dram("ctx", [2, CTX, D], F32, EI)
        cv_d = k.dram("cv", [128, 8, 4], F32, EI)
        wmod_d = k.dram("w_mod", [NL, D, 3 * D], F32, EI)
        bmod_d = k.dram("b_modT", [128, NL, 24], F32, EI)
        gain_d = k.dram("gainT", [128, NL, 8], F32, EI)
        win_d = k.dram("w_in", [NL, D, 3840], F32, EI)
        wup_d = k.dram("w_up", [NL, 4, 256, D], F32, EI)
        wmg_d = k.dram("w_merge", [NL, 4, D, D], F32, EI)
        wout_d = k.dram("w_out", [NL, D, D], F32, EI)
        sv_d = k.dram("smallv", [128, NL, 448], F32, EI)
        lb_d = k.dram("lbT", [128, 2, 2, 4], F32, EI)
        nb_d = k.dram("nbias", [NL, 100, 128, 128], F32, EI)
        rope_d = k.dram("rope", [128, 16, 32], F32, EI)
        cs64_d = k.dram("cs64", [128, 256], BF16, EI)
        dftl_d = k.dram("dftL", [128, 16, 2, LAT], BF16, EI)
        dftc_d = k.dram("dftC", [128, 2, 2, CTX], BF16, EI)
        idb_d = k.dram("identb", [128, 128], BF16, EI)
        idf_d = k.dram("identf", [128, 128], F32, EI)
        cm_d = k.dram("cmask", [64, 256], F32, EI)
        out_d = k.dram("out", [2, LAT, D], F32, "ExternalOutput")
        xs_d = k.dram("xs", [2, T, D], F32, "ExternalOutput" if dbg else "Internal")
        ybr_d = k.dram("ybr", [2, 4, 2, 128, T], BF16, "ExternalOutput" if dbg else "Internal")
        xs_tok = [[Buf() for _ in range(NT)] for _ in range(2)]
        ybr_tok = [[Buf() for _ in range(4)] for _ in range(2)]
        cin = Buf()

        identb = k.sb([128, 128], BF16, "identb")
        identf = k.sb([128, 128], F32, "identf")
        onesf = k.sb([128, 128], F32, "onesf")
        cmask = k.sb([64, 256], F32, "cmask")
        rope = k.sb([128, 16, 32], F32, "rope")
        cs64 = k.sb([128, 256], BF16, "cs64")
        dftc = k.sb([128, 2, 2, CTX], BF16, "dftc")
        sv = k.sb([128, NL, 448], F32, "sv")
        bmod = k.sb([128, NL, 24], F32, "bmod")
        gainT = k.sb([128, NL, 8], F32, "gainT")
        sc = k.sb([128, 8, 4], F32, "sc")
        sc16 = k.sb([128, 8, 4], BF16, "sc16")
        segm = k.sb([128, 512], F32, "segm")
        mhalf = k.sb([128, 16], F32, "mhalf")
        rmask32 = k.sb([128, 4], F32, "rmask32")
        rmask64 = k.sb([128, 2], F32, "rmask64")
        lb_a = k.sb([128, 2, 2, 4], F32, "lb_a")
        lb_c = k.sb([128, 2, 2, 4], F32, "lb_c")
        lb_na = k.sb([128, 2, 2, 4], F32, "lb_na")
        mod = k.sb([128, 24, 4], F32, "mod")
        A1 = k.sb([128, 8, 4], F32, "A1")
        gate_bc = [k.sb([128, D], F32, "gatebc") for _ in range(3)]
        neglam = k.sb([128, 1], F32, "neglam")
        gA = k.sb([128, 2, 32], F32, "gA")
        gS = k.sb([128, 64], F32, "gS")
        gB = k.sb([128, 2, 64], F32, "gB")
        hT = k.sb([128, 8, T], BF16, "hT")
        hT_tok = [Buf() for _ in range(NT)]
        WB0 = k.sb([128, 8, 1024], BF16, "WB0")
        WB1 = k.sb([128, 8192], BF16, "WB1")

        def hts(t0, n):
            return hT_tok[t0 // 128:(t0 + n + 127) // 128]

        DMA("sp", identb[:], idb_d[:], [cin], [identb])
        DMA("sp", identf[:], idf_d[:], [cin], [identf])
        DMA("sp", cmask[:], cm_d[:], [cin], [cmask])
        DMA("sp", rope[:], rope_d[:], [cin], [rope])
        DMA("sp", cs64[:], cs64_d[:], [cin], [cs64])
        DMA("sp", dftc[:], dftc_d[:], [cin], [dftc])
        DMA("sp", sv[:], sv_d[:], [cin], [sv])
        DMA("sp", bmod[:], bmod_d[:], [cin], [bmod])
        DMA("sp", gainT[:], gain_d[:], [cin], [gainT])
        MSET(onesf[:], 1.0, [onesf])
        MSET(mhalf[:], -0.5, [mhalf])
        RED(rmask32[:], identf[:].rearrange("p (j c) -> p j c", c=32), [identf], [rmask32])
        RED(rmask64[:], identf[:].rearrange("p (j c) -> p j c", c=64), [identf], [rmask64])
        MSET(segm[:], 1.0, [segm])
        MSET(segm[:].rearrange("p (c s) -> p c s", s=64)[:, :, 0:1], 0.0, [segm])
        for b in range(2):
            DMA("sp", xs_d.t[b, 0:LAT, :], x_d.t[b], [cin], xs_tok[b][0:16])
            DMA("sp", xs_d.t[b, LAT:T, :], ctx_d.t[b], [cin], xs_tok[b][16:18])

        with ExitStack() as ph:
            k.cur = ph
            cvt = k.sb([128, 8, 4], F32)
            th0 = k.sb([128, 8, 4], F32)
            lbl = k.sb([128, 2, 2, 4], F32)
            lbe = k.sb([128, 2, 2, 4], F32)
            lbs = k.sb([128, 2, 2], F32)
            lbv = k.sb([128, 2, 2, 4], F32)
            lbm = k.sb([128, 2, 2, 4], F32)
            DMA("sp", cvt[:], cv_d[:], [cin], [cvt])
            DMA("sp", lbl[:], lb_d[:], [cin], [lbl])
            ACT(th0[:], cvt[:], AF.Tanh, [cvt], [th0], scale=0.5)
            TS(th0[:], th0[:], 0.5, ALU.mult, [th0], [th0], s2=0.5, op1=ALU.add)
            TT(sc[:], th0[:], cvt[:], ALU.mult, [th0, cvt], [sc])
            CP(sc16[:], sc[:], [sc], [sc16])
            ACT(lbe[:], lbl[:], AF.Exp, [lbl], [lbe])
            RED(lbs[:], lbe[:], [lbe], [lbs])
            RECIP(lbs[:], lbs[:], [lbs], [lbs])
            TT(lbe[:], lbe[:], lbs[:].unsqueeze(3).broadcast_to([128, 2, 2, 4]), ALU.mult, [lbe, lbs], [lbe])
            MSET(lbv[:], 0.0, [lbv])
            for l in range(1, 4):
                TT(lbv[:, :, :, l:l + 1], lbv[:, :, :, l - 1:l], lbe[:, :, :, l:l + 1], ALU.add, [lbv, lbe], [lbv])
            TS(lb_a[:], lbv[:], 1e-20, ALU.max, [lbv], [lb_a])
            TS(lb_na[:], lbv[:], -1.0, ALU.mult, [lbv], [lb_na], s2=1.0, op1=ALU.add)
            TT(lb_c[:], lb_a[:], lb_na[:], ALU.add, [lb_a, lb_na], [lb_c])
            k.barrier()
        k.cur = st

        for l in range(nlayers):
            last = (l == NL - 1)
            lam_init = 0.8 - 0.6 * math.exp(-0.3 * l)
            nq = 16 if last else NT
            nck = 32 if last else NCH

            with ExitStack() as ph:
                k.cur = ph
                pm = k.ps([128, 24, 4], F32)
                pg = k.ps([128, 1024], F32)
                wm32 = [k.sb([128, 8, 512], BF16) for _ in range(2)]
                diag = [k.sb([128, 128], F32) for _ in range(2)]
                lt = k.sb([128, 2, 32], F32)
                le = k.sb([128, 2], F32)
                for ch in range(6):
                    wb_ = wm32[ch % 2]
                    DMA("pool", wb_[:], wmod_d.t[l, :, ch * 512:(ch + 1) * 512].rearrange("(kc p) n -> p kc n", p=128), [cin], [wb_])
                    for jj in range(4):
                        for kc in range(8):
                            MM(pm[:, ch * 4 + jj, :], wb_[:, kc, jj * 128:(jj + 1) * 128], sc16[:, kc, :], kc == 0, kc == 7, [wb_, sc16], [pm])
                TT(mod[:], pm[:], bmod[:, l, :].unsqueeze(2).broadcast_to([128, 24, 4]), ALU.add, [pm, bmod], [mod])
                TS(A1[:], mod[:, 8:16, :], 1.0, ALU.add, [mod], [A1])
                TT(A1[:], A1[:], gainT[:, l, :].unsqueeze(2).broadcast_to([128, 8, 4]), ALU.mult, [A1, gainT], [A1])
                for v in range(3):
                    for t in range(8):
                        dg = diag[t % 2]
                        TS(dg[:], identf[:], mod[:, 16 + t, v:v + 1], ALU.mult, [identf, mod], [dg])
                        MM(pg[:, t * 128:(t + 1) * 128], onesf[:], dg[:], True, True, [onesf, dg], [pg])
                    CP(gate_bc[v][:, 0:512], pg[:, 0:512], [pg], [gate_bc[v]])
                    CP(gate_bc[v][:, 512:1024], pg[:, 512:1024], [pg], [gate_bc[v]], eng="act")
                lv = sv[:, l, 64:192].rearrange("p (a b d) -> p a b d", a=2, b=2)
                TT(lt[:], lv[:, :, 0, :], lv[:, :, 1, :], ALU.mult, [sv], [lt])
                RED(le[:], lt[:], [lt], [le])
                ACT(le[:], le[:], AF.Exp, [le], [le])
                TT(neglam[:], le[:, 1:2], le[:, 0:1], ALU.subtract, [le], [neglam])
                TS(neglam[:], neglam[:], -lam_init, ALU.add, [neglam], [neglam])
                CP(gA[:], sv[:, l, 0:64].rearrange("p (a d) -> p a d", a=2), [sv], [gA])
                TS(gA[:, 0, :], gA[:, 0, :], 32.0 ** -0.5, ALU.mult, [gA], [gA])
                TS(gS[:], sv[:, l, 192:256], 1.0 - lam_init, ALU.mult, [sv], [gS])
                CP(gB[:], sv[:, l, 256:384].rearrange("p (a d) -> p a d", a=2), [sv], [gB])
                TS(gB[:, 0, :], gB[:, 0, :], 0.125, ALU.mult, [gB], [gB])
                k.barrier()
            k.cur = st

            for b in range(2):
                with ExitStack() as ph:
                    k.cur = ph
                    xt = [k.sb([128, D], F32) for _ in range(2)]
                    sq = k.sb([128, D], F32)
                    ssq = [k.sb([128, 1], F32) for _ in range(2)]
                    xn = [k.sb([128, D], BF16) for _ in range(2)]
                    tmp = [k.sb([128, 8, 128], F32) for _ in range(2)]
                    pT = [k.ps([128, 8, 128], BF16) for _ in range(2)]
                    for t in range(NT):
                        v = b if t < 16 else 2
                        x_, s_, n_, p_, m_ = xt[t % 2], ssq[t % 2], xn[t % 2], pT[t % 2], tmp[t % 2]
                        DMA("sp", x_[:], xs_d.t[b, t * 128:(t + 1) * 128, :], [xs_tok[b][t]], [x_])
                        ACT(sq[:], x_[:], AF.Square, [x_], [sq, s_], accum=s_[:])
                        ACT(s_[:], s_[:], AF.Ln, [s_], [s_], scale=1.0 / D, bias=EPS)
                        ACT(s_[:], s_[:], AF.Exp, [s_], [s_], scale=-0.5)
                        TS(n_[:], x_[:], s_[:, 0:1], ALU.mult, [x_, s_], [n_])
                        for kc in range(8):
                            TR(p_[:, kc, :], n_[:, kc * 128:(kc + 1) * 128], identb[:], [n_, identb], [p_])
                        TT(m_[:], p_[:], A1[:, :, v:v + 1].broadcast_to([128, 8, 128]), ALU.mult, [p_, A1], [m_])
                        TT(hT[:, :, t * 128:(t + 1) * 128], m_[:], mod[:, 0:8, v:v + 1].broadcast_to([128, 8, 128]), ALU.add,
                           [m_, mod], [hT_tok[t]], eng="pool")
                    k.barrier()
                k.cur = st

                for br in range(2):
                    with ExitStack() as ph:
                        k.cur = ph
                        isA = (br == 0)
                        dh = 32 if isA else 64
                        ng = 512 // dh
                        nh = ng // 2
                        nmap = 8 if isA else 4
                        c0 = 0 if isA else 768
                        gq = gA if isA else gB
                        kT = k.sb([128, 2, T], BF16)
                        QP = [k.sb([128, T], BF16) for _ in range(nmap)]
                        mps = nmap // 2
                        rmask = rmask32 if isA else rmask64
                        vaug = k.sb([128, NT, 4, 65], BF16)
                        sqt = [k.sb([128, 512], F32) for _ in range(2)]
                        ssg = [k.sb([128, 16], F32) for _ in range(2)]
                        qkn = [k.sb([128, 512], F32) for _ in range(2)]
                        qkb = [k.sb([128, 512], BF16) for _ in range(2)]
                        ta2 = [k.sb([128, 256], F32) for _ in range(2)]
                        tb2 = [k.sb([128, 256], F32) for _ in range(2)]
                        tc2 = [k.sb([128, 256], F32) for _ in range(2)]
                        td2 = [k.sb([128, 256], F32) for _ in range(2)]
                        Pt = [k.sb([128, 512], BF16) for _ in range(3)]
                        Ot2 = [k.sb([128, 8, 65], F32) for _ in range(2)]
                        rsum = k.sb([128, 8], F32)
                        On = k.sb([128, 8, 64], F32)
                        ot = k.sb([128, 256], F32)
                        o2 = k.sb([128, 256], F32)
                        rs4 = k.sb([128, 4], F32)
                        tht = k.sb([128, 256], F32)
                        sgt4 = [k.sb([128, 256], F32) for _ in range(4)]
                        ygb4 = [k.sb([128, 256], BF16) for _ in range(4)]
                        ybt = [k.sb([128, 2, 128], BF16) for _ in range(4)]
                        if isA:
                            OaT = k.sb([66, 8, 512], F32)
                            MSET(OaT[:], 0.0, [OaT])
                        else:
                            nbt = k.sb([128, 100, 128], BF16)
                            DMA("pool", nbt[:], nb_d.t[l].rearrange("n k q -> k n q"), [cin], [nbt])
                        DMA("pool", WB0[:, :, 0:768], win_d.t[l, :, c0:c0 + 768].rearrange("(kc p) n -> p kc n", p=128), [cin], [WB0])
                        g0 = 2816 + 256 * br
                        wg = WB1[:, 0:2048].rearrange("p (kc n) -> p kc n", kc=8)
                        DMA("pool", wg, win_d.t[l, :, g0:g0 + 256].rearrange("(kc p) n -> p kc n", p=128), [cin], [WB1])
                        MSET(vaug[:, :, :, 64:65], 1.0, [vaug])
                        with ExitStack() as ph2:
                            k.cur = ph2
                            pz0 = [k.ps([128, 512], F32) for _ in range(2)]
                            pz1 = [k.ps([128, 512], F32) for _ in range(2)]
                            ptrp = [k.ps([128, 4, 128], BF16) for _ in range(2)]
                            pend = None
                            for t in range(NT):
                                ts_ = slice(t * 128, (t + 1) * 128)
                                z0, z1, pt_ = pz0[t % 2], pz1[t % 2], ptrp[t % 2]
                                sq_, sg_, qn_, qb = sqt[t % 2], ssg[t % 2], qkn[t % 2], qkb[t % 2]
                                for kc in range(8):
                                    MM(z0[:], hT[:, kc, ts_], WB0[:, kc, 0:512], kc == 0, kc == 7, [hT_tok[t], WB0], [z0])
                                for kc in range(8):
                                    MM(z1[:, 0:256], hT[:, kc, ts_], WB0[:, kc, 512:768], kc == 0, kc == 7, [hT_tok[t], WB0], [z1])
                                ACT(sq_[:], z0[:], AF.Square, [z0], [sq_])
                                RED(sg_[:, 0:ng], sq_[:].rearrange("p (g d) -> p g d", d=dh), [sq_], [sg_])
                                ACT(sg_[:, 0:ng], sg_[:, 0:ng], AF.Ln, [sg_], [sg_], scale=1.0 / dh, bias=EPS)
                                ACT(sg_[:, 0:ng], sg_[:, 0:ng], AF.Exp, [sg_], [sg_], scale=-0.5)
                                TT(qn_[:].rearrange("p (g d) -> p g d", d=dh), z0[:].rearrange("p (g d) -> p g d", d=dh),
                                   sg_[:, 0:ng].unsqueeze(2).broadcast_to([128, ng, dh]), ALU.mult, [z0, sg_], [qn_])
                                if isA and t < 16:
                                    TT(qn_[:].rearrange("p (a h d) -> p a h d", a=2, d=dh), qn_[:].rearrange("p (a h d) -> p a h d", a=2, d=dh),
                                       gq[:].unsqueeze(2).broadcast_to([128, 2, nh, dh]), ALU.mult, [qn_, gq], [qn_], eng="pool")
                                    qv = qn_[:].rearrange("p (g two d) -> p g two d", two=2, d=16)
                                    ov = qb[:].rearrange("p (g two d) -> p g two d", two=2, d=16)
                                    cosb = rope[:, t, 0:16].unsqueeze(1).broadcast_to([128, 16, 16])
                                    sinb = rope[:, t, 16:32].unsqueeze(1).broadcast_to([128, 16, 16])
                                    t3 = lambda buf: buf[:].rearrange("p (g d) -> p g d", d=16)
                                    ta, tb, tc_, td = ta2[t % 2], tb2[t % 2], tc2[t % 2], td2[t % 2]
                                    TT(t3(ta), qv[:, :, 0, :], cosb, ALU.mult, [qn_, rope], [ta])
                                    TT(t3(tb), qv[:, :, 1, :], sinb, ALU.mult, [qn_, rope], [tb])
                                    TT(ov[:, :, 0, :], t3(ta), t3(tb), ALU.subtract, [ta, tb], [qb])
                                    TT(t3(tc_), qv[:, :, 0, :], sinb, ALU.mult, [qn_, rope], [tc_])
                                    TT(t3(td), qv[:, :, 1, :], cosb, ALU.mult, [qn_, rope], [td])
                                    TT(ov[:, :, 1, :], t3(tc_), t3(td), ALU.add, [tc_, td], [qb], eng="pool")
                                else:
                                    TT(qb[:].rearrange("p (a h d) -> p a h d", a=2, d=dh), qn_[:].rearrange("p (a h d) -> p a h d", a=2, d=dh),
                                       gq[:].unsqueeze(2).broadcast_to([128, 2, nh, dh]), ALU.mult, [qn_, gq], [qb], eng="pool")
                                CP(vaug[:, t, :, 0:64], z1[:, 0:256].rearrange("p (h d) -> p h d", d=64), [z1], [vaug], eng="act")

                                def trs(t=t, pt_=pt_, qb=qb, ts_=ts_):
                                    for i in range(4):
                                        TR(pt_[:, i, :], qb[:, i * 128:(i + 1) * 128], identb[:], [qb, identb], [pt_])
                                    CP(kT[:, :, ts_], pt_[:, 2:4, :], [pt_], [kT], eng="act")
                                    for m in range(nmap):
                                        ACT(QP[m][:, ts_], pt_[:, m // mps, :], AF.Copy, [pt_, rmask], [QP[m]],
                                            scale=rmask[:, (m % mps):(m % mps) + 1])
                                if not PIPE_PREP:
                                    trs()
                                    continue
                                if pend is not None:
                                    pend()
                                pend = trs
                            if PIPE_PREP:
                                pend()
                            k.barrier()
                        k.cur = ph
                        Sp = [k.ps([128, 512], F32) for _ in range(2)]
                        if isA:
                            accT = [k.ps([65, 512], F32) for _ in range(2)]
                            pOa = k.ps([128, 4, 66], F32)
                            pOb = k.ps([128, 4, 66], F32)
                        else:
                            acc = [k.ps([128, 4, 65], F32) for _ in range(2)]
                        pgz = k.ps([128, 256], F32)
                        ptr = k.ps([128, 4, 128], BF16)

                        pendq = []

                        def finish(qt, Ot):
                            qs_ = slice(qt * 128, (qt + 1) * 128)
                            sg_ = sgt4[qt % 4]
                            yg_ = ygb4[qt % 4]
                            for kc in range(8):
                                MM(pgz[:], hT[:, kc, qs_], wg[:, kc, :], kc == 0, kc == 7, [hT_tok[qt], WB1], [pgz])
                            ACT(tht[:], pgz[:], AF.Tanh, [pgz], [tht], scale=0.5)
                            STT(sg_[:], tht[:], 1.0, pgz[:], ALU.add, ALU.mult, [tht, pgz], [sg_])
                            RECIP(rsum[:, 0:nmap], Ot[:, 0:nmap, 64], [Ot], [rsum])
                            TT(On[:, 0:nmap, :], Ot[:, 0:nmap, 0:64], rsum[:, 0:nmap].unsqueeze(2).broadcast_to([128, nmap, 64]), ALU.mult,
                               [Ot, rsum], [On])
                            if isA:
                                Onv = On[:].rearrange("p (h two) d -> p h two d", two=2)
                                STT(ot[:].rearrange("p (h d) -> p h d", d=64), Onv[:, :, 1, :], neglam[:, 0:1], Onv[:, :, 0, :],
                                    ALU.mult, ALU.add, [On, neglam], [ot])
                                TT(o2[:], ot[:], ot[:], ALU.mult, [ot], [o2], eng="pool")
                                RED(rs4[:], o2[:].rearrange("p (h d) -> p h d", d=64), [o2], [rs4])
                                RSQ(rs4, rs4[:], 128, 4, 1.0 / 64)
                                TT(o2[:].rearrange("p (h d) -> p h d", d=64), ot[:].rearrange("p (h d) -> p h d", d=64),
                                   rs4[:].unsqueeze(2).broadcast_to([128, 4, 64]), ALU.mult, [ot, rs4], [o2])
                                TT(ot[:].rearrange("p (h d) -> p h d", d=64), o2[:].rearrange("p (h d) -> p h d", d=64),
                                   gS[:].unsqueeze(1).broadcast_to([128, 4, 64]), ALU.mult, [o2, gS], [ot], eng="pool")
                                ysrc = ot[:]
                                ybuf = ot
                            else:
                                ysrc = On[:, 0:4, :].rearrange("p h d -> p (h d)")
                                ybuf = On
                            STT(yg_[:], sg_[:], 0.5, ysrc, ALU.mult, ALU.mult, [sg_, ybuf], [yg_])
                            pendq.append(qt)

                        def flush(keep=0):
                            while len(pendq) > keep:
                                qt = pendq.pop(0)
                                qs_ = slice(qt * 128, (qt + 1) * 128)
                                yg_ = ygb4[qt % 4]
                                yb_ = ybt[qt % 4]
                                for i in range(2):
                                    TR(ptr[:, i, :], yg_[:, i * 128:(i + 1) * 128], identb[:], [yg_, identb], [ptr])
                                CP(yb_[:], ptr[:, 0:2, :], [ptr], [yb_])
                                DMA("sp", ybr_d.t[b, br, :, :, qs_].rearrange("c p n -> p c n"), yb_[:], [yb_], [ybr_tok[b][br]])

                        gi = 0
                        if isA:
                            qblocks = [(0, 512), (512, 512), (1024, 512), (1536, 512)] + ([] if last else [(2048, 256)])
                            for (q0, qn) in qblocks:
                                chunks = list(range(NT)) if q0 < LAT else [16, 17]
                                nci = len(chunks)
                                its = [(m, ci, c) for m in range(8) for ci, c in enumerate(chunks)]

                                def qk(i):
                                    m, ci, c = its[i]
                                    slot, pb = m // 4, 32 * (m % 4)
                                    S_ = Sp[(gi + i) % 2]
                                    P_ = Pt[(gi + i) % 3]
                                    MM(S_[:, 0:qn], kT[:, slot, c * 128:(c + 1) * 128], QP[m][:, q0:q0 + qn], True, True, [kT, QP[m]], [S_])
                                    ACT(P_[:, 0:qn], S_[:, 0:qn], AF.Exp, [S_], [P_])

                                def pv(i):
                                    m, ci, c = its[i]
                                    a_ = accT[m % 2]
                                    P_ = Pt[(gi + i) % 3]
                                    MM(a_[:, 0:qn], vaug[:, c, m // 2, :], P_[:, 0:qn], ci == 0, ci == nci - 1, [vaug, P_], [a_])
                                    if ci == nci - 1:
                                        CP(OaT[0:65, m, 0:qn], a_[:, 0:qn], [a_], [OaT])

                                if PIPE_A:
                                    qk(0)
                                for i in range(len(its)):
                                    if PIPE_A:
                                        if i + 1 < len(its):
                                            qk(i + 1)
                                    else:
                                        qk(i)
                                    pv(i)
                                    if i == 8:
                                        flush()
                                gi += len(its)
                                for j in range(qn // 128):
                                    qt = q0 // 128 + j
                                    Ot = Ot2[qt % 2]
                                    for m in range(8):
                                        pO_ = pOa if m < 4 else pOb
                                        TR(pO_[:, m % 4, :], OaT[:, m, j * 128:(j + 1) * 128], identf[0:66, 0:66], [OaT, identf], [pO_])
                                    CP(Ot[:, 0:4, :], pOa[:, :, 0:65], [pOa], [Ot])
                                    CP(Ot[:, 4:8, :], pOb[:, :, 0:65], [pOb], [Ot])
                                    finish(qt, Ot)
                            flush()
                        else:
                            work = []
                            for qt in range(nq):
                                if qt >= 16:
                                    chunks = [(16, None), (17, None)]
                                else:
                                    cs = min(max(qt - 2, 0), 11)
                                    typ = 0 if qt == 0 else 1 if qt == 1 else 3 if qt == 14 else 4 if qt == 15 else 2
                                    chunks = [(cs + i, typ * 5 + i) for i in range(5)] + [(16, None), (17, None)]
                                groups = [chunks[i:i + 4] for i in range(0, len(chunks), 4)]
                                for m in range(4):
                                    for gidx, g in enumerate(groups):
                                        work.append((qt, m, g, gidx == 0, gidx == len(groups) - 1))

                            def scb(i):
                                qt, m, g, fg, lg = work[i]
                                slot, pb = m // 2, 64 * (m % 2)
                                Sb = Sp[i % 2]
                                Pb = Pt[i % 3]
                                S_ = Sb[:].rearrange("p (c q) -> p c q", q=128)
                                P_ = Pb[:].rearrange("p (c q) -> p c q", q=128)
                                for j, (c, bi) in enumerate(g):
                                    kT_ = kT[:, slot, c * 128:(c + 1) * 128]
                                    qT_ = QP[m][:, qt * 128:(qt + 1) * 128]
                                    if bi is not None:
                                        MM(S_[:, j, :], identb[:], nbt[:, bi * 4 + m, :], True, False, [identb, nbt], [Sb])
                                        MM(S_[:, j, :], kT_, qT_, False, True, [kT, QP[m]], [Sb])
                                    else:
                                        MM(S_[:, j, :], kT_, qT_, True, True, [kT, QP[m]], [Sb])
                                ACT(P_[:, 0:len(g), :], S_[:, 0:len(g), :], AF.Exp, [Sb], [Pb])

                            def pvb(i):
                                qt, m, g, fg, lg = work[i]
                                Pb = Pt[i % 3]
                                P_ = Pb[:].rearrange("p (c q) -> p c q", q=128)
                                ab = acc[qt % 2]
                                for j, (c, bi) in enumerate(g):
                                    MM(ab[:, m, :], P_[:, j, :], vaug[:, c, m, :], fg and j == 0, lg and j == len(g) - 1, [Pb, vaug], [ab])
                                if lg and m == 3:
                                    Ot = Ot2[qt % 2]
                                    CP(Ot[:, 0:4, :], ab[:], [ab], [Ot])
                                    finish(qt, Ot)
                                if lg and m == 1:
                                    flush()

                            if PIPE_B:
                                scb(0)
                            for i in range(len(work)):
                                if PIPE_B:
                                    if i + 1 < len(work):
                                        scb(i + 1)
                                else:
                                    scb(i)
                                pvb(i)
                            flush()
                        k.barrier()
                    k.cur = st

                DMA("pool", WB0[:], win_d.t[l, :, 1536:2560].rearrange("(kc p) n -> p kc n", p=128), [cin], [WB0])
                wg = WB1[:, 0:2048].rearrange("p (kc n) -> p kc n", kc=8)
                DMA("pool", wg, win_d.t[l, :, 3328:3584].rearrange("(kc p) n -> p kc n", p=128), [cin], [WB1])
                for ct in range(2):
                    with ExitStack() as ph:
                        k.cur = ph
                        KT = [k.sb([128, T], BF16) for _ in range(2)]
                        QBD = [k.sb([128, NCH, 128], BF16) for _ in range(2)]
                        Ktm2 = [k.sb([64, NCH, 128], BF16) for _ in range(2)]
                        Vc = k.sb([64, NCH, 128], BF16)
                        Spr = [k.sb([128, NCH, 64], BF16) for _ in range(2)]
                        er = [k.sb([128, NCH], F32) for _ in range(2)]
                        Sst2 = [k.sb([128, 64], F32) for _ in range(2)]
                        Stm2 = [k.sb([128, 64], F32) for _ in range(2)]
                        for d in range(2):
                            MSET(QBD[d][:], 0.0, [QBD[d]])
                        with ExitStack() as ph2:
                            k.cur = ph2
                            qh = k.sb([128, T], F32)
                            thq = [k.sb([128, 512], F32) for _ in range(2)]
                            NS = 2
                            tht = [k.sb([128, 512], F32) for _ in range(NS)]
                            gg = [k.sb([128, 512], F32) for _ in range(NS)]
                            kk = [k.sb([128, 512], F32) for _ in range(NS)]
                            uu = [k.sb([128, 512], F32) for _ in range(NS)]
                            Bc = [k.sb([128, 512], F32) for _ in range(NS)]
                            tmpb = [k.sb([128, 512], F32) for _ in range(NS)]
                            tot = [k.sb([128, 8], F32) for _ in range(NS)]
                            E1 = [k.sb([128, 512], F32) for _ in range(NS)]
                            E2 = [k.sb([128, 512], F32) for _ in range(NS)]
                            pq = [k.ps([128, 512], F32) for _ in range(4)]
                            pv = [k.ps([64, 4, 128], F32) for _ in range(2)]
                            for c4 in range(0, NCH, 4):
                                pv_ = pv[(c4 // 4) % 2]
                                for i in range(4):
                                    c = c4 + i
                                    for kc in range(8):
                                        MM(pv_[:, i, :], hT[:, kc, c * 64:(c + 1) * 64], WB0[:, kc, 768 + ct * 128:768 + (ct + 1) * 128],
                                           kc == 0, kc == 7, [hT_tok[c // 2], WB0], [pv_])
                                CP(Vc[:, c4:c4 + 4, :], pv_[:], [pv_], [Vc], eng="act")
                            pi = 0
                            for bi_, (t0, n) in enumerate(BLKS):
                                p_ = pq[pi % 4]
                                pi += 1
                                tq = thq[bi_ % 2]
                                for kc in range(8):
                                    MM(p_[:, 0:n], WB0[:, kc, ct * 128:(ct + 1) * 128], hT[:, kc, t0:t0 + n], kc == 0, kc == 7, hts(t0, n) + [WB0], [p_])
                                ACT(tq[:, 0:n], p_[:, 0:n], AF.Tanh, [p_], [tq], scale=0.5)
                                STT(qh[:, t0:t0 + n], tq[:, 0:n], 1.0, p_[:, 0:n], ALU.add, ALU.mult, [tq, p_], [qh])
                            it = 0
                            for (t0, n) in BLKS:
                                nc_ = n // 64
                                cb = t0 // 64
                                bs_ = slice(t0, t0 + n)
                                for d in range(2):
                                    j = it % NS
                                    it += 1
                                    p_ = pq[pi % 4]
                                    pi += 1
                                    cc = 256 + d * 256 + ct * 128
                                    for kc in range(8):
                                        MM(p_[:, 0:n], WB0[:, kc, cc:cc + 128], hT[:, kc, bs_], kc == 0, kc == 7, hts(t0, n) + [WB0], [p_])
                                    ACT(uu[j][:, 0:n], p_[:, 0:n], AF.Exp, [p_], [uu[j]], scale=-1.0)
                                    ACT(E2[j][:, 0:n], uu[j][:, 0:n], AF.Ln, [uu[j]], [E2[j]], bias=1.0)
                                    ACT(E1[j][:, 0:n], uu[j][:, 0:n], AF.Ln, [uu[j], lb_a, lb_c], [E1[j]], scale=lb_a[:, d, ct, l:l + 1],
                                        bias=lb_c[:, d, ct, l:l + 1])
                                    TT(gg[j][:, 0:n], E1[j][:, 0:n], E2[j][:, 0:n], ALU.subtract, [E1[j], E2[j]], [gg[j]], eng="pool")
                                    TT(tht[j][:, 0:n], p_[:, 0:n], E2[j][:, 0:n], ALU.add, [p_, E2[j]], [tht[j]])
                                    ACT(kk[j][:, 0:n], tht[j][:, 0:n], AF.Exp, [tht[j]], [kk[j]], scale=-1.0)
                                    k.op("dve", [segm, gg[j]], [Bc[j]], lambda e, n=n, j=j: e.tensor_tensor_scan(
                                        out=Bc[j][:, 0:n], data0=segm[:, 0:n], data1=gg[j][:, 0:n], initial=0.0, op0=ALU.mult, op1=ALU.add))
                                    B3 = Bc[j][:, 0:n].rearrange("p (c s) -> p c s", s=64)
                                    CP(tot[j][:, 0:nc_], B3[:, :, 63], [Bc[j]], [tot[j]])
                                    if d == 1:
                                        TT(tmpb[j][:, 0:n], gg[j][:, 0:n], Bc[j][:, 0:n], ALU.subtract, [gg[j], Bc[j]], [tmpb[j]])
                                        TT(B3, tmpb[j][:, 0:n].rearrange("p (c s) -> p c s", s=64),
                                           tot[j][:, 0:nc_].unsqueeze(2).broadcast_to([128, nc_, 64]), ALU.add, [tmpb[j], tot[j]], [Bc[j]])
                                    ACT(er[d][:, cb:cb + nc_], tot[j][:, 0:nc_], AF.Exp, [tot[j]], [er[d]], scale=0.5)
                                    STT(tmpb[j][:, 0:n].rearrange("p (c s) -> p c s", s=64), tot[j][:, 0:nc_].unsqueeze(2).broadcast_to([128, nc_, 64]),
                                        -0.5, B3, ALU.mult, ALU.add, [tot[j], Bc[j]], [tmpb[j]])
                                    TS(tmpb[j][:, 0:n], tmpb[j][:, 0:n], 43.0, ALU.min, [tmpb[j]], [tmpb[j]], s2=-43.0, op1=ALU.max, eng="pool")
                                    ACT(E1[j][:, 0:n], tmpb[j][:, 0:n], AF.Exp, [tmpb[j]], [E1[j]])
                                    ACT(E2[j][:, 0:n], tmpb[j][:, 0:n], AF.Exp, [tmpb[j]], [E2[j]], scale=-1.0)
                                    for hh in range(2):
                                        ps_ = slice(64 * hh, 64 * hh + 64)
                                        STT(QBD[d][ps_, cb:cb + nc_, 64 * hh:64 * hh + 64], qh[ps_, bs_].rearrange("p (c s) -> p c s", s=64), 0.5,
                                            E1[j][ps_, 0:n].rearrange("p (c s) -> p c s", s=64), ALU.mult, ALU.mult, [qh, E1[j]], [QBD[d]])
                                    STT(KT[d][:, bs_], kk[j][:, 0:n], lb_na[:, d, ct, l:l + 1], E2[j][:, 0:n], ALU.mult, ALU.mult,
                                        [kk[j], E2[j], lb_na], [KT[d]])
                            k.barrier()
                        k.cur = ph
                        Am = [k.sb([64, 2, 128], BF16) for _ in range(2)]
                        of_ = k.sb([64, 512], F32)
                        o2 = k.sb([64, 512], F32)
                        rs8 = k.sb([64, 8], F32)
                        th2 = k.sb([64, 512], F32)
                        sg2 = k.sb([64, 512], F32)
                        yg2 = [k.sb([64, 512], BF16) for _ in range(2)]
                        ybt = [k.sb([128, 256], BF16) for _ in range(2)]
                        pbk = k.ps([128, 1024], BF16)
                        ptk = pbk[0:64, :].rearrange("p (a b) -> p a b", b=128)
                        pTy = pbk[:, 0:256]
                        pu = [k.ps([128, 128], F32) for _ in range(2)]
                        pa = [k.ps([64, 2, 128], F32) for _ in range(2)]
                        po = k.ps([64, 4, 128], F32)
                        pgz = k.ps([64, 4, 128], F32)
                        for d in range(2):
                            for c8 in range(0, NCH, 4):
                                for i in range(4):
                                    c = c8 + i
                                    TR(ptk[:, i, :], KT[d][:, c * 64:(c + 1) * 64], identb[:], [KT[d], identb], [pbk])
                                CP(Ktm2[d][:, c8:c8 + 4, :], ptk[:, 0:4, :], [pbk], [Ktm2[d]], eng="act")
                            MSET(Sst2[d][:], 0.0, [Sst2[d]], eng="dve")
                        orders = [[32, 33, 34, 35] + list(range(32)), [35, 34, 33, 32] + list(range(31, -1, -1))]
                        for oi in range(NCH):
                            for d in range(2):
                                c = orders[d][oi]
                                pu_ = pu[d]
                                Sst, Stm = Sst2[d], Stm2[d]
                                MM(pu_[:], Ktm2[d][:, c, :], Vc[:, c, :], True, True, [Ktm2[d], Vc], [pu_])
                                e_ = er[d][:, c:c + 1]
                                ACT(Spr[d][:, c, :], Sst[:], AF.Copy, [Sst, er[d]], [Spr[d]], scale=e_)
                                for hh in range(2):
                                    ps_ = slice(64 * hh, 64 * hh + 64)
                                    STT(Stm[ps_, :], Sst[ps_, :], er[d][ps_, c:c + 1], pu_[ps_, 64 * hh:64 * hh + 64], ALU.mult, ALU.add,
                                        [Sst, er[d], pu_], [Stm])
                                TS(Sst[:], Stm[:], e_, ALU.mult, [Stm, er[d]], [Sst])
                        cm4 = cmask[:].rearrange("p (d h t) -> p d h t", d=2, h=2)

                        def scm(c):
                            pa_ = pa[c % 2]
                            am_ = Am[c % 2]
                            for d in range(2):
                                MM(pa_[:, d, :], KT[d][:, c * 64:(c + 1) * 64], QBD[d][:, c, :], True, True, [KT[d], QBD[d]], [pa_])
                            TT(am_[:].rearrange("p d (h t) -> p d h t", h=2), pa_[:].rearrange("p d (h t) -> p d h t", h=2), cm4, ALU.mult,
                               [pa_, cmask], [am_])

                        ptail = []

                        def tail():
                            while ptail:
                                c4 = ptail.pop(0)
                                yg_ = yg2[(c4 // 4) % 2]
                                yb_ = ybt[(c4 // 4) % 2]
                                for i in range(4):
                                    TR(pTy[:, i * 64:(i + 1) * 64], yg_[:, i * 128:(i + 1) * 128], identb[0:64, 0:64], [yg_, identb], [pbk])
                                CP(yb_[:], pTy, [pbk], [yb_], eng="act")
                                DMA("sp", ybr_d.t[b, 2, ct, :, c4 * 64:c4 * 64 + 256], yb_[:], [yb_], [ybr_tok[b][2]])

                        scm(0)
                        for c4 in range(0, nck, 4):
                            for i in range(4):
                                c = c4 + i
                                if c + 1 < nck:
                                    scm(c + 1)
                                for kc in range(8):
                                    MM(pgz[:, i, :], hT[:, kc, c * 64:(c + 1) * 64], wg[:, kc, ct * 128:(ct + 1) * 128], kc == 0, kc == 7,
                                       [hT_tok[c // 2], WB1], [pgz])
                                am_ = Am[c % 2]
                                for hh in range(2):
                                    hs_ = slice(64 * hh, 64 * hh + 64)
                                    for d in range(2):
                                        MM(po[:, i, hs_], am_[:, d, hs_], Vc[:, c, hs_], d == 0, False, [am_, Vc], [po])
                                        MM(po[:, i, hs_], QBD[d][:, c, hs_], Spr[d][:, c, :], False, d == 1, [QBD[d], Spr[d]], [po])
                            tail()
                            pof = po[:].rearrange("p c n -> p (c n)")
                            pgf = pgz[:].rearrange("p c n -> p (c n)")
                            yg_ = yg2[(c4 // 4) % 2]
                            ACT(th2[:], pgf, AF.Tanh, [pgz], [th2], scale=0.5)
                            STT(sg2[:], th2[:], 1.0, pgf, ALU.add, ALU.mult, [th2, pgz], [sg2])
                            CP(of_[:], pof, [po], [of_], eng="act")
                            ACT(o2[:], pof, AF.Square, [po], [o2])
                            RED(rs8[:], o2[:].rearrange("p (g d) -> p g d", d=64), [o2], [rs8])
                            RSQ(rs8, rs8[:], 64, 8, 1.0 / 64)
                            TT(o2[:].rearrange("p (g d) -> p g d", d=64), of_[:].rearrange("p (g d) -> p g d", d=64),
                               rs8[:].unsqueeze(2).broadcast_to([64, 8, 64]), ALU.mult, [of_, rs8], [o2])
                            TT(of_[:].rearrange("p (g d) -> p g d", d=64), o2[:].rearrange("p (g d) -> p g d", d=64),
                               sv[0:64, l, 384:448].unsqueeze(1).broadcast_to([64, 8, 64]), ALU.mult, [o2, sv], [of_], eng="pool")
                            STT(yg_[:], sg2[:], 0.5, of_[:], ALU.mult, ALU.mult, [sg2, of_], [yg_])
                            ptail.append(c4)
                        tail()
                        k.barrier()
                    k.cur = st

                with ExitStack() as ph:
                    k.cur = ph
                    uT = k.sb([128, 2, T], BF16)
                    PQ = k.sb([128, NT, 2, 256], BF16)
                    tab = [k.sb([128, 16, 2, 256], BF16) for _ in range(2)]
                    tht = k.sb([128, 256], F32)
                    sgt = k.sb([128, 256], F32)
                    ybt = [k.sb([128, 2, 256], BF16) for _ in range(2)]
                    pq = [k.ps([128, 512], F32) for _ in range(2)]
                    pp = [k.ps([128, 2, 256], F32) for _ in range(2)]
                    py = [k.ps([128, 256], F32) for _ in range(2)]
                    pgz = [k.ps([128, 256], F32) for _ in range(2)]
                    DMA("pool", WB0[:, :, 0:256], win_d.t[l, :, 2560:2816].rearrange("(kc p) n -> p kc n", p=128), [cin], [WB0])
                    wg = WB1[:, 0:2048].rearrange("p (kc n) -> p kc n", kc=8)
                    DMA("pool", wg, win_d.t[l, :, 3584:3840].rearrange("(kc p) n -> p kc n", p=128), [cin], [WB1])
                    pi = 0
                    for ct in range(2):
                        for (t0, n) in BLKS:
                            p_ = pq[pi % 2]
                            pi += 1
                            for kc in range(8):
                                MM(p_[:, 0:n], WB0[:, kc, ct * 128:(ct + 1) * 128], hT[:, kc, t0:t0 + n], kc == 0, kc == 7, hts(t0, n) + [WB0], [p_])
                            CP(uT[:, ct, t0:t0 + n], p_[:, 0:n], [p_], [uT], eng="act")
                    for t in range(NT):
                        p_ = pp[t % 2]
                        for ct in range(2):
                            MM(p_[:, ct, :], uT[:, ct, t * 128:(t + 1) * 128], cs64[:], True, True, [uT, cs64], [p_])
                        CP(PQ[:, t, :, :], p_[:], [p_], [PQ], eng=("act" if t % 2 else "dve"))
                    it = 0
                    nblk = 8 if last else 9
                    for nb in range(nblk):
                        if nb < 8:
                            tb_ = tab[nb % 2]
                            DMA("sp", tb_[:], dftl_d.t[:, :, :, nb * 256:(nb + 1) * 256], [cin], [tb_])
                            tcs = list(range(16))
                            tsrc = lambda tc, cs: tb_[:, tc, cs, :]
                            tread = [tb_]
                            t0 = nb * 256
                        else:
                            tcs = [16, 17]
                            tsrc = lambda tc, cs: dftc[:, tc - 16, cs, :]
                            tread = [dftc]
                            t0 = LAT
                        yb_ = ybt[nb % 2]
                        for ct in range(2):
                            y_ = py[it % 2]
                            g_ = pgz[it % 2]
                            it += 1
                            nmm = len(tcs) * 2
                            j = 0
                            for tc in tcs:
                                for cs in range(2):
                                    MM(y_[:], PQ[:, tc, ct, cs * 128:(cs + 1) * 128], tsrc(tc, cs), j == 0, j == nmm - 1, [PQ] + tread, [y_])
                                    j += 1
                            for kc in range(8):
                                MM(g_[:], wg[:, kc, ct * 128:(ct + 1) * 128], hT[:, kc, t0:t0 + 256], kc == 0, kc == 7, hts(t0, 256) + [WB1], [g_])
                            ACT(tht[:], g_[:], AF.Tanh, [g_], [tht], scale=0.5)
                            STT(sgt[:], tht[:], 1.0, g_[:], ALU.add, ALU.mult, [tht, g_], [sgt])
                            STT(yb_[:, ct, :], sgt[:], 0.5, y_[:], ALU.mult, ALU.mult, [sgt, y_], [yb_])
                        DMA("sp", ybr_d.t[b, 3, :, :, t0:t0 + 256].rearrange("c p n -> p c n"), yb_[:], [yb_], [ybr_tok[b][3]])
                    k.barrier()
                k.cur = st

                with ExitStack() as ph:
                    k.cur = ph
                    ybS = k.sb([128, 4, 2, T], BF16)
                    accT = k.sb([128, 8, T], BF16)
                    wu = [k.sb([128, 4, 2, 128], BF16) for _ in range(2)]
                    wm_tok = [Buf(), Buf()]
                    tht = [k.sb([128, 512], F32) for _ in range(2)]
                    tmpm = [k.sb([128, 512], F32) for _ in range(2)]
                    accf = [k.sb([128, 512], F32) for _ in range(2)]
                    xt = [k.sb([128, D], F32) for _ in range(2)]
                    xo = [k.sb([128, D], F32) for _ in range(2)]
                    tmo = [k.sb([128, 512], F32) for _ in range(2)]
                    pM = [k.ps([128, 512], F32) for _ in range(2)]
                    pU = [k.ps([128, 512], F32) for _ in range(2)]
                    pO = [k.ps([128, 512], F32) for _ in range(2)]
                    for i in range(4):
                        DMA("sp", ybS[:, i, :, :], ybr_d.t[b, i].rearrange("c p n -> p c n"), [ybr_tok[b][i]], [ybS])
                    DMA("pool", WB0[:], wout_d.t[l].rearrange("(kc p) n -> p kc n", p=128), [cin], [WB0])
                    mblks = BLKS[:4] if last else BLKS
                    im = 0
                    def wload(ft):
                        fs_ = slice(ft * 128, (ft + 1) * 128)
                        wmv = WB1[:, (ft % 2) * 4096:(ft % 2 + 1) * 4096].rearrange("p (i kc n) -> p i kc n", i=4, kc=8)
                        DMA("pool", wmv, wmg_d.t[l, :, :, fs_].rearrange("i (kc p) n -> p i kc n", p=128), [cin], [wm_tok[ft % 2]])
                        DMA("pool", wu[ft % 2][:], wup_d.t[l, :, :, fs_].rearrange("i (fc p) n -> p i fc n", p=128), [cin], [wu[ft % 2]])

                    if PREFETCH_M:
                        wload(0)
                    for ft in range(8):
                        fs_ = slice(ft * 128, (ft + 1) * 128)
                        wmv = WB1[:, (ft % 2) * 4096:(ft % 2 + 1) * 4096].rearrange("p (i kc n) -> p i kc n", i=4, kc=8)
                        wmt = wm_tok[ft % 2]
                        wu_ = wu[ft % 2]
                        if not PREFETCH_M:
                            wload(ft)
                        elif ft + 1 < 8:
                            wload(ft + 1)
                        for bi_, (t0, n) in enumerate(mblks):
                            af = accf[bi_ % 2]
                            for i in range(4):
                                m_ = pM[im % 2]
                                u_ = pU[im % 2]
                                th_ = tht[im % 2]
                                tm_ = tmpm[im % 2]
                                im += 1
                                for kc in range(8):
                                    MM(m_[:, 0:n], wmv[:, i, kc, :], hT[:, kc, t0:t0 + n], kc == 0, kc == 7, hts(t0, n) + [wmt], [m_])
                                for fc in range(2):
                                    MM(u_[:, 0:n], wu_[:, i, fc, :], ybS[:, i, fc, t0:t0 + n], fc == 0, fc == 1, [wu_, ybS], [u_])
                                ACT(th_[:, 0:n], m_[:, 0:n], AF.Tanh, [m_], [th_], scale=0.5)
                                if i == 0:
                                    STT(af[:, 0:n], th_[:, 0:n], 1.0, u_[:, 0:n], ALU.add, ALU.mult, [th_, u_], [af])
                                else:
                                    STT(tm_[:, 0:n], th_[:, 0:n], 1.0, u_[:, 0:n], ALU.add, ALU.mult, [th_, u_], [tm_])
                                    TT(af[:, 0:n], af[:, 0:n], tm_[:, 0:n], ALU.add, [af, tm_], [af], eng=("dve" if i == 2 else "pool"))
                            ACT(accT[:, ft, t0:t0 + n], af[:, 0:n], AF.Copy, [af], [accT], scale=0.5)
                    ntile = 16 if last else NT
                    io = 0
                    for t in range(ntile):
                        v = b if t < 16 else 2
                        x_ = xt[t % 2]
                        o_ = xo[t % 2]
                        DMA("sp", x_[:], xs_d.t[b, t * 128:(t + 1) * 128, :], [xs_tok[b][t]], [x_])
                        for hf in range(2):
                            p_ = pO[io % 2]
                            tm_ = tmo[io % 2]
                            io += 1
                            cs_ = slice(hf * 512, (hf + 1) * 512)
                            for kc in range(8):
                                MM(p_[:], accT[:, kc, t * 128:(t + 1) * 128], WB0[:, kc, cs_], kc == 0, kc == 7, [accT, WB0], [p_])
                            TT(tm_[:], p_[:], gate_bc[v][:, cs_], ALU.mult, [p_, gate_bc[v]], [tm_])
                            TT(o_[:, cs_], tm_[:], x_[:, cs_], ALU.add, [tm_, x_], [o_], eng="pool")
                        if last:
                            DMA("sp", out_d.t[b, t * 128:(t + 1) * 128, :], o_[:], [o_], [xs_tok[b][t]])
                        else:
                            DMA("sp", xs_d.t[b, t * 128:(t + 1) * 128, :], o_[:], [o_], [xs_tok[b][t]])
                    k.barrier()
                k.cur = st
        k.emit()
    return nc


def _consts():
    bf = ml_dtypes.bfloat16
    c = {}
    c["identb"] = np.eye(128, dtype=np.float32).astype(bf)
    c["identf"] = np.eye(128, dtype=np.float32)
    s = np.arange(64)[:, None]
    t = np.arange(64)[None, :]
    mf = (s <= t).astype(np.float32)
    mb = (s >= t).astype(np.float32)
    cm = np.stack([np.stack([mf, mf], 0), np.stack([mb, mb], 0)], 0)
    c["cmask"] = np.ascontiguousarray(cm.transpose(2, 0, 1, 3).reshape(64, 256))
    n = np.arange(LAT)
    row = (n // 64).astype(np.float32)
    col = (n % 64).astype(np.float32)
    inv = (10000.0 ** (-np.arange(0, 16, 2, dtype=np.float32) / 16)).astype(np.float32)
    ang = np.concatenate([row[:, None] * inv, col[:, None] * inv], -1).astype(np.float32)
    rp = np.concatenate([np.cos(ang), np.sin(ang)], -1).astype(np.float32)
    c["rope"] = np.ascontiguousarray(rp.reshape(16, 128, 32).transpose(1, 0, 2))
    e = np.arange(64)
    a64 = 2 * np.pi * np.outer(e, e) / 64
    C64 = np.cos(a64) / 8.0
    S64 = np.sin(a64) / 8.0
    cs = np.zeros((128, 256), np.float64)
    for g in range(2):
        cs[g * 64:(g + 1) * 64, g * 64:(g + 1) * 64] = C64
        cs[g * 64:(g + 1) * 64, 128 + g * 64:128 + (g + 1) * 64] = S64
    c["cs64"] = cs.astype(np.float32).astype(bf)

    def dft(N):
        tt = np.arange(N)
        a = 2 * np.pi * ((np.outer(tt, tt)) % N) / N
        Cn = np.cos(a) / np.sqrt(N)
        Sn = -np.sin(a) / np.sqrt(N)
        tab = np.stack([Cn, Sn], 1)
        return np.ascontiguousarray(tab.reshape(N // 128, 128, 2, N).transpose(1, 0, 2, 3)).astype(np.float32).astype(bf)

    c["dftL"] = dft(LAT)
    c["dftC"] = dft(CTX)
    return c


def _nbias_index():
    rows, W, wh, ww = 32, 64, 8, 16
    idx = np.full((5, 5, 128, 128, 2), -1, np.int64)
    for typ, qt in enumerate([0, 1, 5, 14, 15]):
        cs = min(max(qt - 2, 0), 11)
        for i in range(5):
            kc = cs + i
            for kk in range(128):
                kr, kcol = 2 * kc + kk // 64, kk % 64
                for q in range(128):
                    r, cq = 2 * qt + q // 64, q % 64
                    r0 = min(max(r - wh // 2, 0), rows - wh)
                    c0 = min(max(cq - ww // 2, 0), W - ww)
                    if r0 <= kr < r0 + wh and c0 <= kcol < c0 + ww:
                        idx[typ, i, kk, q, 0] = kr - r + wh - 1
                        idx[typ, i, kk, q, 1] = min(max(kcol - cq, 1 - ww), ww - 1) + ww - 1
    return idx


_CACHE = {}


def kernel(x, c, ctx, c_ctx, norm_gain, w_mod, b_mod, w_in, da_qk_gain, da_lambda, da_subln_gain,
           na_qk_gain, na_rpb, hg_lb_logits, hg_norm_gain, w_up, w_merge, w_out):
    f = lambda a: np.ascontiguousarray(np.asarray(a, dtype=np.float32))
    x, c, ctx, c_ctx = f(x), f(c), f(ctx), f(c_ctx)
    if "nc" not in _CACHE:
        _CACHE["nc"] = build()
        _CACHE["consts"] = _consts()
        _CACHE["nbidx"] = _nbias_index()
    nc = _CACHE["nc"]
    shared = dict(_CACHE["consts"])
    shared["w_mod"] = f(w_mod)
    shared["w_in"] = f(w_in)
    shared["w_up"] = f(w_up)
    shared["w_merge"] = f(w_merge)
    shared["w_out"] = f(w_out)
    shared["b_modT"] = np.ascontiguousarray(f(b_mod).reshape(NL, 24, 128).transpose(2, 0, 1))
    shared["gainT"] = np.ascontiguousarray(f(norm_gain).reshape(NL, 8, 128).transpose(2, 0, 1))
    smallv = np.concatenate([f(da_qk_gain).reshape(NL, 64), f(da_lambda).reshape(NL, 128), f(da_subln_gain).reshape(NL, 64),
                             f(na_qk_gain).reshape(NL, 128), f(hg_norm_gain).reshape(NL, 64)], -1)
    shared["smallv"] = np.ascontiguousarray(np.broadcast_to(smallv[None], (128, NL, 448)))
    shared["lbT"] = np.ascontiguousarray(f(hg_lb_logits).reshape(2, NL, 2, 128).transpose(3, 0, 2, 1))
    idx = _CACHE["nbidx"]
    rpb = f(na_rpb)
    inw = idx[..., 0] >= 0
    dr = np.where(inw, idx[..., 0], 0)
    dc = np.where(inw, idx[..., 1], 0)
    gath = rpb[:, :, dr, dc]
    gath = np.where(inw[None, None], gath, np.float32(NEG)).astype(np.float32)
    shared["nbias"] = np.ascontiguousarray(gath.transpose(0, 2, 3, 1, 4, 5).reshape(NL, 100, 128, 128))
    in_maps = []
    for i in range(NCORE):
        m = dict(shared)
        m["x"] = np.ascontiguousarray(x[2 * i:2 * i + 2])
        m["ctx"] = np.ascontiguousarray(ctx[2 * i:2 * i + 2])
        cvec = np.stack([c[2 * i], c[2 * i + 1], c_ctx, np.zeros_like(c_ctx)], 0)
        m["cv"] = np.ascontiguousarray(cvec.reshape(4, 8, 128).transpose(2, 1, 0))
        in_maps.append(m)
    res = run_bass_kernel_spmd(nc, in_maps, core_ids=list(range(NCORE)))
    out = np.concatenate([np.asarray(r["out"], dtype=np.float32) for r in res.results], axis=0)
    return out
```
